# Optimizing a Trainium2 kernel written in Bass

```python
import math
import jax
import jax.numpy as jnp
from jax import lax
import numpy as np

D_MODEL = 1024
BATCH = 16
SEQ = 2048
DEPTH = 4

F32 = jnp.float32
MEM_LEN = 256
HEAD_DIM = 64
BLOCK = 128
SWA_HEADS = 8
SWA_KV_HEADS = 2
WINDOW = 128
S5_WIDTH = 512
S5_GROUP = 16
S5_GROUPS = S5_WIDTH // S5_GROUP
S5_STATE = 64
CONV_WIDTH = 512
CONV_K = 31
DIFF_HEADS = 4
DIFF_V_DIM = 2 * HEAD_DIM
CROSS_HEADS = 4
CROSS_HEAD_DIM = 128
CROSS_WIDTH = CROSS_HEADS * CROSS_HEAD_DIM
FFN_DIM = 2816
N_BRANCHES = 4

A_Q = SWA_HEADS * HEAD_DIM
A_KV = SWA_KV_HEADS * HEAD_DIM
D_QK = DIFF_HEADS * 2 * HEAD_DIM
D_V = DIFF_HEADS * DIFF_V_DIM
CONV_IN = 2 * CONV_WIDTH
GATE_COLS = N_BRANCHES * D_MODEL
MIX_IN = A_Q + 2 * A_KV + 2 * D_QK + D_V + S5_WIDTH + CONV_IN + GATE_COLS
MIX_SPLITS = (A_Q, A_Q + A_KV, A_Q + 2 * A_KV, A_Q + 2 * A_KV + D_QK, A_Q + 2 * A_KV + 2 * D_QK, A_Q + 2 * A_KV + 2 * D_QK + D_V, A_Q + 2 * A_KV + 2 * D_QK + D_V + S5_WIDTH, A_Q + 2 * A_KV + 2 * D_QK + D_V + S5_WIDTH + CONV_IN)

DEEPNORM_ALPHA = (2.0 * DEPTH) ** 0.25
DEEPNORM_BETA = (8.0 * DEPTH) ** -0.25
LN_EPS = 1e-5
NEG_INF = -1e30

kernel_name = 'hybrid_gated_swa_s5_conv_diffattn_deepnorm'


def layer_norm(x, g, b):
    xf = x.astype(F32)
    mu = jnp.mean(xf, axis=-1, keepdims=True)
    var = jnp.mean(jnp.square(xf - mu), axis=-1, keepdims=True)
    return ((xf - mu) * lax.rsqrt(var + LN_EPS) * g.astype(F32) + b.astype(F32)).astype(x.dtype)


def alibi_slopes(n):
    return jnp.asarray([2.0 ** (-8.0 * (h + 1) / n) for h in range(n)], F32)


def swiglu_ffn(x, w_in, w_out):
    gate, up = jnp.split(x @ w_in, 2, axis=-1)
    return (jax.nn.silu(gate) * up) @ w_out


def swa_sink_attention(q, k, v, sinks):
    B, L = q.shape[0], q.shape[1]
    nb = L // BLOCK
    rep = SWA_HEADS // SWA_KV_HEADS
    qb = q.reshape(B, nb, BLOCK, SWA_KV_HEADS, rep, HEAD_DIM)

    def band(t):
        tb = t.reshape(B, nb, BLOCK, SWA_KV_HEADS, HEAD_DIM)
        tp = jnp.concatenate([jnp.zeros_like(tb[:, :1]), tb], axis=1)
        return jnp.concatenate([tp[:, :-1], tp[:, 1:]], axis=2)

    kb, vb = band(k), band(v)
    s = jnp.einsum('bnqgrd,bnkgd->bngrqk', qb.astype(F32), kb.astype(F32)) / math.sqrt(HEAD_DIM)
    blk = jnp.arange(nb)[:, None, None] * BLOCK
    qpos = blk + jnp.arange(BLOCK)[None, :, None]
    kpos = blk - BLOCK + jnp.arange(2 * BLOCK)[None, None, :]
    dist = qpos - kpos
    valid = (dist >= 0) & (dist < WINDOW) & (kpos >= 0)
    slopes = alibi_slopes(SWA_HEADS).reshape(SWA_KV_HEADS, rep)[None, None, :, :, None, None]
    s = s - slopes * dist.astype(F32)[None, :, None, None]
    s = jnp.where(valid[None, :, None, None], s, NEG_INF)
    sink = sinks.astype(F32).reshape(SWA_KV_HEADS, rep)[None, None, :, :, None, None]
    m = jnp.maximum(jnp.max(s, axis=-1, keepdims=True), sink)
    p = jnp.exp(s - m)
    p = p / (jnp.sum(p, axis=-1, keepdims=True) + jnp.exp(sink - m))
    o = jnp.einsum('bngrqk,bnkgd->bnqgrd', p.astype(v.dtype), vb)
    return o.reshape(B, L, A_Q)


def s5_ssm(u, a_re, a_im, log_step, b_re, b_im, c_re, c_im, d_skip, glu_w, glu_b):
    B, L = u.shape[0], u.shape[1]
    uf = u.astype(F32).reshape(B, L, S5_GROUPS, S5_GROUP)
    step = jnp.exp(log_step.astype(F32))[:, None]
    ar, ai = a_re.astype(F32), a_im.astype(F32)
    mag = jnp.exp(ar * step)
    abar_r, abar_i = mag * jnp.cos(ai * step), mag * jnp.sin(ai * step)
    den = ar * ar + ai * ai
    nr, ni = abar_r - 1.0, abar_i
    coef_r = (nr * ar + ni * ai) / den
    coef_i = (ni * ar - nr * ai) / den
    br, bi = b_re.astype(F32), b_im.astype(F32)
    bbar_r = coef_r[..., None] * br - coef_i[..., None] * bi
    bbar_i = coef_r[..., None] * bi + coef_i[..., None] * br
    bu_r = jnp.einsum('blgc,gpc->blgp', uf, bbar_r)
    bu_i = jnp.einsum('blgc,gpc->blgp', uf, bbar_i)
    a_r = jnp.broadcast_to(abar_r[None, None], (1, L, S5_GROUPS, S5_STATE))
    a_i = jnp.broadcast_to(abar_i[None, None], (1, L, S5_GROUPS, S5_STATE))

    def combine(e1, e2):
        a1r, a1i, b1r, b1i = e1
        a2r, a2i, b2r, b2i = e2
        return (a1r * a2r - a1i * a2i, a1r * a2i + a1i * a2r,
                a2r * b1r - a2i * b1i + b2r, a2r * b1i + a2i * b1r + b2i)

    _, _, xr, xi = lax.associative_scan(combine, (a_r, a_i, bu_r, bu_i), axis=1)
    y = jnp.einsum('gcp,blgp->blgc', c_re.astype(F32), xr) - jnp.einsum('gcp,blgp->blgc', c_im.astype(F32), xi)
    y = y.reshape(B, L, S5_WIDTH) + d_skip.astype(F32) * uf.reshape(B, L, S5_WIDTH)
    y = jax.nn.gelu(y).astype(u.dtype)
    return y * jax.nn.sigmoid(y @ glu_w + glu_b)


def conformer_conv(h, conv_w, conv_b, ln_g, ln_b):
    val, gate = jnp.split(h, 2, axis=-1)
    g = val * jax.nn.sigmoid(gate)
    y = lax.conv_general_dilated(g, conv_w[:, None, :], window_strides=(1,), padding=[(CONV_K - 1, 0)],
                                 dimension_numbers=('NWC', 'WIO', 'NWC'), feature_group_count=CONV_WIDTH)
    y = layer_norm(y + conv_b, ln_g, ln_b)
    return jax.nn.silu(y)


def diff_attention(q, k, v, lam, norm_g, lambda_init):
    B, L = q.shape[0], q.shape[1]
    nb = L // BLOCK
    kf = k.astype(F32).reshape(B, L, DIFF_HEADS, 2, HEAD_DIM)
    vh = v.reshape(B, L, DIFF_HEADS, DIFF_V_DIM)
    qblocks = jnp.moveaxis(q.reshape(B, nb, BLOCK, DIFF_HEADS, 2, HEAD_DIM), 1, 0)
    slopes = alibi_slopes(DIFF_HEADS)[None, :, None, None, None]
    kpos = jnp.arange(L)

    def one_block(args):
        qb, n = args
        s = jnp.einsum('bqhcd,bkhcd->bhcqk', qb.astype(F32), kf) / math.sqrt(HEAD_DIM)
        qpos = n * BLOCK + jnp.arange(BLOCK)
        dist = qpos[:, None] - kpos[None, :]
        s = jnp.where(dist >= 0, s - slopes * dist.astype(F32), NEG_INF)
        p = jax.nn.softmax(s, axis=-1)
        w = p[:, :, 0] - lam * p[:, :, 1]
        return jnp.einsum('bhqk,bkhe->bqhe', w.astype(v.dtype), vh)

    o = lax.map(one_block, (qblocks, jnp.arange(nb)))
    o = jnp.moveaxis(o, 0, 1).reshape(B, L, DIFF_HEADS, DIFF_V_DIM).astype(F32)
    o = o * lax.rsqrt(jnp.mean(o * o, axis=-1, keepdims=True) + LN_EPS) * norm_g.astype(F32) * (1.0 - lambda_init)
    return o.astype(v.dtype).reshape(B, L, D_V)


def hybrid_mixer(x, w_in, swa_sinks, swa_proj, s5_a_re, s5_a_im, s5_log_step, s5_b_re, s5_b_im, s5_c_re, s5_c_im,
                 s5_d, s5_glu_w, s5_glu_b, s5_proj, conv_w, conv_b, conv_ln_g, conv_ln_b, conv_proj,
                 diff_lq1, diff_lk1, diff_lq2, diff_lk2, diff_norm_g, diff_proj, w_out, lambda_init):
    B, L = x.shape[0], x.shape[1]
    h = x @ w_in
    aq, ak, av, dq, dk, dv, su, cu, gl = jnp.split(h, MIX_SPLITS, axis=-1)
    gates = jax.nn.sigmoid(gl).reshape(B, L, N_BRANCHES, D_MODEL)
    y_a = swa_sink_attention(aq, ak, av, swa_sinks) @ swa_proj
    y_b = s5_ssm(su, s5_a_re, s5_a_im, s5_log_step, s5_b_re, s5_b_im, s5_c_re, s5_c_im, s5_d, s5_glu_w, s5_glu_b) @ s5_proj
    y_c = conformer_conv(cu, conv_w, conv_b, conv_ln_g, conv_ln_b) @ conv_proj
    lam = (jnp.exp(jnp.sum(diff_lq1.astype(F32) * diff_lk1.astype(F32)))
           - jnp.exp(jnp.sum(diff_lq2.astype(F32) * diff_lk2.astype(F32))) + lambda_init)
    y_d = diff_attention(dq, dk, dv, lam, diff_norm_g, lambda_init) @ diff_proj
    merged = gates[:, :, 0] * y_a + gates[:, :, 1] * y_b + gates[:, :, 2] * y_c + gates[:, :, 3] * y_d
    return merged @ w_out


def cross_attention(x, mem_n, wq, wkv, wo):
    B, L = x.shape[0], x.shape[1]
    M = mem_n.shape[1]
    q = (x @ wq).reshape(B, L, CROSS_HEADS, CROSS_HEAD_DIM)
    k, v = jnp.split(mem_n @ wkv, 2, axis=-1)
    k = k.reshape(B, M, CROSS_HEADS, CROSS_HEAD_DIM)
    v = v.reshape(B, M, CROSS_HEADS, CROSS_HEAD_DIM)
    s = jnp.einsum('bqhe,bmhe->bhqm', q.astype(F32), k.astype(F32)) / math.sqrt(CROSS_HEAD_DIM)
    p = jax.nn.softmax(s, axis=-1)
    o = jnp.einsum('bhqm,bmhe->bqhe', p.astype(v.dtype), v).reshape(B, L, CROSS_WIDTH)
    return o @ wo


def setup_inputs(seed: int = 0) -> dict:
    key = jax.random.key(seed)
    keys = jax.random.split(key, 64)
    counter = [0]

    def nk():
        counter[0] += 1
        return keys[counter[0] - 1]

    def nrm(shape, scale):
        return scale * jax.random.normal(nk(), shape, F32)

    def gain(shape):
        return 1.0 + nrm(shape, 0.02)

    Ld, D, F = DEPTH, D_MODEL, FFN_DIM
    G, P, C = S5_GROUPS, S5_STATE, S5_GROUP
    inputs = {}
    inputs['x'] = nrm((BATCH, SEQ, D), 1.0)
    inputs['mem'] = nrm((BATCH, MEM_LEN, D), 1.0)
    inputs['ffn1_w_in'] = nrm((Ld, D, 2 * F), D ** -0.5)
    inputs['ffn1_w_out'] = nrm((Ld, F, D), DEEPNORM_BETA * F ** -0.5)
    inputs['ffn1_ln_g'] = gain((Ld, D))
    inputs['ffn1_ln_b'] = nrm((Ld, D), 0.02)
    inputs['mix_w_in'] = nrm((Ld, D, MIX_IN), D ** -0.5)
    inputs['swa_sinks'] = nrm((Ld, SWA_HEADS), 1.0)
    inputs['swa_proj'] = nrm((Ld, A_Q, D), A_Q ** -0.5)
    inputs['s5_a_re'] = -0.5 + nrm((Ld, G, P), 0.01)
    inputs['s5_a_im'] = math.pi * jnp.arange(P, dtype=F32) + nrm((Ld, G, P), 0.01)
    inputs['s5_log_step'] = jax.random.uniform(nk(), (Ld, G), F32, math.log(1e-3), math.log(1e-1))
    inputs['s5_b_re'] = nrm((Ld, G, P, C), (2.0 * C) ** -0.5)
    inputs['s5_b_im'] = nrm((Ld, G, P, C), (2.0 * C) ** -0.5)
    inputs['s5_c_re'] = nrm((Ld, G, C, P), (2.0 * P) ** -0.5)
    inputs['s5_c_im'] = nrm((Ld, G, C, P), (2.0 * P) ** -0.5)
    inputs['s5_d'] = nrm((Ld, S5_WIDTH), 1.0)
    inputs['s5_glu_w'] = nrm((Ld, S5_WIDTH, S5_WIDTH), S5_WIDTH ** -0.5)
    inputs['s5_glu_b'] = nrm((Ld, S5_WIDTH), 0.02)
    inputs['s5_proj'] = nrm((Ld, S5_WIDTH, D), S5_WIDTH ** -0.5)
    inputs['conv_w'] = nrm((Ld, CONV_K, CONV_WIDTH), CONV_K ** -0.5)
    inputs['conv_b'] = nrm((Ld, CONV_WIDTH), 0.02)
    inputs['conv_ln_g'] = gain((Ld, CONV_WIDTH))
    inputs['conv_ln_b'] = nrm((Ld, CONV_WIDTH), 0.02)
    inputs['conv_proj'] = nrm((Ld, CONV_WIDTH, D), CONV_WIDTH ** -0.5)
    inputs['diff_lq1'] = nrm((Ld, HEAD_DIM), 0.1)
    inputs['diff_lk1'] = nrm((Ld, HEAD_DIM), 0.1)
    inputs['diff_lq2'] = nrm((Ld, HEAD_DIM), 0.1)
    inputs['diff_lk2'] = nrm((Ld, HEAD_DIM), 0.1)
    inputs['diff_norm_g'] = gain((Ld, DIFF_V_DIM))
    inputs['diff_proj'] = nrm((Ld, D_V, D), D_V ** -0.5)
    inputs['mix_w_out'] = nrm((Ld, D, D), DEEPNORM_BETA * D ** -0.5)
    inputs['mix_ln_g'] = gain((Ld, D))
    inputs['mix_ln_b'] = nrm((Ld, D), 0.02)
    inputs['mem_ln_g'] = gain((D,))
    inputs['mem_ln_b'] = nrm((D,), 0.02)
    inputs['cross_wq'] = nrm((Ld, D, CROSS_WIDTH), D ** -0.5)
    inputs['cross_wkv'] = nrm((Ld, D, 2 * CROSS_WIDTH), D ** -0.5)
    inputs['cross_wo'] = nrm((Ld, CROSS_WIDTH, D), DEEPNORM_BETA * CROSS_WIDTH ** -0.5)
    inputs['cross_ln_g'] = gain((Ld, D))
    inputs['cross_ln_b'] = nrm((Ld, D), 0.02)
    inputs['ffn2_w_in'] = nrm((Ld, D, 2 * F), D ** -0.5)
    inputs['ffn2_w_out'] = nrm((Ld, F, D), DEEPNORM_BETA * F ** -0.5)
    inputs['ffn2_ln_g'] = gain((Ld, D))
    inputs['ffn2_ln_b'] = nrm((Ld, D), 0.02)
    return inputs


def reference(x, mem, ffn1_w_in, ffn1_w_out, ffn1_ln_g, ffn1_ln_b, mix_w_in, swa_sinks, swa_proj,
              s5_a_re, s5_a_im, s5_log_step, s5_b_re, s5_b_im, s5_c_re, s5_c_im, s5_d, s5_glu_w, s5_glu_b, s5_proj,
              conv_w, conv_b, conv_ln_g, conv_ln_b, conv_proj,
              diff_lq1, diff_lk1, diff_lq2, diff_lk2, diff_norm_g, diff_proj,
              mix_w_out, mix_ln_g, mix_ln_b, mem_ln_g, mem_ln_b,
              cross_wq, cross_wkv, cross_wo, cross_ln_g, cross_ln_b,
              ffn2_w_in, ffn2_w_out, ffn2_ln_g, ffn2_ln_b):
    mem_n = layer_norm(mem, mem_ln_g, mem_ln_b)
    for l in range(DEPTH):
        f = swiglu_ffn(x, ffn1_w_in[l], ffn1_w_out[l])
        x = layer_norm(DEEPNORM_ALPHA * x + 0.5 * f, ffn1_ln_g[l], ffn1_ln_b[l])
        lambda_init = 0.8 - 0.6 * math.exp(-0.3 * l)
        m = hybrid_mixer(x, mix_w_in[l], swa_sinks[l], swa_proj[l], s5_a_re[l], s5_a_im[l], s5_log_step[l],
                         s5_b_re[l], s5_b_im[l], s5_c_re[l], s5_c_im[l], s5_d[l], s5_glu_w[l], s5_glu_b[l], s5_proj[l],
                         conv_w[l], conv_b[l], conv_ln_g[l], conv_ln_b[l], conv_proj[l],
                         diff_lq1[l], diff_lk1[l], diff_lq2[l], diff_lk2[l], diff_norm_g[l], diff_proj[l],
                         mix_w_out[l], lambda_init)
        x = layer_norm(DEEPNORM_ALPHA * x + m, mix_ln_g[l], mix_ln_b[l])
        c = cross_attention(x, mem_n, cross_wq[l], cross_wkv[l], cross_wo[l])
        x = layer_norm(DEEPNORM_ALPHA * x + c, cross_ln_g[l], cross_ln_b[l])
        f = swiglu_ffn(x, ffn2_w_in[l], ffn2_w_out[l])
        x = layer_norm(DEEPNORM_ALPHA * x + 0.5 * f, ffn2_ln_g[l], ffn2_ln_b[l])
    return x
```

```python
from contextlib import ExitStack
import math
import numpy as np
import concourse.bass as bass
import concourse.mybir as mybir
from concourse.bass_utils import run_bass_kernel_spmd

F32 = mybir.dt.float32
BF16 = mybir.dt.bfloat16
AF = mybir.ActivationFunctionType
ALU = mybir.AluOpType

D = 1024
L = 2048
NT = L // 128
DEPTH = 4
FFN = 2816
NJ = FFN // 128
MIX_IN = 7936
ALPHA = (2.0 * DEPTH) ** 0.25
EPS = 1e-5
NCORES = 8
SEQ_PER_CORE = 2

ENGS = ["pe", "act", "dve", "pool", "sp"]
NSEM = 4
PHASE = 2048
NDSEM = 40


class Buf:
    __slots__ = ("w", "r", "name")

    def __init__(self, name=""):
        self.w = None
        self.r = {}
        self.name = name


class _Rec:
    def __init__(self):
        self.call = None

    def __getattr__(self, name):
        def f(*a, **k):
            self.call = (name, a, k)
            return self
        return f


class Sched:
    def __init__(self, nc, es):
        self.nc = nc
        self.h = dict(pe=nc.tensor, act=nc.scalar, dve=nc.vector, pool=nc.gpsimd, sp=nc.sync)
        self.q = {e: [] for e in ENGS}
        self.cnt = {e: 0 for e in ENGS}
        self.semh = {}
        self.esval = {}
        for e in ENGS:
            for i in range(NSEM):
                self.semh[("e", e, i)] = es.enter_context(nc.semaphore(f"s_{e}{i}"))
                self.esval[("e", e, i)] = 0
        self.dval = [0] * NDSEM
        for i in range(NDSEM):
            self.semh[("d", i)] = es.enter_context(nc.semaphore(f"d{i}"))
        self.dnext = 0
        self.seen = {e: {} for e in ENGS}
        self.last = {}
        self.nwait = 0

    def _wait(self, eng, k, v):
        if self.seen[eng].get(k, 0) >= v:
            return
        self.seen[eng][k] = v
        sem = self.semh[k]
        self.q[eng].append(lambda h, sem=sem, v=v: h.wait_ge(sem, v))
        self.nwait += 1

    def op(self, eng, fn, reads=(), writes=(), dma=False):
        rec = _Rec()
        fn(rec)
        _name, _a, _k = rec.call

        def fn(h, _name=_name, _a=_a, _k=_k):
            return getattr(h, _name)(*_a, **_k)
        deps = {}

        def add(k, v):
            if deps.get(k, 0) < v:
                deps[k] = v

        for b in reads:
            if b.w is not None:
                add(*b.w)
        for b in writes:
            if b.w is not None:
                add(*b.w)
            for k, v in b.r.items():
                if k[0] == "e" and k[1] == eng:
                    continue
                add(k, v)
        for k, v in deps.items():
            if eng == "pe" and k[0] == "e" and k[1] == "pe":
                continue
            self._wait(eng, k, v)
        if dma:
            i = self.dnext
            self.dnext = (i + 1) % NDSEM
            k = ("d", i)
            prev = self.dval[i]
            if prev:
                self._wait(eng, k, prev)
            self.dval[i] = prev + 16
            tok = (k, prev + 16)
            sem = self.semh[k]
            self.q[eng].append(lambda h, fn=fn, sem=sem: fn(h).then_inc(sem, 16))
        else:
            c = self.cnt[eng]
            self.cnt[eng] = c + 1
            k = ("e", eng, (c // PHASE) % NSEM)
            self.esval[k] += 1
            tok = (k, self.esval[k])
            sem = self.semh[k]
            self.q[eng].append(lambda h, fn=fn, sem=sem: fn(h).then_inc(sem, 1))
        self.last[tok[0]] = tok[1]
        for b in writes:
            b.w = tok
            b.r = {}
        for b in reads:
            if b.r.get(tok[0], 0) < tok[1]:
                b.r[tok[0]] = tok[1]
        return tok

    def barrier(self):
        for e in ENGS:
            for k, v in self.last.items():
                if k[0] == "e" and k[1] == e:
                    continue
                self._wait(e, k, v)

    def finish(self, eng="sp"):
        for k, v in self.last.items():
            self._wait(eng, k, v)

    def replay(self, block):
        q = self.q

        @block.tensor
        def _(h):
            for f in q["pe"]:
                f(h)

        @block.scalar
        def _(h):
            for f in q["act"]:
                f(h)

        @block.vector
        def _(h):
            for f in q["dve"]:
                f(h)

        @block.gpsimd
        def _(h):
            for f in q["pool"]:
                f(h)

        @block.sync
        def _(h):
            for f in q["sp"]:
                f(h)


class RR:
    def __init__(self, aps, name="p"):
        self.slots = [a if isinstance(a, tuple) else (a, Buf(f"{name}{i}")) for i, a in enumerate(aps)]
        self.i = 0

    def get(self):
        s = self.slots[self.i]
        self.i = (self.i + 1) % len(self.slots)
        return s


class Arena:
    def __init__(self, t_bf):
        self.tb = t_bf
        self.tf = t_bf.bitcast(F32)
        self.off = 0
        self.cap = t_bf.shape[1] * 2

    def mark(self):
        return self.off

    def reset(self, m):
        self.off = m

    def alloc(self, shape, dt):
        esz = 4 if dt == F32 else 2
        n = int(np.prod(shape))
        self.off = (self.off + 63) // 64 * 64
        o = self.off
        self.off += n * esz
        assert self.off <= self.cap, (self.off, self.cap)
        t = self.tf if dt == F32 else self.tb
        ap = t[:, o // esz: o // esz + n]
        if len(shape) == 2:
            ap = ap.rearrange("p (a b) -> p a b", a=shape[0])
        elif len(shape) == 3:
            ap = ap.rearrange("p (a b c) -> p a b c", a=shape[0], b=shape[1])
        return ap


class K:
    pass


SHAPES = {
    "x": [SEQ_PER_CORE, L, D], "mem": [SEQ_PER_CORE, 256, D],
    "ffn1_w_in": [DEPTH, D, 2 * FFN], "ffn1_w_out": [DEPTH, FFN, D], "ffn1_ln_g": [DEPTH, D], "ffn1_ln_b": [DEPTH, D],
    "mix_w_in": [DEPTH, D, MIX_IN], "swa_sinks": [DEPTH, 8], "swa_proj": [DEPTH, 512, D],
    "s5_a_re": [DEPTH, 32, 64], "s5_a_im": [DEPTH, 32, 64], "s5_log_step": [DEPTH, 32],
    "s5_b_re": [DEPTH, 32, 64, 16], "s5_b_im": [DEPTH, 32, 64, 16], "s5_c_re": [DEPTH, 32, 16, 64],
    "s5_c_im": [DEPTH, 32, 16, 64], "s5_d": [DEPTH, 512], "s5_glu_w": [DEPTH, 512, 512], "s5_glu_b": [DEPTH, 512],
    "s5_proj": [DEPTH, 512, D], "conv_w": [DEPTH, 31, 512], "conv_b": [DEPTH, 512], "conv_ln_g": [DEPTH, 512],
    "conv_ln_b": [DEPTH, 512], "conv_proj": [DEPTH, 512, D], "diff_lq1": [DEPTH, 64], "diff_lk1": [DEPTH, 64],
    "diff_lq2": [DEPTH, 64], "diff_lk2": [DEPTH, 64], "diff_norm_g": [DEPTH, 128], "diff_proj": [DEPTH, 512, D],
    "mix_w_out": [DEPTH, D, D], "mix_ln_g": [DEPTH, D], "mix_ln_b": [DEPTH, D], "mem_ln_g": [D], "mem_ln_b": [D],
    "cross_wq": [DEPTH, D, 512], "cross_wkv": [DEPTH, D, D], "cross_wo": [DEPTH, 512, D],
    "cross_ln_g": [DEPTH, D], "cross_ln_b": [DEPTH, D],
    "ffn2_w_in": [DEPTH, D, 2 * FFN], "ffn2_w_out": [DEPTH, FFN, D], "ffn2_ln_g": [DEPTH, D], "ffn2_ln_b": [DEPTH, D],
}
NEG = -240000.0
TWO_PI = 2.0 * math.pi
CW1 = 6.28125
CW2 = TWO_PI - CW1
MAGIC = 12582912.0
DIFF_SLOPES = [2.0 ** (-8.0 * (h + 1) / 4) for h in range(4)]
SWA_SLOPES = [2.0 ** (-8.0 * (h + 1) / 8) for h in range(8)]
CONSTS = {}
DEBUG = False
LAST = {}


def consts():
    if not CONSTS:
        import ml_dtypes
        bf = ml_dtypes.bfloat16
        CONSTS["c_ident"] = (np.eye(128, dtype=np.float32), F32)
        CONSTS["c_iota"] = (np.tile(np.arange(512, dtype=np.float32)[None, :], (128, 1)), F32)
        pos = np.arange(L)
        pa, pb = (pos // 128).astype(np.float32), (pos % 128).astype(np.float32)
        kb = np.stack([np.ones(L), np.ones(L), 128.0 * pa, pb]).astype(np.float32)
        CONSTS["c_kb"] = (kb.astype(bf), BF16)

        def qb(slopes):
            o = np.zeros((4, len(slopes), L), np.float32)
            for h, s in enumerate(slopes):
                o[0, h] = -8.0 * s * 128.0 * pa
                o[1, h] = -8.0 * s * pb
                o[2, h] = 8.0 * s
                o[3, h] = 8.0 * s
            return o.astype(bf)
        CONSTS["c_qb_diff"] = (qb(DIFF_SLOPES), BF16)
        CONSTS["c_qb_swa"] = (qb(SWA_SLOPES), BF16)
        ki = np.arange(128)[:, None]
        md = np.zeros((128, 4, 512), np.float32)
        qi = np.arange(512)[None, :]
        for rel in range(4):
            md[:, rel, :] = np.where(qi >= 128 * rel + ki, 0.0, NEG)
        CONSTS["c_mask_diff"] = (md.astype(bf), BF16)
        ms = np.zeros((128, 2, 128), np.float32)
        q1 = np.arange(128)[None, :]
        ms[:, 0, :] = np.where(ki > q1, 0.0, NEG)
        ms[:, 1, :] = np.where(ki <= q1, 0.0, NEG)
        CONSTS["c_mask_swa"] = (ms.astype(bf), BF16)
    return CONSTS


def build(layers, parts, branches="abcd", nseq=SEQ_PER_CORE):
    nc = bass.Bass("TRN2", target_bir_lowering=False)
    es = ExitStack()
    with es:
        g = K()
        g.nc = nc
        S = Sched(nc, es)
        g.S = S
        g.dr = {}

        def W(name):
            if name not in g.dr:
                if name.startswith("c_"):
                    arr, dt = consts()[name]
                    g.dr[name] = nc.dram_tensor(name, list(arr.shape), dt, kind="ExternalInput").ap()
                else:
                    g.dr[name] = nc.dram_tensor(name, list(SHAPES[name]), F32, kind="ExternalInput").ap()
            return g.dr[name]
        g.W = W
        g.dbg = {}

        def dbg_out(name, ap, bufs):
            if not DEBUG or name in g.dbg:
                return
            t = nc.dram_tensor("dbg_" + name, list(ap.shape), ap.dtype, kind="ExternalOutput").ap()
            g.dbg[name] = t
            S.op("sp", lambda h: h.dma_start(out=t, in_=ap), reads=bufs, dma=True)
        g.dbg_out = dbg_out
        out = nc.dram_tensor("out", [nseq, L, D], F32, kind="ExternalOutput").ap()

        arena_t = es.enter_context(nc.sbuf_tensor("arena", [128, 106400], BF16))
        A = Arena(arena_t)
        g.A = A
        g.resid = A.alloc([NT, D], F32)
        g.resid_b = [Buf(f"resid{t}") for t in range(NT)]
        g.ident = A.alloc([128], F32)
        g.identb = A.alloc([128], BF16)
        g.ones = A.alloc([128], BF16)
        g.memT = A.alloc([8, 256], BF16)
        g.memT_b = [Buf() for _ in range(8)]
        g.eps_ap = A.alloc([1], F32)
        g.eps4_ap = A.alloc([1], F32)
        g.cb = Buf("consts")
        S.op("sp", lambda h: h.dma_start(out=g.ident, in_=W("c_ident")), writes=[g.cb], dma=True)
        S.op("dve", lambda h: h.tensor_copy(out=g.identb, in_=g.ident), reads=[g.cb], writes=[g.cb])
        S.op("dve", lambda h: h.memset(g.ones, 1.0), writes=[g.cb])
        S.op("dve", lambda h: h.memset(g.eps_ap, EPS), writes=[g.cb])
        S.op("dve", lambda h: h.memset(g.eps4_ap, 4.0 * EPS), writes=[g.cb])
        g.eps_b = g.cb
        g.ident_b = g.cb
        g.banks = [(es.enter_context(nc.psum_tensor(f"ps{i}", [128, 512], F32))[:], Buf(f"ps{i}")) for i in range(8)]
        g.ps = RR(g.banks)
        pmark = A.mark()

        for s in range(nseq):
            for t in range(NT):
                S.op("sp", lambda h, t=t, s=s: h.dma_start(out=g.resid[:, t, :], in_=W("x")[s, t * 128:(t + 1) * 128, :]),
                     writes=[g.resid_b[t]], dma=True)
            if "cross" in parts:
                A.reset(pmark)
                S.barrier()
                prep_mem(g, s)
            for l in layers:
                for p in parts:
                    A.reset(pmark)
                    S.barrier()
                    if p in ("ffn1", "ffn2"):
                        ffn(g, l, p)
                    elif p == "mix":
                        mixer(g, l, branches)
                    elif p == "cross":
                        cross(g, l)
            for t in range(NT):
                S.op("sp", lambda h, t=t, s=s: h.dma_start(out=out[s, t * 128:(t + 1) * 128, :], in_=g.resid[:, t, :]),
                     reads=[g.resid_b[t]], dma=True)
        S.finish("sp")
        with nc.Block() as block:
            S.replay(block)
        g.stats = dict(cnt=dict(S.cnt), nwait=S.nwait)
    return nc, g


def load_w(g, dst, dst_b, src, eng="pool"):
    g.S.op(eng, lambda h: h.dma_start(out=dst, in_=src), writes=[dst_b], dma=True)


def mm(g, out, lhsT, rhs, start, stop, reads, writes):
    g.S.op("pe", lambda h: h.matmul(out, lhsT=lhsT, rhs=rhs, start=start, stop=stop), reads=reads, writes=writes)


def evac(g, i, out, in_, reads, writes, scale=None):
    if i % 2 == 0:
        if scale is None:
            g.S.op("act", lambda h: h.activation(out=out, in_=in_, func=AF.Copy), reads=reads, writes=writes)
        else:
            g.S.op("act", lambda h: h.activation(out=out, in_=in_, func=AF.Copy, scale=scale), reads=reads, writes=writes)
    else:
        if scale is None:
            g.S.op("dve", lambda h: h.tensor_copy(out=out, in_=in_), reads=reads, writes=writes)
        else:
            g.S.op("dve", lambda h: h.tensor_scalar(out=out, in0=in_, scalar1=scale, scalar2=None, op0=ALU.mult),
                   reads=reads, writes=writes)


def gen_xT(g, xT, xT_b, srcs):
    S = g.S
    n = len(srcs)
    for k in range(8):
        ps, psb = g.ps.get()
        for t, (src, sb) in enumerate(srcs):
            S.op("pe", lambda h, ps=ps, t=t, k=k, src=src: h.transpose(
                out=ps[:, t * 128:(t + 1) * 128], in_=src[:, k * 128:(k + 1) * 128], identity=g.ident),
                reads=[sb, g.ident_b], writes=[psb])
        evac(g, k, xT[:, k, 0:128 * n], ps[:, 0:128 * n], [psb], [xT_b[k]])


def resid_tiles(g, t0, n=4):
    return [(g.resid[:, t0 + i, :], g.resid_b[t0 + i]) for i in range(n)]


def ln_alloc(g, n=3):
    A = g.A
    sets = RR([((A.alloc([12], F32), A.alloc([2], F32), A.alloc([1], F32), A.alloc([1], F32)), (Buf("ln1"), Buf("ln2")))
               for _ in range(n)])
    return sets, None


def load_ln(g, gname, bname, l):
    A, S, W = g.A, g.S, g.W
    lng = A.alloc([D], F32)
    lnb = A.alloc([D], F32)
    b = Buf("lnp")
    gs = W(gname)[l:l + 1, :] if l is not None else W(gname).rearrange("(o d) -> o d", o=1)
    bs = W(bname)[l:l + 1, :] if l is not None else W(bname).rearrange("(o d) -> o d", o=1)
    S.op("sp", lambda h: h.dma_start(out=lng, in_=gs.to_broadcast([128, D])), writes=[b], dma=True)
    S.op("sp", lambda h: h.dma_start(out=lnb, in_=bs.to_broadcast([128, D])), writes=[b], dma=True)
    return lng, lnb, b


def ln_tile(g, x, xb, lnp, tmp, tmp_b, eps=None):
    S = g.S
    lng, lnb, lnp_b = lnp
    (st, mv, sd, rs), (b1, b2) = tmp.get()
    for hh in range(2):
        S.op("dve", lambda h, hh=hh: h.bn_stats(out=st[:, hh * 6:(hh + 1) * 6], in_=x[:, hh * 512:(hh + 1) * 512]),
             reads=[xb], writes=[b1])
    S.op("dve", lambda h: h.bn_aggr(out=mv, in_=st), reads=[b1], writes=[b1])
    eps_ap = g.eps_ap if eps is None else eps
    S.op("act", lambda h: h.activation(out=sd, in_=mv[:, 1:2], func=AF.Sqrt, bias=eps_ap, scale=1.0),
         reads=[b1, g.eps_b], writes=[b2])
    S.op("dve", lambda h: h.scalar_tensor_tensor(out=x, in0=x, scalar=mv[:, 0:1], in1=lng, op0=ALU.subtract, op1=ALU.mult),
         reads=[xb, b1, lnp_b], writes=[xb])
    S.op("dve", lambda h: h.reciprocal(out=rs, in_=sd), reads=[b2], writes=[b2])
    S.op("dve", lambda h: h.scalar_tensor_tensor(out=x, in0=x, scalar=rs, in1=lnb, op0=ALU.mult, op1=ALU.add),
         reads=[xb, b2, lnp_b], writes=[xb])


def prep_mem(g, s):
    S, A, W = g.S, g.A, g.W
    mt = A.alloc([2, D], F32)
    mb = [Buf(), Buf()]
    lnp = load_ln(g, "mem_ln_g", "mem_ln_b", None)
    tmp, tmp_b = ln_alloc(g)
    for i in range(2):
        S.op("sp", lambda h, i=i: h.dma_start(out=mt[:, i, :], in_=W("mem")[s, i * 128:(i + 1) * 128, :]),
             writes=[mb[i]], dma=True)
        ln_tile(g, mt[:, i, :], mb[i], lnp, tmp, tmp_b)
    gen_xT(g, g.memT, g.memT_b, [(mt[:, i, :], mb[i]) for i in range(2)])


def ffn(g, l, which):
    S, A, W = g.S, g.A, g.W
    w_in = W(which + "_w_in")
    w_out = W(which + "_w_out")
    xT = A.alloc([8, 1024], BF16)
    xT_b = [[Buf() for _ in range(8)] for _ in range(2)]
    aT = A.alloc([NJ, 1024], BF16)
    aT_b = [[Buf() for _ in range(2)] for _ in range(NJ)]
    wo = RR([(A.alloc([NJ, 512], BF16), Buf()) for _ in range(2)])
    wi = RR([(A.alloc([8, 2, 256], BF16), Buf()) for _ in range(2)])
    lnp = load_ln(g, which + "_ln_g", which + "_ln_b", l)
    sg = RR([(A.alloc([512], F32), Buf()) for _ in range(2)])
    tmp, tmp_b = ln_alloc(g)
    pre = []

    def load_jp(jp):
        w, wb = wi.get()
        for gu in range(2):
            c0 = gu * FFN + jp * 256
            load_w(g, w[:, :, gu, :], wb, w_in[l, :, c0:c0 + 256].rearrange("(k p) n -> p k n", p=128))
        return w, wb
    for c in range(2):
        for sub in range(2):
            gen_xT(g, xT[:, :, sub * 512:(sub + 1) * 512], xT_b[sub], resid_tiles(g, c * 8 + sub * 4))
        for jp in range(NJ // 2):
            w, wb = pre.pop(0) if pre else load_jp(jp)
            for jj in range(2):
                j = jp * 2 + jj
                for sub in range(2):
                    pg, pgb = g.ps.get()
                    pu, pub = g.ps.get()
                    for gu, (pp, ppb) in enumerate(((pg, pgb), (pu, pub))):
                        for k in range(8):
                            mm(g, pp, w[:, k, gu, jj * 128:(jj + 1) * 128], xT[:, k, sub * 512:(sub + 1) * 512],
                               k == 0, k == 7, [wb, xT_b[sub][k]], [ppb])
                    sgt, sgb = sg.get()
                    S.op("act", lambda h, sgt=sgt, pg=pg: h.activation(out=sgt, in_=pg, func=AF.Silu),
                         reads=[pgb], writes=[sgb])
                    S.op("dve", lambda h, sgt=sgt, pu=pu, j=j, sub=sub: h.tensor_tensor(
                        out=aT[:, j, sub * 512:(sub + 1) * 512], in0=pu, in1=sgt, op=ALU.mult),
                        reads=[pub, sgb], writes=[aT_b[j][sub]])
        wos = []
        for q in range(2):
            w, wb = wo.get()
            load_w(g, w, wb, w_out[l, :, q * 512:(q + 1) * 512].rearrange("(j p) n -> p j n", p=128))
            wos.append((w, wb))
        if c == 0:
            pre = [load_jp(0), load_jp(1)]
        for q in range(2):
            w, wb = wos[q]
            for tl in range(8):
                t = c * 8 + tl
                pf, pfb = g.ps.get()
                for j in range(NJ):
                    mm(g, pf, aT[:, j, tl * 128:(tl + 1) * 128], w[:, j, :], j == 0, j == NJ - 1,
                       [wb, aT_b[j][tl // 4]], [pfb])
                xs = g.resid[:, t, q * 512:(q + 1) * 512]
                S.op("dve", lambda h: h.scalar_tensor_tensor(
                    out=xs, in0=xs, scalar=2.0 * ALPHA, in1=pf, op0=ALU.mult, op1=ALU.add),
                    reads=[g.resid_b[t], pfb], writes=[g.resid_b[t]])
        for tl in range(8):
            ln_tile(g, g.resid[:, c * 8 + tl, :], g.resid_b[c * 8 + tl], lnp, tmp, tmp_b, eps=g.eps4_ap)


def cross(g, l):
    S, A, W = g.S, g.A, g.W
    wq = A.alloc([8, 512], BF16)
    wkv = A.alloc([8, 1024], BF16)
    wo = A.alloc([4, 1024], BF16)
    wq_b, wkv_b, wo_b = Buf(), Buf(), Buf()
    load_w(g, wkv, wkv_b, W("cross_wkv")[l].rearrange("(k p) n -> p k n", p=128))
    load_w(g, wq, wq_b, W("cross_wq")[l].rearrange("(k p) n -> p k n", p=128))
    load_w(g, wo, wo_b, W("cross_wo")[l].rearrange("(k p) n -> p k n", p=128))
    KT = A.alloc([4, 256], BF16)
    KT_b = [Buf() for _ in range(4)]
    V = A.alloc([2, 512], BF16)
    V_b = [Buf() for _ in range(2)]
    for hd in range(4):
        ps, psb = g.ps.get()
        for k in range(8):
            mm(g, ps[:, 0:256], wkv[:, k, hd * 128:(hd + 1) * 128], g.memT[:, k, :], k == 0, k == 7,
               [wkv_b, g.memT_b[k]], [psb])
        evac(g, hd, KT[:, hd, :], ps[:, 0:256], [psb], [KT_b[hd]])
    for mt in range(2):
        ps, psb = g.ps.get()
        for k in range(8):
            mm(g, ps, g.memT[:, k, mt * 128:(mt + 1) * 128], wkv[:, k, 512:1024], k == 0, k == 7,
               [wkv_b, g.memT_b[k]], [psb])
        evac(g, mt, V[:, mt, :], ps, [psb], [V_b[mt]])
    xT = A.alloc([8, 512], BF16)
    xT_b = [Buf() for _ in range(8)]
    QT = RR([(A.alloc([512], BF16), Buf()) for _ in range(2)])
    PT = RR([(A.alloc([512], BF16), Buf()) for _ in range(4)])
    RI = RR([(A.alloc([512], F32), Buf()) for _ in range(2)])
    OT = A.alloc([4, 512], BF16)
    OT_b = [Buf() for _ in range(4)]
    lnp = load_ln(g, "cross_ln_g", "cross_ln_b", l)
    tmp, tmp_b = ln_alloc(g)
    sc = 1.0 / math.sqrt(128.0)
    for c in range(4):
        gen_xT(g, xT, xT_b, resid_tiles(g, c * 4))
        for hp in range(2):
            qts, ptss = {}, {}
            for hd in (2 * hp, 2 * hp + 1):
                ps, psb = g.ps.get()
                for k in range(8):
                    mm(g, ps, wq[:, k, hd * 128:(hd + 1) * 128], xT[:, k, :], k == 0, k == 7, [wq_b, xT_b[k]], [psb])
                qt, qtb = QT.get()
                evac(g, hd, qt, ps, [psb], [qtb])
                qts[hd] = (qt, qtb)
            for hd in (2 * hp, 2 * hp + 1):
                qt, qtb = qts[hd]
                pts = []
                for mt in range(2):
                    ps, psb = g.ps.get()
                    mm(g, ps, KT[:, hd, mt * 128:(mt + 1) * 128], qt, True, True, [KT_b[hd], qtb], [psb])
                    pt, ptb = PT.get()
                    S.op("act", lambda h: h.activation(out=pt, in_=ps, func=AF.Exp, scale=sc), reads=[psb], writes=[ptb])
                    pts.append((pt, ptb))
                ptss[hd] = pts
            for hd in (2 * hp, 2 * hp + 1):
                pts = ptss[hd]
                po, pob = g.ps.get()
                pr, prb = g.ps.get()
                for mt in range(2):
                    mm(g, po, V[:, mt, hd * 128:(hd + 1) * 128], pts[mt][0], mt == 0, mt == 1, [V_b[mt], pts[mt][1]], [pob])
                for mt in range(2):
                    mm(g, pr, g.ones, pts[mt][0], mt == 0, mt == 1, [g.cb, pts[mt][1]], [prb])
                ri, rib = RI.get()
                S.op("dve", lambda h: h.reciprocal(out=ri, in_=pr), reads=[prb], writes=[rib])
                S.op("dve", lambda h: h.tensor_tensor(out=OT[:, hd, :], in0=po, in1=ri, op=ALU.mult),
                     reads=[pob, rib], writes=[OT_b[hd]])
        for tl in range(4):
            t = c * 4 + tl
            for half in range(2):
                ps, psb = g.ps.get()
                for hd in range(4):
                    mm(g, ps, OT[:, hd, tl * 128:(tl + 1) * 128], wo[:, hd, half * 512:(half + 1) * 512],
                       hd == 0, hd == 3, [OT_b[hd], wo_b], [psb])
                xs = g.resid[:, t, half * 512:(half + 1) * 512]
                S.op("dve", lambda h, xs=xs, ps=ps: h.scalar_tensor_tensor(
                    out=xs, in0=xs, scalar=ALPHA, in1=ps, op0=ALU.mult, op1=ALU.add),
                    reads=[g.resid_b[t], psb], writes=[g.resid_b[t]])
            ln_tile(g, g.resid[:, t, :], g.resid_b[t], lnp, tmp, tmp_b)


def view3(ap, a):
    return ap.rearrange("p (a b) -> p a b", a=a)


def bc_mid(ap2, n):
    return ap2.unsqueeze(1).to_broadcast([ap2.shape[0], n, ap2.shape[1]])


def bc_last(ap2, n):
    return ap2.unsqueeze(2).to_broadcast([ap2.shape[0], ap2.shape[1], n])


def mixer(g, l, branches):
    S, A, W = g.S, g.A, g.W
    OT = {br: A.alloc([4, L], BF16) for br in "abcd"}
    OT_b = {br: [[Buf() for _ in range(4)] for _ in range(4)] for br in "abcd"}
    m0 = A.mark()
    fns = dict(a=swa_branch, b=s5_branch, c=conv_branch, d=diff_branch)
    for br in "dacb":
        A.reset(m0)
        S.barrier()
        if br in branches:
            fns[br](g, l, OT[br], OT_b[br])
        else:
            allb = [b for row in OT_b[br] for b in row]
            S.op("pool", lambda h, br=br: h.memset(OT[br], 0.0), writes=allb)
    A.reset(m0)
    S.barrier()
    lam_init = None
    mT = A.alloc([8, L], BF16)
    mT_b = [[Buf() for _ in range(4)] for _ in range(8)]
    m2 = A.mark()
    xT = A.alloc([8, 512], BF16)
    xT_b = [Buf() for _ in range(8)]
    wg = RR([(A.alloc([8, 4, 128], BF16), Buf()) for _ in range(2)])
    wp = RR([(A.alloc([4, 4, 128], BF16), Buf()) for _ in range(2)])
    SG = RR([(A.alloc([512], F32), Buf()) for _ in range(2)])
    PRT = [(A.alloc([512], F32), Buf()) for _ in range(3)]
    projs = dict(a="swa_proj", b="s5_proj", c="conv_proj", d="diff_proj")
    for jd in range(8):
        w, wb = wg.get()
        p, pb = wp.get()
        for i, br in enumerate("abcd"):
            c0 = 3840 + i * 1024 + jd * 128
            load_w(g, w[:, :, i, :], wb, W("mix_w_in")[l, :, c0:c0 + 128].rearrange("(k p) n -> p k n", p=128))
            load_w(g, p[:, i, :, :], pb, W(projs[br])[l, :, jd * 128:(jd + 1) * 128].rearrange("(k p) n -> p k n", p=128))
        for c in range(4):
            gen_xT(g, xT, xT_b, resid_tiles(g, c * 4))
            slot = [PRT[0], PRT[1], PRT[2], PRT[1]]
            for i, br in enumerate("abcd"):
                pgt, pgb = g.ps.get()
                for k in range(8):
                    mm(g, pgt, w[:, k, i, :], xT[:, k, :], k == 0, k == 7, [wb, xT_b[k]], [pgb])
                py, pyb = g.ps.get()
                for kc in range(4):
                    mm(g, py, p[:, i, kc, :], OT[br][:, kc, c * 512:(c + 1) * 512], kc == 0, kc == 3,
                       [pb, OT_b[br][kc][c]], [pyb])
                sg, sgb = SG.get()
                S.op("act", lambda h: h.activation(out=sg, in_=pgt, func=AF.Sigmoid), reads=[pgb], writes=[sgb])
                pr, prb = slot[i]
                S.op("dve", lambda h: h.tensor_tensor(out=pr, in0=py, in1=sg, op=ALU.mult), reads=[pyb, sgb], writes=[prb])
                if i == 1:
                    S.op("pool", lambda h: h.tensor_tensor(out=PRT[0][0], in0=PRT[0][0], in1=PRT[1][0], op=ALU.add),
                         reads=[PRT[0][1], PRT[1][1]], writes=[PRT[0][1]])
                if i == 3:
                    S.op("pool", lambda h: h.tensor_tensor(out=PRT[2][0], in0=PRT[2][0], in1=PRT[1][0], op=ALU.add),
                         reads=[PRT[2][1], PRT[1][1]], writes=[PRT[2][1]])
            S.op("dve", lambda h: h.tensor_tensor(out=mT[:, jd, c * 512:(c + 1) * 512], in0=PRT[0][0], in1=PRT[2][0], op=ALU.add),
                 reads=[PRT[0][1], PRT[2][1]], writes=[mT_b[jd][c]])
    A.reset(m2)
    S.barrier()
    wo = A.alloc([8, 1024], BF16)
    wo_b = Buf()
    load_w(g, wo, wo_b, W("mix_w_out")[l].rearrange("(k p) n -> p k n", p=128))
    lnp = load_ln(g, "mix_ln_g", "mix_ln_b", l)
    tmp, tmp_b = ln_alloc(g)
    for t in range(NT):
        for half in range(2):
            ps, psb = g.ps.get()
            for k in range(8):
                mm(g, ps, mT[:, k, t * 128:(t + 1) * 128], wo[:, k, half * 512:(half + 1) * 512], k == 0, k == 7,
                   [mT_b[k][t // 4], wo_b], [psb])
            xs = g.resid[:, t, half * 512:(half + 1) * 512]
            S.op("dve", lambda h, xs=xs, ps=ps: h.scalar_tensor_tensor(
                out=xs, in0=xs, scalar=ALPHA, in1=ps, op0=ALU.mult, op1=ALU.add),
                reads=[g.resid_b[t], psb], writes=[g.resid_b[t]])
        ln_tile(g, g.resid[:, t, :], g.resid_b[t], lnp, tmp, tmp_b)


def diff_branch(g, l, OT, OT_b):
    S, A, W = g.S, g.A, g.W
    lam_init = 0.8 - 0.6 * math.exp(-0.3 * l)
    xT = A.alloc([8, 512], BF16)
    xT_b = [Buf() for _ in range(8)]
    wD = RR([(A.alloc([8, 384], BF16), Buf()) for _ in range(2)])
    QT = A.alloc([2, L], BF16)
    KT = A.alloc([2, L], BF16)
    V = A.alloc([NT, 128], BF16)
    QT_b = [[Buf() for _ in range(4)] for _ in range(2)]
    KT_b = [[Buf() for _ in range(4)] for _ in range(2)]
    V_b = [Buf() for _ in range(NT)]
    qbias_b, kbias_b = Buf(), Buf()
    PT = RR([(A.alloc([512], BF16), Buf()) for _ in range(4)])
    SM = RR([(A.alloc([512], F32), Buf()) for _ in range(2)])
    F = RR([(A.alloc([512], F32), Buf()) for _ in range(6)])
    mask_diff = A.alloc([4, 512], BF16)
    mk_b = Buf()
    S.op("sp", lambda h: h.dma_start(out=mask_diff, in_=W("c_mask_diff")), writes=[mk_b], dma=True)
    lq = A.alloc([4, 64], F32)
    sc = A.alloc([8], F32)
    ng = A.alloc([1], F32)
    sb = Buf()
    for i, nm in enumerate(("diff_lq1", "diff_lk1", "diff_lq2", "diff_lk2")):
        S.op("sp", lambda h, i=i, nm=nm: h.dma_start(out=lq[:, i, :], in_=W(nm)[l:l + 1, :].to_broadcast([128, 64])),
             writes=[sb], dma=True)
    S.op("sp", lambda h: h.dma_start(out=ng, in_=W("diff_norm_g")[l].rearrange("(p o) -> p o", o=1)), writes=[sb], dma=True)
    for i in range(2):
        S.op("dve", lambda h, i=i: h.tensor_tensor(out=lq[:, 2 * i, :], in0=lq[:, 2 * i, :], in1=lq[:, 2 * i + 1, :], op=ALU.mult),
             reads=[sb], writes=[sb])
        S.op("dve", lambda h, i=i: h.tensor_reduce(out=sc[:, i:i + 1], in_=lq[:, 2 * i, :], axis=mybir.AxisListType.X, op=ALU.add),
             reads=[sb], writes=[sb])
        S.op("act", lambda h, i=i: h.activation(out=sc[:, 2 + i:3 + i], in_=sc[:, i:i + 1], func=AF.Exp), reads=[sb], writes=[sb])
    S.op("dve", lambda h: h.tensor_tensor(out=sc[:, 4:5], in0=sc[:, 3:4], in1=sc[:, 2:3], op=ALU.subtract), reads=[sb], writes=[sb])
    S.op("dve", lambda h: h.tensor_scalar(out=sc[:, 5:6], in0=sc[:, 4:5], scalar1=-lam_init, scalar2=None, op0=ALU.add),
         reads=[sb], writes=[sb])
    nlam = sc[:, 5:6]
    for c in range(2):
        S.op("sp", lambda h, c=c: h.dma_start(out=KT[64:68, c, :], in_=W("c_kb")), writes=[kbias_b], dma=True)
    acc = g.banks[0:4]
    rot = RR(g.banks[4:8])
    for hd in range(4):
        w, wb = wD.get()
        for i, c0 in enumerate((768, 1280, 1792)):
            load_w(g, w[:, :, i * 128:(i + 1) * 128],  wb,
                   W("mix_w_in")[l, :, c0 + hd * 128:c0 + (hd + 1) * 128].rearrange("(k p) n -> p k n", p=128))
        for c in range(2):
            S.op("sp", lambda h, c=c, hd=hd: h.dma_start(out=QT[64:68, c, :], in_=W("c_qb_diff")[:, hd, :]),
                 writes=[qbias_b], dma=True)
        for c in range(4):
            gen_xT(g, xT, xT_b, resid_tiles(g, c * 4))
            for qk, (T_, T_b) in enumerate(((QT, QT_b), (KT, KT_b))):
                for comp in range(2):
                    ps, psb = g.ps.get()
                    col = qk * 128 + comp * 64
                    for k in range(8):
                        mm(g, ps[0:64, :], w[:, k, col:col + 64], xT[:, k, :], k == 0, k == 7, [wb, xT_b[k]], [psb])
                    evac(g, comp + qk, T_[0:64, comp, c * 512:(c + 1) * 512], ps[0:64, :], [psb], [T_b[comp][c]])
            for tl in range(4):
                ps, psb = g.ps.get()
                for k in range(8):
                    mm(g, ps[:, 0:128], xT[:, k, tl * 128:(tl + 1) * 128], w[:, k, 256:384], k == 0, k == 7,
                       [wb, xT_b[k]], [psb])
                evac(g, tl, V[:, c * 4 + tl, :], ps[:, 0:128], [psb], [V_b[c * 4 + tl]])
        for qc in range(4):
            nkb = 4 * qc + 4
            for comp in range(2):
                po, pob = acc[2 * comp]
                pr, prb = acc[2 * comp + 1]
                def s_mm(kb):
                    ps, psb = rot.get()
                    mm(g, ps, KT[0:68, comp, kb * 128:(kb + 1) * 128], QT[0:68, comp, qc * 512:(qc + 1) * 512], True, True,
                       [KT_b[comp][kb // 4], QT_b[comp][qc], qbias_b, kbias_b], [psb])
                    return ps, psb
                nxt = [s_mm(0)] + ([s_mm(1)] if nkb > 1 else [])
                for kb in range(nkb):
                    ps, psb = nxt.pop(0)
                    if kb + 2 < nkb:
                        nxt.append(s_mm(kb + 2))
                    pt, ptb = PT.get()
                    if kb >= 4 * qc:
                        sm, smb = SM.get()
                        S.op("dve", lambda h: h.tensor_tensor(
                            out=sm, in0=ps, in1=mask_diff[:, kb - 4 * qc, :], op=ALU.add), reads=[psb, mk_b], writes=[smb])
                        S.op("act", lambda h: h.activation(out=pt, in_=sm, func=AF.Exp, scale=0.125),
                             reads=[smb], writes=[ptb])
                    else:
                        S.op("act", lambda h: h.activation(out=pt, in_=ps, func=AF.Exp, scale=0.125),
                             reads=[psb], writes=[ptb])
                    mm(g, po, V[:, kb, :], pt, kb == 0, kb == nkb - 1, [V_b[kb], ptb], [pob])
                    mm(g, pr, g.ones, pt, kb == 0, kb == nkb - 1, [g.cb, ptb], [prb])
            ts = []
            for comp in range(2):
                po, pob = acc[2 * comp]
                pr, prb = acc[2 * comp + 1]
                ri, rib = F.get()
                S.op("dve", lambda h, ri=ri, pr=pr: h.reciprocal(out=ri, in_=pr), reads=[prb], writes=[rib])
                t_, tb_ = F.get()
                S.op("dve", lambda h, t_=t_, po=po, ri=ri: h.tensor_tensor(out=t_, in0=po, in1=ri, op=ALU.mult),
                     reads=[pob, rib], writes=[tb_])
                ts.append((t_, tb_))
            o, ob = F.get()
            S.op("dve", lambda h, o=o, t0=ts[0][0], t1=ts[1][0]: h.scalar_tensor_tensor(
                out=o, in0=t1, scalar=nlam, in1=t0, op0=ALU.mult, op1=ALU.add),
                reads=[ts[0][1], ts[1][1], sb], writes=[ob])
            sq, sqb = PT.get()
            S.op("act", lambda h, sq=sq, o=o: h.activation(out=sq, in_=o, func=AF.Square), reads=[ob], writes=[sqb])
            pm, pmb = rot.get()
            mm(g, pm, g.ones, sq, True, True, [g.cb, sqb], [pmb])
            sd, sdb = F.get()
            S.op("act", lambda h, sd=sd, pm=pm: h.activation(out=sd, in_=pm, func=AF.Sqrt, bias=g.eps_ap, scale=1.0 / 128.0),
                 reads=[pmb, g.cb], writes=[sdb])
            S.op("dve", lambda h, sd=sd: h.reciprocal(out=sd, in_=sd), reads=[sdb], writes=[sdb])
            S.op("dve", lambda h, o=o, sd=sd: h.tensor_tensor(out=o, in0=o, in1=sd, op=ALU.mult), reads=[ob, sdb], writes=[ob])
            S.op("dve", lambda h, o=o, hd=hd, qc=qc: h.tensor_scalar(
                out=OT[:, hd, qc * 512:(qc + 1) * 512], in0=o, scalar1=ng, scalar2=1.0 - lam_init, op0=ALU.mult, op1=ALU.mult),
                reads=[ob, sb], writes=[OT_b[hd][qc]])


def swa_branch(g, l, OT, OT_b):
    S, A, W = g.S, g.A, g.W
    xT = A.alloc([8, 512], BF16)
    xT_b = [Buf() for _ in range(8)]
    wA = RR([(A.alloc([8, 384], BF16), Buf()) for _ in range(2)])
    QT = A.alloc([4, L], BF16)
    KT = A.alloc([L], BF16)
    V = A.alloc([NT, 64], BF16)
    QT_b = [[Buf() for _ in range(4)] for _ in range(4)]
    KT_b = [Buf() for _ in range(4)]
    V_b = [Buf() for _ in range(NT)]
    qbias_b, kbias_b = Buf(), Buf()
    PT = RR([(A.alloc([4, 128], BF16), Buf()) for _ in range(4)])
    SM = RR([(A.alloc([4, 128], F32), Buf()) for _ in range(2)])
    DN = RR([(A.alloc([2, 128], F32), Buf()) for _ in range(2)])
    esink = A.alloc([2], F32)
    es_b = Buf()
    mask_swa = A.alloc([2, 128], BF16)
    mk_b = Buf()
    S.op("sp", lambda h: h.dma_start(out=mask_swa, in_=W("c_mask_swa")), writes=[mk_b], dma=True)
    S.op("sp", lambda h: h.dma_start(out=KT[64:68, :], in_=W("c_kb")), writes=[kbias_b], dma=True)
    acc = g.banks[0:2]
    rot = RR(g.banks[2:8])
    for grp in range(2):
        w, wb = wA.get()
        for (d0, c0, n) in ((0, grp * 256, 256), (256, 512 + grp * 64, 64), (320, 640 + grp * 64, 64)):
            load_w(g, w[:, :, d0:d0 + n], wb, W("mix_w_in")[l, :, c0:c0 + n].rearrange("(k p) n -> p k n", p=128))
        for r in range(4):
            S.op("sp", lambda h, r=r, grp=grp: h.dma_start(out=QT[64:68, r, :], in_=W("c_qb_swa")[:, 4 * grp + r, :]),
                 writes=[qbias_b], dma=True)
        for par in range(2):
            for j in range(2):
                hidx = 4 * grp + 2 * j + par
                S.op("sp", lambda h, par=par, j=j, hidx=hidx: h.dma_start(
                    out=esink[64 * par:64 * par + 64, j:j + 1],
                    in_=W("swa_sinks")[l:l + 1, hidx:hidx + 1].to_broadcast([64, 1])), writes=[es_b], dma=True)
        S.op("act", lambda h: h.activation(out=esink, in_=esink, func=AF.Exp), reads=[es_b], writes=[es_b])
        for c in range(4):
            gen_xT(g, xT, xT_b, resid_tiles(g, c * 4))
            for r in range(4):
                ps, psb = g.ps.get()
                for k in range(8):
                    mm(g, ps[0:64, :], w[:, k, r * 64:(r + 1) * 64], xT[:, k, :], k == 0, k == 7, [wb, xT_b[k]], [psb])
                evac(g, r, QT[0:64, r, c * 512:(c + 1) * 512], ps[0:64, :], [psb], [QT_b[r][c]])
            ps, psb = g.ps.get()
            for k in range(8):
                mm(g, ps[0:64, :], w[:, k, 256:320], xT[:, k, :], k == 0, k == 7, [wb, xT_b[k]], [psb])
            evac(g, 1, KT[0:64, c * 512:(c + 1) * 512], ps[0:64, :], [psb], [KT_b[c]])
            for tl in range(4):
                ps, psb = g.ps.get()
                for k in range(8):
                    mm(g, ps[:, 0:64], xT[:, k, tl * 128:(tl + 1) * 128], w[:, k, 320:384], k == 0, k == 7,
                       [wb, xT_b[k]], [psb])
                evac(g, tl, V[:, c * 4 + tl, :], ps[:, 0:64], [psb], [V_b[c * 4 + tl]])
        def s_blk(n):
            kbs = [n - 1, n] if n > 0 else [n]
            pts = []
            for kb in kbs:
                mi = 0 if kb == n - 1 else 1
                ps, psb = rot.get()
                mm(g, ps, KT[0:68, kb * 128:(kb + 1) * 128], QT[0:68, :, n * 128:(n + 1) * 128], True, True,
                   [KT_b[kb // 4], kbias_b, qbias_b] + [QT_b[r][n // 4] for r in range(4)], [psb])
                sm, smb = SM.get()
                S.op("dve", lambda h: h.tensor_tensor(
                    out=sm, in0=view3(ps, 4), in1=bc_mid(mask_swa[:, mi, :], 4), op=ALU.add),
                    reads=[psb, mk_b], writes=[smb])
                pt, ptb = PT.get()
                S.op("act", lambda h: h.activation(out=pt, in_=sm, func=AF.Exp, scale=0.125),
                     reads=[smb], writes=[ptb])
                pts.append((pt, ptb, kb))
            return pts
        nxt = s_blk(0)
        for n in range(NT):
            pts = nxt
            if n + 1 < NT:
                nxt = s_blk(n + 1)
            po, pob = acc[0]
            pr, prb = acc[1]
            for par in range(2):
                for i, (pt, ptb, kb) in enumerate(pts):
                    mm(g, po[64 * par:64 * par + 64, 0:256], V[:, kb, :], pt[:, par::2, :], i == 0, i == len(pts) - 1,
                       [V_b[kb], ptb], [pob])
                for i, (pt, ptb, kb) in enumerate(pts):
                    mm(g, pr[64 * par:64 * par + 64, 0:256], g.ones[:, 0:64], pt[:, par::2, :], i == 0, i == len(pts) - 1,
                       [g.cb, ptb], [prb])
            dn, dnb = DN.get()
            S.op("dve", lambda h: h.tensor_tensor(
                out=dn, in0=view3(pr[:, 0:256], 2), in1=bc_last(esink, 128), op=ALU.add), reads=[prb, es_b], writes=[dnb])
            S.op("dve", lambda h: h.reciprocal(out=dn, in_=dn), reads=[dnb], writes=[dnb])
            S.op("dve", lambda h: h.tensor_tensor(
                out=OT[:, 2 * grp:2 * grp + 2, n * 128:(n + 1) * 128], in0=view3(po[:, 0:256], 2), in1=dn, op=ALU.mult),
                reads=[pob, dnb], writes=[OT_b[2 * grp][n // 4], OT_b[2 * grp + 1][n // 4]])


def conv_branch(g, l, OT, OT_b):
    S, A, W = g.S, g.A, g.W
    gT = A.alloc([4, 30 + L], BF16)
    gT_b = [[Buf() for _ in range(5)] for _ in range(4)]
    cwT = A.alloc([4, 32], F32)
    cbv = A.alloc([3, 4], F32)
    pb_ = Buf()
    m1 = A.mark()
    xT = A.alloc([8, 512], BF16)
    xT_b = [Buf() for _ in range(8)]
    wC = A.alloc([8, 1024], BF16)
    wC_b = Buf()
    cwn = A.alloc([512], F32)
    SG = RR([(A.alloc([512], F32), Buf()) for _ in range(2)])
    load_w(g, wC, wC_b, W("mix_w_in")[l, :, 2816:3840].rearrange("(k p) n -> p k n", p=128))
    S.op("pool", lambda h: h.memset(cwn[0:32, :], 0.0), writes=[pb_])
    S.op("sp", lambda h: h.dma_start(out=cwn[0:31, :], in_=W("conv_w")[l]), writes=[pb_], dma=True)
    for i, nm in enumerate(("conv_b", "conv_ln_g", "conv_ln_b")):
        S.op("sp", lambda h, i=i, nm=nm: h.dma_start(out=cbv[:, i, :], in_=W(nm)[l].rearrange("(c p) -> p c", p=128),
                                                     allow_slow_non_contiguous=True), writes=[pb_], dma=True)
    for cc in range(4):
        ps, psb = g.ps.get()
        S.op("pe", lambda h, ps=ps, cc=cc: h.transpose(out=ps[:, 0:32], in_=cwn[0:32, cc * 128:(cc + 1) * 128],
                                                       identity=g.ident[0:32, 0:32]), reads=[pb_, g.cb], writes=[psb])
        S.op("dve", lambda h, ps=ps, cc=cc: h.tensor_copy(out=cwT[:, cc, 0:31], in_=ps[:, 0:31]), reads=[psb], writes=[pb_])
        S.op("pool", lambda h, cc=cc: h.memset(gT[:, cc, 0:30], 0.0), writes=[gT_b[cc][4]])
    for c in range(4):
        gen_xT(g, xT, xT_b, resid_tiles(g, c * 4))
        for cc in range(4):
            pv, pvb = g.ps.get()
            pg, pgb = g.ps.get()
            for k in range(8):
                mm(g, pv, wC[:, k, cc * 128:(cc + 1) * 128], xT[:, k, :], k == 0, k == 7, [wC_b, xT_b[k]], [pvb])
            for k in range(8):
                mm(g, pg, wC[:, k, 512 + cc * 128:512 + (cc + 1) * 128], xT[:, k, :], k == 0, k == 7, [wC_b, xT_b[k]], [pgb])
            sg, sgb = SG.get()
            S.op("act", lambda h, sg=sg, pg=pg: h.activation(out=sg, in_=pg, func=AF.Sigmoid), reads=[pgb], writes=[sgb])
            S.op("dve", lambda h, sg=sg, pv=pv, cc=cc, c=c: h.tensor_tensor(
                out=gT[:, cc, 30 + c * 512:30 + (c + 1) * 512], in0=pv, in1=sg, op=ALU.mult),
                reads=[pvb, sgb], writes=[gT_b[cc][c]])
    g.dbg_out("gT", gT, [b for r in gT_b for b in r])
    g.dbg_out("cwT", cwT, [pb_])
    g.dbg_out("cbv", cbv, [pb_])
    A.reset(m1)
    S.barrier()
    Dg = A.alloc([4, 31, 128], BF16)
    Dg_b = [Buf() for _ in range(4)]
    yb = A.alloc([4, 512], F32)
    sq = A.alloc([4, 512], BF16)
    ybh = A.alloc([4, 512], BF16)
    ybh_b = [Buf() for _ in range(4)]
    yb_b = [Buf() for _ in range(4)]
    sq_b = [Buf() for _ in range(4)]
    Fm = [(A.alloc([512], F32), Buf()) for _ in range(3)]
    def build_dg(cc):
        for j in range(31):
            S.op("dve" if j % 2 == 0 else "pool", lambda h: h.tensor_scalar(
                out=Dg[:, cc, j, :], in0=g.identb, scalar1=cwT[:, cc, j:j + 1], scalar2=None, op0=ALU.mult),
                reads=[pb_, g.cb], writes=[Dg_b[cc]])

    def conv_mm(c):
        banks = []
        if c == 0:
            build_dg(0)
        for cc in range(4):
            if c == 0 and cc + 1 < 4:
                build_dg(cc + 1)
            ps, psb = g.ps.get()
            rd = [Dg_b[cc], gT_b[cc][c]] + ([gT_b[cc][c - 1]] if c > 0 else [gT_b[cc][4]])
            for j in range(31):
                mm(g, ps, Dg[:, cc, j, :], gT[:, cc, c * 512 + j:c * 512 + j + 512], j == 0, j == 30, rd, [psb])
            banks.append((ps, psb))
        return banks
    nxt_banks = conv_mm(0)
    for c in range(4):
        banks = nxt_banks
        for cc in range(4):
            ps, psb = banks[cc]
            S.op("act", lambda h: h.activation(out=yb[:, cc, :], in_=ps, func=AF.Identity,
                                               bias=cbv[:, 0, cc:cc + 1], scale=1.0),
                 reads=[psb, pb_], writes=[yb_b[cc]])
            S.op("act", lambda h: h.activation(out=sq[:, cc, :], in_=ps, func=AF.Square,
                                               bias=cbv[:, 0, cc:cc + 1], scale=1.0),
                 reads=[psb, pb_], writes=[sq_b[cc]])
            S.op("act", lambda h: h.activation(out=ybh[:, cc, :], in_=ps, func=AF.Identity,
                                               bias=cbv[:, 0, cc:cc + 1], scale=1.0),
                 reads=[psb, pb_], writes=[ybh_b[cc]])
        if c + 1 < 4:
            nxt_banks = conv_mm(c + 1)
        if c == 0:
            g.dbg_out("yb0", yb, yb_b)
            g.dbg_out("sq0", sq, sq_b)
        pm1, pm1b = g.ps.get()
        pm2, pm2b = g.ps.get()
        for cc in range(4):
            mm(g, pm1, g.ones, ybh[:, cc, :], cc == 0, cc == 3, [g.cb, ybh_b[cc]], [pm1b])
        for cc in range(4):
            mm(g, pm2, g.ones, sq[:, cc, :], cc == 0, cc == 3, [g.cb, sq_b[cc]], [pm2b])
        (mean, meb), (msq, msb), (rs, rsb) = Fm
        S.op("act", lambda h: h.activation(out=mean, in_=pm1, func=AF.Copy, scale=1.0 / 512.0), reads=[pm1b], writes=[meb])
        S.op("act", lambda h: h.activation(out=msq, in_=pm1, func=AF.Square, scale=1.0 / 512.0), reads=[pm1b], writes=[msb])
        S.op("dve", lambda h: h.scalar_tensor_tensor(out=rs, in0=pm2, scalar=1.0 / 512.0, in1=msq, op0=ALU.mult, op1=ALU.subtract),
             reads=[pm2b, msb], writes=[rsb])
        S.op("act", lambda h: h.activation(out=rs, in_=rs, func=AF.Sqrt, bias=g.eps_ap, scale=1.0), reads=[rsb, g.cb], writes=[rsb])
        S.op("dve", lambda h: h.reciprocal(out=rs, in_=rs), reads=[rsb], writes=[rsb])
        if c == 0:
            g.dbg_out("mean0", mean, [meb])
            g.dbg_out("msq0", msq, [msb])
            g.dbg_out("rs0", rs, [rsb])
        for cc in range(4):
            S.op("pool", lambda h, cc=cc: h.tensor_tensor(out=yb[:, cc, :], in0=yb[:, cc, :], in1=mean, op=ALU.subtract),
                 reads=[yb_b[cc], meb], writes=[yb_b[cc]])
            S.op("dve", lambda h, cc=cc: h.tensor_tensor(out=yb[:, cc, :], in0=yb[:, cc, :], in1=rs, op=ALU.mult),
                 reads=[yb_b[cc], rsb], writes=[yb_b[cc]])
            S.op("act", lambda h, cc=cc, c=c: h.activation(out=OT[:, cc, c * 512:(c + 1) * 512], in_=yb[:, cc, :], func=AF.Silu,
                                                           bias=cbv[:, 2, cc:cc + 1], scale=cbv[:, 1, cc:cc + 1]),
                 reads=[yb_b[cc], pb_], writes=[OT_b[cc][c]])
    g.dbg_out("OTc", OT, [b for r in OT_b for b in r])


def sincos(g, x, s_out, c_out, t1, t2, xb, outb=None, t3=None, t4=None, cb2=None):
    S = g.S
    chains = [(0.0, s_out, t1, t2, Buf()), (math.pi / 2.0, c_out, t3 if t3 is not None else t1, t4 if t4 is not None else t2,
                                           Buf() if t3 is not None else None)]
    if t3 is None:
        chains[1] = chains[1][:4] + (chains[0][4],)
    steps = []
    for shift, out, a, b, cb in chains:
        st = [
            ("dve", lambda h, a=a, shift=shift: h.tensor_scalar(out=a, in0=x, scalar1=1.0 / TWO_PI, scalar2=shift / TWO_PI + MAGIC,
                                                                 op0=ALU.mult, op1=ALU.add)),
            ("dve", lambda h, a=a: h.tensor_scalar(out=a, in0=a, scalar1=-MAGIC, scalar2=None, op0=ALU.add)),
            ("dve", lambda h, a=a, b=b: h.scalar_tensor_tensor(out=b, in0=a, scalar=-CW1, in1=x, op0=ALU.mult, op1=ALU.add)),
            ("dve", lambda h, a=a, b=b: h.scalar_tensor_tensor(out=b, in0=a, scalar=-CW2, in1=b, op0=ALU.mult, op1=ALU.add)),
            ("dve", lambda h, b=b, shift=shift: h.tensor_scalar(out=b, in0=b, scalar1=shift, scalar2=3.1415925, op0=ALU.add, op1=ALU.min)),
            ("dve", lambda h, b=b: h.tensor_scalar(out=b, in0=b, scalar1=-3.1415925, scalar2=None, op0=ALU.max)),
            ("act", lambda h, b=b, out=out: h.activation(out=out, in_=b, func=AF.Sin)),
        ]
        steps.append((st, cb))
    order = []
    if t3 is None:
        for st, cb in steps:
            order += [(e, fn, cb) for e, fn in st]
    else:
        for i in range(7):
            for st, cb in steps:
                order.append(st[i] + (cb,))
    c2 = chains[1][4]
    for idx, (e, fn, cb) in enumerate(order):
        last = (e == "act")
        cbs = cb2 if (cb2 is not None and cb is c2 and t3 is not None) else [cb]
        wr = list(cbs) + ([xb] if last else []) + ([outb] if (last and outb is not None) else [])
        S.op(e, fn, reads=[xb] + list(cbs), writes=wr)


def s5_branch(g, l, OT, OT_b):
    S, A, W = g.S, g.A, g.W
    yT, yT_b = OT, OT_b

    def f(shape):
        return A.alloc(shape, F32)
    sb = Buf("s5small")

    def dv(fn):
        S.op("dve", fn, reads=[sb], writes=[sb])

    def ac(fn):
        S.op("act", fn, reads=[sb], writes=[sb])
    WBr = A.alloc([16, 128], BF16)
    WBi = A.alloc([16, 128], BF16)
    WCr = A.alloc([16, 64], BF16)
    WCn = A.alloc([16, 64], BF16)
    wmat_b = Buf()
    mag, s128, c128 = f([16]), f([16]), f([16])
    dsk, glb = f([4]), f([4])
    uT = A.alloc([4, L], BF16)
    uT_b = [[Buf() for _ in range(4)] for _ in range(4)]
    th = f([16])
    m1 = A.mark()
    ar, ai, ls, sn, cs, t1, t2 = (f([16]) for _ in range(7))
    abr, abi, den, nr, cr, ci, t3 = (f([16]) for _ in range(7))
    Bs_r, Bs_i = f([16, 128]), f([16, 128])
    braw_r, braw_i = f([16, 16]), f([16, 16])
    tb1 = f([4, 16])
    S.op("sp", lambda h: h.dma_start(out=ar, in_=W("s5_a_re")[l].rearrange("(ct gl) p -> (gl p) ct", gl=2),
                                     allow_slow_non_contiguous=True), writes=[sb], dma=True)
    S.op("sp", lambda h: h.dma_start(out=ai, in_=W("s5_a_im")[l].rearrange("(ct gl) p -> (gl p) ct", gl=2),
                                     allow_slow_non_contiguous=True), writes=[sb], dma=True)
    for gl in range(2):
        S.op("sp", lambda h, gl=gl: h.dma_start(
            out=ls[64 * gl:64 * gl + 64, :],
            in_=W("s5_log_step")[l].rearrange("(ct gl) -> gl ct", gl=2)[gl:gl + 1, :].to_broadcast([64, 16]),
            allow_slow_non_contiguous=True), writes=[sb], dma=True)
    S.op("sp", lambda h: h.dma_start(out=braw_r, in_=W("s5_b_re")[l].rearrange("(ct gl) p c -> (gl p) ct c", gl=2)),
         writes=[sb], dma=True)
    S.op("sp", lambda h: h.dma_start(out=braw_i, in_=W("s5_b_im")[l].rearrange("(ct gl) p c -> (gl p) ct c", gl=2)),
         writes=[sb], dma=True)
    for i, nm in enumerate(("s5_d", "s5_glu_b")):
        dst = (dsk, glb)[i]
        S.op("sp", lambda h, dst=dst, nm=nm: h.dma_start(out=dst, in_=W(nm)[l].rearrange("(c p) -> p c", p=128),
                                                         allow_slow_non_contiguous=True), writes=[sb], dma=True)
    S.op("pool", lambda h: h.memset(Bs_r, 0.0), writes=[sb])
    S.op("pool", lambda h: h.memset(Bs_i, 0.0), writes=[sb])
    ac(lambda h: h.activation(out=ls, in_=ls, func=AF.Exp))
    dv(lambda h: h.tensor_tensor(out=t1, in0=ar, in1=ls, op=ALU.mult))
    ac(lambda h: h.activation(out=mag, in_=t1, func=AF.Exp))
    dv(lambda h: h.tensor_tensor(out=th, in0=ai, in1=ls, op=ALU.mult))
    sincos(g, th, sn, cs, t1, t2, sb)
    dv(lambda h: h.tensor_tensor(out=abr, in0=mag, in1=cs, op=ALU.mult))
    dv(lambda h: h.tensor_tensor(out=abi, in0=mag, in1=sn, op=ALU.mult))
    dv(lambda h: h.tensor_tensor(out=t1, in0=ar, in1=ar, op=ALU.mult))
    dv(lambda h: h.tensor_tensor(out=den, in0=ai, in1=ai, op=ALU.mult))
    dv(lambda h: h.tensor_tensor(out=den, in0=den, in1=t1, op=ALU.add))
    dv(lambda h: h.reciprocal(out=den, in_=den))
    dv(lambda h: h.tensor_scalar(out=nr, in0=abr, scalar1=-1.0, scalar2=None, op0=ALU.add))
    dv(lambda h: h.tensor_tensor(out=t1, in0=nr, in1=ar, op=ALU.mult))
    dv(lambda h: h.tensor_tensor(out=t2, in0=abi, in1=ai, op=ALU.mult))
    dv(lambda h: h.tensor_tensor(out=t1, in0=t1, in1=t2, op=ALU.add))
    dv(lambda h: h.tensor_tensor(out=cr, in0=t1, in1=den, op=ALU.mult))
    dv(lambda h: h.tensor_tensor(out=t1, in0=abi, in1=ar, op=ALU.mult))
    dv(lambda h: h.tensor_tensor(out=t2, in0=nr, in1=ai, op=ALU.mult))
    dv(lambda h: h.tensor_tensor(out=t1, in0=t1, in1=t2, op=ALU.subtract))
    dv(lambda h: h.tensor_tensor(out=ci, in0=t1, in1=den, op=ALU.mult))
    dv(lambda h: h.tensor_scalar(out=t3, in0=th, scalar1=512.0, scalar2=None, op0=ALU.mult))
    sincos(g, t3, s128, c128, t1, t2, sb)
    for q in range(4):
        for gl in range(2):
            rows = slice(64 * gl, 64 * gl + 64)
            c0 = 32 * q + 16 * gl
            crb = bc_last(cr[rows, q::4], 16)
            cib = bc_last(ci[rows, q::4], 16)
            br_, bi_ = braw_r[rows, q::4, :], braw_i[rows, q::4, :]
            o_r, o_i = Bs_r[rows, q::4, c0:c0 + 16], Bs_i[rows, q::4, c0:c0 + 16]
            tt = tb1[rows]
            dv(lambda h, tt=tt, bi_=bi_, cib=cib: h.tensor_tensor(out=tt, in0=bi_, in1=cib, op=ALU.mult))
            dv(lambda h, o_r=o_r, br_=br_, crb=crb: h.tensor_tensor(out=o_r, in0=br_, in1=crb, op=ALU.mult))
            dv(lambda h, o_r=o_r, tt=tt: h.tensor_tensor(out=o_r, in0=o_r, in1=tt, op=ALU.subtract))
            dv(lambda h, tt=tt, br_=br_, cib=cib: h.tensor_tensor(out=tt, in0=br_, in1=cib, op=ALU.mult))
            dv(lambda h, o_i=o_i, bi_=bi_, crb=crb: h.tensor_tensor(out=o_i, in0=bi_, in1=crb, op=ALU.mult))
            dv(lambda h, o_i=o_i, tt=tt: h.tensor_tensor(out=o_i, in0=o_i, in1=tt, op=ALU.add))
    for ct in range(16):
        q = ct % 4
        for i, (src, dst) in enumerate(((Bs_r, WBr), (Bs_i, WBi))):
            ps, psb = g.ps.get()
            S.op("pe", lambda h, ps=ps, src=src, ct=ct: h.transpose(out=ps[:, 0:128], in_=src[:, ct, :], identity=g.ident),
                 reads=[sb, g.cb], writes=[psb])
            er = slice(64, 128) if q == 3 else slice(32 * q, 32 * q + 32)
            evac(g, i, dst[er, ct, :], ps[er, 0:128], [psb], [wmat_b])
    A.reset(m1)
    S.barrier()
    Cs_r, Cs_i = f([16, 128]), f([16, 128])
    S.op("pool", lambda h: h.memset(Cs_r[0:64], 0.0), writes=[sb])
    S.op("pool", lambda h: h.memset(Cs_i[0:64], 0.0), writes=[sb])
    for gl in range(2):
        for par in range(2):
            for (dst, nm) in ((Cs_r, "s5_c_re"), (Cs_i, "s5_c_im")):
                p0 = 32 * par + 16 * gl
                S.op("sp", lambda h, gl=gl, par=par, dst=dst, nm=nm, p0=p0: h.dma_start(
                    out=dst[p0:p0 + 16, par::2, 64 * gl:64 * gl + 64],
                    in_=W(nm)[l].rearrange("(ct gl) c p -> gl c ct p", gl=2)[gl][:, par::2, :]), writes=[sb], dma=True)
    for ct in range(16):
        for i, (src, dst, scl) in enumerate(((Cs_r, WCr, None), (Cs_i, WCn, -1.0))):
            ps, psb = g.ps.get()
            S.op("pe", lambda h, ps=ps, src=src, ct=ct: h.transpose(out=ps[:, 0:64], in_=src[0:64, ct, :],
                                                                    identity=g.ident[0:64, 0:64]),
                 reads=[sb, g.cb], writes=[psb])
            evac(g, i, dst[:, ct, :], ps[:, 0:64], [psb], [wmat_b], scale=scl)
    A.reset(m1)
    S.barrier()
    xT = A.alloc([8, 512], BF16)
    xT_b = [Buf() for _ in range(8)]
    wS = A.alloc([8, 512], BF16)
    wS_b = Buf()
    load_w(g, wS, wS_b, W("mix_w_in")[l, :, 2304:2816].rearrange("(k p) n -> p k n", p=128))
    for c in range(4):
        gen_xT(g, xT, xT_b, resid_tiles(g, c * 4))
        for cc in range(4):
            ps, psb = g.ps.get()
            for k in range(8):
                mm(g, ps, wS[:, k, cc * 128:(cc + 1) * 128], xT[:, k, :], k == 0, k == 7, [wS_b, xT_b[k]], [psb])
            evac(g, cc, uT[:, cc, c * 512:(c + 1) * 512], ps, [psb], [uT_b[cc][c]])
    A.reset(m1)
    S.barrier()
    iota = f([512])
    S.op("sp", lambda h: h.dma_start(out=iota, in_=W("c_iota")), writes=[sb], dma=True)
    TB = RR([((f([512]), f([512])), Buf()) for _ in range(2)])
    angj, sj1, sj2 = f([512]), f([512]), f([512])
    sjb = Buf()
    brp, bip, ta, tb = f([512]), f([512]), f([512]), f([512])
    dB = Buf()
    brp_b, bip_b, ta_b, tb_b = Buf(), Buf(), Buf(), Buf()
    W2 = RR([((f([512]), f([512])), (Buf(), Buf())) for _ in range(2)])
    pa, pb2, pc, pd = f([512]), f([512]), f([512]), f([512])
    pa_b, pb_b, pc_b, pd_b = Buf(), Buf(), Buf(), Buf()
    INI = RR([(f([4]), Buf()) for _ in range(2)])
    XR = RR([(A.alloc([512], BF16), Buf()) for _ in range(2)])
    XI = RR([(A.alloc([512], BF16), Buf()) for _ in range(2)])
    vv, v2, vb = brp, bip, dB
    yacc = g.banks[0:4]
    rot = RR(g.banks[4:8])
    for cc in range(4):
        for q in range(4):
            ct = 4 * cc + q
            rows = slice(64, 128) if q == 3 else slice(32 * q, 32 * q + 32)
            orow = slice(64 * (q // 2), 64 * (q // 2) + 64)
            (sinT, cosT), tbb = TB.get()
            S.op("dve", lambda h: h.tensor_scalar(out=angj, in0=iota, scalar1=th[:, ct:ct + 1], scalar2=None, op0=ALU.mult),
                 reads=[sb, sjb], writes=[sjb])
            sincos(g, angj, sinT, cosT, sj1, sj2, sjb, outb=tbb, t3=ta, t4=tb, cb2=[ta_b, tb_b])
            rho = mag[:, ct:ct + 1].to_broadcast([128, 512])
            cc_, ss_ = c128[:, ct:ct + 1], s128[:, ct:ct + 1]
            prev = None
            for c in range(4):
                py, pyb = yacc[c]
                pbr, pbrb = rot.get()
                pbi, pbib = rot.get()
                mm(g, pbr, WBr[rows, ct, :], uT[rows, cc, c * 512:(c + 1) * 512], True, True, [wmat_b, uT_b[cc][c]], [pbrb])
                mm(g, pbi, WBi[rows, ct, :], uT[rows, cc, c * 512:(c + 1) * 512], True, True, [wmat_b, uT_b[cc][c]], [pbib])
                S.op("dve", lambda h: h.tensor_tensor(out=ta, in0=pbi, in1=sinT, op=ALU.mult), reads=[pbib, tbb], writes=[ta_b])
                S.op("dve", lambda h: h.tensor_tensor(out=tb, in0=pbr, in1=sinT, op=ALU.mult), reads=[pbrb, tbb], writes=[tb_b])
                S.op("dve", lambda h: h.tensor_tensor(out=brp, in0=pbr, in1=cosT, op=ALU.mult), reads=[pbrb, tbb], writes=[brp_b])
                S.op("dve", lambda h: h.tensor_tensor(out=bip, in0=pbi, in1=cosT, op=ALU.mult), reads=[pbib, tbb], writes=[bip_b])
                S.op("dve", lambda h: h.tensor_tensor(out=brp, in0=brp, in1=ta, op=ALU.add), reads=[brp_b, ta_b], writes=[brp_b])
                S.op("dve", lambda h: h.tensor_tensor(out=bip, in0=bip, in1=tb, op=ALU.subtract), reads=[bip_b, tb_b], writes=[bip_b])
                (wr, wi), (wr_b, wi_b) = W2.get()
                if prev is None:
                    i_re, i_im, ird = 0.0, 0.0, []
                else:
                    (pwr, pwi), (pwr_b, pwi_b) = prev
                    ini, inib = INI.get()
                    lr, li = pwr[:, 511:512], pwi[:, 511:512]
                    rdl = [pwr_b, pwi_b, sb]
                    S.op("dve", lambda h: h.tensor_scalar(out=ini[:, 0:1], in0=li, scalar1=ss_, scalar2=None, op0=ALU.mult),
                         reads=rdl, writes=[inib])
                    S.op("dve", lambda h: h.tensor_scalar(out=ini[:, 2:3], in0=li, scalar1=cc_, scalar2=None, op0=ALU.mult),
                         reads=rdl, writes=[inib])
                    S.op("dve", lambda h: h.scalar_tensor_tensor(out=ini[:, 1:2], in0=lr, scalar=cc_, in1=ini[:, 0:1],
                                                                 op0=ALU.mult, op1=ALU.subtract), reads=rdl + [inib], writes=[inib])
                    S.op("dve", lambda h: h.scalar_tensor_tensor(out=ini[:, 3:4], in0=lr, scalar=ss_, in1=ini[:, 2:3],
                                                                 op0=ALU.mult, op1=ALU.add), reads=rdl + [inib], writes=[inib])
                    i_re, i_im, ird = ini[:, 1:2], ini[:, 3:4], [inib]
                S.op("dve", lambda h: h.tensor_tensor_scan(out=wr, data0=rho, data1=brp, initial=i_re, op0=ALU.mult, op1=ALU.add),
                     reads=[brp_b, sb] + ird, writes=[wr_b])
                S.op("dve", lambda h: h.tensor_tensor_scan(out=wi, data0=rho, data1=bip, initial=i_im, op0=ALU.mult, op1=ALU.add),
                     reads=[bip_b, sb] + ird, writes=[wi_b])
                prev = ((wr, wi), (wr_b, wi_b))
                xr, xrb = XR.get()
                xi, xib = XI.get()
                S.op("pool", lambda h: h.tensor_tensor(out=pa, in0=wr, in1=cosT, op=ALU.mult), reads=[wr_b, tbb, sjb], writes=[pa_b])
                S.op("pool", lambda h: h.tensor_tensor(out=pb2, in0=wi, in1=sinT, op=ALU.mult), reads=[wi_b, tbb, sjb], writes=[pb_b])
                S.op("pool", lambda h: h.tensor_tensor(out=pc, in0=wi, in1=cosT, op=ALU.mult), reads=[wi_b, tbb, sjb], writes=[pc_b])
                S.op("pool", lambda h: h.tensor_tensor(out=pd, in0=wr, in1=sinT, op=ALU.mult), reads=[wr_b, tbb, sjb], writes=[pd_b])
                S.op("pool", lambda h: h.tensor_tensor(out=xr, in0=pa, in1=pb2, op=ALU.subtract), reads=[pa_b, pb_b], writes=[xrb])
                S.op("pool", lambda h: h.tensor_tensor(out=xi, in0=pc, in1=pd, op=ALU.add), reads=[pc_b, pd_b], writes=[xib])
                mm(g, py[orow, :], WCr[:, ct, :], xr, q % 2 == 0, False, [wmat_b, xrb], [pyb])
                mm(g, py[orow, :], WCn[:, ct, :], xi, False, q % 2 == 1, [wmat_b, xib], [pyb])
        for c in range(4):
            py, pyb = yacc[c]
            S.op("dve", lambda h: h.scalar_tensor_tensor(
                out=vv, in0=uT[:, cc, c * 512:(c + 1) * 512], scalar=dsk[:, cc:cc + 1], in1=py, op0=ALU.mult, op1=ALU.add),
                reads=[pyb, uT_b[cc][c], sb, vb], writes=[vb])
            S.op("act", lambda h: h.activation(out=v2, in_=vv, func=AF.Square), reads=[vb], writes=[vb])
            S.op("dve", lambda h: h.tensor_scalar(out=v2, in0=v2, scalar1=0.044715, scalar2=1.0, op0=ALU.mult, op1=ALU.add),
                 reads=[vb], writes=[vb])
            S.op("dve", lambda h: h.tensor_tensor(out=v2, in0=v2, in1=vv, op=ALU.mult), reads=[vb], writes=[vb])
            S.op("act", lambda h: h.activation(out=v2, in_=v2, func=AF.Sigmoid, scale=1.5957691216057308), reads=[vb], writes=[vb])
            S.op("dve", lambda h: h.tensor_tensor(out=yT[:, cc, c * 512:(c + 1) * 512], in0=vv, in1=v2, op=ALU.mult),
                 reads=[vb], writes=[yT_b[cc][c]])
    A.reset(m1)
    S.barrier()
    gw = A.alloc([4, 512], BF16)
    gw_b = Buf()
    load_w(g, gw, gw_b, W("s5_glu_w")[l].rearrange("(k p) n -> p k n", p=128))
    vv = f([512])
    vb = Buf()
    for c in range(4):
        zs = []
        for oc in range(4):
            ps, psb = g.banks[oc]
            for kc in range(4):
                mm(g, ps, gw[:, kc, oc * 128:(oc + 1) * 128], yT[:, kc, c * 512:(c + 1) * 512], kc == 0, kc == 3,
                   [gw_b, yT_b[kc][c]], [psb])
            zs.append((ps, psb))
        for oc in range(4):
            ps, psb = zs[oc]
            S.op("act", lambda h, ps=ps, oc=oc: h.activation(out=vv, in_=ps, func=AF.Sigmoid, bias=glb[:, oc:oc + 1], scale=1.0),
                 reads=[psb, sb, vb], writes=[vb])
            S.op("dve", lambda h, oc=oc, c=c: h.tensor_tensor(out=yT[:, oc, c * 512:(c + 1) * 512],
                                                              in0=yT[:, oc, c * 512:(c + 1) * 512], in1=vv, op=ALU.mult),
                 reads=[vb, yT_b[oc][c]], writes=[yT_b[oc][c]])


PARTS = ("ffn1", "mix", "cross", "ffn2")
_cache = {}


def run(inputs, layers, parts, xin, branches="abcd"):
    key = (tuple(layers), tuple(parts), branches)
    if key not in _cache:
        _cache[key] = build(layers, parts, branches)
    nc, g = _cache[key]
    in_maps = []
    shared = {}
    for name in g.dr:
        if name in ("x", "mem"):
            continue
        if name.startswith("c_"):
            shared[name] = consts()[name][0]
        else:
            shared[name] = np.ascontiguousarray(np.asarray(inputs[name], dtype=np.float32))
    for c in range(NCORES):
        m = dict(shared)
        m["x"] = np.ascontiguousarray(xin[c * SEQ_PER_CORE:(c + 1) * SEQ_PER_CORE])
        if "mem" in g.dr:
            m["mem"] = np.ascontiguousarray(np.asarray(inputs["mem"], dtype=np.float32)[c * SEQ_PER_CORE:(c + 1) * SEQ_PER_CORE])
        in_maps.append(m)
    res = run_bass_kernel_spmd(nc, in_maps, core_ids=list(range(NCORES)))
    if DEBUG:
        LAST.update({k: v for k, v in res.results[0].items()})
    return np.concatenate([r["out"] for r in res.results], axis=0)


FUSED = True


def kernel(**inputs):
    x = np.asarray(inputs["x"], dtype=np.float32)
    if FUSED:
        return run(inputs, list(range(DEPTH)), PARTS, x)
    for l in range(DEPTH):
        x = run(inputs, [l], PARTS, x)
    return x
```

```python
from contextlib import ExitStack
import math
import numpy as np
import concourse.bass as bass
import concourse.mybir as mybir
from concourse.bass_utils import run_bass_kernel_spmd

F32 = mybir.dt.float32
BF16 = mybir.dt.bfloat16
AF = mybir.ActivationFunctionType
ALU = mybir.AluOpType

D = 1024
L = 2048
NT = L // 128
DEPTH = 4
FFN = 2816
NJ = FFN // 128
MIX_IN = 7936
ALPHA = (2.0 * DEPTH) ** 0.25
EPS = 1e-5
NCORES = 8
SEQ_PER_CORE = 2

ENGS = ["pe", "act", "dve", "pool", "sp"]
NSEM = 4
PHASE = 2048
NDSEM = 40


class Buf:
    __slots__ = ("w", "r", "name")

    def __init__(self, name=""):
        self.w = None
        self.r = {}
        self.name = name


class _Rec:
    def __init__(self):
        self.call = None

    def __getattr__(self, name):
        def f(*a, **k):
            self.call = (name, a, k)
            return self
        return f


class Sched:
    def __init__(self, nc, es):
        self.nc = nc
        self.h = dict(pe=nc.tensor, act=nc.scalar, dve=nc.vector, pool=nc.gpsimd, sp=nc.sync)
        self.q = {e: [] for e in ENGS}
        self.cnt = {e: 0 for e in ENGS}
        self.semh = {}
        self.esval = {}
        for e in ENGS:
            for i in range(NSEM):
                self.semh[("e", e, i)] = es.enter_context(nc.semaphore(f"s_{e}{i}"))
                self.esval[("e", e, i)] = 0
        self.dval = [0] * NDSEM
        for i in range(NDSEM):
            self.semh[("d", i)] = es.enter_context(nc.semaphore(f"d{i}"))
        self.dnext = 0
        self.seen = {e: {} for e in ENGS}
        self.last = {}
        self.nwait = 0

    def _wait(self, eng, k, v):
        if self.seen[eng].get(k, 0) >= v:
            return
        self.seen[eng][k] = v
        sem = self.semh[k]
        self.q[eng].append(lambda h, sem=sem, v=v: h.wait_ge(sem, v))
        self.nwait += 1

    def op(self, eng, fn, reads=(), writes=(), dma=False):
        rec = _Rec()
        fn(rec)
        _name, _a, _k = rec.call

        def fn(h, _name=_name, _a=_a, _k=_k):
            return getattr(h, _name)(*_a, **_k)
        deps = {}

        def add(k, v):
            if deps.get(k, 0) < v:
                deps[k] = v

        for b in reads:
            if b.w is not None:
                add(*b.w)
        for b in writes:
            if b.w is not None:
                add(*b.w)
            for k, v in b.r.items():
                if k[0] == "e" and k[1] == eng:
                    continue
                add(k, v)
        for k, v in deps.items():
            if eng == "pe" and k[0] == "e" and k[1] == "pe":
                continue
            self._wait(eng, k, v)
        if dma:
            i = self.dnext
            self.dnext = (i + 1) % NDSEM
            k = ("d", i)
            prev = self.dval[i]
            if prev:
                self._wait(eng, k, prev)
            self.dval[i] = prev + 16
            tok = (k, prev + 16)
            sem = self.semh[k]
            self.q[eng].append(lambda h, fn=fn, sem=sem: fn(h).then_inc(sem, 16))
        else:
            c = self.cnt[eng]
            self.cnt[eng] = c + 1
            k = ("e", eng, (c // PHASE) % NSEM)
            self.esval[k] += 1
            tok = (k, self.esval[k])
            sem = self.semh[k]
            self.q[eng].append(lambda h, fn=fn, sem=sem: fn(h).then_inc(sem, 1))
        self.last[tok[0]] = tok[1]
        for b in writes:
            b.w = tok
            b.r = {}
        for b in reads:
            if b.r.get(tok[0], 0) < tok[1]:
                b.r[tok[0]] = tok[1]
        return tok

    def barrier(self):
        for e in ENGS:
            for k, v in self.last.items():
                if k[0] == "e" and k[1] == e:
                    continue
                self._wait(e, k, v)

    def finish(self, eng="sp"):
        for k, v in self.last.items():
            self._wait(eng, k, v)

    def replay(self, block):
        q = self.q

        @block.tensor
        def _(h):
            for f in q["pe"]:
                f(h)

        @block.scalar
        def _(h):
            for f in q["act"]:
                f(h)

        @block.vector
        def _(h):
            for f in q["dve"]:
                f(h)

        @block.gpsimd
        def _(h):
            for f in q["pool"]:
                f(h)

        @block.sync
        def _(h):
            for f in q["sp"]:
                f(h)


class RR:
    def __init__(self, aps, name="p"):
        self.slots = [a if isinstance(a, tuple) else (a, Buf(f"{name}{i}")) for i, a in enumerate(aps)]
        self.i = 0

    def get(self):
        s = self.slots[self.i]
        self.i = (self.i + 1) % len(self.slots)
        return s


class Arena:
    def __init__(self, t_bf):
        self.tb = t_bf
        self.tf = t_bf.bitcast(F32)
        self.off = 0
        self.cap = t_bf.shape[1] * 2

    def mark(self):
        return self.off

    def reset(self, m):
        self.off = m

    def alloc(self, shape, dt):
        esz = 4 if dt == F32 else 2
        n = int(np.prod(shape))
        self.off = (self.off + 63) // 64 * 64
        o = self.off
        self.off += n * esz
        assert self.off <= self.cap, (self.off, self.cap)
        t = self.tf if dt == F32 else self.tb
        ap = t[:, o // esz: o // esz + n]
        if len(shape) == 2:
            ap = ap.rearrange("p (a b) -> p a b", a=shape[0])
        elif len(shape) == 3:
            ap = ap.rearrange("p (a b c) -> p a b c", a=shape[0], b=shape[1])
        return ap


class K:
    pass


SHAPES = {
    "x": [SEQ_PER_CORE, L, D], "mem": [SEQ_PER_CORE, 256, D],
    "ffn1_w_in": [DEPTH, D, 2 * FFN], "ffn1_w_out": [DEPTH, FFN, D], "ffn1_ln_g": [DEPTH, D], "ffn1_ln_b": [DEPTH, D],
    "mix_w_in": [DEPTH, D, MIX_IN], "swa_sinks": [DEPTH, 8], "swa_proj": [DEPTH, 512, D],
    "s5_a_re": [DEPTH, 32, 64], "s5_a_im": [DEPTH, 32, 64], "s5_log_step": [DEPTH, 32],
    "s5_b_re": [DEPTH, 32, 64, 16], "s5_b_im": [DEPTH, 32, 64, 16], "s5_c_re": [DEPTH, 32, 16, 64],
    "s5_c_im": [DEPTH, 32, 16, 64], "s5_d": [DEPTH, 512], "s5_glu_w": [DEPTH, 512, 512], "s5_glu_b": [DEPTH, 512],
    "s5_proj": [DEPTH, 512, D], "conv_w": [DEPTH, 31, 512], "conv_b": [DEPTH, 512], "conv_ln_g": [DEPTH, 512],
    "conv_ln_b": [DEPTH, 512], "conv_proj": [DEPTH, 512, D], "diff_lq1": [DEPTH, 64], "diff_lk1": [DEPTH, 64],
    "diff_lq2": [DEPTH, 64], "diff_lk2": [DEPTH, 64], "diff_norm_g": [DEPTH, 128], "diff_proj": [DEPTH, 512, D],
    "mix_w_out": [DEPTH, D, D], "mix_ln_g": [DEPTH, D], "mix_ln_b": [DEPTH, D], "mem_ln_g": [D], "mem_ln_b": [D],
    "cross_wq": [DEPTH, D, 512], "cross_wkv": [DEPTH, D, D], "cross_wo": [DEPTH, 512, D],
    "cross_ln_g": [DEPTH, D], "cross_ln_b": [DEPTH, D],
    "ffn2_w_in": [DEPTH, D, 2 * FFN], "ffn2_w_out": [DEPTH, FFN, D], "ffn2_ln_g": [DEPTH, D], "ffn2_ln_b": [DEPTH, D],
}
NEG = -240000.0
TWO_PI = 2.0 * math.pi
CW1 = 6.28125
CW2 = TWO_PI - CW1
MAGIC = 12582912.0
DIFF_SLOPES = [2.0 ** (-8.0 * (h + 1) / 4) for h in range(4)]
SWA_SLOPES = [2.0 ** (-8.0 * (h + 1) / 8) for h in range(8)]
CONSTS = {}
DEBUG = False
LAST = {}


def consts():
    if not CONSTS:
        import ml_dtypes
        bf = ml_dtypes.bfloat16
        CONSTS["c_ident"] = (np.eye(128, dtype=np.float32), F32)
        CONSTS["c_iota"] = (np.tile(np.arange(512, dtype=np.float32)[None, :], (128, 1)), F32)
        pos = np.arange(L)
        pa, pb = (pos // 128).astype(np.float32), (pos % 128).astype(np.float32)
        kb = np.stack([np.ones(L), np.ones(L), 128.0 * pa, pb]).astype(np.float32)
        CONSTS["c_kb"] = (kb.astype(bf), BF16)

        def qb(slopes):
            o = np.zeros((4, len(slopes), L), np.float32)
            for h, s in enumerate(slopes):
                o[0, h] = -8.0 * s * 128.0 * pa
                o[1, h] = -8.0 * s * pb
                o[2, h] = 8.0 * s
                o[3, h] = 8.0 * s
            return o.astype(bf)
        CONSTS["c_qb_diff"] = (qb(DIFF_SLOPES), BF16)
        CONSTS["c_qb_swa"] = (qb(SWA_SLOPES), BF16)
        ki = np.arange(128)[:, None]
        md = np.zeros((128, 4, 512), np.float32)
        qi = np.arange(512)[None, :]
        for rel in range(4):
            md[:, rel, :] = np.where(qi >= 128 * rel + ki, 0.0, NEG)
        CONSTS["c_mask_diff"] = (md.astype(bf), BF16)
        ms = np.zeros((128, 2, 128), np.float32)
        q1 = np.arange(128)[None, :]
        ms[:, 0, :] = np.where(ki > q1, 0.0, NEG)
        ms[:, 1, :] = np.where(ki <= q1, 0.0, NEG)
        CONSTS["c_mask_swa"] = (ms.astype(bf), BF16)
    return CONSTS


def build(layers, parts, branches="abcd", nseq=SEQ_PER_CORE):
    nc = bass.Bass("TRN2", target_bir_lowering=False)
    es = ExitStack()
    with es:
        g = K()
        g.nc = nc
        S = Sched(nc, es)
        g.S = S
        g.dr = {}

        def W(name):
            if name not in g.dr:
                if name.startswith("c_"):
                    arr, dt = consts()[name]
                    g.dr[name] = nc.dram_tensor(name, list(arr.shape), dt, kind="ExternalInput").ap()
                else:
                    g.dr[name] = nc.dram_tensor(name, list(SHAPES[name]), F32, kind="ExternalInput").ap()
            return g.dr[name]
        g.W = W
        g.dbg = {}

        def dbg_out(name, ap, bufs):
            if not DEBUG or name in g.dbg:
                return
            t = nc.dram_tensor("dbg_" + name, list(ap.shape), ap.dtype, kind="ExternalOutput").ap()
            g.dbg[name] = t
            S.op("sp", lambda h: h.dma_start(out=t, in_=ap), reads=bufs, dma=True)
        g.dbg_out = dbg_out
        out = nc.dram_tensor("out", [nseq, L, D], F32, kind="ExternalOutput").ap()

        arena_t = es.enter_context(nc.sbuf_tensor("arena", [128, 106400], BF16))
        A = Arena(arena_t)
        g.A = A
        g.resid = A.alloc([NT, D], F32)
        g.resid_b = [Buf(f"resid{t}") for t in range(NT)]
        g.ident = A.alloc([128], F32)
        g.identb = A.alloc([128], BF16)
        g.ones = A.alloc([128], BF16)
        g.memT = A.alloc([8, 256], BF16)
        g.memT_b = [Buf() for _ in range(8)]
        g.eps_ap = A.alloc([1], F32)
        g.eps4_ap = A.alloc([1], F32)
        g.cb = Buf("consts")
        S.op("sp", lambda h: h.dma_start(out=g.ident, in_=W("c_ident")), writes=[g.cb], dma=True)
        S.op("dve", lambda h: h.tensor_copy(out=g.identb, in_=g.ident), reads=[g.cb], writes=[g.cb])
        S.op("dve", lambda h: h.memset(g.ones, 1.0), writes=[g.cb])
        S.op("dve", lambda h: h.memset(g.eps_ap, EPS), writes=[g.cb])
        S.op("dve", lambda h: h.memset(g.eps4_ap, 4.0 * EPS), writes=[g.cb])
        g.eps_b = g.cb
        g.ident_b = g.cb
        g.banks = [(es.enter_context(nc.psum_tensor(f"ps{i}", [128, 512], F32))[:], Buf(f"ps{i}")) for i in range(8)]
        g.ps = RR(g.banks)
        pmark = A.mark()

        for s in range(nseq):
            for t in range(NT):
                S.op("sp", lambda h, t=t, s=s: h.dma_start(out=g.resid[:, t, :], in_=W("x")[s, t * 128:(t + 1) * 128, :]),
                     writes=[g.resid_b[t]], dma=True)
            if "cross" in parts:
                A.reset(pmark)
                S.barrier()
                prep_mem(g, s)
            for l in layers:
                for p in parts:
                    A.reset(pmark)
                    S.barrier()
                    if p in ("ffn1", "ffn2"):
                        ffn(g, l, p)
                    elif p == "mix":
                        mixer(g, l, branches)
                    elif p == "cross":
                        cross(g, l)
            for t in range(NT):
                S.op("sp", lambda h, t=t, s=s: h.dma_start(out=out[s, t * 128:(t + 1) * 128, :], in_=g.resid[:, t, :]),
                     reads=[g.resid_b[t]], dma=True)
        S.finish("sp")
        with nc.Block() as block:
            S.replay(block)
        g.stats = dict(cnt=dict(S.cnt), nwait=S.nwait)
    return nc, g


def load_w(g, dst, dst_b, src, eng="pool"):
    g.S.op(eng, lambda h: h.dma_start(out=dst, in_=src), writes=[dst_b], dma=True)


def mm(g, out, lhsT, rhs, start, stop, reads, writes):
    g.S.op("pe", lambda h: h.matmul(out, lhsT=lhsT, rhs=rhs, start=start, stop=stop), reads=reads, writes=writes)


def evac(g, i, out, in_, reads, writes, scale=None):
    if i % 2 == 0:
        if scale is None:
            g.S.op("act", lambda h: h.activation(out=out, in_=in_, func=AF.Copy), reads=reads, writes=writes)
        else:
            g.S.op("act", lambda h: h.activation(out=out, in_=in_, func=AF.Copy, scale=scale), reads=reads, writes=writes)
    else:
        if scale is None:
            g.S.op("dve", lambda h: h.tensor_copy(out=out, in_=in_), reads=reads, writes=writes)
        else:
            g.S.op("dve", lambda h: h.tensor_scalar(out=out, in0=in_, scalar1=scale, scalar2=None, op0=ALU.mult),
                   reads=reads, writes=writes)


def gen_xT(g, xT, xT_b, srcs):
    S = g.S
    n = len(srcs)
    for k in range(8):
        ps, psb = g.ps.get()
        for t, (src, sb) in enumerate(srcs):
            S.op("pe", lambda h, ps=ps, t=t, k=k, src=src: h.transpose(
                out=ps[:, t * 128:(t + 1) * 128], in_=src[:, k * 128:(k + 1) * 128], identity=g.ident),
                reads=[sb, g.ident_b], writes=[psb])
        evac(g, k, xT[:, k, 0:128 * n], ps[:, 0:128 * n], [psb], [xT_b[k]])


def resid_tiles(g, t0, n=4):
    return [(g.resid[:, t0 + i, :], g.resid_b[t0 + i]) for i in range(n)]


def ln_alloc(g, n=3):
    A = g.A
    sets = RR([((A.alloc([12], F32), A.alloc([2], F32), A.alloc([1], F32), A.alloc([1], F32)), (Buf("ln1"), Buf("ln2")))
               for _ in range(n)])
    return sets, None


def load_ln(g, gname, bname, l):
    A, S, W = g.A, g.S, g.W
    lng = A.alloc([D], F32)
    lnb = A.alloc([D], F32)
    b = Buf("lnp")
    gs = W(gname)[l:l + 1, :] if l is not None else W(gname).rearrange("(o d) -> o d", o=1)
    bs = W(bname)[l:l + 1, :] if l is not None else W(bname).rearrange("(o d) -> o d", o=1)
    S.op("sp", lambda h: h.dma_start(out=lng, in_=gs.to_broadcast([128, D])), writes=[b], dma=True)
    S.op("sp", lambda h: h.dma_start(out=lnb, in_=bs.to_broadcast([128, D])), writes=[b], dma=True)
    return lng, lnb, b


def ln_tile(g, x, xb, lnp, tmp, tmp_b, eps=None):
    S = g.S
    lng, lnb, lnp_b = lnp
    (st, mv, sd, rs), (b1, b2) = tmp.get()
    for hh in range(2):
        S.op("dve", lambda h, hh=hh: h.bn_stats(out=st[:, hh * 6:(hh + 1) * 6], in_=x[:, hh * 512:(hh + 1) * 512]),
             reads=[xb], writes=[b1])
    S.op("dve", lambda h: h.bn_aggr(out=mv, in_=st), reads=[b1], writes=[b1])
    eps_ap = g.eps_ap if eps is None else eps
    S.op("act", lambda h: h.activation(out=sd, in_=mv[:, 1:2], func=AF.Sqrt, bias=eps_ap, scale=1.0),
         reads=[b1, g.eps_b], writes=[b2])
    S.op("dve", lambda h: h.scalar_tensor_tensor(out=x, in0=x, scalar=mv[:, 0:1], in1=lng, op0=ALU.subtract, op1=ALU.mult),
         reads=[xb, b1, lnp_b], writes=[xb])
    S.op("dve", lambda h: h.reciprocal(out=rs, in_=sd), reads=[b2], writes=[b2])
    S.op("dve", lambda h: h.scalar_tensor_tensor(out=x, in0=x, scalar=rs, in1=lnb, op0=ALU.mult, op1=ALU.add),
         reads=[xb, b2, lnp_b], writes=[xb])


def prep_mem(g, s):
    S, A, W = g.S, g.A, g.W
    mt = A.alloc([2, D], F32)
    mb = [Buf(), Buf()]
    lnp = load_ln(g, "mem_ln_g", "mem_ln_b", None)
    tmp, tmp_b = ln_alloc(g)
    for i in range(2):
        S.op("sp", lambda h, i=i: h.dma_start(out=mt[:, i, :], in_=W("mem")[s, i * 128:(i + 1) * 128, :]),
             writes=[mb[i]], dma=True)
        ln_tile(g, mt[:, i, :], mb[i], lnp, tmp, tmp_b)
    gen_xT(g, g.memT, g.memT_b, [(mt[:, i, :], mb[i]) for i in range(2)])


def ffn(g, l, which):
    S, A, W = g.S, g.A, g.W
    w_in = W(which + "_w_in")
    w_out = W(which + "_w_out")
    xT = A.alloc([8, 1024], BF16)
    xT_b = [[Buf() for _ in range(8)] for _ in range(2)]
    aT = A.alloc([NJ, 1024], BF16)
    aT_b = [[Buf() for _ in range(2)] for _ in range(NJ)]
    wo = RR([(A.alloc([NJ, 512], BF16), Buf()) for _ in range(2)])
    wi = RR([(A.alloc([8, 2, 256], BF16), Buf()) for _ in range(2)])
    lnp = load_ln(g, which + "_ln_g", which + "_ln_b", l)
    sg = RR([(A.alloc([512], F32), Buf()) for _ in range(2)])
    tmp, tmp_b = ln_alloc(g)
    pre = []

    def load_jp(jp):
        w, wb = wi.get()
        for gu in range(2):
            c0 = gu * FFN + jp * 256
            load_w(g, w[:, :, gu, :], wb, w_in[l, :, c0:c0 + 256].rearrange("(k p) n -> p k n", p=128))
        return w, wb
    for c in range(2):
        for sub in range(2):
            gen_xT(g, xT[:, :, sub * 512:(sub + 1) * 512], xT_b[sub], resid_tiles(g, c * 8 + sub * 4))
        for jp in range(NJ // 2):
            w, wb = pre.pop(0) if pre else load_jp(jp)
            for jj in range(2):
                j = jp * 2 + jj
                for sub in range(2):
                    pg, pgb = g.ps.get()
                    pu, pub = g.ps.get()
                    for gu, (pp, ppb) in enumerate(((pg, pgb), (pu, pub))):
                        for k in range(8):
                            mm(g, pp, w[:, k, gu, jj * 128:(jj + 1) * 128], xT[:, k, sub * 512:(sub + 1) * 512],
                               k == 0, k == 7, [wb, xT_b[sub][k]], [ppb])
                    sgt, sgb = sg.get()
                    S.op("act", lambda h, sgt=sgt, pg=pg: h.activation(out=sgt, in_=pg, func=AF.Silu),
                         reads=[pgb], writes=[sgb])
                    S.op("dve", lambda h, sgt=sgt, pu=pu, j=j, sub=sub: h.tensor_tensor(
                        out=aT[:, j, sub * 512:(sub + 1) * 512], in0=pu, in1=sgt, op=ALU.mult),
                        reads=[pub, sgb], writes=[aT_b[j][sub]])
        wos = []
        for q in range(2):
            w, wb = wo.get()
            load_w(g, w, wb, w_out[l, :, q * 512:(q + 1) * 512].rearrange("(j p) n -> p j n", p=128))
            wos.append((w, wb))
        if c == 0:
            pre = [load_jp(0), load_jp(1)]
        for q in range(2):
            w, wb = wos[q]
            for tl in range(8):
                t = c * 8 + tl
                pf, pfb = g.ps.get()
                for j in range(NJ):
                    mm(g, pf, aT[:, j, tl * 128:(tl + 1) * 128], w[:, j, :], j == 0, j == NJ - 1,
                       [wb, aT_b[j][tl // 4]], [pfb])
                xs = g.resid[:, t, q * 512:(q + 1) * 512]
                S.op("dve", lambda h: h.scalar_tensor_tensor(
                    out=xs, in0=xs, scalar=2.0 * ALPHA, in1=pf, op0=ALU.mult, op1=ALU.add),
                    reads=[g.resid_b[t], pfb], writes=[g.resid_b[t]])
        for tl in range(8):
            ln_tile(g, g.resid[:, c * 8 + tl, :], g.resid_b[c * 8 + tl], lnp, tmp, tmp_b, eps=g.eps4_ap)


def cross(g, l):
    S, A, W = g.S, g.A, g.W
    wq = A.alloc([8, 512], BF16)
    wkv = A.alloc([8, 1024], BF16)
    wo = A.alloc([4, 1024], BF16)
    wq_b, wkv_b, wo_b = Buf(), Buf(), Buf()
    load_w(g, wkv, wkv_b, W("cross_wkv")[l].rearrange("(k p) n -> p k n", p=128))
    load_w(g, wq, wq_b, W("cross_wq")[l].rearrange("(k p) n -> p k n", p=128))
    load_w(g, wo, wo_b, W("cross_wo")[l].rearrange("(k p) n -> p k n", p=128))
    KT = A.alloc([4, 256], BF16)
    KT_b = [Buf() for _ in range(4)]
    V = A.alloc([2, 512], BF16)
    V_b = [Buf() for _ in range(2)]
    for hd in range(4):
        ps, psb = g.ps.get()
        for k in range(8):
            mm(g, ps[:, 0:256], wkv[:, k, hd * 128:(hd + 1) * 128], g.memT[:, k, :], k == 0, k == 7,
               [wkv_b, g.memT_b[k]], [psb])
        evac(g, hd, KT[:, hd, :], ps[:, 0:256], [psb], [KT_b[hd]])
    for mt in range(2):
        ps, psb = g.ps.get()
        for k in range(8):
            mm(g, ps, g.memT[:, k, mt * 128:(mt + 1) * 128], wkv[:, k, 512:1024], k == 0, k == 7,
               [wkv_b, g.memT_b[k]], [psb])
        evac(g, mt, V[:, mt, :], ps, [psb], [V_b[mt]])
    xT = A.alloc([8, 512], BF16)
    xT_b = [Buf() for _ in range(8)]
    QT = RR([(A.alloc([512], BF16), Buf()) for _ in range(2)])
    PT = RR([(A.alloc([512], BF16), Buf()) for _ in range(4)])
    RI = RR([(A.alloc([512], F32), Buf()) for _ in range(2)])
    OT = A.alloc([4, 512], BF16)
    OT_b = [Buf() for _ in range(4)]
    lnp = load_ln(g, "cross_ln_g", "cross_ln_b", l)
    tmp, tmp_b = ln_alloc(g)
    sc = 1.0 / math.sqrt(128.0)
    for c in range(4):
        gen_xT(g, xT, xT_b, resid_tiles(g, c * 4))
        for hp in range(2):
            qts, ptss = {}, {}
            for hd in (2 * hp, 2 * hp + 1):
                ps, psb = g.ps.get()
                for k in range(8):
                    mm(g, ps, wq[:, k, hd * 128:(hd + 1) * 128], xT[:, k, :], k == 0, k == 7, [wq_b, xT_b[k]], [psb])
                qt, qtb = QT.get()
                evac(g, hd, qt, ps, [psb], [qtb])
                qts[hd] = (qt, qtb)
            for hd in (2 * hp, 2 * hp + 1):
                qt, qtb = qts[hd]
                pts = []
                for mt in range(2):
                    ps, psb = g.ps.get()
                    mm(g, ps, KT[:, hd, mt * 128:(mt + 1) * 128], qt, True, True, [KT_b[hd], qtb], [psb])
                    pt, ptb = PT.get()
                    S.op("act", lambda h: h.activation(out=pt, in_=ps, func=AF.Exp, scale=sc), reads=[psb], writes=[ptb])
                    pts.append((pt, ptb))
                ptss[hd] = pts
            for hd in (2 * hp, 2 * hp + 1):
                pts = ptss[hd]
                po, pob = g.ps.get()
                pr, prb = g.ps.get()
                for mt in range(2):
                    mm(g, po, V[:, mt, hd * 128:(hd + 1) * 128], pts[mt][0], mt == 0, mt == 1, [V_b[mt], pts[mt][1]], [pob])
                for mt in range(2):
                    mm(g, pr, g.ones, pts[mt][0], mt == 0, mt == 1, [g.cb, pts[mt][1]], [prb])
                ri, rib = RI.get()
                S.op("dve", lambda h: h.reciprocal(out=ri, in_=pr), reads=[prb], writes=[rib])
                S.op("dve", lambda h: h.tensor_tensor(out=OT[:, hd, :], in0=po, in1=ri, op=ALU.mult),
                     reads=[pob, rib], writes=[OT_b[hd]])
        for tl in range(4):
            t = c * 4 + tl
            for half in range(2):
                ps, psb = g.ps.get()
                for hd in range(4):
                    mm(g, ps, OT[:, hd, tl * 128:(tl + 1) * 128], wo[:, hd, half * 512:(half + 1) * 512],
                       hd == 0, hd == 3, [OT_b[hd], wo_b], [psb])
                xs = g.resid[:, t, half * 512:(half + 1) * 512]
                S.op("dve", lambda h, xs=xs, ps=ps: h.scalar_tensor_tensor(
                    out=xs, in0=xs, scalar=ALPHA, in1=ps, op0=ALU.mult, op1=ALU.add),
                    reads=[g.resid_b[t], psb], writes=[g.resid_b[t]])
            ln_tile(g, g.resid[:, t, :], g.resid_b[t], lnp, tmp, tmp_b)


def view3(ap, a):
    return ap.rearrange("p (a b) -> p a b", a=a)


def bc_mid(ap2, n):
    return ap2.unsqueeze(1).to_broadcast([ap2.shape[0], n, ap2.shape[1]])


def bc_last(ap2, n):
    return ap2.unsqueeze(2).to_broadcast([ap2.shape[0], ap2.shape[1], n])


def mixer(g, l, branches):
    S, A, W = g.S, g.A, g.W
    OT = {br: A.alloc([4, L], BF16) for br in "abcd"}
    OT_b = {br: [[Buf() for _ in range(4)] for _ in range(4)] for br in "abcd"}
    m0 = A.mark()
    fns = dict(a=swa_branch, b=s5_branch, c=conv_branch, d=diff_branch)
    for br in "dacb":
        A.reset(m0)
        S.barrier()
        if br in branches:
            fns[br](g, l, OT[br], OT_b[br])
        else:
            allb = [b for row in OT_b[br] for b in row]
            S.op("pool", lambda h, br=br: h.memset(OT[br], 0.0), writes=allb)
    A.reset(m0)
    S.barrier()
    lam_init = None
    mT = A.alloc([8, L], BF16)
    mT_b = [[Buf() for _ in range(4)] for _ in range(8)]
    m2 = A.mark()
    xT = A.alloc([8, 512], BF16)
    xT_b = [Buf() for _ in range(8)]
    wg = RR([(A.alloc([8, 4, 128], BF16), Buf()) for _ in range(2)])
    wp = RR([(A.alloc([4, 4, 128], BF16), Buf()) for _ in range(2)])
    SG = RR([(A.alloc([512], F32), Buf()) for _ in range(2)])
    PRT = [(A.alloc([512], F32), Buf()) for _ in range(3)]
    projs = dict(a="swa_proj", b="s5_proj", c="conv_proj", d="diff_proj")
    for jd in range(8):
        w, wb = wg.get()
        p, pb = wp.get()
        for i, br in enumerate("abcd"):
            c0 = 3840 + i * 1024 + jd * 128
            load_w(g, w[:, :, i, :], wb, W("mix_w_in")[l, :, c0:c0 + 128].rearrange("(k p) n -> p k n", p=128))
            load_w(g, p[:, i, :, :], pb, W(projs[br])[l, :, jd * 128:(jd + 1) * 128].rearrange("(k p) n -> p k n", p=128))
        for c in range(4):
            gen_xT(g, xT, xT_b, resid_tiles(g, c * 4))
            slot = [PRT[0], PRT[1], PRT[2], PRT[1]]
            for i, br in enumerate("abcd"):
                pgt, pgb = g.ps.get()
                for k in range(8):
                    mm(g, pgt, w[:, k, i, :], xT[:, k, :], k == 0, k == 7, [wb, xT_b[k]], [pgb])
                py, pyb = g.ps.get()
                for kc in range(4):
                    mm(g, py, p[:, i, kc, :], OT[br][:, kc, c * 512:(c + 1) * 512], kc == 0, kc == 3,
                       [pb, OT_b[br][kc][c]], [pyb])
                sg, sgb = SG.get()
                S.op("act", lambda h: h.activation(out=sg, in_=pgt, func=AF.Sigmoid), reads=[pgb], writes=[sgb])
                pr, prb = slot[i]
                S.op("dve", lambda h: h.tensor_tensor(out=pr, in0=py, in1=sg, op=ALU.mult), reads=[pyb, sgb], writes=[prb])
                if i == 1:
                    S.op("pool", lambda h: h.tensor_tensor(out=PRT[0][0], in0=PRT[0][0], in1=PRT[1][0], op=ALU.add),
                         reads=[PRT[0][1], PRT[1][1]], writes=[PRT[0][1]])
                if i == 3:
                    S.op("pool", lambda h: h.tensor_tensor(out=PRT[2][0], in0=PRT[2][0], in1=PRT[1][0], op=ALU.add),
                         reads=[PRT[2][1], PRT[1][1]], writes=[PRT[2][1]])
            S.op("dve", lambda h: h.tensor_tensor(out=mT[:, jd, c * 512:(c + 1) * 512], in0=PRT[0][0], in1=PRT[2][0], op=ALU.add),
                 reads=[PRT[0][1], PRT[2][1]], writes=[mT_b[jd][c]])
    A.reset(m2)
    S.barrier()
    wo = A.alloc([8, 1024], BF16)
    wo_b = Buf()
    load_w(g, wo, wo_b, W("mix_w_out")[l].rearrange("(k p) n -> p k n", p=128))
    lnp = load_ln(g, "mix_ln_g", "mix_ln_b", l)
    tmp, tmp_b = ln_alloc(g)
    for t in range(NT):
        for half in range(2):
            ps, psb = g.ps.get()
            for k in range(8):
                mm(g, ps, mT[:, k, t * 128:(t + 1) * 128], wo[:, k, half * 512:(half + 1) * 512], k == 0, k == 7,
                   [mT_b[k][t // 4], wo_b], [psb])
            xs = g.resid[:, t, half * 512:(half + 1) * 512]
            S.op("dve", lambda h, xs=xs, ps=ps: h.scalar_tensor_tensor(
                out=xs, in0=xs, scalar=ALPHA, in1=ps, op0=ALU.mult, op1=ALU.add),
                reads=[g.resid_b[t], psb], writes=[g.resid_b[t]])
        ln_tile(g, g.resid[:, t, :], g.resid_b[t], lnp, tmp, tmp_b)


def diff_branch(g, l, OT, OT_b):
    S, A, W = g.S, g.A, g.W
    lam_init = 0.8 - 0.6 * math.exp(-0.3 * l)
    xT = A.alloc([8, 512], BF16)
    xT_b = [Buf() for _ in range(8)]
    wD = RR([(A.alloc([8, 384], BF16), Buf()) for _ in range(2)])
    QT = A.alloc([2, L], BF16)
    KT = A.alloc([2, L], BF16)
    V = A.alloc([NT, 128], BF16)
    QT_b = [[Buf() for _ in range(4)] for _ in range(2)]
    KT_b = [[Buf() for _ in range(4)] for _ in range(2)]
    V_b = [Buf() for _ in range(NT)]
    qbias_b, kbias_b = Buf(), Buf()
    PT = RR([(A.alloc([512], BF16), Buf()) for _ in range(4)])
    SM = RR([(A.alloc([512], F32), Buf()) for _ in range(2)])
    F = RR([(A.alloc([512], F32), Buf()) for _ in range(6)])
    mask_diff = A.alloc([4, 512], BF16)
    mk_b = Buf()
    S.op("sp", lambda h: h.dma_start(out=mask_diff, in_=W("c_mask_diff")), writes=[mk_b], dma=True)
    lq = A.alloc([4, 64], F32)
    sc = A.alloc([8], F32)
    ng = A.alloc([1], F32)
    sb = Buf()
    for i, nm in enumerate(("diff_lq1", "diff_lk1", "diff_lq2", "diff_lk2")):
        S.op("sp", lambda h, i=i, nm=nm: h.dma_start(out=lq[:, i, :], in_=W(nm)[l:l + 1, :].to_broadcast([128, 64])),
             writes=[sb], dma=True)
    S.op("sp", lambda h: h.dma_start(out=ng, in_=W("diff_norm_g")[l].rearrange("(p o) -> p o", o=1)), writes=[sb], dma=True)
    for i in range(2):
        S.op("dve", lambda h, i=i: h.tensor_tensor(out=lq[:, 2 * i, :], in0=lq[:, 2 * i, :], in1=lq[:, 2 * i + 1, :], op=ALU.mult),
             reads=[sb], writes=[sb])
        S.op("dve", lambda h, i=i: h.tensor_reduce(out=sc[:, i:i + 1], in_=lq[:, 2 * i, :], axis=mybir.AxisListType.X, op=ALU.add),
             reads=[sb], writes=[sb])
        S.op("act", lambda h, i=i: h.activation(out=sc[:, 2 + i:3 + i], in_=sc[:, i:i + 1], func=AF.Exp), reads=[sb], writes=[sb])
    S.op("dve", lambda h: h.tensor_tensor(out=sc[:, 4:5], in0=sc[:, 3:4], in1=sc[:, 2:3], op=ALU.subtract), reads=[sb], writes=[sb])
    S.op("dve", lambda h: h.tensor_scalar(out=sc[:, 5:6], in0=sc[:, 4:5], scalar1=-lam_init, scalar2=None, op0=ALU.add),
         reads=[sb], writes=[sb])
    nlam = sc[:, 5:6]
    for c in range(2):
        S.op("sp", lambda h, c=c: h.dma_start(out=KT[64:68, c, :], in_=W("c_kb")), writes=[kbias_b], dma=True)
    acc = g.banks[0:4]
    rot = RR(g.banks[4:8])
    for hd in range(4):
        w, wb = wD.get()
        for i, c0 in enumerate((768, 1280, 1792)):
            load_w(g, w[:, :, i * 128:(i + 1) * 128],  wb,
                   W("mix_w_in")[l, :, c0 + hd * 128:c0 + (hd + 1) * 128].rearrange("(k p) n -> p k n", p=128))
        for c in range(2):
            S.op("sp", lambda h, c=c, hd=hd: h.dma_start(out=QT[64:68, c, :], in_=W("c_qb_diff")[:, hd, :]),
                 writes=[qbias_b], dma=True)
        for c in range(4):
            gen_xT(g, xT, xT_b, resid_tiles(g, c * 4))
            for qk, (T_, T_b) in enumerate(((QT, QT_b), (KT, KT_b))):
                for comp in range(2):
                    ps, psb = g.ps.get()
                    col = qk * 128 + comp * 64
                    for k in range(8):
                        mm(g, ps[0:64, :], w[:, k, col:col + 64], xT[:, k, :], k == 0, k == 7, [wb, xT_b[k]], [psb])
                    evac(g, comp + qk, T_[0:64, comp, c * 512:(c + 1) * 512], ps[0:64, :], [psb], [T_b[comp][c]])
            for tl in range(4):
                ps, psb = g.ps.get()
                for k in range(8):
                    mm(g, ps[:, 0:128], xT[:, k, tl * 128:(tl + 1) * 128], w[:, k, 256:384], k == 0, k == 7,
                       [wb, xT_b[k]], [psb])
                evac(g, tl, V[:, c * 4 + tl, :], ps[:, 0:128], [psb], [V_b[c * 4 + tl]])
        for qc in range(4):
            nkb = 4 * qc + 4
            for comp in range(2):
                po, pob = acc[2 * comp]
                pr, prb = acc[2 * comp + 1]
                def s_mm(kb):
                    ps, psb = rot.get()
                    mm(g, ps, KT[0:68, comp, kb * 128:(kb + 1) * 128], QT[0:68, comp, qc * 512:(qc + 1) * 512], True, True,
                       [KT_b[comp][kb // 4], QT_b[comp][qc], qbias_b, kbias_b], [psb])
                    return ps, psb
                nxt = [s_mm(0)] + ([s_mm(1)] if nkb > 1 else [])
                for kb in range(nkb):
                    ps, psb = nxt.pop(0)
                    if kb + 2 < nkb:
                        nxt.append(s_mm(kb + 2))
                    pt, ptb = PT.get()
                    if kb >= 4 * qc:
                        sm, smb = SM.get()
                        S.op("dve", lambda h: h.tensor_tensor(
                            out=sm, in0=ps, in1=mask_diff[:, kb - 4 * qc, :], op=ALU.add), reads=[psb, mk_b], writes=[smb])
                        S.op("act", lambda h: h.activation(out=pt, in_=sm, func=AF.Exp, scale=0.125),
                             reads=[smb], writes=[ptb])
                    else:
                        S.op("act", lambda h: h.activation(out=pt, in_=ps, func=AF.Exp, scale=0.125),
                             reads=[psb], writes=[ptb])
                    mm(g, po, V[:, kb, :], pt, kb == 0, kb == nkb - 1, [V_b[kb], ptb], [pob])
                    mm(g, pr, g.ones, pt, kb == 0, kb == nkb - 1, [g.cb, ptb], [prb])
            ts = []
            for comp in range(2):
                po, pob = acc[2 * comp]
                pr, prb = acc[2 * comp + 1]
                ri, rib = F.get()
                S.op("dve", lambda h, ri=ri, pr=pr: h.reciprocal(out=ri, in_=pr), reads=[prb], writes=[rib])
                t_, tb_ = F.get()
                S.op("dve", lambda h, t_=t_, po=po, ri=ri: h.tensor_tensor(out=t_, in0=po, in1=ri, op=ALU.mult),
                     reads=[pob, rib], writes=[tb_])
                ts.append((t_, tb_))
            o, ob = F.get()
            S.op("dve", lambda h, o=o, t0=ts[0][0], t1=ts[1][0]: h.scalar_tensor_tensor(
                out=o, in0=t1, scalar=nlam, in1=t0, op0=ALU.mult, op1=ALU.add),
                reads=[ts[0][1], ts[1][1], sb], writes=[ob])
            sq, sqb = PT.get()
            S.op("act", lambda h, sq=sq, o=o: h.activation(out=sq, in_=o, func=AF.Square), reads=[ob], writes=[sqb])
            pm, pmb = rot.get()
            mm(g, pm, g.ones, sq, True, True, [g.cb, sqb], [pmb])
            sd, sdb = F.get()
            S.op("act", lambda h, sd=sd, pm=pm: h.activation(out=sd, in_=pm, func=AF.Sqrt, bias=g.eps_ap, scale=1.0 / 128.0),
                 reads=[pmb, g.cb], writes=[sdb])
            S.op("dve", lambda h, sd=sd: h.reciprocal(out=sd, in_=sd), reads=[sdb], writes=[sdb])
            S.op("dve", lambda h, o=o, sd=sd: h.tensor_tensor(out=o, in0=o, in1=sd, op=ALU.mult), reads=[ob, sdb], writes=[ob])
            S.op("dve", lambda h, o=o, hd=hd, qc=qc: h.tensor_scalar(
                out=OT[:, hd, qc * 512:(qc + 1) * 512], in0=o, scalar1=ng, scalar2=1.0 - lam_init, op0=ALU.mult, op1=ALU.mult),
                reads=[ob, sb], writes=[OT_b[hd][qc]])


def swa_branch(g, l, OT, OT_b):
    S, A, W = g.S, g.A, g.W
    xT = A.alloc([8, 512], BF16)
    xT_b = [Buf() for _ in range(8)]
    wA = RR([(A.alloc([8, 384], BF16), Buf()) for _ in range(2)])
    QT = A.alloc([4, L], BF16)
    KT = A.alloc([L], BF16)
    V = A.alloc([NT, 64], BF16)
    QT_b = [[Buf() for _ in range(4)] for _ in range(4)]
    KT_b = [Buf() for _ in range(4)]
    V_b = [Buf() for _ in range(NT)]
    qbias_b, kbias_b = Buf(), Buf()
    PT = RR([(A.alloc([4, 128], BF16), Buf()) for _ in range(4)])
    SM = RR([(A.alloc([4, 128], F32), Buf()) for _ in range(2)])
    DN = RR([(A.alloc([2, 128], F32), Buf()) for _ in range(2)])
    esink = A.alloc([2], F32)
    es_b = Buf()
    mask_swa = A.alloc([2, 128], BF16)
    mk_b = Buf()
    S.op("sp", lambda h: h.dma_start(out=mask_swa, in_=W("c_mask_swa")), writes=[mk_b], dma=True)
    S.op("sp", lambda h: h.dma_start(out=KT[64:68, :], in_=W("c_kb")), writes=[kbias_b], dma=True)
    acc = g.banks[0:2]
    rot = RR(g.banks[2:8])
    for grp in range(2):
        w, wb = wA.get()
        for (d0, c0, n) in ((0, grp * 256, 256), (256, 512 + grp * 64, 64), (320, 640 + grp * 64, 64)):
            load_w(g, w[:, :, d0:d0 + n], wb, W("mix_w_in")[l, :, c0:c0 + n].rearrange("(k p) n -> p k n", p=128))
        for r in range(4):
            S.op("sp", lambda h, r=r, grp=grp: h.dma_start(out=QT[64:68, r, :], in_=W("c_qb_swa")[:, 4 * grp + r, :]),
                 writes=[qbias_b], dma=True)
        for par in range(2):
            for j in range(2):
                hidx = 4 * grp + 2 * j + par
                S.op("sp", lambda h, par=par, j=j, hidx=hidx: h.dma_start(
                    out=esink[64 * par:64 * par + 64, j:j + 1],
                    in_=W("swa_sinks")[l:l + 1, hidx:hidx + 1].to_broadcast([64, 1])), writes=[es_b], dma=True)
        S.op("act", lambda h: h.activation(out=esink, in_=esink, func=AF.Exp), reads=[es_b], writes=[es_b])
        for c in range(4):
            gen_xT(g, xT, xT_b, resid_tiles(g, c * 4))
            for r in range(4):
                ps, psb = g.ps.get()
                for k in range(8):
                    mm(g, ps[0:64, :], w[:, k, r * 64:(r + 1) * 64], xT[:, k, :], k == 0, k == 7, [wb, xT_b[k]], [psb])
                evac(g, r, QT[0:64, r, c * 512:(c + 1) * 512], ps[0:64, :], [psb], [QT_b[r][c]])
            ps, psb = g.ps.get()
            for k in range(8):
                mm(g, ps[0:64, :], w[:, k, 256:320], xT[:, k, :], k == 0, k == 7, [wb, xT_b[k]], [psb])
            evac(g, 1, KT[0:64, c * 512:(c + 1) * 512], ps[0:64, :], [psb], [KT_b[c]])
            for tl in range(4):
                ps, psb = g.ps.get()
                for k in range(8):
                    mm(g, ps[:, 0:64], xT[:, k, tl * 128:(tl + 1) * 128], w[:, k, 320:384], k == 0, k == 7,
                       [wb, xT_b[k]], [psb])
                evac(g, tl, V[:, c * 4 + tl, :], ps[:, 0:64], [psb], [V_b[c * 4 + tl]])
        def s_blk(n):
            kbs = [n - 1, n] if n > 0 else [n]
            pts = []
            for kb in kbs:
                mi = 0 if kb == n - 1 else 1
                ps, psb = rot.get()
                mm(g, ps, KT[0:68, kb * 128:(kb + 1) * 128], QT[0:68, :, n * 128:(n + 1) * 128], True, True,
                   [KT_b[kb // 4], kbias_b, qbias_b] + [QT_b[r][n // 4] for r in range(4)], [psb])
                sm, smb = SM.get()
                S.op("dve", lambda h: h.tensor_tensor(
                    out=sm, in0=view3(ps, 4), in1=bc_mid(mask_swa[:, mi, :], 4), op=ALU.add),
                    reads=[psb, mk_b], writes=[smb])
                pt, ptb = PT.get()
                S.op("act", lambda h: h.activation(out=pt, in_=sm, func=AF.Exp, scale=0.125),
                     reads=[smb], writes=[ptb])
                pts.append((pt, ptb, kb))
            return pts
        nxt = s_blk(0)
        for n in range(NT):
            pts = nxt
            if n + 1 < NT:
                nxt = s_blk(n + 1)
            po, pob = acc[0]
            pr, prb = acc[1]
            for par in range(2):
                for i, (pt, ptb, kb) in enumerate(pts):
                    mm(g, po[64 * par:64 * par + 64, 0:256], V[:, kb, :], pt[:, par::2, :], i == 0, i == len(pts) - 1,
                       [V_b[kb], ptb], [pob])
                for i, (pt, ptb, kb) in enumerate(pts):
                    mm(g, pr[64 * par:64 * par + 64, 0:256], g.ones[:, 0:64], pt[:, par::2, :], i == 0, i == len(pts) - 1,
                       [g.cb, ptb], [prb])
            dn, dnb = DN.get()
            S.op("dve", lambda h: h.tensor_tensor(
                out=dn, in0=view3(pr[:, 0:256], 2), in1=bc_last(esink, 128), op=ALU.add), reads=[prb, es_b], writes=[dnb])
            S.op("dve", lambda h: h.reciprocal(out=dn, in_=dn), reads=[dnb], writes=[dnb])
            S.op("dve", lambda h: h.tensor_tensor(
                out=OT[:, 2 * grp:2 * grp + 2, n * 128:(n + 1) * 128], in0=view3(po[:, 0:256], 2), in1=dn, op=ALU.mult),
                reads=[pob, dnb], writes=[OT_b[2 * grp][n // 4], OT_b[2 * grp + 1][n // 4]])


def conv_branch(g, l, OT, OT_b):
    S, A, W = g.S, g.A, g.W
    gT = A.alloc([4, 30 + L], BF16)
    gT_b = [[Buf() for _ in range(5)] for _ in range(4)]
    cwT = A.alloc([4, 32], F32)
    cbv = A.alloc([3, 4], F32)
    pb_ = Buf()
    m1 = A.mark()
    xT = A.alloc([8, 512], BF16)
    xT_b = [Buf() for _ in range(8)]
    wC = A.alloc([8, 1024], BF16)
    wC_b = Buf()
    cwn = A.alloc([512], F32)
    SG = RR([(A.alloc([512], F32), Buf()) for _ in range(2)])
    load_w(g, wC, wC_b, W("mix_w_in")[l, :, 2816:3840].rearrange("(k p) n -> p k n", p=128))
    S.op("pool", lambda h: h.memset(cwn[0:32, :], 0.0), writes=[pb_])
    S.op("sp", lambda h: h.dma_start(out=cwn[0:31, :], in_=W("conv_w")[l]), writes=[pb_], dma=True)
    for i, nm in enumerate(("conv_b", "conv_ln_g", "conv_ln_b")):
        S.op("sp", lambda h, i=i, nm=nm: h.dma_start(out=cbv[:, i, :], in_=W(nm)[l].rearrange("(c p) -> p c", p=128),
                                                     allow_slow_non_contiguous=True), writes=[pb_], dma=True)
    for cc in range(4):
        ps, psb = g.ps.get()
        S.op("pe", lambda h, ps=ps, cc=cc: h.transpose(out=ps[:, 0:32], in_=cwn[0:32, cc * 128:(cc + 1) * 128],
                                                       identity=g.ident[0:32, 0:32]), reads=[pb_, g.cb], writes=[psb])
        S.op("dve", lambda h, ps=ps, cc=cc: h.tensor_copy(out=cwT[:, cc, 0:31], in_=ps[:, 0:31]), reads=[psb], writes=[pb_])
        S.op("pool", lambda h, cc=cc: h.memset(gT[:, cc, 0:30], 0.0), writes=[gT_b[cc][4]])
    for c in range(4):
        gen_xT(g, xT, xT_b, resid_tiles(g, c * 4))
        for cc in range(4):
            pv, pvb = g.ps.get()
            pg, pgb = g.ps.get()
            for k in range(8):
                mm(g, pv, wC[:, k, cc * 128:(cc + 1) * 128], xT[:, k, :], k == 0, k == 7, [wC_b, xT_b[k]], [pvb])
            for k in range(8):
                mm(g, pg, wC[:, k, 512 + cc * 128:512 + (cc + 1) * 128], xT[:, k, :], k == 0, k == 7, [wC_b, xT_b[k]], [pgb])
            sg, sgb = SG.get()
            S.op("act", lambda h, sg=sg, pg=pg: h.activation(out=sg, in_=pg, func=AF.Sigmoid), reads=[pgb], writes=[sgb])
            S.op("dve", lambda h, sg=sg, pv=pv, cc=cc, c=c: h.tensor_tensor(
                out=gT[:, cc, 30 + c * 512:30 + (c + 1) * 512], in0=pv, in1=sg, op=ALU.mult),
                reads=[pvb, sgb], writes=[gT_b[cc][c]])
    g.dbg_out("gT", gT, [b for r in gT_b for b in r])
    g.dbg_out("cwT", cwT, [pb_])
    g.dbg_out("cbv", cbv, [pb_])
    A.reset(m1)
    S.barrier()
    Dg = A.alloc([4, 31, 128], BF16)
    Dg_b = [Buf() for _ in range(4)]
    yb = A.alloc([4, 512], F32)
    sq = A.alloc([4, 512], BF16)
    ybh = A.alloc([4, 512], BF16)
    ybh_b = [Buf() for _ in range(4)]
    yb_b = [Buf() for _ in range(4)]
    sq_b = [Buf() for _ in range(4)]
    Fm = [(A.alloc([512], F32), Buf()) for _ in range(3)]
    def build_dg(cc):
        for j in range(31):
            S.op("dve" if j % 2 == 0 else "pool", lambda h: h.tensor_scalar(
                out=Dg[:, cc, j, :], in0=g.identb, scalar1=cwT[:, cc, j:j + 1], scalar2=None, op0=ALU.mult),
                reads=[pb_, g.cb], writes=[Dg_b[cc]])

    def conv_mm(c):
        banks = []
        if c == 0:
            build_dg(0)
        for cc in range(4):
            if c == 0 and cc + 1 < 4:
                build_dg(cc + 1)
            ps, psb = g.ps.get()
            rd = [Dg_b[cc], gT_b[cc][c]] + ([gT_b[cc][c - 1]] if c > 0 else [gT_b[cc][4]])
            for j in range(31):
                mm(g, ps, Dg[:, cc, j, :], gT[:, cc, c * 512 + j:c * 512 + j + 512], j == 0, j == 30, rd, [psb])
            banks.append((ps, psb))
        return banks
    nxt_banks = conv_mm(0)
    for c in range(4):
        banks = nxt_banks
        for cc in range(4):
            ps, psb = banks[cc]
            S.op("act", lambda h: h.activation(out=yb[:, cc, :], in_=ps, func=AF.Identity,
                                               bias=cbv[:, 0, cc:cc + 1], scale=1.0),
                 reads=[psb, pb_], writes=[yb_b[cc]])
            S.op("act", lambda h: h.activation(out=sq[:, cc, :], in_=ps, func=AF.Square,
                                               bias=cbv[:, 0, cc:cc + 1], scale=1.0),
                 reads=[psb, pb_], writes=[sq_b[cc]])
            S.op("act", lambda h: h.activation(out=ybh[:, cc, :], in_=ps, func=AF.Identity,
                                               bias=cbv[:, 0, cc:cc + 1], scale=1.0),
                 reads=[psb, pb_], writes=[ybh_b[cc]])
        if c + 1 < 4:
            nxt_banks = conv_mm(c + 1)
        if c == 0:
            g.dbg_out("yb0", yb, yb_b)
            g.dbg_out("sq0", sq, sq_b)
        pm1, pm1b = g.ps.get()
        pm2, pm2b = g.ps.get()
        for cc in range(4):
            mm(g, pm1, g.ones, ybh[:, cc, :], cc == 0, cc == 3, [g.cb, ybh_b[cc]], [pm1b])
        for cc in range(4):
            mm(g, pm2, g.ones, sq[:, cc, :], cc == 0, cc == 3, [g.cb, sq_b[cc]], [pm2b])
        (mean, meb), (msq, msb), (rs, rsb) = Fm
        S.op("act", lambda h: h.activation(out=mean, in_=pm1, func=AF.Copy, scale=1.0 / 512.0), reads=[pm1b], writes=[meb])
        S.op("act", lambda h: h.activation(out=msq, in_=pm1, func=AF.Square, scale=1.0 / 512.0), reads=[pm1b], writes=[msb])
        S.op("dve", lambda h: h.scalar_tensor_tensor(out=rs, in0=pm2, scalar=1.0 / 512.0, in1=msq, op0=ALU.mult, op1=ALU.subtract),
             reads=[pm2b, msb], writes=[rsb])
        S.op("act", lambda h: h.activation(out=rs, in_=rs, func=AF.Sqrt, bias=g.eps_ap, scale=1.0), reads=[rsb, g.cb], writes=[rsb])
        S.op("dve", lambda h: h.reciprocal(out=rs, in_=rs), reads=[rsb], writes=[rsb])
        if c == 0:
            g.dbg_out("mean0", mean, [meb])
            g.dbg_out("msq0", msq, [msb])
            g.dbg_out("rs0", rs, [rsb])
        for cc in range(4):
            S.op("pool", lambda h, cc=cc: h.tensor_tensor(out=yb[:, cc, :], in0=yb[:, cc, :], in1=mean, op=ALU.subtract),
                 reads=[yb_b[cc], meb], writes=[yb_b[cc]])
            S.op("dve", lambda h, cc=cc: h.tensor_tensor(out=yb[:, cc, :], in0=yb[:, cc, :], in1=rs, op=ALU.mult),
                 reads=[yb_b[cc], rsb], writes=[yb_b[cc]])
            S.op("act", lambda h, cc=cc, c=c: h.activation(out=OT[:, cc, c * 512:(c + 1) * 512], in_=yb[:, cc, :], func=AF.Silu,
                                                           bias=cbv[:, 2, cc:cc + 1], scale=cbv[:, 1, cc:cc + 1]),
                 reads=[yb_b[cc], pb_], writes=[OT_b[cc][c]])
    g.dbg_out("OTc", OT, [b for r in OT_b for b in r])


def sincos(g, x, s_out, c_out, t1, t2, xb, outb=None, t3=None, t4=None, cb2=None):
    S = g.S
    chains = [(0.0, s_out, t1, t2, Buf()), (math.pi / 2.0, c_out, t3 if t3 is not None else t1, t4 if t4 is not None else t2,
                                           Buf() if t3 is not None else None)]
    if t3 is None:
        chains[1] = chains[1][:4] + (chains[0][4],)
    steps = []
    for shift, out, a, b, cb in chains:
        st = [
            ("dve", lambda h, a=a, shift=shift: h.tensor_scalar(out=a, in0=x, scalar1=1.0 / TWO_PI, scalar2=shift / TWO_PI + MAGIC,
                                                                 op0=ALU.mult, op1=ALU.add)),
            ("dve", lambda h, a=a: h.tensor_scalar(out=a, in0=a, scalar1=-MAGIC, scalar2=None, op0=ALU.add)),
            ("dve", lambda h, a=a, b=b: h.scalar_tensor_tensor(out=b, in0=a, scalar=-CW1, in1=x, op0=ALU.mult, op1=ALU.add)),
            ("dve", lambda h, a=a, b=b: h.scalar_tensor_tensor(out=b, in0=a, scalar=-CW2, in1=b, op0=ALU.mult, op1=ALU.add)),
            ("dve", lambda h, b=b, shift=shift: h.tensor_scalar(out=b, in0=b, scalar1=shift, scalar2=3.1415925, op0=ALU.add, op1=ALU.min)),
            ("dve", lambda h, b=b: h.tensor_scalar(out=b, in0=b, scalar1=-3.1415925, scalar2=None, op0=ALU.max)),
            ("act", lambda h, b=b, out=out: h.activation(out=out, in_=b, func=AF.Sin)),
        ]
        steps.append((st, cb))
    order = []
    if t3 is None:
        for st, cb in steps:
            order += [(e, fn, cb) for e, fn in st]
    else:
        for i in range(7):
            for st, cb in steps:
                order.append(st[i] + (cb,))
    c2 = chains[1][4]
    for idx, (e, fn, cb) in enumerate(order):
        last = (e == "act")
        cbs = cb2 if (cb2 is not None and cb is c2 and t3 is not None) else [cb]
        wr = list(cbs) + ([xb] if last else []) + ([outb] if (last and outb is not None) else [])
        S.op(e, fn, reads=[xb] + list(cbs), writes=wr)


def s5_branch(g, l, OT, OT_b):
    S, A, W = g.S, g.A, g.W
    yT, yT_b = OT, OT_b

    def f(shape):
        return A.alloc(shape, F32)
    sb = Buf("s5small")

    def dv(fn):
        S.op("dve", fn, reads=[sb], writes=[sb])

    def ac(fn):
        S.op("act", fn, reads=[sb], writes=[sb])
    WBr = A.alloc([16, 128], BF16)
    WBi = A.alloc([16, 128], BF16)
    WCr = A.alloc([16, 64], BF16)
    WCn = A.alloc([16, 64], BF16)
    wmat_b = Buf()
    mag, s128, c128 = f([16]), f([16]), f([16])
    dsk, glb = f([4]), f([4])
    uT = A.alloc([4, L], BF16)
    uT_b = [[Buf() for _ in range(4)] for _ in range(4)]
    th = f([16])
    m1 = A.mark()
    ar, ai, ls, sn, cs, t1, t2 = (f([16]) for _ in range(7))
    abr, abi, den, nr, cr, ci, t3 = (f([16]) for _ in range(7))
    Bs_r, Bs_i = f([16, 128]), f([16, 128])
    braw_r, braw_i = f([16, 16]), f([16, 16])
    tb1 = f([4, 16])
    S.op("sp", lambda h: h.dma_start(out=ar, in_=W("s5_a_re")[l].rearrange("(ct gl) p -> (gl p) ct", gl=2),
                                     allow_slow_non_contiguous=True), writes=[sb], dma=True)
    S.op("sp", lambda h: h.dma_start(out=ai, in_=W("s5_a_im")[l].rearrange("(ct gl) p -> (gl p) ct", gl=2),
                                     allow_slow_non_contiguous=True), writes=[sb], dma=True)
    for gl in range(2):
        S.op("sp", lambda h, gl=gl: h.dma_start(
            out=ls[64 * gl:64 * gl + 64, :],
            in_=W("s5_log_step")[l].rearrange("(ct gl) -> gl ct", gl=2)[gl:gl + 1, :].to_broadcast([64, 16]),
            allow_slow_non_contiguous=True), writes=[sb], dma=True)
    S.op("sp", lambda h: h.dma_start(out=braw_r, in_=W("s5_b_re")[l].rearrange("(ct gl) p c -> (gl p) ct c", gl=2)),
         writes=[sb], dma=True)
    S.op("sp", lambda h: h.dma_start(out=braw_i, in_=W("s5_b_im")[l].rearrange("(ct gl) p c -> (gl p) ct c", gl=2)),
         writes=[sb], dma=True)
    for i, nm in enumerate(("s5_d", "s5_glu_b")):
        dst = (dsk, glb)[i]
        S.op("sp", lambda h, dst=dst, nm=nm: h.dma_start(out=dst, in_=W(nm)[l].rearrange("(c p) -> p c", p=128),
                                                         allow_slow_non_contiguous=True), writes=[sb], dma=True)
    S.op("pool", lambda h: h.memset(Bs_r, 0.0), writes=[sb])
    S.op("pool", lambda h: h.memset(Bs_i, 0.0), writes=[sb])
    ac(lambda h: h.activation(out=ls, in_=ls, func=AF.Exp))
    dv(lambda h: h.tensor_tensor(out=t1, in0=ar, in1=ls, op=ALU.mult))
    ac(lambda h: h.activation(out=mag, in_=t1, func=AF.Exp))
    dv(lambda h: h.tensor_tensor(out=th, in0=ai, in1=ls, op=ALU.mult))
    sincos(g, th, sn, cs, t1, t2, sb)
    dv(lambda h: h.tensor_tensor(out=abr, in0=mag, in1=cs, op=ALU.mult))
    dv(lambda h: h.tensor_tensor(out=abi, in0=mag, in1=sn, op=ALU.mult))
    dv(lambda h: h.tensor_tensor(out=t1, in0=ar, in1=ar, op=ALU.mult))
    dv(lambda h: h.tensor_tensor(out=den, in0=ai, in1=ai, op=ALU.mult))
    dv(lambda h: h.tensor_tensor(out=den, in0=den, in1=t1, op=ALU.add))
    dv(lambda h: h.reciprocal(out=den, in_=den))
    dv(lambda h: h.tensor_scalar(out=nr, in0=abr, scalar1=-1.0, scalar2=None, op0=ALU.add))
    dv(lambda h: h.tensor_tensor(out=t1, in0=nr, in1=ar, op=ALU.mult))
    dv(lambda h: h.tensor_tensor(out=t2, in0=abi, in1=ai, op=ALU.mult))
    dv(lambda h: h.tensor_tensor(out=t1, in0=t1, in1=t2, op=ALU.add))
    dv(lambda h: h.tensor_tensor(out=cr, in0=t1, in1=den, op=ALU.mult))
    dv(lambda h: h.tensor_tensor(out=t1, in0=abi, in1=ar, op=ALU.mult))
    dv(lambda h: h.tensor_tensor(out=t2, in0=nr, in1=ai, op=ALU.mult))
    dv(lambda h: h.tensor_tensor(out=t1, in0=t1, in1=t2, op=ALU.subtract))
    dv(lambda h: h.tensor_tensor(out=ci, in0=t1, in1=den, op=ALU.mult))
    dv(lambda h: h.tensor_scalar(out=t3, in0=th, scalar1=512.0, scalar2=None, op0=ALU.mult))
    sincos(g, t3, s128, c128, t1, t2, sb)
    for q in range(4):
        for gl in range(2):
            rows = slice(64 * gl, 64 * gl + 64)
            c0 = 32 * q + 16 * gl
            crb = bc_last(cr[rows, q::4], 16)
            cib = bc_last(ci[rows, q::4], 16)
            br_, bi_ = braw_r[rows, q::4, :], braw_i[rows, q::4, :]
            o_r, o_i = Bs_r[rows, q::4, c0:c0 + 16], Bs_i[rows, q::4, c0:c0 + 16]
            tt = tb1[rows]
            dv(lambda h, tt=tt, bi_=bi_, cib=cib: h.tensor_tensor(out=tt, in0=bi_, in1=cib, op=ALU.mult))
            dv(lambda h, o_r=o_r, br_=br_, crb=crb: h.tensor_tensor(out=o_r, in0=br_, in1=crb, op=ALU.mult))
            dv(lambda h, o_r=o_r, tt=tt: h.tensor_tensor(out=o_r, in0=o_r, in1=tt, op=ALU.subtract))
            dv(lambda h, tt=tt, br_=br_, cib=cib: h.tensor_tensor(out=tt, in0=br_, in1=cib, op=ALU.mult))
            dv(lambda h, o_i=o_i, bi_=bi_, crb=crb: h.tensor_tensor(out=o_i, in0=bi_, in1=crb, op=ALU.mult))
            dv(lambda h, o_i=o_i, tt=tt: h.tensor_tensor(out=o_i, in0=o_i, in1=tt, op=ALU.add))
    for ct in range(16):
        q = ct % 4
        for i, (src, dst) in enumerate(((Bs_r, WBr), (Bs_i, WBi))):
            ps, psb = g.ps.get()
            S.op("pe", lambda h, ps=ps, src=src, ct=ct: h.transpose(out=ps[:, 0:128], in_=src[:, ct, :], identity=g.ident),
                 reads=[sb, g.cb], writes=[psb])
            er = slice(64, 128) if q == 3 else slice(32 * q, 32 * q + 32)
            evac(g, i, dst[er, ct, :], ps[er, 0:128], [psb], [wmat_b])
    A.reset(m1)
    S.barrier()
    Cs_r, Cs_i = f([16, 128]), f([16, 128])
    S.op("pool", lambda h: h.memset(Cs_r[0:64], 0.0), writes=[sb])
    S.op("pool", lambda h: h.memset(Cs_i[0:64], 0.0), writes=[sb])
    for gl in range(2):
        for par in range(2):
            for (dst, nm) in ((Cs_r, "s5_c_re"), (Cs_i, "s5_c_im")):
                p0 = 32 * par + 16 * gl
                S.op("sp", lambda h, gl=gl, par=par, dst=dst, nm=nm, p0=p0: h.dma_start(
                    out=dst[p0:p0 + 16, par::2, 64 * gl:64 * gl + 64],
                    in_=W(nm)[l].rearrange("(ct gl) c p -> gl c ct p", gl=2)[gl][:, par::2, :]), writes=[sb], dma=True)
    for ct in range(16):
        for i, (src, dst, scl) in enumerate(((Cs_r, WCr, None), (Cs_i, WCn, -1.0))):
            ps, psb = g.ps.get()
            S.op("pe", lambda h, ps=ps, src=src, ct=ct: h.transpose(out=ps[:, 0:64], in_=src[0:64, ct, :],
                                                                    identity=g.ident[0:64, 0:64]),
                 reads=[sb, g.cb], writes=[psb])
            evac(g, i, dst[:, ct, :], ps[:, 0:64], [psb], [wmat_b], scale=scl)
    A.reset(m1)
    S.barrier()
    xT = A.alloc([8, 512], BF16)
    xT_b = [Buf() for _ in range(8)]
    wS = A.alloc([8, 512], BF16)
    wS_b = Buf()
    load_w(g, wS, wS_b, W("mix_w_in")[l, :, 2304:2816].rearrange("(k p) n -> p k n", p=128))
    for c in range(4):
        gen_xT(g, xT, xT_b, resid_tiles(g, c * 4))
        for cc in range(4):
            ps, psb = g.ps.get()
            for k in range(8):
                mm(g, ps, wS[:, k, cc * 128:(cc + 1) * 128], xT[:, k, :], k == 0, k == 7, [wS_b, xT_b[k]], [psb])
            evac(g, cc, uT[:, cc, c * 512:(c + 1) * 512], ps, [psb], [uT_b[cc][c]])
    A.reset(m1)
    S.barrier()
    iota = f([512])
    S.op("sp", lambda h: h.dma_start(out=iota, in_=W("c_iota")), writes=[sb], dma=True)
    TB = RR([((f([512]), f([512])), Buf()) for _ in range(2)])
    angj, sj1, sj2 = f([512]), f([512]), f([512])
    sjb = Buf()
    brp, bip, ta, tb = f([512]), f([512]), f([512]), f([512])
    dB = Buf()
    brp_b, bip_b, ta_b, tb_b = Buf(), Buf(), Buf(), Buf()
    W2 = RR([((f([512]), f([512])), (Buf(), Buf())) for _ in range(2)])
    pa, pb2, pc, pd = f([512]), f([512]), f([512]), f([512])
    pa_b, pb_b, pc_b, pd_b = Buf(), Buf(), Buf(), Buf()
    INI = RR([(f([4]), Buf()) for _ in range(2)])
    XR = RR([(A.alloc([512], BF16), Buf()) for _ in range(2)])
    XI = RR([(A.alloc([512], BF16), Buf()) for _ in range(2)])
    vv, v2, vb = brp, bip, dB
    yacc = g.banks[0:4]
    rot = RR(g.banks[4:8])
    bu_q = []

    def bu_mm(cc, q, c):
        ct = 4 * cc + q
        rows = slice(64, 128) if q == 3 else slice(32 * q, 32 * q + 32)
        pbr, pbrb = rot.get()
        pbi, pbib = rot.get()
        mm(g, pbr, WBr[rows, ct, :], uT[rows, cc, c * 512:(c + 1) * 512], True, True, [wmat_b, uT_b[cc][c]], [pbrb])
        mm(g, pbi, WBi[rows, ct, :], uT[rows, cc, c * 512:(c + 1) * 512], True, True, [wmat_b, uT_b[cc][c]], [pbib])
        return pbr, pbrb, pbi, pbib
    for cc in range(4):
        for q in range(4):
            ct = 4 * cc + q
            rows = slice(64, 128) if q == 3 else slice(32 * q, 32 * q + 32)
            orow = slice(64 * (q // 2), 64 * (q // 2) + 64)
            (sinT, cosT), tbb = TB.get()
            S.op("dve", lambda h: h.tensor_scalar(out=angj, in0=iota, scalar1=th[:, ct:ct + 1], scalar2=None, op0=ALU.mult),
                 reads=[sb, sjb], writes=[sjb])
            sincos(g, angj, sinT, cosT, sj1, sj2, sjb, outb=tbb, t3=ta, t4=tb, cb2=[ta_b, tb_b])
            rho = mag[:, ct:ct + 1].to_broadcast([128, 512])
            cc_, ss_ = c128[:, ct:ct + 1], s128[:, ct:ct + 1]
            prev = None
            for c in range(4):
                py, pyb = yacc[c]
                if not bu_q:
                    bu_q.append(bu_mm(cc, q, c))
                pbr, pbrb, pbi, pbib = bu_q.pop(0)
                nx = (cc, q, c + 1) if c + 1 < 4 else ((cc, q + 1, 0) if q + 1 < 4 else ((cc + 1, 0, 0) if cc + 1 < 4 else None))
                if nx is not None:
                    bu_q.append(bu_mm(*nx))
                S.op("dve", lambda h: h.tensor_tensor(out=ta, in0=pbi, in1=sinT, op=ALU.mult), reads=[pbib, tbb], writes=[ta_b])
                S.op("dve", lambda h: h.tensor_tensor(out=tb, in0=pbr, in1=sinT, op=ALU.mult), reads=[pbrb, tbb], writes=[tb_b])
                S.op("dve", lambda h: h.tensor_tensor(out=brp, in0=pbr, in1=cosT, op=ALU.mult), reads=[pbrb, tbb], writes=[brp_b])
                S.op("dve", lambda h: h.tensor_tensor(out=bip, in0=pbi, in1=cosT, op=ALU.mult), reads=[pbib, tbb], writes=[bip_b])
                S.op("dve", lambda h: h.tensor_tensor(out=brp, in0=brp, in1=ta, op=ALU.add), reads=[brp_b, ta_b], writes=[brp_b])
                S.op("dve", lambda h: h.tensor_tensor(out=bip, in0=bip, in1=tb, op=ALU.subtract), reads=[bip_b, tb_b], writes=[bip_b])
                (wr, wi), (wr_b, wi_b) = W2.get()
                if prev is None:
                    i_re, i_im, ird = 0.0, 0.0, []
                else:
                    (pwr, pwi), (pwr_b, pwi_b) = prev
                    ini, inib = INI.get()
                    lr, li = pwr[:, 511:512], pwi[:, 511:512]
                    rdl = [pwr_b, pwi_b, sb]
                    S.op("dve", lambda h: h.tensor_scalar(out=ini[:, 0:1], in0=li, scalar1=ss_, scalar2=None, op0=ALU.mult),
                         reads=rdl, writes=[inib])
                    S.op("dve", lambda h: h.tensor_scalar(out=ini[:, 2:3], in0=li, scalar1=cc_, scalar2=None, op0=ALU.mult),
                         reads=rdl, writes=[inib])
                    S.op("dve", lambda h: h.scalar_tensor_tensor(out=ini[:, 1:2], in0=lr, scalar=cc_, in1=ini[:, 0:1],
                                                                 op0=ALU.mult, op1=ALU.subtract), reads=rdl + [inib], writes=[inib])
                    S.op("dve", lambda h: h.scalar_tensor_tensor(out=ini[:, 3:4], in0=lr, scalar=ss_, in1=ini[:, 2:3],
                                                                 op0=ALU.mult, op1=ALU.add), reads=rdl + [inib], writes=[inib])
                    i_re, i_im, ird = ini[:, 1:2], ini[:, 3:4], [inib]
                S.op("dve", lambda h: h.tensor_tensor_scan(out=wr, data0=rho, data1=brp, initial=i_re, op0=ALU.mult, op1=ALU.add),
                     reads=[brp_b, sb] + ird, writes=[wr_b])
                S.op("dve", lambda h: h.tensor_tensor_scan(out=wi, data0=rho, data1=bip, initial=i_im, op0=ALU.mult, op1=ALU.add),
                     reads=[bip_b, sb] + ird, writes=[wi_b])
                prev = ((wr, wi), (wr_b, wi_b))
                xr, xrb = XR.get()
                xi, xib = XI.get()
                S.op("pool", lambda h: h.tensor_tensor(out=pa, in0=wr, in1=cosT, op=ALU.mult), reads=[wr_b, tbb, sjb], writes=[pa_b])
                S.op("pool", lambda h: h.tensor_tensor(out=pb2, in0=wi, in1=sinT, op=ALU.mult), reads=[wi_b, tbb, sjb], writes=[pb_b])
                S.op("pool", lambda h: h.tensor_tensor(out=pc, in0=wi, in1=cosT, op=ALU.mult), reads=[wi_b, tbb, sjb], writes=[pc_b])
                S.op("pool", lambda h: h.tensor_tensor(out=pd, in0=wr, in1=sinT, op=ALU.mult), reads=[wr_b, tbb, sjb], writes=[pd_b])
                S.op("pool", lambda h: h.tensor_tensor(out=xr, in0=pa, in1=pb2, op=ALU.subtract), reads=[pa_b, pb_b], writes=[xrb])
                S.op("pool", lambda h: h.tensor_tensor(out=xi, in0=pc, in1=pd, op=ALU.add), reads=[pc_b, pd_b], writes=[xib])
                mm(g, py[orow, :], WCr[:, ct, :], xr, q % 2 == 0, False, [wmat_b, xrb], [pyb])
                mm(g, py[orow, :], WCn[:, ct, :], xi, False, q % 2 == 1, [wmat_b, xib], [pyb])
        for c in range(4):
            py, pyb = yacc[c]
            S.op("dve", lambda h: h.scalar_tensor_tensor(
                out=vv, in0=uT[:, cc, c * 512:(c + 1) * 512], scalar=dsk[:, cc:cc + 1], in1=py, op0=ALU.mult, op1=ALU.add),
                reads=[pyb, uT_b[cc][c], sb, vb], writes=[vb])
            S.op("act", lambda h: h.activation(out=v2, in_=vv, func=AF.Square), reads=[vb], writes=[vb])
            S.op("dve", lambda h: h.tensor_scalar(out=v2, in0=v2, scalar1=0.044715, scalar2=1.0, op0=ALU.mult, op1=ALU.add),
                 reads=[vb], writes=[vb])
            S.op("dve", lambda h: h.tensor_tensor(out=v2, in0=v2, in1=vv, op=ALU.mult), reads=[vb], writes=[vb])
            S.op("act", lambda h: h.activation(out=v2, in_=v2, func=AF.Sigmoid, scale=1.5957691216057308), reads=[vb], writes=[vb])
            S.op("dve", lambda h: h.tensor_tensor(out=yT[:, cc, c * 512:(c + 1) * 512], in0=vv, in1=v2, op=ALU.mult),
                 reads=[vb], writes=[yT_b[cc][c]])
    A.reset(m1)
    S.barrier()
    gw = A.alloc([4, 512], BF16)
    gw_b = Buf()
    load_w(g, gw, gw_b, W("s5_glu_w")[l].rearrange("(k p) n -> p k n", p=128))
    vv = f([512])
    vb = Buf()
    for c in range(4):
        zs = []
        for oc in range(4):
            ps, psb = g.banks[oc]
            for kc in range(4):
                mm(g, ps, gw[:, kc, oc * 128:(oc + 1) * 128], yT[:, kc, c * 512:(c + 1) * 512], kc == 0, kc == 3,
                   [gw_b, yT_b[kc][c]], [psb])
            zs.append((ps, psb))
        for oc in range(4):
            ps, psb = zs[oc]
            S.op("act", lambda h, ps=ps, oc=oc: h.activation(out=vv, in_=ps, func=AF.Sigmoid, bias=glb[:, oc:oc + 1], scale=1.0),
                 reads=[psb, sb, vb], writes=[vb])
            S.op("dve", lambda h, oc=oc, c=c: h.tensor_tensor(out=yT[:, oc, c * 512:(c + 1) * 512],
                                                              in0=yT[:, oc, c * 512:(c + 1) * 512], in1=vv, op=ALU.mult),
                 reads=[vb, yT_b[oc][c]], writes=[yT_b[oc][c]])


PARTS = ("ffn1", "mix", "cross", "ffn2")
_cache = {}


def run(inputs, layers, parts, xin, branches="abcd"):
    key = (tuple(layers), tuple(parts), branches)
    if key not in _cache:
        _cache[key] = build(layers, parts, branches)
    nc, g = _cache[key]
    in_maps = []
    shared = {}
    for name in g.dr:
        if name in ("x", "mem"):
            continue
        if name.startswith("c_"):
            shared[name] = consts()[name][0]
        else:
            shared[name] = np.ascontiguousarray(np.asarray(inputs[name], dtype=np.float32))
    for c in range(NCORES):
        m = dict(shared)
        m["x"] = np.ascontiguousarray(xin[c * SEQ_PER_CORE:(c + 1) * SEQ_PER_CORE])
        if "mem" in g.dr:
            m["mem"] = np.ascontiguousarray(np.asarray(inputs["mem"], dtype=np.float32)[c * SEQ_PER_CORE:(c + 1) * SEQ_PER_CORE])
        in_maps.append(m)
    res = run_bass_kernel_spmd(nc, in_maps, core_ids=list(range(NCORES)))
    if DEBUG:
        LAST.update({k: v for k, v in res.results[0].items()})
    return np.concatenate([r["out"] for r in res.results], axis=0)


FUSED = True


def kernel(**inputs):
    x = np.asarray(inputs["x"], dtype=np.float32)
    if FUSED:
        return run(inputs, list(range(DEPTH)), PARTS, x)
    for l in range(DEPTH):
        x = run(inputs, [l], PARTS, x)
    return x
```

```python
from contextlib import ExitStack
import math
import numpy as np
import concourse.bass as bass
import concourse.mybir as mybir
from concourse.bass_utils import run_bass_kernel_spmd

F32 = mybir.dt.float32
BF16 = mybir.dt.bfloat16
AF = mybir.ActivationFunctionType
ALU = mybir.AluOpType

D = 1024
L = 2048
NT = L // 128
DEPTH = 4
FFN = 2816
NJ = FFN // 128
MIX_IN = 7936
ALPHA = (2.0 * DEPTH) ** 0.25
EPS = 1e-5
NCORES = 8
SEQ_PER_CORE = 2

ENGS = ["pe", "act", "dve", "pool", "sp"]
NSEM = 4
PHASE = 2048
NDSEM = 40


class Buf:
    __slots__ = ("w", "r", "name")

    def __init__(self, name=""):
        self.w = None
        self.r = {}
        self.name = name


class _Rec:
    def __init__(self):
        self.call = None

    def __getattr__(self, name):
        def f(*a, **k):
            self.call = (name, a, k)
            return self
        return f


class Sched:
    def __init__(self, nc, es):
        self.nc = nc
        self.h = dict(pe=nc.tensor, act=nc.scalar, dve=nc.vector, pool=nc.gpsimd, sp=nc.sync)
        self.q = {e: [] for e in ENGS}
        self.cnt = {e: 0 for e in ENGS}
        self.semh = {}
        self.esval = {}
        for e in ENGS:
            for i in range(NSEM):
                self.semh[("e", e, i)] = es.enter_context(nc.semaphore(f"s_{e}{i}"))
                self.esval[("e", e, i)] = 0
        self.dval = [0] * NDSEM
        for i in range(NDSEM):
            self.semh[("d", i)] = es.enter_context(nc.semaphore(f"d{i}"))
        self.dnext = 0
        self.seen = {e: {} for e in ENGS}
        self.last = {}
        self.nwait = 0

    def _wait(self, eng, k, v):
        if self.seen[eng].get(k, 0) >= v:
            return
        self.seen[eng][k] = v
        sem = self.semh[k]
        self.q[eng].append(lambda h, sem=sem, v=v: h.wait_ge(sem, v))
        self.nwait += 1

    def op(self, eng, fn, reads=(), writes=(), dma=False):
        rec = _Rec()
        fn(rec)
        _name, _a, _k = rec.call

        def fn(h, _name=_name, _a=_a, _k=_k):
            return getattr(h, _name)(*_a, **_k)
        deps = {}

        def add(k, v):
            if deps.get(k, 0) < v:
                deps[k] = v

        for b in reads:
            if b.w is not None:
                add(*b.w)
        for b in writes:
            if b.w is not None:
                add(*b.w)
            for k, v in b.r.items():
                if k[0] == "e" and k[1] == eng:
                    continue
                add(k, v)
        for k, v in deps.items():
            if eng == "pe" and k[0] == "e" and k[1] == "pe":
                continue
            self._wait(eng, k, v)
        if dma:
            i = self.dnext
            self.dnext = (i + 1) % NDSEM
            k = ("d", i)
            prev = self.dval[i]
            if prev:
                self._wait(eng, k, prev)
            self.dval[i] = prev + 16
            tok = (k, prev + 16)
            sem = self.semh[k]
            self.q[eng].append(lambda h, fn=fn, sem=sem: fn(h).then_inc(sem, 16))
        else:
            c = self.cnt[eng]
            self.cnt[eng] = c + 1
            k = ("e", eng, (c // PHASE) % NSEM)
            self.esval[k] += 1
            tok = (k, self.esval[k])
            sem = self.semh[k]
            self.q[eng].append(lambda h, fn=fn, sem=sem: fn(h).then_inc(sem, 1))
        self.last[tok[0]] = tok[1]
        for b in writes:
            b.w = tok
            b.r = {}
        for b in reads:
            if b.r.get(tok[0], 0) < tok[1]:
                b.r[tok[0]] = tok[1]
        return tok

    def barrier(self):
        for e in ENGS:
            for k, v in self.last.items():
                if k[0] == "e" and k[1] == e:
                    continue
                self._wait(e, k, v)

    def finish(self, eng="sp"):
        for k, v in self.last.items():
            self._wait(eng, k, v)

    def replay(self, block):
        q = self.q

        @block.tensor
        def _(h):
            for f in q["pe"]:
                f(h)

        @block.scalar
        def _(h):
            for f in q["act"]:
                f(h)

        @block.vector
        def _(h):
            for f in q["dve"]:
                f(h)

        @block.gpsimd
        def _(h):
            for f in q["pool"]:
                f(h)

        @block.sync
        def _(h):
            for f in q["sp"]:
                f(h)


class RR:
    def __init__(self, aps, name="p"):
        self.slots = [a if isinstance(a, tuple) else (a, Buf(f"{name}{i}")) for i, a in enumerate(aps)]
        self.i = 0

    def get(self):
        s = self.slots[self.i]
        self.i = (self.i + 1) % len(self.slots)
        return s


class Arena:
    def __init__(self, t_bf):
        self.tb = t_bf
        self.tf = t_bf.bitcast(F32)
        self.off = 0
        self.cap = t_bf.shape[1] * 2

    def mark(self):
        return self.off

    def reset(self, m):
        self.off = m

    def alloc(self, shape, dt):
        esz = 4 if dt == F32 else 2
        n = int(np.prod(shape))
        self.off = (self.off + 63) // 64 * 64
        o = self.off
        self.off += n * esz
        assert self.off <= self.cap, (self.off, self.cap)
        t = self.tf if dt == F32 else self.tb
        ap = t[:, o // esz: o // esz + n]
        if len(shape) == 2:
            ap = ap.rearrange("p (a b) -> p a b", a=shape[0])
        elif len(shape) == 3:
            ap = ap.rearrange("p (a b c) -> p a b c", a=shape[0], b=shape[1])
        return ap


class K:
    pass


SHAPES = {
    "x": [SEQ_PER_CORE, L, D], "mem": [SEQ_PER_CORE, 256, D],
    "ffn1_w_in": [DEPTH, D, 2 * FFN], "ffn1_w_out": [DEPTH, FFN, D], "ffn1_ln_g": [DEPTH, D], "ffn1_ln_b": [DEPTH, D],
    "mix_w_in": [DEPTH, D, MIX_IN], "swa_sinks": [DEPTH, 8], "swa_proj": [DEPTH, 512, D],
    "s5_a_re": [DEPTH, 32, 64], "s5_a_im": [DEPTH, 32, 64], "s5_log_step": [DEPTH, 32],
    "s5_b_re": [DEPTH, 32, 64, 16], "s5_b_im": [DEPTH, 32, 64, 16], "s5_c_re": [DEPTH, 32, 16, 64],
    "s5_c_im": [DEPTH, 32, 16, 64], "s5_d": [DEPTH, 512], "s5_glu_w": [DEPTH, 512, 512], "s5_glu_b": [DEPTH, 512],
    "s5_proj": [DEPTH, 512, D], "conv_w": [DEPTH, 31, 512], "conv_b": [DEPTH, 512], "conv_ln_g": [DEPTH, 512],
    "conv_ln_b": [DEPTH, 512], "conv_proj": [DEPTH, 512, D], "diff_lq1": [DEPTH, 64], "diff_lk1": [DEPTH, 64],
    "diff_lq2": [DEPTH, 64], "diff_lk2": [DEPTH, 64], "diff_norm_g": [DEPTH, 128], "diff_proj": [DEPTH, 512, D],
    "mix_w_out": [DEPTH, D, D], "mix_ln_g": [DEPTH, D], "mix_ln_b": [DEPTH, D], "mem_ln_g": [D], "mem_ln_b": [D],
    "cross_wq": [DEPTH, D, 512], "cross_wkv": [DEPTH, D, D], "cross_wo": [DEPTH, 512, D],
    "cross_ln_g": [DEPTH, D], "cross_ln_b": [DEPTH, D],
    "ffn2_w_in": [DEPTH, D, 2 * FFN], "ffn2_w_out": [DEPTH, FFN, D], "ffn2_ln_g": [DEPTH, D], "ffn2_ln_b": [DEPTH, D],
}
NEG = -240000.0
TWO_PI = 2.0 * math.pi
CW1 = 6.28125
CW2 = TWO_PI - CW1
MAGIC = 12582912.0
DIFF_SLOPES = [2.0 ** (-8.0 * (h + 1) / 4) for h in range(4)]
SWA_SLOPES = [2.0 ** (-8.0 * (h + 1) / 8) for h in range(8)]
CONSTS = {}
DEBUG = False
LAST = {}


def consts():
    if not CONSTS:
        import ml_dtypes
        bf = ml_dtypes.bfloat16
        CONSTS["c_ident"] = (np.eye(128, dtype=np.float32), F32)
        CONSTS["c_iota"] = (np.tile(np.arange(512, dtype=np.float32)[None, :], (128, 1)), F32)
        pos = np.arange(L)
        pa, pb = (pos // 128).astype(np.float32), (pos % 128).astype(np.float32)
        kb = np.stack([np.ones(L), np.ones(L), 128.0 * pa, pb]).astype(np.float32)
        CONSTS["c_kb"] = (kb.astype(bf), BF16)

        def qb(slopes):
            o = np.zeros((4, len(slopes), L), np.float32)
            for h, s in enumerate(slopes):
                o[0, h] = -8.0 * s * 128.0 * pa
                o[1, h] = -8.0 * s * pb
                o[2, h] = 8.0 * s
                o[3, h] = 8.0 * s
            return o.astype(bf)
        CONSTS["c_qb_diff"] = (qb(DIFF_SLOPES), BF16)
        CONSTS["c_qb_swa"] = (qb(SWA_SLOPES), BF16)
        ki = np.arange(128)[:, None]
        md = np.zeros((128, 4, 512), np.float32)
        qi = np.arange(512)[None, :]
        for rel in range(4):
            md[:, rel, :] = np.where(qi >= 128 * rel + ki, 0.0, NEG)
        CONSTS["c_mask_diff"] = (md.astype(bf), BF16)
        ms = np.zeros((128, 2, 128), np.float32)
        q1 = np.arange(128)[None, :]
        ms[:, 0, :] = np.where(ki > q1, 0.0, NEG)
        ms[:, 1, :] = np.where(ki <= q1, 0.0, NEG)
        CONSTS["c_mask_swa"] = (ms.astype(bf), BF16)
    return CONSTS


def build(layers, parts, branches="abcd", nseq=SEQ_PER_CORE):
    nc = bass.Bass("TRN2", target_bir_lowering=False)
    es = ExitStack()
    with es:
        g = K()
        g.nc = nc
        S = Sched(nc, es)
        g.S = S
        g.dr = {}

        def W(name):
            if name not in g.dr:
                if name.startswith("c_"):
                    arr, dt = consts()[name]
                    g.dr[name] = nc.dram_tensor(name, list(arr.shape), dt, kind="ExternalInput").ap()
                else:
                    g.dr[name] = nc.dram_tensor(name, list(SHAPES[name]), F32, kind="ExternalInput").ap()
            return g.dr[name]
        g.W = W
        g.dbg = {}

        def dbg_out(name, ap, bufs):
            if not DEBUG or name in g.dbg:
                return
            t = nc.dram_tensor("dbg_" + name, list(ap.shape), ap.dtype, kind="ExternalOutput").ap()
            g.dbg[name] = t
            S.op("sp", lambda h: h.dma_start(out=t, in_=ap), reads=bufs, dma=True)
        g.dbg_out = dbg_out
        out = nc.dram_tensor("out", [nseq, L, D], F32, kind="ExternalOutput").ap()

        arena_t = es.enter_context(nc.sbuf_tensor("arena", [128, 106400], BF16))
        A = Arena(arena_t)
        g.A = A
        g.resid = A.alloc([NT, D], F32)
        g.resid_b = [Buf(f"resid{t}") for t in range(NT)]
        g.ident = A.alloc([128], F32)
        g.identb = A.alloc([128], BF16)
        g.ones = A.alloc([128], BF16)
        g.memT = A.alloc([8, 256], BF16)
        g.memT_b = [Buf() for _ in range(8)]
        g.eps_ap = A.alloc([1], F32)
        g.eps4_ap = A.alloc([1], F32)
        g.cb = Buf("consts")
        S.op("sp", lambda h: h.dma_start(out=g.ident, in_=W("c_ident")), writes=[g.cb], dma=True)
        S.op("dve", lambda h: h.tensor_copy(out=g.identb, in_=g.ident), reads=[g.cb], writes=[g.cb])
        S.op("dve", lambda h: h.memset(g.ones, 1.0), writes=[g.cb])
        S.op("dve", lambda h: h.memset(g.eps_ap, EPS), writes=[g.cb])
        S.op("dve", lambda h: h.memset(g.eps4_ap, 4.0 * EPS), writes=[g.cb])
        g.eps_b = g.cb
        g.ident_b = g.cb
        g.banks = [(es.enter_context(nc.psum_tensor(f"ps{i}", [128, 512], F32))[:], Buf(f"ps{i}")) for i in range(8)]
        g.ps = RR(g.banks)
        pmark = A.mark()

        for s in range(nseq):
            for t in range(NT):
                S.op("sp", lambda h, t=t, s=s: h.dma_start(out=g.resid[:, t, :], in_=W("x")[s, t * 128:(t + 1) * 128, :]),
                     writes=[g.resid_b[t]], dma=True)
            if "cross" in parts:
                A.reset(pmark)
                S.barrier()
                prep_mem(g, s)
            for l in layers:
                for p in parts:
                    A.reset(pmark)
                    S.barrier()
                    if p in ("ffn1", "ffn2"):
                        ffn(g, l, p)
                    elif p == "mix":
                        mixer(g, l, branches)
                    elif p == "cross":
                        cross(g, l)
            for t in range(NT):
                S.op("sp", lambda h, t=t, s=s: h.dma_start(out=out[s, t * 128:(t + 1) * 128, :], in_=g.resid[:, t, :]),
                     reads=[g.resid_b[t]], dma=True)
        S.finish("sp")
        with nc.Block() as block:
            S.replay(block)
        g.stats = dict(cnt=dict(S.cnt), nwait=S.nwait)
    return nc, g


def load_w(g, dst, dst_b, src, eng="pool"):
    g.S.op(eng, lambda h: h.dma_start(out=dst, in_=src), writes=[dst_b], dma=True)


def mm(g, out, lhsT, rhs, start, stop, reads, writes):
    g.S.op("pe", lambda h: h.matmul(out, lhsT=lhsT, rhs=rhs, start=start, stop=stop), reads=reads, writes=writes)


def evac(g, i, out, in_, reads, writes, scale=None):
    if i % 2 == 0:
        if scale is None:
            g.S.op("act", lambda h: h.activation(out=out, in_=in_, func=AF.Copy), reads=reads, writes=writes)
        else:
            g.S.op("act", lambda h: h.activation(out=out, in_=in_, func=AF.Copy, scale=scale), reads=reads, writes=writes)
    else:
        if scale is None:
            g.S.op("dve", lambda h: h.tensor_copy(out=out, in_=in_), reads=reads, writes=writes)
        else:
            g.S.op("dve", lambda h: h.tensor_scalar(out=out, in0=in_, scalar1=scale, scalar2=None, op0=ALU.mult),
                   reads=reads, writes=writes)


def gen_xT(g, xT, xT_b, srcs):
    S = g.S
    n = len(srcs)
    for k in range(8):
        ps, psb = g.ps.get()
        for t, (src, sb) in enumerate(srcs):
            S.op("pe", lambda h, ps=ps, t=t, k=k, src=src: h.transpose(
                out=ps[:, t * 128:(t + 1) * 128], in_=src[:, k * 128:(k + 1) * 128], identity=g.ident),
                reads=[sb, g.ident_b], writes=[psb])
        evac(g, k, xT[:, k, 0:128 * n], ps[:, 0:128 * n], [psb], [xT_b[k]])


def resid_tiles(g, t0, n=4):
    return [(g.resid[:, t0 + i, :], g.resid_b[t0 + i]) for i in range(n)]


def ln_alloc(g, n=3):
    A = g.A
    sets = RR([((A.alloc([12], F32), A.alloc([2], F32), A.alloc([1], F32), A.alloc([1], F32)), (Buf("ln1"), Buf("ln2")))
               for _ in range(n)])
    return sets, None


def load_ln(g, gname, bname, l):
    A, S, W = g.A, g.S, g.W
    lng = A.alloc([D], F32)
    lnb = A.alloc([D], F32)
    b = Buf("lnp")
    gs = W(gname)[l:l + 1, :] if l is not None else W(gname).rearrange("(o d) -> o d", o=1)
    bs = W(bname)[l:l + 1, :] if l is not None else W(bname).rearrange("(o d) -> o d", o=1)
    S.op("sp", lambda h: h.dma_start(out=lng, in_=gs.to_broadcast([128, D])), writes=[b], dma=True)
    S.op("sp", lambda h: h.dma_start(out=lnb, in_=bs.to_broadcast([128, D])), writes=[b], dma=True)
    return lng, lnb, b


def ln_tile(g, x, xb, lnp, tmp, tmp_b, eps=None):
    S = g.S
    lng, lnb, lnp_b = lnp
    (st, mv, sd, rs), (b1, b2) = tmp.get()
    for hh in range(2):
        S.op("dve", lambda h, hh=hh: h.bn_stats(out=st[:, hh * 6:(hh + 1) * 6], in_=x[:, hh * 512:(hh + 1) * 512]),
             reads=[xb], writes=[b1])
    S.op("dve", lambda h: h.bn_aggr(out=mv, in_=st), reads=[b1], writes=[b1])
    eps_ap = g.eps_ap if eps is None else eps
    S.op("act", lambda h: h.activation(out=sd, in_=mv[:, 1:2], func=AF.Sqrt, bias=eps_ap, scale=1.0),
         reads=[b1, g.eps_b], writes=[b2])
    S.op("dve", lambda h: h.scalar_tensor_tensor(out=x, in0=x, scalar=mv[:, 0:1], in1=lng, op0=ALU.subtract, op1=ALU.mult),
         reads=[xb, b1, lnp_b], writes=[xb])
    S.op("dve", lambda h: h.reciprocal(out=rs, in_=sd), reads=[b2], writes=[b2])
    S.op("dve", lambda h: h.scalar_tensor_tensor(out=x, in0=x, scalar=rs, in1=lnb, op0=ALU.mult, op1=ALU.add),
         reads=[xb, b2, lnp_b], writes=[xb])


def prep_mem(g, s):
    S, A, W = g.S, g.A, g.W
    mt = A.alloc([2, D], F32)
    mb = [Buf(), Buf()]
    lnp = load_ln(g, "mem_ln_g", "mem_ln_b", None)
    tmp, tmp_b = ln_alloc(g)
    for i in range(2):
        S.op("sp", lambda h, i=i: h.dma_start(out=mt[:, i, :], in_=W("mem")[s, i * 128:(i + 1) * 128, :]),
             writes=[mb[i]], dma=True)
        ln_tile(g, mt[:, i, :], mb[i], lnp, tmp, tmp_b)
    gen_xT(g, g.memT, g.memT_b, [(mt[:, i, :], mb[i]) for i in range(2)])


def ffn(g, l, which):
    S, A, W = g.S, g.A, g.W
    w_in = W(which + "_w_in")
    w_out = W(which + "_w_out")
    xT = A.alloc([8, 1024], BF16)
    xT_b = [[Buf() for _ in range(8)] for _ in range(2)]
    aT = A.alloc([NJ, 1024], BF16)
    aT_b = [[Buf() for _ in range(2)] for _ in range(NJ)]
    wo = RR([(A.alloc([NJ, 512], BF16), Buf()) for _ in range(2)])
    wi = RR([(A.alloc([8, 2, 256], BF16), Buf()) for _ in range(2)])
    lnp = load_ln(g, which + "_ln_g", which + "_ln_b", l)
    sg = RR([(A.alloc([512], F32), Buf()) for _ in range(2)])
    tmp, tmp_b = ln_alloc(g)
    pre = []
    pending_ln = []

    def load_jp(jp):
        w, wb = wi.get()
        for gu in range(2):
            c0 = gu * FFN + jp * 256
            load_w(g, w[:, :, gu, :], wb, w_in[l, :, c0:c0 + 256].rearrange("(k p) n -> p k n", p=128))
        return w, wb
    for c in range(2):
        for sub in range(2):
            gen_xT(g, xT[:, :, sub * 512:(sub + 1) * 512], xT_b[sub], resid_tiles(g, c * 8 + sub * 4))
        for jp in range(NJ // 2):
            w, wb = pre.pop(0) if pre else load_jp(jp)
            if pending_ln and jp >= 1:
                tt_ = pending_ln.pop(0)
                ln_tile(g, g.resid[:, tt_, :], g.resid_b[tt_], lnp, tmp, tmp_b, eps=g.eps4_ap)
            for jj in range(2):
                j = jp * 2 + jj
                for sub in range(2):
                    pg, pgb = g.ps.get()
                    pu, pub = g.ps.get()
                    for gu, (pp, ppb) in enumerate(((pg, pgb), (pu, pub))):
                        for k in range(8):
                            mm(g, pp, w[:, k, gu, jj * 128:(jj + 1) * 128], xT[:, k, sub * 512:(sub + 1) * 512],
                               k == 0, k == 7, [wb, xT_b[sub][k]], [ppb])
                    sgt, sgb = sg.get()
                    S.op("act", lambda h, sgt=sgt, pg=pg: h.activation(out=sgt, in_=pg, func=AF.Silu),
                         reads=[pgb], writes=[sgb])
                    S.op("dve", lambda h, sgt=sgt, pu=pu, j=j, sub=sub: h.tensor_tensor(
                        out=aT[:, j, sub * 512:(sub + 1) * 512], in0=pu, in1=sgt, op=ALU.mult),
                        reads=[pub, sgb], writes=[aT_b[j][sub]])
        wos = []
        for q in range(2):
            w, wb = wo.get()
            load_w(g, w, wb, w_out[l, :, q * 512:(q + 1) * 512].rearrange("(j p) n -> p j n", p=128))
            wos.append((w, wb))
        if c == 0:
            pre = [load_jp(0), load_jp(1)]
        for q in range(2):
            w, wb = wos[q]
            for tl in range(8):
                t = c * 8 + tl
                pf, pfb = g.ps.get()
                for j in range(NJ):
                    mm(g, pf, aT[:, j, tl * 128:(tl + 1) * 128], w[:, j, :], j == 0, j == NJ - 1,
                       [wb, aT_b[j][tl // 4]], [pfb])
                xs = g.resid[:, t, q * 512:(q + 1) * 512]
                S.op("dve", lambda h: h.scalar_tensor_tensor(
                    out=xs, in0=xs, scalar=2.0 * ALPHA, in1=pf, op0=ALU.mult, op1=ALU.add),
                    reads=[g.resid_b[t], pfb], writes=[g.resid_b[t]])
        if c == 0:
            pending_ln = [tl for tl in range(8)]
        else:
            for tt_ in pending_ln:
                ln_tile(g, g.resid[:, tt_, :], g.resid_b[tt_], lnp, tmp, tmp_b, eps=g.eps4_ap)
            pending_ln = []
            for tl in range(8):
                ln_tile(g, g.resid[:, c * 8 + tl, :], g.resid_b[c * 8 + tl], lnp, tmp, tmp_b, eps=g.eps4_ap)


def cross(g, l):
    S, A, W = g.S, g.A, g.W
    wq = A.alloc([8, 512], BF16)
    wkv = A.alloc([8, 1024], BF16)
    wo = A.alloc([4, 1024], BF16)
    wq_b, wkv_b, wo_b = Buf(), Buf(), Buf()
    load_w(g, wkv, wkv_b, W("cross_wkv")[l].rearrange("(k p) n -> p k n", p=128))
    load_w(g, wq, wq_b, W("cross_wq")[l].rearrange("(k p) n -> p k n", p=128))
    load_w(g, wo, wo_b, W("cross_wo")[l].rearrange("(k p) n -> p k n", p=128))
    KT = A.alloc([4, 256], BF16)
    KT_b = [Buf() for _ in range(4)]
    V = A.alloc([2, 512], BF16)
    V_b = [Buf() for _ in range(2)]
    for hd in range(4):
        ps, psb = g.ps.get()
        for k in range(8):
            mm(g, ps[:, 0:256], wkv[:, k, hd * 128:(hd + 1) * 128], g.memT[:, k, :], k == 0, k == 7,
               [wkv_b, g.memT_b[k]], [psb])
        evac(g, hd, KT[:, hd, :], ps[:, 0:256], [psb], [KT_b[hd]])
    for mt in range(2):
        ps, psb = g.ps.get()
        for k in range(8):
            mm(g, ps, g.memT[:, k, mt * 128:(mt + 1) * 128], wkv[:, k, 512:1024], k == 0, k == 7,
               [wkv_b, g.memT_b[k]], [psb])
        evac(g, mt, V[:, mt, :], ps, [psb], [V_b[mt]])
    xT = A.alloc([8, 512], BF16)
    xT_b = [Buf() for _ in range(8)]
    QT = RR([(A.alloc([512], BF16), Buf()) for _ in range(2)])
    PT = RR([(A.alloc([512], BF16), Buf()) for _ in range(4)])
    RI = RR([(A.alloc([512], F32), Buf()) for _ in range(2)])
    OT = A.alloc([4, 512], BF16)
    OT_b = [Buf() for _ in range(4)]
    lnp = load_ln(g, "cross_ln_g", "cross_ln_b", l)
    tmp, tmp_b = ln_alloc(g)
    sc = 1.0 / math.sqrt(128.0)
    for c in range(4):
        gen_xT(g, xT, xT_b, resid_tiles(g, c * 4))
        for hp in range(2):
            qts, ptss = {}, {}
            for hd in (2 * hp, 2 * hp + 1):
                ps, psb = g.ps.get()
                for k in range(8):
                    mm(g, ps, wq[:, k, hd * 128:(hd + 1) * 128], xT[:, k, :], k == 0, k == 7, [wq_b, xT_b[k]], [psb])
                qt, qtb = QT.get()
                evac(g, hd, qt, ps, [psb], [qtb])
                qts[hd] = (qt, qtb)
            for hd in (2 * hp, 2 * hp + 1):
                qt, qtb = qts[hd]
                pts = []
                for mt in range(2):
                    ps, psb = g.ps.get()
                    mm(g, ps, KT[:, hd, mt * 128:(mt + 1) * 128], qt, True, True, [KT_b[hd], qtb], [psb])
                    pt, ptb = PT.get()
                    S.op("act", lambda h: h.activation(out=pt, in_=ps, func=AF.Exp, scale=sc), reads=[psb], writes=[ptb])
                    pts.append((pt, ptb))
                ptss[hd] = pts
            for hd in (2 * hp, 2 * hp + 1):
                pts = ptss[hd]
                po, pob = g.ps.get()
                pr, prb = g.ps.get()
                for mt in range(2):
                    mm(g, po, V[:, mt, hd * 128:(hd + 1) * 128], pts[mt][0], mt == 0, mt == 1, [V_b[mt], pts[mt][1]], [pob])
                for mt in range(2):
                    mm(g, pr, g.ones, pts[mt][0], mt == 0, mt == 1, [g.cb, pts[mt][1]], [prb])
                ri, rib = RI.get()
                S.op("dve", lambda h: h.reciprocal(out=ri, in_=pr), reads=[prb], writes=[rib])
                S.op("dve", lambda h: h.tensor_tensor(out=OT[:, hd, :], in0=po, in1=ri, op=ALU.mult),
                     reads=[pob, rib], writes=[OT_b[hd]])
        for tl in range(4):
            t = c * 4 + tl
            for half in range(2):
                ps, psb = g.ps.get()
                for hd in range(4):
                    mm(g, ps, OT[:, hd, tl * 128:(tl + 1) * 128], wo[:, hd, half * 512:(half + 1) * 512],
                       hd == 0, hd == 3, [OT_b[hd], wo_b], [psb])
                xs = g.resid[:, t, half * 512:(half + 1) * 512]
                S.op("dve", lambda h, xs=xs, ps=ps: h.scalar_tensor_tensor(
                    out=xs, in0=xs, scalar=ALPHA, in1=ps, op0=ALU.mult, op1=ALU.add),
                    reads=[g.resid_b[t], psb], writes=[g.resid_b[t]])
            ln_tile(g, g.resid[:, t, :], g.resid_b[t], lnp, tmp, tmp_b)


def view3(ap, a):
    return ap.rearrange("p (a b) -> p a b", a=a)


def bc_mid(ap2, n):
    return ap2.unsqueeze(1).to_broadcast([ap2.shape[0], n, ap2.shape[1]])


def bc_last(ap2, n):
    return ap2.unsqueeze(2).to_broadcast([ap2.shape[0], ap2.shape[1], n])


def mixer(g, l, branches):
    S, A, W = g.S, g.A, g.W
    OT = {br: A.alloc([4, L], BF16) for br in "abcd"}
    OT_b = {br: [[Buf() for _ in range(4)] for _ in range(4)] for br in "abcd"}
    m0 = A.mark()
    fns = dict(a=swa_branch, b=s5_branch, c=conv_branch, d=diff_branch)
    for br in "dacb":
        A.reset(m0)
        S.barrier()
        if br in branches:
            fns[br](g, l, OT[br], OT_b[br])
        else:
            allb = [b for row in OT_b[br] for b in row]
            S.op("pool", lambda h, br=br: h.memset(OT[br], 0.0), writes=allb)
    A.reset(m0)
    S.barrier()
    lam_init = None
    mT = A.alloc([8, L], BF16)
    mT_b = [[Buf() for _ in range(4)] for _ in range(8)]
    m2 = A.mark()
    xT = A.alloc([8, 512], BF16)
    xT_b = [Buf() for _ in range(8)]
    wg = RR([(A.alloc([8, 4, 128], BF16), Buf()) for _ in range(2)])
    wp = RR([(A.alloc([4, 4, 128], BF16), Buf()) for _ in range(2)])
    SG = RR([(A.alloc([512], F32), Buf()) for _ in range(2)])
    PRT = [(A.alloc([512], F32), Buf()) for _ in range(3)]
    projs = dict(a="swa_proj", b="s5_proj", c="conv_proj", d="diff_proj")
    for jd in range(8):
        w, wb = wg.get()
        p, pb = wp.get()
        for i, br in enumerate("abcd"):
            c0 = 3840 + i * 1024 + jd * 128
            load_w(g, w[:, :, i, :], wb, W("mix_w_in")[l, :, c0:c0 + 128].rearrange("(k p) n -> p k n", p=128))
            load_w(g, p[:, i, :, :], pb, W(projs[br])[l, :, jd * 128:(jd + 1) * 128].rearrange("(k p) n -> p k n", p=128))
        for c in range(4):
            gen_xT(g, xT, xT_b, resid_tiles(g, c * 4))
            slot = [PRT[0], PRT[1], PRT[2], PRT[1]]
            for i, br in enumerate("abcd"):
                pgt, pgb = g.ps.get()
                for k in range(8):
                    mm(g, pgt, w[:, k, i, :], xT[:, k, :], k == 0, k == 7, [wb, xT_b[k]], [pgb])
                py, pyb = g.ps.get()
                for kc in range(4):
                    mm(g, py, p[:, i, kc, :], OT[br][:, kc, c * 512:(c + 1) * 512], kc == 0, kc == 3,
                       [pb, OT_b[br][kc][c]], [pyb])
                sg, sgb = SG.get()
                S.op("act", lambda h: h.activation(out=sg, in_=pgt, func=AF.Sigmoid), reads=[pgb], writes=[sgb])
                pr, prb = slot[i]
                S.op("dve", lambda h: h.tensor_tensor(out=pr, in0=py, in1=sg, op=ALU.mult), reads=[pyb, sgb], writes=[prb])
                if i == 1:
                    S.op("pool", lambda h: h.tensor_tensor(out=PRT[0][0], in0=PRT[0][0], in1=PRT[1][0], op=ALU.add),
                         reads=[PRT[0][1], PRT[1][1]], writes=[PRT[0][1]])
                if i == 3:
                    S.op("pool", lambda h: h.tensor_tensor(out=PRT[2][0], in0=PRT[2][0], in1=PRT[1][0], op=ALU.add),
                         reads=[PRT[2][1], PRT[1][1]], writes=[PRT[2][1]])
            S.op("dve", lambda h: h.tensor_tensor(out=mT[:, jd, c * 512:(c + 1) * 512], in0=PRT[0][0], in1=PRT[2][0], op=ALU.add),
                 reads=[PRT[0][1], PRT[2][1]], writes=[mT_b[jd][c]])
    A.reset(m2)
    S.barrier()
    wo = A.alloc([8, 1024], BF16)
    wo_b = Buf()
    load_w(g, wo, wo_b, W("mix_w_out")[l].rearrange("(k p) n -> p k n", p=128))
    lnp = load_ln(g, "mix_ln_g", "mix_ln_b", l)
    tmp, tmp_b = ln_alloc(g)
    for t in range(NT):
        for half in range(2):
            ps, psb = g.ps.get()
            for k in range(8):
                mm(g, ps, mT[:, k, t * 128:(t + 1) * 128], wo[:, k, half * 512:(half + 1) * 512], k == 0, k == 7,
                   [mT_b[k][t // 4], wo_b], [psb])
            xs = g.resid[:, t, half * 512:(half + 1) * 512]
            S.op("dve", lambda h, xs=xs, ps=ps: h.scalar_tensor_tensor(
                out=xs, in0=xs, scalar=ALPHA, in1=ps, op0=ALU.mult, op1=ALU.add),
                reads=[g.resid_b[t], psb], writes=[g.resid_b[t]])
        ln_tile(g, g.resid[:, t, :], g.resid_b[t], lnp, tmp, tmp_b)


def diff_branch(g, l, OT, OT_b):
    S, A, W = g.S, g.A, g.W
    lam_init = 0.8 - 0.6 * math.exp(-0.3 * l)
    xT = A.alloc([8, 512], BF16)
    xT_b = [Buf() for _ in range(8)]
    wD = RR([(A.alloc([8, 384], BF16), Buf()) for _ in range(2)])
    QT = A.alloc([2, L], BF16)
    KT = A.alloc([2, L], BF16)
    V = A.alloc([NT, 128], BF16)
    QT_b = [[Buf() for _ in range(4)] for _ in range(2)]
    KT_b = [[Buf() for _ in range(4)] for _ in range(2)]
    V_b = [Buf() for _ in range(NT)]
    qbias_b, kbias_b = Buf(), Buf()
    PT = RR([(A.alloc([512], BF16), Buf()) for _ in range(4)])
    SM = RR([(A.alloc([512], F32), Buf()) for _ in range(2)])
    F = RR([(A.alloc([512], F32), Buf()) for _ in range(6)])
    mask_diff = A.alloc([4, 512], BF16)
    mk_b = Buf()
    S.op("sp", lambda h: h.dma_start(out=mask_diff, in_=W("c_mask_diff")), writes=[mk_b], dma=True)
    lq = A.alloc([4, 64], F32)
    sc = A.alloc([8], F32)
    ng = A.alloc([1], F32)
    sb = Buf()
    for i, nm in enumerate(("diff_lq1", "diff_lk1", "diff_lq2", "diff_lk2")):
        S.op("sp", lambda h, i=i, nm=nm: h.dma_start(out=lq[:, i, :], in_=W(nm)[l:l + 1, :].to_broadcast([128, 64])),
             writes=[sb], dma=True)
    S.op("sp", lambda h: h.dma_start(out=ng, in_=W("diff_norm_g")[l].rearrange("(p o) -> p o", o=1)), writes=[sb], dma=True)
    for i in range(2):
        S.op("dve", lambda h, i=i: h.tensor_tensor(out=lq[:, 2 * i, :], in0=lq[:, 2 * i, :], in1=lq[:, 2 * i + 1, :], op=ALU.mult),
             reads=[sb], writes=[sb])
        S.op("dve", lambda h, i=i: h.tensor_reduce(out=sc[:, i:i + 1], in_=lq[:, 2 * i, :], axis=mybir.AxisListType.X, op=ALU.add),
             reads=[sb], writes=[sb])
        S.op("act", lambda h, i=i: h.activation(out=sc[:, 2 + i:3 + i], in_=sc[:, i:i + 1], func=AF.Exp), reads=[sb], writes=[sb])
    S.op("dve", lambda h: h.tensor_tensor(out=sc[:, 4:5], in0=sc[:, 3:4], in1=sc[:, 2:3], op=ALU.subtract), reads=[sb], writes=[sb])
    S.op("dve", lambda h: h.tensor_scalar(out=sc[:, 5:6], in0=sc[:, 4:5], scalar1=-lam_init, scalar2=None, op0=ALU.add),
         reads=[sb], writes=[sb])
    nlam = sc[:, 5:6]
    for c in range(2):
        S.op("sp", lambda h, c=c: h.dma_start(out=KT[64:68, c, :], in_=W("c_kb")), writes=[kbias_b], dma=True)
    acc = g.banks[0:4]
    rot = RR(g.banks[4:8])
    for hd in range(4):
        w, wb = wD.get()
        for i, c0 in enumerate((768, 1280, 1792)):
            load_w(g, w[:, :, i * 128:(i + 1) * 128],  wb,
                   W("mix_w_in")[l, :, c0 + hd * 128:c0 + (hd + 1) * 128].rearrange("(k p) n -> p k n", p=128))
        for c in range(2):
            S.op("sp", lambda h, c=c, hd=hd: h.dma_start(out=QT[64:68, c, :], in_=W("c_qb_diff")[:, hd, :]),
                 writes=[qbias_b], dma=True)
        for c in range(4):
            gen_xT(g, xT, xT_b, resid_tiles(g, c * 4))
            for qk, (T_, T_b) in enumerate(((QT, QT_b), (KT, KT_b))):
                for comp in range(2):
                    ps, psb = g.ps.get()
                    col = qk * 128 + comp * 64
                    for k in range(8):
                        mm(g, ps[0:64, :], w[:, k, col:col + 64], xT[:, k, :], k == 0, k == 7, [wb, xT_b[k]], [psb])
                    evac(g, comp + qk, T_[0:64, comp, c * 512:(c + 1) * 512], ps[0:64, :], [psb], [T_b[comp][c]])
            for tl in range(4):
                ps, psb = g.ps.get()
                for k in range(8):
                    mm(g, ps[:, 0:128], xT[:, k, tl * 128:(tl + 1) * 128], w[:, k, 256:384], k == 0, k == 7,
                       [wb, xT_b[k]], [psb])
                evac(g, tl, V[:, c * 4 + tl, :], ps[:, 0:128], [psb], [V_b[c * 4 + tl]])
        for qc in range(4):
            nkb = 4 * qc + 4
            for comp in range(2):
                po, pob = acc[2 * comp]
                pr, prb = acc[2 * comp + 1]
                def s_mm(kb):
                    ps, psb = rot.get()
                    mm(g, ps, KT[0:68, comp, kb * 128:(kb + 1) * 128], QT[0:68, comp, qc * 512:(qc + 1) * 512], True, True,
                       [KT_b[comp][kb // 4], QT_b[comp][qc], qbias_b, kbias_b], [psb])
                    return ps, psb
                nxt = [s_mm(0)] + ([s_mm(1)] if nkb > 1 else [])
                for kb in range(nkb):
                    ps, psb = nxt.pop(0)
                    if kb + 2 < nkb:
                        nxt.append(s_mm(kb + 2))
                    pt, ptb = PT.get()
                    if kb >= 4 * qc:
                        sm, smb = SM.get()
                        S.op("dve", lambda h: h.tensor_tensor(
                            out=sm, in0=ps, in1=mask_diff[:, kb - 4 * qc, :], op=ALU.add), reads=[psb, mk_b], writes=[smb])
                        S.op("act", lambda h: h.activation(out=pt, in_=sm, func=AF.Exp, scale=0.125),
                             reads=[smb], writes=[ptb])
                    else:
                        S.op("act", lambda h: h.activation(out=pt, in_=ps, func=AF.Exp, scale=0.125),
                             reads=[psb], writes=[ptb])
                    mm(g, po, V[:, kb, :], pt, kb == 0, kb == nkb - 1, [V_b[kb], ptb], [pob])
                    mm(g, pr, g.ones, pt, kb == 0, kb == nkb - 1, [g.cb, ptb], [prb])
            ts = []
            for comp in range(2):
                po, pob = acc[2 * comp]
                pr, prb = acc[2 * comp + 1]
                ri, rib = F.get()
                S.op("dve", lambda h, ri=ri, pr=pr: h.reciprocal(out=ri, in_=pr), reads=[prb], writes=[rib])
                t_, tb_ = F.get()
                S.op("dve", lambda h, t_=t_, po=po, ri=ri: h.tensor_tensor(out=t_, in0=po, in1=ri, op=ALU.mult),
                     reads=[pob, rib], writes=[tb_])
                ts.append((t_, tb_))
            o, ob = F.get()
            S.op("dve", lambda h, o=o, t0=ts[0][0], t1=ts[1][0]: h.scalar_tensor_tensor(
                out=o, in0=t1, scalar=nlam, in1=t0, op0=ALU.mult, op1=ALU.add),
                reads=[ts[0][1], ts[1][1], sb], writes=[ob])
            sq, sqb = PT.get()
            S.op("act", lambda h, sq=sq, o=o: h.activation(out=sq, in_=o, func=AF.Square), reads=[ob], writes=[sqb])
            pm, pmb = rot.get()
            mm(g, pm, g.ones, sq, True, True, [g.cb, sqb], [pmb])
            sd, sdb = F.get()
            S.op("act", lambda h, sd=sd, pm=pm: h.activation(out=sd, in_=pm, func=AF.Sqrt, bias=g.eps_ap, scale=1.0 / 128.0),
                 reads=[pmb, g.cb], writes=[sdb])
            S.op("dve", lambda h, sd=sd: h.reciprocal(out=sd, in_=sd), reads=[sdb], writes=[sdb])
            S.op("dve", lambda h, o=o, sd=sd: h.tensor_tensor(out=o, in0=o, in1=sd, op=ALU.mult), reads=[ob, sdb], writes=[ob])
            S.op("dve", lambda h, o=o, hd=hd, qc=qc: h.tensor_scalar(
                out=OT[:, hd, qc * 512:(qc + 1) * 512], in0=o, scalar1=ng, scalar2=1.0 - lam_init, op0=ALU.mult, op1=ALU.mult),
                reads=[ob, sb], writes=[OT_b[hd][qc]])


def swa_branch(g, l, OT, OT_b):
    S, A, W = g.S, g.A, g.W
    xT = A.alloc([8, 512], BF16)
    xT_b = [Buf() for _ in range(8)]
    wA = RR([(A.alloc([8, 384], BF16), Buf()) for _ in range(2)])
    QT = A.alloc([4, L], BF16)
    KT = A.alloc([L], BF16)
    V = A.alloc([NT, 64], BF16)
    QT_b = [[Buf() for _ in range(4)] for _ in range(4)]
    KT_b = [Buf() for _ in range(4)]
    V_b = [Buf() for _ in range(NT)]
    qbias_b, kbias_b = Buf(), Buf()
    PT = RR([(A.alloc([4, 128], BF16), Buf()) for _ in range(4)])
    SM = RR([(A.alloc([4, 128], F32), Buf()) for _ in range(2)])
    DN = RR([(A.alloc([2, 128], F32), Buf()) for _ in range(2)])
    esink = A.alloc([2], F32)
    es_b = Buf()
    mask_swa = A.alloc([2, 128], BF16)
    mk_b = Buf()
    S.op("sp", lambda h: h.dma_start(out=mask_swa, in_=W("c_mask_swa")), writes=[mk_b], dma=True)
    S.op("sp", lambda h: h.dma_start(out=KT[64:68, :], in_=W("c_kb")), writes=[kbias_b], dma=True)
    acc = g.banks[0:2]
    rot = RR(g.banks[2:8])
    for grp in range(2):
        w, wb = wA.get()
        for (d0, c0, n) in ((0, grp * 256, 256), (256, 512 + grp * 64, 64), (320, 640 + grp * 64, 64)):
            load_w(g, w[:, :, d0:d0 + n], wb, W("mix_w_in")[l, :, c0:c0 + n].rearrange("(k p) n -> p k n", p=128))
        for r in range(4):
            S.op("sp", lambda h, r=r, grp=grp: h.dma_start(out=QT[64:68, r, :], in_=W("c_qb_swa")[:, 4 * grp + r, :]),
                 writes=[qbias_b], dma=True)
        for par in range(2):
            for j in range(2):
                hidx = 4 * grp + 2 * j + par
                S.op("sp", lambda h, par=par, j=j, hidx=hidx: h.dma_start(
                    out=esink[64 * par:64 * par + 64, j:j + 1],
                    in_=W("swa_sinks")[l:l + 1, hidx:hidx + 1].to_broadcast([64, 1])), writes=[es_b], dma=True)
        S.op("act", lambda h: h.activation(out=esink, in_=esink, func=AF.Exp), reads=[es_b], writes=[es_b])
        for c in range(4):
            gen_xT(g, xT, xT_b, resid_tiles(g, c * 4))
            for r in range(4):
                ps, psb = g.ps.get()
                for k in range(8):
                    mm(g, ps[0:64, :], w[:, k, r * 64:(r + 1) * 64], xT[:, k, :], k == 0, k == 7, [wb, xT_b[k]], [psb])
                evac(g, r, QT[0:64, r, c * 512:(c + 1) * 512], ps[0:64, :], [psb], [QT_b[r][c]])
            ps, psb = g.ps.get()
            for k in range(8):
                mm(g, ps[0:64, :], w[:, k, 256:320], xT[:, k, :], k == 0, k == 7, [wb, xT_b[k]], [psb])
            evac(g, 1, KT[0:64, c * 512:(c + 1) * 512], ps[0:64, :], [psb], [KT_b[c]])
            for tl in range(4):
                ps, psb = g.ps.get()
                for k in range(8):
                    mm(g, ps[:, 0:64], xT[:, k, tl * 128:(tl + 1) * 128], w[:, k, 320:384], k == 0, k == 7,
                       [wb, xT_b[k]], [psb])
                evac(g, tl, V[:, c * 4 + tl, :], ps[:, 0:64], [psb], [V_b[c * 4 + tl]])
        def s_blk(n):
            kbs = [n - 1, n] if n > 0 else [n]
            pts = []
            for kb in kbs:
                mi = 0 if kb == n - 1 else 1
                ps, psb = rot.get()
                mm(g, ps, KT[0:68, kb * 128:(kb + 1) * 128], QT[0:68, :, n * 128:(n + 1) * 128], True, True,
                   [KT_b[kb // 4], kbias_b, qbias_b] + [QT_b[r][n // 4] for r in range(4)], [psb])
                sm, smb = SM.get()
                S.op("dve", lambda h: h.tensor_tensor(
                    out=sm, in0=view3(ps, 4), in1=bc_mid(mask_swa[:, mi, :], 4), op=ALU.add),
                    reads=[psb, mk_b], writes=[smb])
                pt, ptb = PT.get()
                S.op("act", lambda h: h.activation(out=pt, in_=sm, func=AF.Exp, scale=0.125),
                     reads=[smb], writes=[ptb])
                pts.append((pt, ptb, kb))
            return pts
        nxt = s_blk(0)
        for n in range(NT):
            pts = nxt
            if n + 1 < NT:
                nxt = s_blk(n + 1)
            po, pob = acc[0]
            pr, prb = acc[1]
            for par in range(2):
                for i, (pt, ptb, kb) in enumerate(pts):
                    mm(g, po[64 * par:64 * par + 64, 0:256], V[:, kb, :], pt[:, par::2, :], i == 0, i == len(pts) - 1,
                       [V_b[kb], ptb], [pob])
                for i, (pt, ptb, kb) in enumerate(pts):
                    mm(g, pr[64 * par:64 * par + 64, 0:256], g.ones[:, 0:64], pt[:, par::2, :], i == 0, i == len(pts) - 1,
                       [g.cb, ptb], [prb])
            dn, dnb = DN.get()
            S.op("dve", lambda h: h.tensor_tensor(
                out=dn, in0=view3(pr[:, 0:256], 2), in1=bc_last(esink, 128), op=ALU.add), reads=[prb, es_b], writes=[dnb])
            S.op("dve", lambda h: h.reciprocal(out=dn, in_=dn), reads=[dnb], writes=[dnb])
            S.op("dve", lambda h: h.tensor_tensor(
                out=OT[:, 2 * grp:2 * grp + 2, n * 128:(n + 1) * 128], in0=view3(po[:, 0:256], 2), in1=dn, op=ALU.mult),
                reads=[pob, dnb], writes=[OT_b[2 * grp][n // 4], OT_b[2 * grp + 1][n // 4]])


def conv_branch(g, l, OT, OT_b):
    S, A, W = g.S, g.A, g.W
    gT = A.alloc([4, 30 + L], BF16)
    gT_b = [[Buf() for _ in range(5)] for _ in range(4)]
    cwT = A.alloc([4, 32], F32)
    cbv = A.alloc([3, 4], F32)
    pb_ = Buf()
    m1 = A.mark()
    xT = A.alloc([8, 512], BF16)
    xT_b = [Buf() for _ in range(8)]
    wC = A.alloc([8, 1024], BF16)
    wC_b = Buf()
    cwn = A.alloc([512], F32)
    SG = RR([(A.alloc([512], F32), Buf()) for _ in range(2)])
    load_w(g, wC, wC_b, W("mix_w_in")[l, :, 2816:3840].rearrange("(k p) n -> p k n", p=128))
    S.op("pool", lambda h: h.memset(cwn[0:32, :], 0.0), writes=[pb_])
    S.op("sp", lambda h: h.dma_start(out=cwn[0:31, :], in_=W("conv_w")[l]), writes=[pb_], dma=True)
    for i, nm in enumerate(("conv_b", "conv_ln_g", "conv_ln_b")):
        S.op("sp", lambda h, i=i, nm=nm: h.dma_start(out=cbv[:, i, :], in_=W(nm)[l].rearrange("(c p) -> p c", p=128),
                                                     allow_slow_non_contiguous=True), writes=[pb_], dma=True)
    for cc in range(4):
        ps, psb = g.ps.get()
        S.op("pe", lambda h, ps=ps, cc=cc: h.transpose(out=ps[:, 0:32], in_=cwn[0:32, cc * 128:(cc + 1) * 128],
                                                       identity=g.ident[0:32, 0:32]), reads=[pb_, g.cb], writes=[psb])
        S.op("dve", lambda h, ps=ps, cc=cc: h.tensor_copy(out=cwT[:, cc, 0:31], in_=ps[:, 0:31]), reads=[psb], writes=[pb_])
        S.op("pool", lambda h, cc=cc: h.memset(gT[:, cc, 0:30], 0.0), writes=[gT_b[cc][4]])
    for c in range(4):
        gen_xT(g, xT, xT_b, resid_tiles(g, c * 4))
        for cc in range(4):
            pv, pvb = g.ps.get()
            pg, pgb = g.ps.get()
            for k in range(8):
                mm(g, pv, wC[:, k, cc * 128:(cc + 1) * 128], xT[:, k, :], k == 0, k == 7, [wC_b, xT_b[k]], [pvb])
            for k in range(8):
                mm(g, pg, wC[:, k, 512 + cc * 128:512 + (cc + 1) * 128], xT[:, k, :], k == 0, k == 7, [wC_b, xT_b[k]], [pgb])
            sg, sgb = SG.get()
            S.op("act", lambda h, sg=sg, pg=pg: h.activation(out=sg, in_=pg, func=AF.Sigmoid), reads=[pgb], writes=[sgb])
            S.op("dve", lambda h, sg=sg, pv=pv, cc=cc, c=c: h.tensor_tensor(
                out=gT[:, cc, 30 + c * 512:30 + (c + 1) * 512], in0=pv, in1=sg, op=ALU.mult),
                reads=[pvb, sgb], writes=[gT_b[cc][c]])
    g.dbg_out("gT", gT, [b for r in gT_b for b in r])
    g.dbg_out("cwT", cwT, [pb_])
    g.dbg_out("cbv", cbv, [pb_])
    A.reset(m1)
    S.barrier()
    Dg = A.alloc([4, 31, 128], BF16)
    Dg_b = [Buf() for _ in range(4)]
    yb = A.alloc([4, 512], F32)
    sq = A.alloc([4, 512], BF16)
    ybh = A.alloc([4, 512], BF16)
    ybh_b = [Buf() for _ in range(4)]
    yb_b = [Buf() for _ in range(4)]
    sq_b = [Buf() for _ in range(4)]
    Fm = [(A.alloc([512], F32), Buf()) for _ in range(3)]
    def build_dg(cc):
        for j in range(31):
            S.op("dve" if j % 2 == 0 else "pool", lambda h: h.tensor_scalar(
                out=Dg[:, cc, j, :], in0=g.identb, scalar1=cwT[:, cc, j:j + 1], scalar2=None, op0=ALU.mult),
                reads=[pb_, g.cb], writes=[Dg_b[cc]])

    def conv_mm(c):
        banks = []
        if c == 0:
            build_dg(0)
        for cc in range(4):
            if c == 0 and cc + 1 < 4:
                build_dg(cc + 1)
            ps, psb = g.ps.get()
            rd = [Dg_b[cc], gT_b[cc][c]] + ([gT_b[cc][c - 1]] if c > 0 else [gT_b[cc][4]])
            for j in range(31):
                mm(g, ps, Dg[:, cc, j, :], gT[:, cc, c * 512 + j:c * 512 + j + 512], j == 0, j == 30, rd, [psb])
            banks.append((ps, psb))
        return banks
    nxt_banks = conv_mm(0)
    for c in range(4):
        banks = nxt_banks
        for cc in range(4):
            ps, psb = banks[cc]
            S.op("act", lambda h: h.activation(out=yb[:, cc, :], in_=ps, func=AF.Identity,
                                               bias=cbv[:, 0, cc:cc + 1], scale=1.0),
                 reads=[psb, pb_], writes=[yb_b[cc]])
            S.op("act", lambda h: h.activation(out=sq[:, cc, :], in_=ps, func=AF.Square,
                                               bias=cbv[:, 0, cc:cc + 1], scale=1.0),
                 reads=[psb, pb_], writes=[sq_b[cc]])
            S.op("act", lambda h: h.activation(out=ybh[:, cc, :], in_=ps, func=AF.Identity,
                                               bias=cbv[:, 0, cc:cc + 1], scale=1.0),
                 reads=[psb, pb_], writes=[ybh_b[cc]])
        if c + 1 < 4:
            nxt_banks = conv_mm(c + 1)
        if c == 0:
            g.dbg_out("yb0", yb, yb_b)
            g.dbg_out("sq0", sq, sq_b)
        pm1, pm1b = g.ps.get()
        pm2, pm2b = g.ps.get()
        for cc in range(4):
            mm(g, pm1, g.ones, ybh[:, cc, :], cc == 0, cc == 3, [g.cb, ybh_b[cc]], [pm1b])
        for cc in range(4):
            mm(g, pm2, g.ones, sq[:, cc, :], cc == 0, cc == 3, [g.cb, sq_b[cc]], [pm2b])
        (mean, meb), (msq, msb), (rs, rsb) = Fm
        S.op("act", lambda h: h.activation(out=mean, in_=pm1, func=AF.Copy, scale=1.0 / 512.0), reads=[pm1b], writes=[meb])
        S.op("act", lambda h: h.activation(out=msq, in_=pm1, func=AF.Square, scale=1.0 / 512.0), reads=[pm1b], writes=[msb])
        S.op("dve", lambda h: h.scalar_tensor_tensor(out=rs, in0=pm2, scalar=1.0 / 512.0, in1=msq, op0=ALU.mult, op1=ALU.subtract),
             reads=[pm2b, msb], writes=[rsb])
        S.op("act", lambda h: h.activation(out=rs, in_=rs, func=AF.Sqrt, bias=g.eps_ap, scale=1.0), reads=[rsb, g.cb], writes=[rsb])
        S.op("dve", lambda h: h.reciprocal(out=rs, in_=rs), reads=[rsb], writes=[rsb])
        if c == 0:
            g.dbg_out("mean0", mean, [meb])
            g.dbg_out("msq0", msq, [msb])
            g.dbg_out("rs0", rs, [rsb])
        for cc in range(4):
            S.op("pool", lambda h, cc=cc: h.tensor_tensor(out=yb[:, cc, :], in0=yb[:, cc, :], in1=mean, op=ALU.subtract),
                 reads=[yb_b[cc], meb], writes=[yb_b[cc]])
            S.op("dve", lambda h, cc=cc: h.tensor_tensor(out=yb[:, cc, :], in0=yb[:, cc, :], in1=rs, op=ALU.mult),
                 reads=[yb_b[cc], rsb], writes=[yb_b[cc]])
            S.op("act", lambda h, cc=cc, c=c: h.activation(out=OT[:, cc, c * 512:(c + 1) * 512], in_=yb[:, cc, :], func=AF.Silu,
                                                           bias=cbv[:, 2, cc:cc + 1], scale=cbv[:, 1, cc:cc + 1]),
                 reads=[yb_b[cc], pb_], writes=[OT_b[cc][c]])
    g.dbg_out("OTc", OT, [b for r in OT_b for b in r])


def sincos(g, x, s_out, c_out, t1, t2, xb, outb=None, t3=None, t4=None, cb2=None):
    S = g.S
    chains = [(0.0, s_out, t1, t2, Buf()), (math.pi / 2.0, c_out, t3 if t3 is not None else t1, t4 if t4 is not None else t2,
                                           Buf() if t3 is not None else None)]
    if t3 is None:
        chains[1] = chains[1][:4] + (chains[0][4],)
    steps = []
    for shift, out, a, b, cb in chains:
        st = [
            ("dve", lambda h, a=a, shift=shift: h.tensor_scalar(out=a, in0=x, scalar1=1.0 / TWO_PI, scalar2=shift / TWO_PI + MAGIC,
                                                                 op0=ALU.mult, op1=ALU.add)),
            ("dve", lambda h, a=a: h.tensor_scalar(out=a, in0=a, scalar1=-MAGIC, scalar2=None, op0=ALU.add)),
            ("dve", lambda h, a=a, b=b: h.scalar_tensor_tensor(out=b, in0=a, scalar=-CW1, in1=x, op0=ALU.mult, op1=ALU.add)),
            ("dve", lambda h, a=a, b=b: h.scalar_tensor_tensor(out=b, in0=a, scalar=-CW2, in1=b, op0=ALU.mult, op1=ALU.add)),
            ("dve", lambda h, b=b, shift=shift: h.tensor_scalar(out=b, in0=b, scalar1=shift, scalar2=3.1415925, op0=ALU.add, op1=ALU.min)),
            ("dve", lambda h, b=b: h.tensor_scalar(out=b, in0=b, scalar1=-3.1415925, scalar2=None, op0=ALU.max)),
            ("act", lambda h, b=b, out=out: h.activation(out=out, in_=b, func=AF.Sin)),
        ]
        steps.append((st, cb))
    order = []
    if t3 is None:
        for st, cb in steps:
            order += [(e, fn, cb) for e, fn in st]
    else:
        for i in range(7):
            for st, cb in steps:
                order.append(st[i] + (cb,))
    c2 = chains[1][4]
    for idx, (e, fn, cb) in enumerate(order):
        last = (e == "act")
        cbs = cb2 if (cb2 is not None and cb is c2 and t3 is not None) else [cb]
        wr = list(cbs) + ([xb] if last else []) + ([outb] if (last and outb is not None) else [])
        S.op(e, fn, reads=[xb] + list(cbs), writes=wr)


def s5_branch(g, l, OT, OT_b):
    S, A, W = g.S, g.A, g.W
    yT, yT_b = OT, OT_b

    def f(shape):
        return A.alloc(shape, F32)
    sb = Buf("s5small")

    def dv(fn):
        S.op("dve", fn, reads=[sb], writes=[sb])

    def ac(fn):
        S.op("act", fn, reads=[sb], writes=[sb])
    WBr = A.alloc([16, 128], BF16)
    WBi = A.alloc([16, 128], BF16)
    WCr = A.alloc([16, 64], BF16)
    WCn = A.alloc([16, 64], BF16)
    wmat_b = Buf()
    mag, s128, c128 = f([16]), f([16]), f([16])
    dsk, glb = f([4]), f([4])
    uT = A.alloc([4, L], BF16)
    uT_b = [[Buf() for _ in range(4)] for _ in range(4)]
    th = f([16])
    m1 = A.mark()
    ar, ai, ls, sn, cs, t1, t2 = (f([16]) for _ in range(7))
    abr, abi, den, nr, cr, ci, t3 = (f([16]) for _ in range(7))
    Bs_r, Bs_i = f([16, 128]), f([16, 128])
    braw_r, braw_i = f([16, 16]), f([16, 16])
    tb1 = f([4, 16])
    S.op("sp", lambda h: h.dma_start(out=ar, in_=W("s5_a_re")[l].rearrange("(ct gl) p -> (gl p) ct", gl=2),
                                     allow_slow_non_contiguous=True), writes=[sb], dma=True)
    S.op("sp", lambda h: h.dma_start(out=ai, in_=W("s5_a_im")[l].rearrange("(ct gl) p -> (gl p) ct", gl=2),
                                     allow_slow_non_contiguous=True), writes=[sb], dma=True)
    for gl in range(2):
        S.op("sp", lambda h, gl=gl: h.dma_start(
            out=ls[64 * gl:64 * gl + 64, :],
            in_=W("s5_log_step")[l].rearrange("(ct gl) -> gl ct", gl=2)[gl:gl + 1, :].to_broadcast([64, 16]),
            allow_slow_non_contiguous=True), writes=[sb], dma=True)
    S.op("sp", lambda h: h.dma_start(out=braw_r, in_=W("s5_b_re")[l].rearrange("(ct gl) p c -> (gl p) ct c", gl=2)),
         writes=[sb], dma=True)
    S.op("sp", lambda h: h.dma_start(out=braw_i, in_=W("s5_b_im")[l].rearrange("(ct gl) p c -> (gl p) ct c", gl=2)),
         writes=[sb], dma=True)
    for i, nm in enumerate(("s5_d", "s5_glu_b")):
        dst = (dsk, glb)[i]
        S.op("sp", lambda h, dst=dst, nm=nm: h.dma_start(out=dst, in_=W(nm)[l].rearrange("(c p) -> p c", p=128),
                                                         allow_slow_non_contiguous=True), writes=[sb], dma=True)
    S.op("pool", lambda h: h.memset(Bs_r, 0.0), writes=[sb])
    S.op("pool", lambda h: h.memset(Bs_i, 0.0), writes=[sb])
    ac(lambda h: h.activation(out=ls, in_=ls, func=AF.Exp))
    dv(lambda h: h.tensor_tensor(out=t1, in0=ar, in1=ls, op=ALU.mult))
    ac(lambda h: h.activation(out=mag, in_=t1, func=AF.Exp))
    dv(lambda h: h.tensor_tensor(out=th, in0=ai, in1=ls, op=ALU.mult))
    sincos(g, th, sn, cs, t1, t2, sb)
    dv(lambda h: h.tensor_tensor(out=abr, in0=mag, in1=cs, op=ALU.mult))
    dv(lambda h: h.tensor_tensor(out=abi, in0=mag, in1=sn, op=ALU.mult))
    dv(lambda h: h.tensor_tensor(out=t1, in0=ar, in1=ar, op=ALU.mult))
    dv(lambda h: h.tensor_tensor(out=den, in0=ai, in1=ai, op=ALU.mult))
    dv(lambda h: h.tensor_tensor(out=den, in0=den, in1=t1, op=ALU.add))
    dv(lambda h: h.reciprocal(out=den, in_=den))
    dv(lambda h: h.tensor_scalar(out=nr, in0=abr, scalar1=-1.0, scalar2=None, op0=ALU.add))
    dv(lambda h: h.tensor_tensor(out=t1, in0=nr, in1=ar, op=ALU.mult))
    dv(lambda h: h.tensor_tensor(out=t2, in0=abi, in1=ai, op=ALU.mult))
    dv(lambda h: h.tensor_tensor(out=t1, in0=t1, in1=t2, op=ALU.add))
    dv(lambda h: h.tensor_tensor(out=cr, in0=t1, in1=den, op=ALU.mult))
    dv(lambda h: h.tensor_tensor(out=t1, in0=abi, in1=ar, op=ALU.mult))
    dv(lambda h: h.tensor_tensor(out=t2, in0=nr, in1=ai, op=ALU.mult))
    dv(lambda h: h.tensor_tensor(out=t1, in0=t1, in1=t2, op=ALU.subtract))
    dv(lambda h: h.tensor_tensor(out=ci, in0=t1, in1=den, op=ALU.mult))
    dv(lambda h: h.tensor_scalar(out=t3, in0=th, scalar1=512.0, scalar2=None, op0=ALU.mult))
    sincos(g, t3, s128, c128, t1, t2, sb)
    for q in range(4):
        for gl in range(2):
            rows = slice(64 * gl, 64 * gl + 64)
            c0 = 32 * q + 16 * gl
            crb = bc_last(cr[rows, q::4], 16)
            cib = bc_last(ci[rows, q::4], 16)
            br_, bi_ = braw_r[rows, q::4, :], braw_i[rows, q::4, :]
            o_r, o_i = Bs_r[rows, q::4, c0:c0 + 16], Bs_i[rows, q::4, c0:c0 + 16]
            tt = tb1[rows]
            dv(lambda h, tt=tt, bi_=bi_, cib=cib: h.tensor_tensor(out=tt, in0=bi_, in1=cib, op=ALU.mult))
            dv(lambda h, o_r=o_r, br_=br_, crb=crb: h.tensor_tensor(out=o_r, in0=br_, in1=crb, op=ALU.mult))
            dv(lambda h, o_r=o_r, tt=tt: h.tensor_tensor(out=o_r, in0=o_r, in1=tt, op=ALU.subtract))
            dv(lambda h, tt=tt, br_=br_, cib=cib: h.tensor_tensor(out=tt, in0=br_, in1=cib, op=ALU.mult))
            dv(lambda h, o_i=o_i, bi_=bi_, crb=crb: h.tensor_tensor(out=o_i, in0=bi_, in1=crb, op=ALU.mult))
            dv(lambda h, o_i=o_i, tt=tt: h.tensor_tensor(out=o_i, in0=o_i, in1=tt, op=ALU.add))
    for ct in range(16):
        q = ct % 4
        for i, (src, dst) in enumerate(((Bs_r, WBr), (Bs_i, WBi))):
            ps, psb = g.ps.get()
            S.op("pe", lambda h, ps=ps, src=src, ct=ct: h.transpose(out=ps[:, 0:128], in_=src[:, ct, :], identity=g.ident),
                 reads=[sb, g.cb], writes=[psb])
            er = slice(64, 128) if q == 3 else slice(32 * q, 32 * q + 32)
            evac(g, i, dst[er, ct, :], ps[er, 0:128], [psb], [wmat_b])
    A.reset(m1)
    S.barrier()
    Cs_r, Cs_i = f([16, 128]), f([16, 128])
    S.op("pool", lambda h: h.memset(Cs_r[0:64], 0.0), writes=[sb])
    S.op("pool", lambda h: h.memset(Cs_i[0:64], 0.0), writes=[sb])
    for gl in range(2):
        for par in range(2):
            for (dst, nm) in ((Cs_r, "s5_c_re"), (Cs_i, "s5_c_im")):
                p0 = 32 * par + 16 * gl
                S.op("sp", lambda h, gl=gl, par=par, dst=dst, nm=nm, p0=p0: h.dma_start(
                    out=dst[p0:p0 + 16, par::2, 64 * gl:64 * gl + 64],
                    in_=W(nm)[l].rearrange("(ct gl) c p -> gl c ct p", gl=2)[gl][:, par::2, :]), writes=[sb], dma=True)
    for ct in range(16):
        for i, (src, dst, scl) in enumerate(((Cs_r, WCr, None), (Cs_i, WCn, -1.0))):
            ps, psb = g.ps.get()
            S.op("pe", lambda h, ps=ps, src=src, ct=ct: h.transpose(out=ps[:, 0:64], in_=src[0:64, ct, :],
                                                                    identity=g.ident[0:64, 0:64]),
                 reads=[sb, g.cb], writes=[psb])
            evac(g, i, dst[:, ct, :], ps[:, 0:64], [psb], [wmat_b], scale=scl)
    A.reset(m1)
    S.barrier()
    xT = A.alloc([8, 512], BF16)
    xT_b = [Buf() for _ in range(8)]
    wS = A.alloc([8, 512], BF16)
    wS_b = Buf()
    load_w(g, wS, wS_b, W("mix_w_in")[l, :, 2304:2816].rearrange("(k p) n -> p k n", p=128))
    for c in range(4):
        gen_xT(g, xT, xT_b, resid_tiles(g, c * 4))
        for cc in range(4):
            ps, psb = g.ps.get()
            for k in range(8):
                mm(g, ps, wS[:, k, cc * 128:(cc + 1) * 128], xT[:, k, :], k == 0, k == 7, [wS_b, xT_b[k]], [psb])
            evac(g, cc, uT[:, cc, c * 512:(c + 1) * 512], ps, [psb], [uT_b[cc][c]])
    A.reset(m1)
    S.barrier()
    iota = f([512])
    S.op("sp", lambda h: h.dma_start(out=iota, in_=W("c_iota")), writes=[sb], dma=True)
    TB = RR([((f([512]), f([512])), Buf()) for _ in range(2)])
    angj, sj1, sj2 = f([512]), f([512]), f([512])
    sjb = Buf()
    brp, bip, ta, tb = f([512]), f([512]), f([512]), f([512])
    dB = Buf()
    brp_b, bip_b, ta_b, tb_b = Buf(), Buf(), Buf(), Buf()
    W2 = RR([((f([512]), f([512])), (Buf(), Buf())) for _ in range(2)])
    pa, pb2, pc, pd = f([512]), f([512]), f([512]), f([512])
    pa_b, pb_b, pc_b, pd_b = Buf(), Buf(), Buf(), Buf()
    INI = RR([(f([4]), Buf()) for _ in range(2)])
    XR = RR([(A.alloc([512], BF16), Buf()) for _ in range(2)])
    XI = RR([(A.alloc([512], BF16), Buf()) for _ in range(2)])
    vv, v2, vb = brp, bip, dB
    yacc = g.banks[0:4]
    rot = RR(g.banks[4:8])
    bu_q = []

    def bu_mm(cc, q, c):
        ct = 4 * cc + q
        rows = slice(64, 128) if q == 3 else slice(32 * q, 32 * q + 32)
        pbr, pbrb = rot.get()
        pbi, pbib = rot.get()
        mm(g, pbr, WBr[rows, ct, :], uT[rows, cc, c * 512:(c + 1) * 512], True, True, [wmat_b, uT_b[cc][c]], [pbrb])
        mm(g, pbi, WBi[rows, ct, :], uT[rows, cc, c * 512:(c + 1) * 512], True, True, [wmat_b, uT_b[cc][c]], [pbib])
        return pbr, pbrb, pbi, pbib
    for cc in range(4):
        for q in range(4):
            ct = 4 * cc + q
            rows = slice(64, 128) if q == 3 else slice(32 * q, 32 * q + 32)
            orow = slice(64 * (q // 2), 64 * (q // 2) + 64)
            (sinT, cosT), tbb = TB.get()
            S.op("dve", lambda h: h.tensor_scalar(out=angj, in0=iota, scalar1=th[:, ct:ct + 1], scalar2=None, op0=ALU.mult),
                 reads=[sb, sjb], writes=[sjb])
            sincos(g, angj, sinT, cosT, sj1, sj2, sjb, outb=tbb, t3=ta, t4=tb, cb2=[ta_b, tb_b])
            rho = mag[:, ct:ct + 1].to_broadcast([128, 512])
            cc_, ss_ = c128[:, ct:ct + 1], s128[:, ct:ct + 1]
            prev = None
            for c in range(4):
                py, pyb = yacc[c]
                if not bu_q:
                    bu_q.append(bu_mm(cc, q, c))
                pbr, pbrb, pbi, pbib = bu_q.pop(0)
                nx = (cc, q, c + 1) if c + 1 < 4 else ((cc, q + 1, 0) if q + 1 < 4 else ((cc + 1, 0, 0) if cc + 1 < 4 else None))
                if nx is not None:
                    bu_q.append(bu_mm(*nx))
                S.op("dve", lambda h: h.tensor_tensor(out=ta, in0=pbi, in1=sinT, op=ALU.mult), reads=[pbib, tbb], writes=[ta_b])
                S.op("dve", lambda h: h.tensor_tensor(out=tb, in0=pbr, in1=sinT, op=ALU.mult), reads=[pbrb, tbb], writes=[tb_b])
                S.op("dve", lambda h: h.tensor_tensor(out=brp, in0=pbr, in1=cosT, op=ALU.mult), reads=[pbrb, tbb], writes=[brp_b])
                S.op("dve", lambda h: h.tensor_tensor(out=bip, in0=pbi, in1=cosT, op=ALU.mult), reads=[pbib, tbb], writes=[bip_b])
                S.op("dve", lambda h: h.tensor_tensor(out=brp, in0=brp, in1=ta, op=ALU.add), reads=[brp_b, ta_b], writes=[brp_b])
                S.op("dve", lambda h: h.tensor_tensor(out=bip, in0=bip, in1=tb, op=ALU.subtract), reads=[bip_b, tb_b], writes=[bip_b])
                (wr, wi), (wr_b, wi_b) = W2.get()
                if prev is None:
                    i_re, i_im, ird = 0.0, 0.0, []
                else:
                    (pwr, pwi), (pwr_b, pwi_b) = prev
                    ini, inib = INI.get()
                    lr, li = pwr[:, 511:512], pwi[:, 511:512]
                    rdl = [pwr_b, pwi_b, sb]
                    S.op("dve", lambda h: h.tensor_scalar(out=ini[:, 0:1], in0=li, scalar1=ss_, scalar2=None, op0=ALU.mult),
                         reads=rdl, writes=[inib])
                    S.op("dve", lambda h: h.tensor_scalar(out=ini[:, 2:3], in0=li, scalar1=cc_, scalar2=None, op0=ALU.mult),
                         reads=rdl, writes=[inib])
                    S.op("dve", lambda h: h.scalar_tensor_tensor(out=ini[:, 1:2], in0=lr, scalar=cc_, in1=ini[:, 0:1],
                                                                 op0=ALU.mult, op1=ALU.subtract), reads=rdl + [inib], writes=[inib])
                    S.op("dve", lambda h: h.scalar_tensor_tensor(out=ini[:, 3:4], in0=lr, scalar=ss_, in1=ini[:, 2:3],
                                                                 op0=ALU.mult, op1=ALU.add), reads=rdl + [inib], writes=[inib])
                    i_re, i_im, ird = ini[:, 1:2], ini[:, 3:4], [inib]
                S.op("dve", lambda h: h.tensor_tensor_scan(out=wr, data0=rho, data1=brp, initial=i_re, op0=ALU.mult, op1=ALU.add),
                     reads=[brp_b, sb] + ird, writes=[wr_b])
                S.op("dve", lambda h: h.tensor_tensor_scan(out=wi, data0=rho, data1=bip, initial=i_im, op0=ALU.mult, op1=ALU.add),
                     reads=[bip_b, sb] + ird, writes=[wi_b])
                prev = ((wr, wi), (wr_b, wi_b))
                xr, xrb = XR.get()
                xi, xib = XI.get()
                S.op("pool", lambda h: h.tensor_tensor(out=pa, in0=wr, in1=cosT, op=ALU.mult), reads=[wr_b, tbb, sjb], writes=[pa_b])
                S.op("pool", lambda h: h.tensor_tensor(out=pb2, in0=wi, in1=sinT, op=ALU.mult), reads=[wi_b, tbb, sjb], writes=[pb_b])
                S.op("pool", lambda h: h.tensor_tensor(out=pc, in0=wi, in1=cosT, op=ALU.mult), reads=[wi_b, tbb, sjb], writes=[pc_b])
                S.op("pool", lambda h: h.tensor_tensor(out=pd, in0=wr, in1=sinT, op=ALU.mult), reads=[wr_b, tbb, sjb], writes=[pd_b])
                S.op("pool", lambda h: h.tensor_tensor(out=xr, in0=pa, in1=pb2, op=ALU.subtract), reads=[pa_b, pb_b], writes=[xrb])
                S.op("pool", lambda h: h.tensor_tensor(out=xi, in0=pc, in1=pd, op=ALU.add), reads=[pc_b, pd_b], writes=[xib])
                mm(g, py[orow, :], WCr[:, ct, :], xr, q % 2 == 0, False, [wmat_b, xrb], [pyb])
                mm(g, py[orow, :], WCn[:, ct, :], xi, False, q % 2 == 1, [wmat_b, xib], [pyb])
        for c in range(4):
            py, pyb = yacc[c]
            S.op("dve", lambda h: h.scalar_tensor_tensor(
                out=vv, in0=uT[:, cc, c * 512:(c + 1) * 512], scalar=dsk[:, cc:cc + 1], in1=py, op0=ALU.mult, op1=ALU.add),
                reads=[pyb, uT_b[cc][c], sb, vb], writes=[vb])
            S.op("act", lambda h: h.activation(out=v2, in_=vv, func=AF.Square), reads=[vb], writes=[vb])
            S.op("dve", lambda h: h.tensor_scalar(out=v2, in0=v2, scalar1=0.044715, scalar2=1.0, op0=ALU.mult, op1=ALU.add),
                 reads=[vb], writes=[vb])
            S.op("dve", lambda h: h.tensor_tensor(out=v2, in0=v2, in1=vv, op=ALU.mult), reads=[vb], writes=[vb])
            S.op("act", lambda h: h.activation(out=v2, in_=v2, func=AF.Sigmoid, scale=1.5957691216057308), reads=[vb], writes=[vb])
            S.op("dve", lambda h: h.tensor_tensor(out=yT[:, cc, c * 512:(c + 1) * 512], in0=vv, in1=v2, op=ALU.mult),
                 reads=[vb], writes=[yT_b[cc][c]])
    A.reset(m1)
    S.barrier()
    gw = A.alloc([4, 512], BF16)
    gw_b = Buf()
    load_w(g, gw, gw_b, W("s5_glu_w")[l].rearrange("(k p) n -> p k n", p=128))
    vv = f([512])
    vb = Buf()
    for c in range(4):
        zs = []
        for oc in range(4):
            ps, psb = g.banks[oc]
            for kc in range(4):
                mm(g, ps, gw[:, kc, oc * 128:(oc + 1) * 128], yT[:, kc, c * 512:(c + 1) * 512], kc == 0, kc == 3,
                   [gw_b, yT_b[kc][c]], [psb])
            zs.append((ps, psb))
        for oc in range(4):
            ps, psb = zs[oc]
            S.op("act", lambda h, ps=ps, oc=oc: h.activation(out=vv, in_=ps, func=AF.Sigmoid, bias=glb[:, oc:oc + 1], scale=1.0),
                 reads=[psb, sb, vb], writes=[vb])
            S.op("dve", lambda h, oc=oc, c=c: h.tensor_tensor(out=yT[:, oc, c * 512:(c + 1) * 512],
                                                              in0=yT[:, oc, c * 512:(c + 1) * 512], in1=vv, op=ALU.mult),
                 reads=[vb, yT_b[oc][c]], writes=[yT_b[oc][c]])


PARTS = ("ffn1", "mix", "cross", "ffn2")
_cache = {}


def run(inputs, layers, parts, xin, branches="abcd"):
    key = (tuple(layers), tuple(parts), branches)
    if key not in _cache:
        _cache[key] = build(layers, parts, branches)
    nc, g = _cache[key]
    in_maps = []
    shared = {}
    for name in g.dr:
        if name in ("x", "mem"):
            continue
        if name.startswith("c_"):
            shared[name] = consts()[name][0]
        else:
            shared[name] = np.ascontiguousarray(np.asarray(inputs[name], dtype=np.float32))
    for c in range(NCORES):
        m = dict(shared)
        m["x"] = np.ascontiguousarray(xin[c * SEQ_PER_CORE:(c + 1) * SEQ_PER_CORE])
        if "mem" in g.dr:
            m["mem"] = np.ascontiguousarray(np.asarray(inputs["mem"], dtype=np.float32)[c * SEQ_PER_CORE:(c + 1) * SEQ_PER_CORE])
        in_maps.append(m)
    res = run_bass_kernel_spmd(nc, in_maps, core_ids=list(range(NCORES)))
    if DEBUG:
        LAST.update({k: v for k, v in res.results[0].items()})
    return np.concatenate([r["out"] for r in res.results], axis=0)


FUSED = True


def kernel(**inputs):
    x = np.asarray(inputs["x"], dtype=np.float32)
    if FUSED:
        return run(inputs, list(range(DEPTH)), PARTS, x)
    for l in range(DEPTH):
        x = run(inputs, [l], PARTS, x)
    return x
```

```python
from contextlib import ExitStack
import math
import numpy as np
import concourse.bass as bass
import concourse.mybir as mybir
from concourse.bass_utils import run_bass_kernel_spmd

F32 = mybir.dt.float32
BF16 = mybir.dt.bfloat16
AF = mybir.ActivationFunctionType
ALU = mybir.AluOpType

D = 1024
L = 2048
NT = L // 128
DEPTH = 4
FFN = 2816
NJ = FFN // 128
MIX_IN = 7936
ALPHA = (2.0 * DEPTH) ** 0.25
EPS = 1e-5
NCORES = 8
SEQ_PER_CORE = 2

ENGS = ["pe", "act", "dve", "pool", "sp"]
NSEM = 4
PHASE = 2048
NDSEM = 40


class Buf:
    __slots__ = ("w", "r", "name")

    def __init__(self, name=""):
        self.w = None
        self.r = {}
        self.name = name


class _Rec:
    def __init__(self):
        self.call = None

    def __getattr__(self, name):
        def f(*a, **k):
            self.call = (name, a, k)
            return self
        return f


class Sched:
    def __init__(self, nc, es):
        self.nc = nc
        self.h = dict(pe=nc.tensor, act=nc.scalar, dve=nc.vector, pool=nc.gpsimd, sp=nc.sync)
        self.q = {e: [] for e in ENGS}
        self.cnt = {e: 0 for e in ENGS}
        self.semh = {}
        self.esval = {}
        for e in ENGS:
            for i in range(NSEM):
                self.semh[("e", e, i)] = es.enter_context(nc.semaphore(f"s_{e}{i}"))
                self.esval[("e", e, i)] = 0
        self.dval = [0] * NDSEM
        for i in range(NDSEM):
            self.semh[("d", i)] = es.enter_context(nc.semaphore(f"d{i}"))
        self.dnext = 0
        self.seen = {e: {} for e in ENGS}
        self.last = {}
        self.nwait = 0

    def _wait(self, eng, k, v):
        if self.seen[eng].get(k, 0) >= v:
            return
        self.seen[eng][k] = v
        sem = self.semh[k]
        self.q[eng].append(lambda h, sem=sem, v=v: h.wait_ge(sem, v))
        self.nwait += 1

    def op(self, eng, fn, reads=(), writes=(), dma=False):
        rec = _Rec()
        fn(rec)
        _name, _a, _k = rec.call

        def fn(h, _name=_name, _a=_a, _k=_k):
            return getattr(h, _name)(*_a, **_k)
        deps = {}

        def add(k, v):
            if deps.get(k, 0) < v:
                deps[k] = v

        cow = {}
        for b in reads:
            if b.w:
                for k, v in b.w.items():
                    add(k, v)
        for b in writes:
            co = bool(dma and b.w and not b.r and all(k[0] == "d" for k in b.w))
            cow[id(b)] = co
            if b.w and not co:
                for k, v in b.w.items():
                    add(k, v)
            for k, v in b.r.items():
                if k[0] == "e" and k[1] == eng:
                    continue
                add(k, v)
        for k, v in deps.items():
            if eng == "pe" and k[0] == "e" and k[1] == "pe":
                continue
            self._wait(eng, k, v)
        if dma:
            i = self.dnext
            self.dnext = (i + 1) % NDSEM
            k = ("d", i)
            prev = self.dval[i]
            if prev:
                self._wait(eng, k, prev)
            self.dval[i] = prev + 16
            tok = (k, prev + 16)
            sem = self.semh[k]
            self.q[eng].append(lambda h, fn=fn, sem=sem: fn(h).then_inc(sem, 16))
        else:
            c = self.cnt[eng]
            self.cnt[eng] = c + 1
            k = ("e", eng, (c // PHASE) % NSEM)
            self.esval[k] += 1
            tok = (k, self.esval[k])
            sem = self.semh[k]
            self.q[eng].append(lambda h, fn=fn, sem=sem: fn(h).then_inc(sem, 1))
        self.last[tok[0]] = tok[1]
        for b in writes:
            if cow.get(id(b)):
                b.w[tok[0]] = tok[1]
            else:
                b.w = {tok[0]: tok[1]}
            b.r = {}
        for b in reads:
            if b.r.get(tok[0], 0) < tok[1]:
                b.r[tok[0]] = tok[1]
        return tok

    def barrier(self):
        for e in ENGS:
            for k, v in self.last.items():
                if k[0] == "e" and k[1] == e:
                    continue
                self._wait(e, k, v)

    def finish(self, eng="sp"):
        for k, v in self.last.items():
            self._wait(eng, k, v)

    def replay(self, block):
        q = self.q

        @block.tensor
        def _(h):
            for f in q["pe"]:
                f(h)

        @block.scalar
        def _(h):
            for f in q["act"]:
                f(h)

        @block.vector
        def _(h):
            for f in q["dve"]:
                f(h)

        @block.gpsimd
        def _(h):
            for f in q["pool"]:
                f(h)

        @block.sync
        def _(h):
            for f in q["sp"]:
                f(h)


class RR:
    def __init__(self, aps, name="p"):
        self.slots = [a if isinstance(a, tuple) else (a, Buf(f"{name}{i}")) for i, a in enumerate(aps)]
        self.i = 0

    def get(self):
        s = self.slots[self.i]
        self.i = (self.i + 1) % len(self.slots)
        return s


class Arena:
    def __init__(self, t_bf):
        self.tb = t_bf
        self.tf = t_bf.bitcast(F32)
        self.off = 0
        self.cap = t_bf.shape[1] * 2

    def mark(self):
        return self.off

    def reset(self, m):
        self.off = m

    def alloc(self, shape, dt):
        esz = 4 if dt == F32 else 2
        n = int(np.prod(shape))
        self.off = (self.off + 63) // 64 * 64
        o = self.off
        self.off += n * esz
        assert self.off <= self.cap, (self.off, self.cap)
        t = self.tf if dt == F32 else self.tb
        ap = t[:, o // esz: o // esz + n]
        if len(shape) == 2:
            ap = ap.rearrange("p (a b) -> p a b", a=shape[0])
        elif len(shape) == 3:
            ap = ap.rearrange("p (a b c) -> p a b c", a=shape[0], b=shape[1])
        return ap


class K:
    pass


SHAPES = {
    "x": [SEQ_PER_CORE, L, D], "mem": [SEQ_PER_CORE, 256, D],
    "ffn1_w_in": [DEPTH, D, 2 * FFN], "ffn1_w_out": [DEPTH, FFN, D], "ffn1_ln_g": [DEPTH, D], "ffn1_ln_b": [DEPTH, D],
    "mix_w_in": [DEPTH, D, MIX_IN], "swa_sinks": [DEPTH, 8], "swa_proj": [DEPTH, 512, D],
    "s5_a_re": [DEPTH, 32, 64], "s5_a_im": [DEPTH, 32, 64], "s5_log_step": [DEPTH, 32],
    "s5_b_re": [DEPTH, 32, 64, 16], "s5_b_im": [DEPTH, 32, 64, 16], "s5_c_re": [DEPTH, 32, 16, 64],
    "s5_c_im": [DEPTH, 32, 16, 64], "s5_d": [DEPTH, 512], "s5_glu_w": [DEPTH, 512, 512], "s5_glu_b": [DEPTH, 512],
    "s5_proj": [DEPTH, 512, D], "conv_w": [DEPTH, 31, 512], "conv_b": [DEPTH, 512], "conv_ln_g": [DEPTH, 512],
    "conv_ln_b": [DEPTH, 512], "conv_proj": [DEPTH, 512, D], "diff_lq1": [DEPTH, 64], "diff_lk1": [DEPTH, 64],
    "diff_lq2": [DEPTH, 64], "diff_lk2": [DEPTH, 64], "diff_norm_g": [DEPTH, 128], "diff_proj": [DEPTH, 512, D],
    "mix_w_out": [DEPTH, D, D], "mix_ln_g": [DEPTH, D], "mix_ln_b": [DEPTH, D], "mem_ln_g": [D], "mem_ln_b": [D],
    "cross_wq": [DEPTH, D, 512], "cross_wkv": [DEPTH, D, D], "cross_wo": [DEPTH, 512, D],
    "cross_ln_g": [DEPTH, D], "cross_ln_b": [DEPTH, D],
    "ffn2_w_in": [DEPTH, D, 2 * FFN], "ffn2_w_out": [DEPTH, FFN, D], "ffn2_ln_g": [DEPTH, D], "ffn2_ln_b": [DEPTH, D],
}
NEG = -240000.0
TWO_PI = 2.0 * math.pi
CW1 = 6.28125
CW2 = TWO_PI - CW1
MAGIC = 12582912.0
DIFF_SLOPES = [2.0 ** (-8.0 * (h + 1) / 4) for h in range(4)]
SWA_SLOPES = [2.0 ** (-8.0 * (h + 1) / 8) for h in range(8)]
CONSTS = {}
DEBUG = False
LAST = {}


def consts():
    if not CONSTS:
        import ml_dtypes
        bf = ml_dtypes.bfloat16
        CONSTS["c_ident"] = (np.eye(128, dtype=np.float32), F32)
        CONSTS["c_iota"] = (np.tile(np.arange(512, dtype=np.float32)[None, :], (128, 1)), F32)
        pos = np.arange(L)
        pa, pb = (pos // 128).astype(np.float32), (pos % 128).astype(np.float32)
        kb = np.stack([np.ones(L), np.ones(L), 128.0 * pa, pb]).astype(np.float32)
        CONSTS["c_kb"] = (kb.astype(bf), BF16)

        def qb(slopes):
            o = np.zeros((4, len(slopes), L), np.float32)
            for h, s in enumerate(slopes):
                o[0, h] = -8.0 * s * 128.0 * pa
                o[1, h] = -8.0 * s * pb
                o[2, h] = 8.0 * s
                o[3, h] = 8.0 * s
            return o.astype(bf)
        CONSTS["c_qb_diff"] = (qb(DIFF_SLOPES), BF16)
        CONSTS["c_qb_swa"] = (qb(SWA_SLOPES), BF16)
        ki = np.arange(128)[:, None]
        md = np.zeros((128, 4, 512), np.float32)
        qi = np.arange(512)[None, :]
        for rel in range(4):
            md[:, rel, :] = np.where(qi >= 128 * rel + ki, 0.0, NEG)
        CONSTS["c_mask_diff"] = (md.astype(bf), BF16)
        ms = np.zeros((128, 2, 128), np.float32)
        q1 = np.arange(128)[None, :]
        ms[:, 0, :] = np.where(ki > q1, 0.0, NEG)
        ms[:, 1, :] = np.where(ki <= q1, 0.0, NEG)
        CONSTS["c_mask_swa"] = (ms.astype(bf), BF16)
    return CONSTS


def build(layers, parts, branches="abcd", nseq=SEQ_PER_CORE):
    nc = bass.Bass("TRN2", target_bir_lowering=False)
    es = ExitStack()
    with es:
        g = K()
        g.nc = nc
        S = Sched(nc, es)
        g.S = S
        g.dr = {}

        def W(name):
            if name not in g.dr:
                if name.startswith("c_"):
                    arr, dt = consts()[name]
                    g.dr[name] = nc.dram_tensor(name, list(arr.shape), dt, kind="ExternalInput").ap()
                else:
                    g.dr[name] = nc.dram_tensor(name, list(SHAPES[name]), F32, kind="ExternalInput").ap()
            return g.dr[name]
        g.W = W
        g.dbg = {}

        def dbg_out(name, ap, bufs):
            if not DEBUG or name in g.dbg:
                return
            t = nc.dram_tensor("dbg_" + name, list(ap.shape), ap.dtype, kind="ExternalOutput").ap()
            g.dbg[name] = t
            S.op("sp", lambda h: h.dma_start(out=t, in_=ap), reads=bufs, dma=True)
        g.dbg_out = dbg_out
        out = nc.dram_tensor("out", [nseq, L, D], F32, kind="ExternalOutput").ap()

        arena_t = es.enter_context(nc.sbuf_tensor("arena", [128, 106400], BF16))
        A = Arena(arena_t)
        g.A = A
        g.resid = A.alloc([NT, D], F32)
        g.resid_b = [Buf(f"resid{t}") for t in range(NT)]
        g.ident = A.alloc([128], F32)
        g.identb = A.alloc([128], BF16)
        g.ones = A.alloc([128], BF16)
        g.memT = A.alloc([8, 256], BF16)
        g.memT_b = [Buf() for _ in range(8)]
        g.eps_ap = A.alloc([1], F32)
        g.eps4_ap = A.alloc([1], F32)
        g.cb = Buf("consts")
        S.op("sp", lambda h: h.dma_start(out=g.ident, in_=W("c_ident")), writes=[g.cb], dma=True)
        S.op("dve", lambda h: h.tensor_copy(out=g.identb, in_=g.ident), reads=[g.cb], writes=[g.cb])
        S.op("dve", lambda h: h.memset(g.ones, 1.0), writes=[g.cb])
        S.op("dve", lambda h: h.memset(g.eps_ap, EPS), writes=[g.cb])
        S.op("dve", lambda h: h.memset(g.eps4_ap, 4.0 * EPS), writes=[g.cb])
        g.eps_b = g.cb
        g.ident_b = g.cb
        g.banks = [(es.enter_context(nc.psum_tensor(f"ps{i}", [128, 512], F32))[:], Buf(f"ps{i}")) for i in range(8)]
        g.ps = RR(g.banks)
        pmark = A.mark()

        for s in range(nseq):
            for t in range(NT):
                S.op("sp", lambda h, t=t, s=s: h.dma_start(out=g.resid[:, t, :], in_=W("x")[s, t * 128:(t + 1) * 128, :]),
                     writes=[g.resid_b[t]], dma=True)
            if "cross" in parts:
                A.reset(pmark)
                S.barrier()
                prep_mem(g, s)
            for l in layers:
                for p in parts:
                    A.reset(pmark)
                    S.barrier()
                    if p in ("ffn1", "ffn2"):
                        ffn(g, l, p)
                    elif p == "mix":
                        mixer(g, l, branches)
                    elif p == "cross":
                        cross(g, l)
            for t in range(NT):
                S.op("sp", lambda h, t=t, s=s: h.dma_start(out=out[s, t * 128:(t + 1) * 128, :], in_=g.resid[:, t, :]),
                     reads=[g.resid_b[t]], dma=True)
        S.finish("sp")
        with nc.Block() as block:
            S.replay(block)
        g.stats = dict(cnt=dict(S.cnt), nwait=S.nwait)
    return nc, g


def load_w(g, dst, dst_b, src, eng="pool"):
    g.S.op(eng, lambda h: h.dma_start(out=dst, in_=src), writes=[dst_b], dma=True)


def mm(g, out, lhsT, rhs, start, stop, reads, writes):
    g.S.op("pe", lambda h: h.matmul(out, lhsT=lhsT, rhs=rhs, start=start, stop=stop), reads=reads, writes=writes)


def evac(g, i, out, in_, reads, writes, scale=None):
    if i % 2 == 0:
        if scale is None:
            g.S.op("act", lambda h: h.activation(out=out, in_=in_, func=AF.Copy), reads=reads, writes=writes)
        else:
            g.S.op("act", lambda h: h.activation(out=out, in_=in_, func=AF.Copy, scale=scale), reads=reads, writes=writes)
    else:
        if scale is None:
            g.S.op("dve", lambda h: h.tensor_copy(out=out, in_=in_), reads=reads, writes=writes)
        else:
            g.S.op("dve", lambda h: h.tensor_scalar(out=out, in0=in_, scalar1=scale, scalar2=None, op0=ALU.mult),
                   reads=reads, writes=writes)


def gen_xT(g, xT, xT_b, srcs):
    S = g.S
    n = len(srcs)
    for k in range(8):
        ps, psb = g.ps.get()
        for t, (src, sb) in enumerate(srcs):
            S.op("pe", lambda h, ps=ps, t=t, k=k, src=src: h.transpose(
                out=ps[:, t * 128:(t + 1) * 128], in_=src[:, k * 128:(k + 1) * 128], identity=g.ident),
                reads=[sb, g.ident_b], writes=[psb])
        evac(g, k, xT[:, k, 0:128 * n], ps[:, 0:128 * n], [psb], [xT_b[k]])


def resid_tiles(g, t0, n=4):
    return [(g.resid[:, t0 + i, :], g.resid_b[t0 + i]) for i in range(n)]


def ln_alloc(g, n=3):
    A = g.A
    sets = RR([((A.alloc([12], F32), A.alloc([2], F32), A.alloc([1], F32), A.alloc([1], F32)), (Buf("ln1"), Buf("ln2")))
               for _ in range(n)])
    return sets, None


def load_ln(g, gname, bname, l):
    A, S, W = g.A, g.S, g.W
    lng = A.alloc([D], F32)
    lnb = A.alloc([D], F32)
    b = Buf("lnp")
    gs = W(gname)[l:l + 1, :] if l is not None else W(gname).rearrange("(o d) -> o d", o=1)
    bs = W(bname)[l:l + 1, :] if l is not None else W(bname).rearrange("(o d) -> o d", o=1)
    S.op("sp", lambda h: h.dma_start(out=lng, in_=gs.to_broadcast([128, D])), writes=[b], dma=True)
    S.op("sp", lambda h: h.dma_start(out=lnb, in_=bs.to_broadcast([128, D])), writes=[b], dma=True)
    return lng, lnb, b


def ln_tile(g, x, xb, lnp, tmp, tmp_b, eps=None):
    S = g.S
    lng, lnb, lnp_b = lnp
    (st, mv, sd, rs), (b1, b2) = tmp.get()
    for hh in range(2):
        S.op("dve", lambda h, hh=hh: h.bn_stats(out=st[:, hh * 6:(hh + 1) * 6], in_=x[:, hh * 512:(hh + 1) * 512]),
             reads=[xb], writes=[b1])
    S.op("dve", lambda h: h.bn_aggr(out=mv, in_=st), reads=[b1], writes=[b1])
    eps_ap = g.eps_ap if eps is None else eps
    S.op("act", lambda h: h.activation(out=sd, in_=mv[:, 1:2], func=AF.Sqrt, bias=eps_ap, scale=1.0),
         reads=[b1, g.eps_b], writes=[b2])
    S.op("dve", lambda h: h.scalar_tensor_tensor(out=x, in0=x, scalar=mv[:, 0:1], in1=lng, op0=ALU.subtract, op1=ALU.mult),
         reads=[xb, b1, lnp_b], writes=[xb])
    S.op("dve", lambda h: h.reciprocal(out=rs, in_=sd), reads=[b2], writes=[b2])
    S.op("dve", lambda h: h.scalar_tensor_tensor(out=x, in0=x, scalar=rs, in1=lnb, op0=ALU.mult, op1=ALU.add),
         reads=[xb, b2, lnp_b], writes=[xb])


def prep_mem(g, s):
    S, A, W = g.S, g.A, g.W
    mt = A.alloc([2, D], F32)
    mb = [Buf(), Buf()]
    lnp = load_ln(g, "mem_ln_g", "mem_ln_b", None)
    tmp, tmp_b = ln_alloc(g)
    for i in range(2):
        S.op("sp", lambda h, i=i: h.dma_start(out=mt[:, i, :], in_=W("mem")[s, i * 128:(i + 1) * 128, :]),
             writes=[mb[i]], dma=True)
        ln_tile(g, mt[:, i, :], mb[i], lnp, tmp, tmp_b)
    gen_xT(g, g.memT, g.memT_b, [(mt[:, i, :], mb[i]) for i in range(2)])


def ffn(g, l, which):
    S, A, W = g.S, g.A, g.W
    w_in = W(which + "_w_in")
    w_out = W(which + "_w_out")
    xT = A.alloc([8, 1024], BF16)
    xT_b = [[Buf() for _ in range(8)] for _ in range(2)]
    aT = A.alloc([NJ, 1024], BF16)
    aT_b = [[Buf() for _ in range(2)] for _ in range(NJ)]
    wo = RR([(A.alloc([NJ, 512], BF16), Buf()) for _ in range(2)])
    wi = RR([(A.alloc([8, 2, 256], BF16), Buf()) for _ in range(2)])
    lnp = load_ln(g, which + "_ln_g", which + "_ln_b", l)
    sg = RR([(A.alloc([512], F32), Buf()) for _ in range(2)])
    tmp, tmp_b = ln_alloc(g)
    pre = []

    def load_jp(jp):
        w, wb = wi.get()
        for gu in range(2):
            c0 = gu * FFN + jp * 256
            load_w(g, w[:, :, gu, :], wb, w_in[l, :, c0:c0 + 256].rearrange("(k p) n -> p k n", p=128))
        return w, wb
    for c in range(2):
        for sub in range(2):
            gen_xT(g, xT[:, :, sub * 512:(sub + 1) * 512], xT_b[sub], resid_tiles(g, c * 8 + sub * 4))
        for jp in range(NJ // 2):
            w, wb = pre.pop(0) if pre else load_jp(jp)
            for jj in range(2):
                j = jp * 2 + jj
                for sub in range(2):
                    pg, pgb = g.ps.get()
                    pu, pub = g.ps.get()
                    for gu, (pp, ppb) in enumerate(((pg, pgb), (pu, pub))):
                        for k in range(8):
                            mm(g, pp, w[:, k, gu, jj * 128:(jj + 1) * 128], xT[:, k, sub * 512:(sub + 1) * 512],
                               k == 0, k == 7, [wb, xT_b[sub][k]], [ppb])
                    sgt, sgb = sg.get()
                    S.op("act", lambda h, sgt=sgt, pg=pg: h.activation(out=sgt, in_=pg, func=AF.Silu),
                         reads=[pgb], writes=[sgb])
                    S.op("dve", lambda h, sgt=sgt, pu=pu, j=j, sub=sub: h.tensor_tensor(
                        out=aT[:, j, sub * 512:(sub + 1) * 512], in0=pu, in1=sgt, op=ALU.mult),
                        reads=[pub, sgb], writes=[aT_b[j][sub]])
        wos = []
        for q in range(2):
            w, wb = wo.get()
            load_w(g, w, wb, w_out[l, :, q * 512:(q + 1) * 512].rearrange("(j p) n -> p j n", p=128))
            wos.append((w, wb))
        if c == 0:
            pre = [load_jp(0), load_jp(1)]
        for q in range(2):
            w, wb = wos[q]
            for tl in range(8):
                t = c * 8 + tl
                pf, pfb = g.ps.get()
                for j in range(NJ):
                    mm(g, pf, aT[:, j, tl * 128:(tl + 1) * 128], w[:, j, :], j == 0, j == NJ - 1,
                       [wb, aT_b[j][tl // 4]], [pfb])
                xs = g.resid[:, t, q * 512:(q + 1) * 512]
                S.op("dve", lambda h: h.scalar_tensor_tensor(
                    out=xs, in0=xs, scalar=2.0 * ALPHA, in1=pf, op0=ALU.mult, op1=ALU.add),
                    reads=[g.resid_b[t], pfb], writes=[g.resid_b[t]])
        for tl in range(8):
            ln_tile(g, g.resid[:, c * 8 + tl, :], g.resid_b[c * 8 + tl], lnp, tmp, tmp_b, eps=g.eps4_ap)


def cross(g, l):
    S, A, W = g.S, g.A, g.W
    wq = A.alloc([8, 512], BF16)
    wkv = A.alloc([8, 1024], BF16)
    wo = A.alloc([4, 1024], BF16)
    wq_b, wkv_b, wo_b = Buf(), Buf(), Buf()
    load_w(g, wkv, wkv_b, W("cross_wkv")[l].rearrange("(k p) n -> p k n", p=128))
    load_w(g, wq, wq_b, W("cross_wq")[l].rearrange("(k p) n -> p k n", p=128))
    load_w(g, wo, wo_b, W("cross_wo")[l].rearrange("(k p) n -> p k n", p=128))
    KT = A.alloc([4, 256], BF16)
    KT_b = [Buf() for _ in range(4)]
    V = A.alloc([2, 512], BF16)
    V_b = [Buf() for _ in range(2)]
    for hd in range(4):
        ps, psb = g.ps.get()
        for k in range(8):
            mm(g, ps[:, 0:256], wkv[:, k, hd * 128:(hd + 1) * 128], g.memT[:, k, :], k == 0, k == 7,
               [wkv_b, g.memT_b[k]], [psb])
        evac(g, hd, KT[:, hd, :], ps[:, 0:256], [psb], [KT_b[hd]])
    for mt in range(2):
        ps, psb = g.ps.get()
        for k in range(8):
            mm(g, ps, g.memT[:, k, mt * 128:(mt + 1) * 128], wkv[:, k, 512:1024], k == 0, k == 7,
               [wkv_b, g.memT_b[k]], [psb])
        evac(g, mt, V[:, mt, :], ps, [psb], [V_b[mt]])
    xT = A.alloc([8, 512], BF16)
    xT_b = [Buf() for _ in range(8)]
    QT = RR([(A.alloc([512], BF16), Buf()) for _ in range(2)])
    PT = RR([(A.alloc([512], BF16), Buf()) for _ in range(4)])
    RI = RR([(A.alloc([512], F32), Buf()) for _ in range(2)])
    OT = A.alloc([4, 512], BF16)
    OT_b = [Buf() for _ in range(4)]
    lnp = load_ln(g, "cross_ln_g", "cross_ln_b", l)
    tmp, tmp_b = ln_alloc(g)
    sc = 1.0 / math.sqrt(128.0)
    for c in range(4):
        gen_xT(g, xT, xT_b, resid_tiles(g, c * 4))
        for hp in range(2):
            qts, ptss = {}, {}
            for hd in (2 * hp, 2 * hp + 1):
                ps, psb = g.ps.get()
                for k in range(8):
                    mm(g, ps, wq[:, k, hd * 128:(hd + 1) * 128], xT[:, k, :], k == 0, k == 7, [wq_b, xT_b[k]], [psb])
                qt, qtb = QT.get()
                evac(g, hd, qt, ps, [psb], [qtb])
                qts[hd] = (qt, qtb)
            for hd in (2 * hp, 2 * hp + 1):
                qt, qtb = qts[hd]
                pts = []
                for mt in range(2):
                    ps, psb = g.ps.get()
                    mm(g, ps, KT[:, hd, mt * 128:(mt + 1) * 128], qt, True, True, [KT_b[hd], qtb], [psb])
                    pt, ptb = PT.get()
                    S.op("act", lambda h: h.activation(out=pt, in_=ps, func=AF.Exp, scale=sc), reads=[psb], writes=[ptb])
                    pts.append((pt, ptb))
                ptss[hd] = pts
            for hd in (2 * hp, 2 * hp + 1):
                pts = ptss[hd]
                po, pob = g.ps.get()
                pr, prb = g.ps.get()
                for mt in range(2):
                    mm(g, po, V[:, mt, hd * 128:(hd + 1) * 128], pts[mt][0], mt == 0, mt == 1, [V_b[mt], pts[mt][1]], [pob])
                for mt in range(2):
                    mm(g, pr, g.ones, pts[mt][0], mt == 0, mt == 1, [g.cb, pts[mt][1]], [prb])
                ri, rib = RI.get()
                S.op("dve", lambda h: h.reciprocal(out=ri, in_=pr), reads=[prb], writes=[rib])
                S.op("dve", lambda h: h.tensor_tensor(out=OT[:, hd, :], in0=po, in1=ri, op=ALU.mult),
                     reads=[pob, rib], writes=[OT_b[hd]])
        for tl in range(4):
            t = c * 4 + tl
            for half in range(2):
                ps, psb = g.ps.get()
                for hd in range(4):
                    mm(g, ps, OT[:, hd, tl * 128:(tl + 1) * 128], wo[:, hd, half * 512:(half + 1) * 512],
                       hd == 0, hd == 3, [OT_b[hd], wo_b], [psb])
                xs = g.resid[:, t, half * 512:(half + 1) * 512]
                S.op("dve", lambda h, xs=xs, ps=ps: h.scalar_tensor_tensor(
                    out=xs, in0=xs, scalar=ALPHA, in1=ps, op0=ALU.mult, op1=ALU.add),
                    reads=[g.resid_b[t], psb], writes=[g.resid_b[t]])
            ln_tile(g, g.resid[:, t, :], g.resid_b[t], lnp, tmp, tmp_b)


def view3(ap, a):
    return ap.rearrange("p (a b) -> p a b", a=a)


def bc_mid(ap2, n):
    return ap2.unsqueeze(1).to_broadcast([ap2.shape[0], n, ap2.shape[1]])


def bc_last(ap2, n):
    return ap2.unsqueeze(2).to_broadcast([ap2.shape[0], ap2.shape[1], n])


def mixer(g, l, branches):
    S, A, W = g.S, g.A, g.W
    OT = {br: A.alloc([4, L], BF16) for br in "abcd"}
    OT_b = {br: [[Buf() for _ in range(4)] for _ in range(4)] for br in "abcd"}
    m0 = A.mark()
    fns = dict(a=swa_branch, b=s5_branch, c=conv_branch, d=diff_branch)
    for br in "dacb":
        A.reset(m0)
        S.barrier()
        if br in branches:
            fns[br](g, l, OT[br], OT_b[br])
        else:
            allb = [b for row in OT_b[br] for b in row]
            S.op("pool", lambda h, br=br: h.memset(OT[br], 0.0), writes=allb)
    A.reset(m0)
    S.barrier()
    lam_init = None
    mT = A.alloc([8, L], BF16)
    mT_b = [[Buf() for _ in range(4)] for _ in range(8)]
    m2 = A.mark()
    xT = A.alloc([8, 512], BF16)
    xT_b = [Buf() for _ in range(8)]
    wg = RR([(A.alloc([8, 4, 128], BF16), Buf()) for _ in range(2)])
    wp = RR([(A.alloc([4, 4, 128], BF16), Buf()) for _ in range(2)])
    SG = RR([(A.alloc([512], F32), Buf()) for _ in range(2)])
    PRT = [(A.alloc([512], F32), Buf()) for _ in range(3)]
    projs = dict(a="swa_proj", b="s5_proj", c="conv_proj", d="diff_proj")
    for jd in range(8):
        w, wb = wg.get()
        p, pb = wp.get()
        for i, br in enumerate("abcd"):
            c0 = 3840 + i * 1024 + jd * 128
            load_w(g, w[:, :, i, :], wb, W("mix_w_in")[l, :, c0:c0 + 128].rearrange("(k p) n -> p k n", p=128))
            load_w(g, p[:, i, :, :], pb, W(projs[br])[l, :, jd * 128:(jd + 1) * 128].rearrange("(k p) n -> p k n", p=128))
        for c in range(4):
            gen_xT(g, xT, xT_b, resid_tiles(g, c * 4))
            slot = [PRT[0], PRT[1], PRT[2], PRT[1]]
            for i, br in enumerate("abcd"):
                pgt, pgb = g.ps.get()
                for k in range(8):
                    mm(g, pgt, w[:, k, i, :], xT[:, k, :], k == 0, k == 7, [wb, xT_b[k]], [pgb])
                py, pyb = g.ps.get()
                for kc in range(4):
                    mm(g, py, p[:, i, kc, :], OT[br][:, kc, c * 512:(c + 1) * 512], kc == 0, kc == 3,
                       [pb, OT_b[br][kc][c]], [pyb])
                sg, sgb = SG.get()
                S.op("act", lambda h: h.activation(out=sg, in_=pgt, func=AF.Sigmoid), reads=[pgb], writes=[sgb])
                pr, prb = slot[i]
                S.op("dve", lambda h: h.tensor_tensor(out=pr, in0=py, in1=sg, op=ALU.mult), reads=[pyb, sgb], writes=[prb])
                if i == 1:
                    S.op("pool", lambda h: h.tensor_tensor(out=PRT[0][0], in0=PRT[0][0], in1=PRT[1][0], op=ALU.add),
                         reads=[PRT[0][1], PRT[1][1]], writes=[PRT[0][1]])
                if i == 3:
                    S.op("pool", lambda h: h.tensor_tensor(out=PRT[2][0], in0=PRT[2][0], in1=PRT[1][0], op=ALU.add),
                         reads=[PRT[2][1], PRT[1][1]], writes=[PRT[2][1]])
            S.op("dve", lambda h: h.tensor_tensor(out=mT[:, jd, c * 512:(c + 1) * 512], in0=PRT[0][0], in1=PRT[2][0], op=ALU.add),
                 reads=[PRT[0][1], PRT[2][1]], writes=[mT_b[jd][c]])
    A.reset(m2)
    S.barrier()
    wo = A.alloc([8, 1024], BF16)
    wo_b = Buf()
    load_w(g, wo, wo_b, W("mix_w_out")[l].rearrange("(k p) n -> p k n", p=128))
    lnp = load_ln(g, "mix_ln_g", "mix_ln_b", l)
    tmp, tmp_b = ln_alloc(g)
    for t in range(NT):
        for half in range(2):
            ps, psb = g.ps.get()
            for k in range(8):
                mm(g, ps, mT[:, k, t * 128:(t + 1) * 128], wo[:, k, half * 512:(half + 1) * 512], k == 0, k == 7,
                   [mT_b[k][t // 4], wo_b], [psb])
            xs = g.resid[:, t, half * 512:(half + 1) * 512]
            S.op("dve", lambda h, xs=xs, ps=ps: h.scalar_tensor_tensor(
                out=xs, in0=xs, scalar=ALPHA, in1=ps, op0=ALU.mult, op1=ALU.add),
                reads=[g.resid_b[t], psb], writes=[g.resid_b[t]])
        ln_tile(g, g.resid[:, t, :], g.resid_b[t], lnp, tmp, tmp_b)


def diff_branch(g, l, OT, OT_b):
    S, A, W = g.S, g.A, g.W
    lam_init = 0.8 - 0.6 * math.exp(-0.3 * l)
    xT = A.alloc([8, 512], BF16)
    xT_b = [Buf() for _ in range(8)]
    wD = RR([(A.alloc([8, 384], BF16), Buf()) for _ in range(2)])
    QT = A.alloc([2, L], BF16)
    KT = A.alloc([2, L], BF16)
    V = A.alloc([NT, 128], BF16)
    QT_b = [[Buf() for _ in range(4)] for _ in range(2)]
    KT_b = [[Buf() for _ in range(4)] for _ in range(2)]
    V_b = [Buf() for _ in range(NT)]
    qbias_b, kbias_b = Buf(), Buf()
    PT = RR([(A.alloc([512], BF16), Buf()) for _ in range(4)])
    SM = RR([(A.alloc([512], F32), Buf()) for _ in range(2)])
    F = RR([(A.alloc([512], F32), Buf()) for _ in range(6)])
    mask_diff = A.alloc([4, 512], BF16)
    mk_b = Buf()
    S.op("sp", lambda h: h.dma_start(out=mask_diff, in_=W("c_mask_diff")), writes=[mk_b], dma=True)
    lq = A.alloc([4, 64], F32)
    sc = A.alloc([8], F32)
    ng = A.alloc([1], F32)
    sb = Buf()
    for i, nm in enumerate(("diff_lq1", "diff_lk1", "diff_lq2", "diff_lk2")):
        S.op("sp", lambda h, i=i, nm=nm: h.dma_start(out=lq[:, i, :], in_=W(nm)[l:l + 1, :].to_broadcast([128, 64])),
             writes=[sb], dma=True)
    S.op("sp", lambda h: h.dma_start(out=ng, in_=W("diff_norm_g")[l].rearrange("(p o) -> p o", o=1)), writes=[sb], dma=True)
    for i in range(2):
        S.op("dve", lambda h, i=i: h.tensor_tensor(out=lq[:, 2 * i, :], in0=lq[:, 2 * i, :], in1=lq[:, 2 * i + 1, :], op=ALU.mult),
             reads=[sb], writes=[sb])
        S.op("dve", lambda h, i=i: h.tensor_reduce(out=sc[:, i:i + 1], in_=lq[:, 2 * i, :], axis=mybir.AxisListType.X, op=ALU.add),
             reads=[sb], writes=[sb])
        S.op("act", lambda h, i=i: h.activation(out=sc[:, 2 + i:3 + i], in_=sc[:, i:i + 1], func=AF.Exp), reads=[sb], writes=[sb])
    S.op("dve", lambda h: h.tensor_tensor(out=sc[:, 4:5], in0=sc[:, 3:4], in1=sc[:, 2:3], op=ALU.subtract), reads=[sb], writes=[sb])
    S.op("dve", lambda h: h.tensor_scalar(out=sc[:, 5:6], in0=sc[:, 4:5], scalar1=-lam_init, scalar2=None, op0=ALU.add),
         reads=[sb], writes=[sb])
    nlam = sc[:, 5:6]
    for c in range(2):
        S.op("sp", lambda h, c=c: h.dma_start(out=KT[64:68, c, :], in_=W("c_kb")), writes=[kbias_b], dma=True)
    acc = g.banks[0:4]
    rot = RR(g.banks[4:8])
    for hd in range(4):
        w, wb = wD.get()
        for i, c0 in enumerate((768, 1280, 1792)):
            load_w(g, w[:, :, i * 128:(i + 1) * 128],  wb,
                   W("mix_w_in")[l, :, c0 + hd * 128:c0 + (hd + 1) * 128].rearrange("(k p) n -> p k n", p=128))
        for c in range(2):
            S.op("sp", lambda h, c=c, hd=hd: h.dma_start(out=QT[64:68, c, :], in_=W("c_qb_diff")[:, hd, :]),
                 writes=[qbias_b], dma=True)
        for c in range(4):
            gen_xT(g, xT, xT_b, resid_tiles(g, c * 4))
            for qk, (T_, T_b) in enumerate(((QT, QT_b), (KT, KT_b))):
                for comp in range(2):
                    ps, psb = g.ps.get()
                    col = qk * 128 + comp * 64
                    for k in range(8):
                        mm(g, ps[0:64, :], w[:, k, col:col + 64], xT[:, k, :], k == 0, k == 7, [wb, xT_b[k]], [psb])
                    evac(g, comp + qk, T_[0:64, comp, c * 512:(c + 1) * 512], ps[0:64, :], [psb], [T_b[comp][c]])
            for tl in range(4):
                ps, psb = g.ps.get()
                for k in range(8):
                    mm(g, ps[:, 0:128], xT[:, k, tl * 128:(tl + 1) * 128], w[:, k, 256:384], k == 0, k == 7,
                       [wb, xT_b[k]], [psb])
                evac(g, tl, V[:, c * 4 + tl, :], ps[:, 0:128], [psb], [V_b[c * 4 + tl]])
        for qc in range(4):
            nkb = 4 * qc + 4
            for comp in range(2):
                po, pob = acc[2 * comp]
                pr, prb = acc[2 * comp + 1]
                def s_mm(kb):
                    ps, psb = rot.get()
                    mm(g, ps, KT[0:68, comp, kb * 128:(kb + 1) * 128], QT[0:68, comp, qc * 512:(qc + 1) * 512], True, True,
                       [KT_b[comp][kb // 4], QT_b[comp][qc], qbias_b, kbias_b], [psb])
                    return ps, psb
                nxt = [s_mm(0)] + ([s_mm(1)] if nkb > 1 else [])
                for kb in range(nkb):
                    ps, psb = nxt.pop(0)
                    if kb + 2 < nkb:
                        nxt.append(s_mm(kb + 2))
                    pt, ptb = PT.get()
                    if kb >= 4 * qc:
                        sm, smb = SM.get()
                        S.op("dve", lambda h: h.tensor_tensor(
                            out=sm, in0=ps, in1=mask_diff[:, kb - 4 * qc, :], op=ALU.add), reads=[psb, mk_b], writes=[smb])
                        S.op("act", lambda h: h.activation(out=pt, in_=sm, func=AF.Exp, scale=0.125),
                             reads=[smb], writes=[ptb])
                    else:
                        S.op("act", lambda h: h.activation(out=pt, in_=ps, func=AF.Exp, scale=0.125),
                             reads=[psb], writes=[ptb])
                    mm(g, po, V[:, kb, :], pt, kb == 0, kb == nkb - 1, [V_b[kb], ptb], [pob])
                    mm(g, pr, g.ones, pt, kb == 0, kb == nkb - 1, [g.cb, ptb], [prb])
            ts = []
            for comp in range(2):
                po, pob = acc[2 * comp]
                pr, prb = acc[2 * comp + 1]
                ri, rib = F.get()
                S.op("dve", lambda h, ri=ri, pr=pr: h.reciprocal(out=ri, in_=pr), reads=[prb], writes=[rib])
                t_, tb_ = F.get()
                S.op("dve", lambda h, t_=t_, po=po, ri=ri: h.tensor_tensor(out=t_, in0=po, in1=ri, op=ALU.mult),
                     reads=[pob, rib], writes=[tb_])
                ts.append((t_, tb_))
            o, ob = F.get()
            S.op("dve", lambda h, o=o, t0=ts[0][0], t1=ts[1][0]: h.scalar_tensor_tensor(
                out=o, in0=t1, scalar=nlam, in1=t0, op0=ALU.mult, op1=ALU.add),
                reads=[ts[0][1], ts[1][1], sb], writes=[ob])
            sq, sqb = PT.get()
            S.op("act", lambda h, sq=sq, o=o: h.activation(out=sq, in_=o, func=AF.Square), reads=[ob], writes=[sqb])
            pm, pmb = rot.get()
            mm(g, pm, g.ones, sq, True, True, [g.cb, sqb], [pmb])
            sd, sdb = F.get()
            S.op("act", lambda h, sd=sd, pm=pm: h.activation(out=sd, in_=pm, func=AF.Sqrt, bias=g.eps_ap, scale=1.0 / 128.0),
                 reads=[pmb, g.cb], writes=[sdb])
            S.op("dve", lambda h, sd=sd: h.reciprocal(out=sd, in_=sd), reads=[sdb], writes=[sdb])
            S.op("dve", lambda h, o=o, sd=sd: h.tensor_tensor(out=o, in0=o, in1=sd, op=ALU.mult), reads=[ob, sdb], writes=[ob])
            S.op("dve", lambda h, o=o, hd=hd, qc=qc: h.tensor_scalar(
                out=OT[:, hd, qc * 512:(qc + 1) * 512], in0=o, scalar1=ng, scalar2=1.0 - lam_init, op0=ALU.mult, op1=ALU.mult),
                reads=[ob, sb], writes=[OT_b[hd][qc]])


def swa_branch(g, l, OT, OT_b):
    S, A, W = g.S, g.A, g.W
    xT = A.alloc([8, 512], BF16)
    xT_b = [Buf() for _ in range(8)]
    wA = RR([(A.alloc([8, 384], BF16), Buf()) for _ in range(2)])
    QT = A.alloc([4, L], BF16)
    KT = A.alloc([L], BF16)
    V = A.alloc([NT, 64], BF16)
    QT_b = [[Buf() for _ in range(4)] for _ in range(4)]
    KT_b = [Buf() for _ in range(4)]
    V_b = [Buf() for _ in range(NT)]
    qbias_b, kbias_b = Buf(), Buf()
    PT = RR([(A.alloc([4, 128], BF16), Buf()) for _ in range(4)])
    SM = RR([(A.alloc([4, 128], F32), Buf()) for _ in range(2)])
    DN = RR([(A.alloc([2, 128], F32), Buf()) for _ in range(2)])
    esink = A.alloc([2], F32)
    es_b = Buf()
    mask_swa = A.alloc([2, 128], BF16)
    mk_b = Buf()
    S.op("sp", lambda h: h.dma_start(out=mask_swa, in_=W("c_mask_swa")), writes=[mk_b], dma=True)
    S.op("sp", lambda h: h.dma_start(out=KT[64:68, :], in_=W("c_kb")), writes=[kbias_b], dma=True)
    acc = g.banks[0:2]
    rot = RR(g.banks[2:8])
    for grp in range(2):
        w, wb = wA.get()
        for (d0, c0, n) in ((0, grp * 256, 256), (256, 512 + grp * 64, 64), (320, 640 + grp * 64, 64)):
            load_w(g, w[:, :, d0:d0 + n], wb, W("mix_w_in")[l, :, c0:c0 + n].rearrange("(k p) n -> p k n", p=128))
        for r in range(4):
            S.op("sp", lambda h, r=r, grp=grp: h.dma_start(out=QT[64:68, r, :], in_=W("c_qb_swa")[:, 4 * grp + r, :]),
                 writes=[qbias_b], dma=True)
        for par in range(2):
            for j in range(2):
                hidx = 4 * grp + 2 * j + par
                S.op("sp", lambda h, par=par, j=j, hidx=hidx: h.dma_start(
                    out=esink[64 * par:64 * par + 64, j:j + 1],
                    in_=W("swa_sinks")[l:l + 1, hidx:hidx + 1].to_broadcast([64, 1])), writes=[es_b], dma=True)
        S.op("act", lambda h: h.activation(out=esink, in_=esink, func=AF.Exp), reads=[es_b], writes=[es_b])
        for c in range(4):
            gen_xT(g, xT, xT_b, resid_tiles(g, c * 4))
            for r in range(4):
                ps, psb = g.ps.get()
                for k in range(8):
                    mm(g, ps[0:64, :], w[:, k, r * 64:(r + 1) * 64], xT[:, k, :], k == 0, k == 7, [wb, xT_b[k]], [psb])
                evac(g, r, QT[0:64, r, c * 512:(c + 1) * 512], ps[0:64, :], [psb], [QT_b[r][c]])
            ps, psb = g.ps.get()
            for k in range(8):
                mm(g, ps[0:64, :], w[:, k, 256:320], xT[:, k, :], k == 0, k == 7, [wb, xT_b[k]], [psb])
            evac(g, 1, KT[0:64, c * 512:(c + 1) * 512], ps[0:64, :], [psb], [KT_b[c]])
            for tl in range(4):
                ps, psb = g.ps.get()
                for k in range(8):
                    mm(g, ps[:, 0:64], xT[:, k, tl * 128:(tl + 1) * 128], w[:, k, 320:384], k == 0, k == 7,
                       [wb, xT_b[k]], [psb])
                evac(g, tl, V[:, c * 4 + tl, :], ps[:, 0:64], [psb], [V_b[c * 4 + tl]])
        def s_blk(n):
            kbs = [n - 1, n] if n > 0 else [n]
            pts = []
            for kb in kbs:
                mi = 0 if kb == n - 1 else 1
                ps, psb = rot.get()
                mm(g, ps, KT[0:68, kb * 128:(kb + 1) * 128], QT[0:68, :, n * 128:(n + 1) * 128], True, True,
                   [KT_b[kb // 4], kbias_b, qbias_b] + [QT_b[r][n // 4] for r in range(4)], [psb])
                sm, smb = SM.get()
                S.op("dve", lambda h: h.tensor_tensor(
                    out=sm, in0=view3(ps, 4), in1=bc_mid(mask_swa[:, mi, :], 4), op=ALU.add),
                    reads=[psb, mk_b], writes=[smb])
                pt, ptb = PT.get()
                S.op("act", lambda h: h.activation(out=pt, in_=sm, func=AF.Exp, scale=0.125),
                     reads=[smb], writes=[ptb])
                pts.append((pt, ptb, kb))
            return pts
        nxt = s_blk(0)
        for n in range(NT):
            pts = nxt
            if n + 1 < NT:
                nxt = s_blk(n + 1)
            po, pob = acc[0]
            pr, prb = acc[1]
            for par in range(2):
                for i, (pt, ptb, kb) in enumerate(pts):
                    mm(g, po[64 * par:64 * par + 64, 0:256], V[:, kb, :], pt[:, par::2, :], i == 0, i == len(pts) - 1,
                       [V_b[kb], ptb], [pob])
                for i, (pt, ptb, kb) in enumerate(pts):
                    mm(g, pr[64 * par:64 * par + 64, 0:256], g.ones[:, 0:64], pt[:, par::2, :], i == 0, i == len(pts) - 1,
                       [g.cb, ptb], [prb])
            dn, dnb = DN.get()
            S.op("dve", lambda h: h.tensor_tensor(
                out=dn, in0=view3(pr[:, 0:256], 2), in1=bc_last(esink, 128), op=ALU.add), reads=[prb, es_b], writes=[dnb])
            S.op("dve", lambda h: h.reciprocal(out=dn, in_=dn), reads=[dnb], writes=[dnb])
            S.op("dve", lambda h: h.tensor_tensor(
                out=OT[:, 2 * grp:2 * grp + 2, n * 128:(n + 1) * 128], in0=view3(po[:, 0:256], 2), in1=dn, op=ALU.mult),
                reads=[pob, dnb], writes=[OT_b[2 * grp][n // 4], OT_b[2 * grp + 1][n // 4]])


def conv_branch(g, l, OT, OT_b):
    S, A, W = g.S, g.A, g.W
    gT = A.alloc([4, 30 + L], BF16)
    gT_b = [[Buf() for _ in range(5)] for _ in range(4)]
    cwT = A.alloc([4, 32], F32)
    cbv = A.alloc([3, 4], F32)
    pb_ = Buf()
    m1 = A.mark()
    xT = A.alloc([8, 512], BF16)
    xT_b = [Buf() for _ in range(8)]
    wC = A.alloc([8, 1024], BF16)
    wC_b = Buf()
    cwn = A.alloc([512], F32)
    SG = RR([(A.alloc([512], F32), Buf()) for _ in range(2)])
    load_w(g, wC, wC_b, W("mix_w_in")[l, :, 2816:3840].rearrange("(k p) n -> p k n", p=128))
    S.op("pool", lambda h: h.memset(cwn[0:32, :], 0.0), writes=[pb_])
    S.op("sp", lambda h: h.dma_start(out=cwn[0:31, :], in_=W("conv_w")[l]), writes=[pb_], dma=True)
    for i, nm in enumerate(("conv_b", "conv_ln_g", "conv_ln_b")):
        S.op("sp", lambda h, i=i, nm=nm: h.dma_start(out=cbv[:, i, :], in_=W(nm)[l].rearrange("(c p) -> p c", p=128),
                                                     allow_slow_non_contiguous=True), writes=[pb_], dma=True)
    for cc in range(4):
        ps, psb = g.ps.get()
        S.op("pe", lambda h, ps=ps, cc=cc: h.transpose(out=ps[:, 0:32], in_=cwn[0:32, cc * 128:(cc + 1) * 128],
                                                       identity=g.ident[0:32, 0:32]), reads=[pb_, g.cb], writes=[psb])
        S.op("dve", lambda h, ps=ps, cc=cc: h.tensor_copy(out=cwT[:, cc, 0:31], in_=ps[:, 0:31]), reads=[psb], writes=[pb_])
        S.op("pool", lambda h, cc=cc: h.memset(gT[:, cc, 0:30], 0.0), writes=[gT_b[cc][4]])
    for c in range(4):
        gen_xT(g, xT, xT_b, resid_tiles(g, c * 4))
        for cc in range(4):
            pv, pvb = g.ps.get()
            pg, pgb = g.ps.get()
            for k in range(8):
                mm(g, pv, wC[:, k, cc * 128:(cc + 1) * 128], xT[:, k, :], k == 0, k == 7, [wC_b, xT_b[k]], [pvb])
            for k in range(8):
                mm(g, pg, wC[:, k, 512 + cc * 128:512 + (cc + 1) * 128], xT[:, k, :], k == 0, k == 7, [wC_b, xT_b[k]], [pgb])
            sg, sgb = SG.get()
            S.op("act", lambda h, sg=sg, pg=pg: h.activation(out=sg, in_=pg, func=AF.Sigmoid), reads=[pgb], writes=[sgb])
            S.op("dve", lambda h, sg=sg, pv=pv, cc=cc, c=c: h.tensor_tensor(
                out=gT[:, cc, 30 + c * 512:30 + (c + 1) * 512], in0=pv, in1=sg, op=ALU.mult),
                reads=[pvb, sgb], writes=[gT_b[cc][c]])
    g.dbg_out("gT", gT, [b for r in gT_b for b in r])
    g.dbg_out("cwT", cwT, [pb_])
    g.dbg_out("cbv", cbv, [pb_])
    A.reset(m1)
    S.barrier()
    Dg = A.alloc([4, 31, 128], BF16)
    Dg_b = [Buf() for _ in range(4)]
    yb = A.alloc([4, 512], F32)
    sq = A.alloc([4, 512], BF16)
    ybh = A.alloc([4, 512], BF16)
    ybh_b = [Buf() for _ in range(4)]
    yb_b = [Buf() for _ in range(4)]
    sq_b = [Buf() for _ in range(4)]
    Fm = [(A.alloc([512], F32), Buf()) for _ in range(3)]
    def build_dg(cc):
        for j in range(31):
            S.op("dve" if j % 2 == 0 else "pool", lambda h: h.tensor_scalar(
                out=Dg[:, cc, j, :], in0=g.identb, scalar1=cwT[:, cc, j:j + 1], scalar2=None, op0=ALU.mult),
                reads=[pb_, g.cb], writes=[Dg_b[cc]])

    def conv_mm(c):
        banks = []
        if c == 0:
            build_dg(0)
        for cc in range(4):
            if c == 0 and cc + 1 < 4:
                build_dg(cc + 1)
            ps, psb = g.ps.get()
            rd = [Dg_b[cc], gT_b[cc][c]] + ([gT_b[cc][c - 1]] if c > 0 else [gT_b[cc][4]])
            for j in range(31):
                mm(g, ps, Dg[:, cc, j, :], gT[:, cc, c * 512 + j:c * 512 + j + 512], j == 0, j == 30, rd, [psb])
            banks.append((ps, psb))
        return banks
    nxt_banks = conv_mm(0)
    for c in range(4):
        banks = nxt_banks
        for cc in range(4):
            ps, psb = banks[cc]
            S.op("act", lambda h: h.activation(out=yb[:, cc, :], in_=ps, func=AF.Identity,
                                               bias=cbv[:, 0, cc:cc + 1], scale=1.0),
                 reads=[psb, pb_], writes=[yb_b[cc]])
            S.op("act", lambda h: h.activation(out=sq[:, cc, :], in_=ps, func=AF.Square,
                                               bias=cbv[:, 0, cc:cc + 1], scale=1.0),
                 reads=[psb, pb_], writes=[sq_b[cc]])
            S.op("act", lambda h: h.activation(out=ybh[:, cc, :], in_=ps, func=AF.Identity,
                                               bias=cbv[:, 0, cc:cc + 1], scale=1.0),
                 reads=[psb, pb_], writes=[ybh_b[cc]])
        if c + 1 < 4:
            nxt_banks = conv_mm(c + 1)
        if c == 0:
            g.dbg_out("yb0", yb, yb_b)
            g.dbg_out("sq0", sq, sq_b)
        pm1, pm1b = g.ps.get()
        pm2, pm2b = g.ps.get()
        for cc in range(4):
            mm(g, pm1, g.ones, ybh[:, cc, :], cc == 0, cc == 3, [g.cb, ybh_b[cc]], [pm1b])
        for cc in range(4):
            mm(g, pm2, g.ones, sq[:, cc, :], cc == 0, cc == 3, [g.cb, sq_b[cc]], [pm2b])
        (mean, meb), (msq, msb), (rs, rsb) = Fm
        S.op("act", lambda h: h.activation(out=mean, in_=pm1, func=AF.Copy, scale=1.0 / 512.0), reads=[pm1b], writes=[meb])
        S.op("act", lambda h: h.activation(out=msq, in_=pm1, func=AF.Square, scale=1.0 / 512.0), reads=[pm1b], writes=[msb])
        S.op("dve", lambda h: h.scalar_tensor_tensor(out=rs, in0=pm2, scalar=1.0 / 512.0, in1=msq, op0=ALU.mult, op1=ALU.subtract),
             reads=[pm2b, msb], writes=[rsb])
        S.op("act", lambda h: h.activation(out=rs, in_=rs, func=AF.Sqrt, bias=g.eps_ap, scale=1.0), reads=[rsb, g.cb], writes=[rsb])
        S.op("dve", lambda h: h.reciprocal(out=rs, in_=rs), reads=[rsb], writes=[rsb])
        if c == 0:
            g.dbg_out("mean0", mean, [meb])
            g.dbg_out("msq0", msq, [msb])
            g.dbg_out("rs0", rs, [rsb])
        for cc in range(4):
            S.op("pool", lambda h, cc=cc: h.tensor_tensor(out=yb[:, cc, :], in0=yb[:, cc, :], in1=mean, op=ALU.subtract),
                 reads=[yb_b[cc], meb], writes=[yb_b[cc]])
            S.op("dve", lambda h, cc=cc: h.tensor_tensor(out=yb[:, cc, :], in0=yb[:, cc, :], in1=rs, op=ALU.mult),
                 reads=[yb_b[cc], rsb], writes=[yb_b[cc]])
            S.op("act", lambda h, cc=cc, c=c: h.activation(out=OT[:, cc, c * 512:(c + 1) * 512], in_=yb[:, cc, :], func=AF.Silu,
                                                           bias=cbv[:, 2, cc:cc + 1], scale=cbv[:, 1, cc:cc + 1]),
                 reads=[yb_b[cc], pb_], writes=[OT_b[cc][c]])
    g.dbg_out("OTc", OT, [b for r in OT_b for b in r])


def sincos(g, x, s_out, c_out, t1, t2, xb, outb=None, t3=None, t4=None, cb2=None):
    S = g.S
    chains = [(0.0, s_out, t1, t2, Buf()), (math.pi / 2.0, c_out, t3 if t3 is not None else t1, t4 if t4 is not None else t2,
                                           Buf() if t3 is not None else None)]
    if t3 is None:
        chains[1] = chains[1][:4] + (chains[0][4],)
    steps = []
    for shift, out, a, b, cb in chains:
        st = [
            ("dve", lambda h, a=a, shift=shift: h.tensor_scalar(out=a, in0=x, scalar1=1.0 / TWO_PI, scalar2=shift / TWO_PI + MAGIC,
                                                                 op0=ALU.mult, op1=ALU.add)),
            ("dve", lambda h, a=a: h.tensor_scalar(out=a, in0=a, scalar1=-MAGIC, scalar2=None, op0=ALU.add)),
            ("dve", lambda h, a=a, b=b: h.scalar_tensor_tensor(out=b, in0=a, scalar=-CW1, in1=x, op0=ALU.mult, op1=ALU.add)),
            ("dve", lambda h, a=a, b=b: h.scalar_tensor_tensor(out=b, in0=a, scalar=-CW2, in1=b, op0=ALU.mult, op1=ALU.add)),
            ("dve", lambda h, b=b, shift=shift: h.tensor_scalar(out=b, in0=b, scalar1=shift, scalar2=3.1415925, op0=ALU.add, op1=ALU.min)),
            ("dve", lambda h, b=b: h.tensor_scalar(out=b, in0=b, scalar1=-3.1415925, scalar2=None, op0=ALU.max)),
            ("act", lambda h, b=b, out=out: h.activation(out=out, in_=b, func=AF.Sin)),
        ]
        steps.append((st, cb))
    order = []
    if t3 is None:
        for st, cb in steps:
            order += [(e, fn, cb) for e, fn in st]
    else:
        for i in range(7):
            for st, cb in steps:
                order.append(st[i] + (cb,))
    c2 = chains[1][4]
    for idx, (e, fn, cb) in enumerate(order):
        last = (e == "act")
        cbs = cb2 if (cb2 is not None and cb is c2 and t3 is not None) else [cb]
        wr = list(cbs) + ([xb] if last else []) + ([outb] if (last and outb is not None) else [])
        S.op(e, fn, reads=[xb] + list(cbs), writes=wr)


def s5_branch(g, l, OT, OT_b):
    S, A, W = g.S, g.A, g.W
    yT, yT_b = OT, OT_b

    def f(shape):
        return A.alloc(shape, F32)
    sb = Buf("s5small")

    def dv(fn):
        S.op("dve", fn, reads=[sb], writes=[sb])

    def ac(fn):
        S.op("act", fn, reads=[sb], writes=[sb])
    WBr = A.alloc([16, 128], BF16)
    WBi = A.alloc([16, 128], BF16)
    WCr = A.alloc([16, 64], BF16)
    WCn = A.alloc([16, 64], BF16)
    wmat_b = Buf()
    mag, s128, c128 = f([16]), f([16]), f([16])
    dsk, glb = f([4]), f([4])
    uT = A.alloc([4, L], BF16)
    uT_b = [[Buf() for _ in range(4)] for _ in range(4)]
    th = f([16])
    m1 = A.mark()
    ar, ai, ls, sn, cs, t1, t2 = (f([16]) for _ in range(7))
    abr, abi, den, nr, cr, ci, t3 = (f([16]) for _ in range(7))
    Bs_r, Bs_i = f([16, 128]), f([16, 128])
    braw_r, braw_i = f([16, 16]), f([16, 16])
    tb1 = f([4, 16])
    S.op("sp", lambda h: h.dma_start(out=ar, in_=W("s5_a_re")[l].rearrange("(ct gl) p -> (gl p) ct", gl=2),
                                     allow_slow_non_contiguous=True), writes=[sb], dma=True)
    S.op("sp", lambda h: h.dma_start(out=ai, in_=W("s5_a_im")[l].rearrange("(ct gl) p -> (gl p) ct", gl=2),
                                     allow_slow_non_contiguous=True), writes=[sb], dma=True)
    for gl in range(2):
        S.op("sp", lambda h, gl=gl: h.dma_start(
            out=ls[64 * gl:64 * gl + 64, :],
            in_=W("s5_log_step")[l].rearrange("(ct gl) -> gl ct", gl=2)[gl:gl + 1, :].to_broadcast([64, 16]),
            allow_slow_non_contiguous=True), writes=[sb], dma=True)
    S.op("sp", lambda h: h.dma_start(out=braw_r, in_=W("s5_b_re")[l].rearrange("(ct gl) p c -> (gl p) ct c", gl=2)),
         writes=[sb], dma=True)
    S.op("sp", lambda h: h.dma_start(out=braw_i, in_=W("s5_b_im")[l].rearrange("(ct gl) p c -> (gl p) ct c", gl=2)),
         writes=[sb], dma=True)
    for i, nm in enumerate(("s5_d", "s5_glu_b")):
        dst = (dsk, glb)[i]
        S.op("sp", lambda h, dst=dst, nm=nm: h.dma_start(out=dst, in_=W(nm)[l].rearrange("(c p) -> p c", p=128),
                                                         allow_slow_non_contiguous=True), writes=[sb], dma=True)
    S.op("pool", lambda h: h.memset(Bs_r, 0.0), writes=[sb])
    S.op("pool", lambda h: h.memset(Bs_i, 0.0), writes=[sb])
    ac(lambda h: h.activation(out=ls, in_=ls, func=AF.Exp))
    dv(lambda h: h.tensor_tensor(out=t1, in0=ar, in1=ls, op=ALU.mult))
    ac(lambda h: h.activation(out=mag, in_=t1, func=AF.Exp))
    dv(lambda h: h.tensor_tensor(out=th, in0=ai, in1=ls, op=ALU.mult))
    sincos(g, th, sn, cs, t1, t2, sb)
    dv(lambda h: h.tensor_tensor(out=abr, in0=mag, in1=cs, op=ALU.mult))
    dv(lambda h: h.tensor_tensor(out=abi, in0=mag, in1=sn, op=ALU.mult))
    dv(lambda h: h.tensor_tensor(out=t1, in0=ar, in1=ar, op=ALU.mult))
    dv(lambda h: h.tensor_tensor(out=den, in0=ai, in1=ai, op=ALU.mult))
    dv(lambda h: h.tensor_tensor(out=den, in0=den, in1=t1, op=ALU.add))
    dv(lambda h: h.reciprocal(out=den, in_=den))
    dv(lambda h: h.tensor_scalar(out=nr, in0=abr, scalar1=-1.0, scalar2=None, op0=ALU.add))
    dv(lambda h: h.tensor_tensor(out=t1, in0=nr, in1=ar, op=ALU.mult))
    dv(lambda h: h.tensor_tensor(out=t2, in0=abi, in1=ai, op=ALU.mult))
    dv(lambda h: h.tensor_tensor(out=t1, in0=t1, in1=t2, op=ALU.add))
    dv(lambda h: h.tensor_tensor(out=cr, in0=t1, in1=den, op=ALU.mult))
    dv(lambda h: h.tensor_tensor(out=t1, in0=abi, in1=ar, op=ALU.mult))
    dv(lambda h: h.tensor_tensor(out=t2, in0=nr, in1=ai, op=ALU.mult))
    dv(lambda h: h.tensor_tensor(out=t1, in0=t1, in1=t2, op=ALU.subtract))
    dv(lambda h: h.tensor_tensor(out=ci, in0=t1, in1=den, op=ALU.mult))
    dv(lambda h: h.tensor_scalar(out=t3, in0=th, scalar1=512.0, scalar2=None, op0=ALU.mult))
    sincos(g, t3, s128, c128, t1, t2, sb)
    for q in range(4):
        for gl in range(2):
            rows = slice(64 * gl, 64 * gl + 64)
            c0 = 32 * q + 16 * gl
            crb = bc_last(cr[rows, q::4], 16)
            cib = bc_last(ci[rows, q::4], 16)
            br_, bi_ = braw_r[rows, q::4, :], braw_i[rows, q::4, :]
            o_r, o_i = Bs_r[rows, q::4, c0:c0 + 16], Bs_i[rows, q::4, c0:c0 + 16]
            tt = tb1[rows]
            dv(lambda h, tt=tt, bi_=bi_, cib=cib: h.tensor_tensor(out=tt, in0=bi_, in1=cib, op=ALU.mult))
            dv(lambda h, o_r=o_r, br_=br_, crb=crb: h.tensor_tensor(out=o_r, in0=br_, in1=crb, op=ALU.mult))
            dv(lambda h, o_r=o_r, tt=tt: h.tensor_tensor(out=o_r, in0=o_r, in1=tt, op=ALU.subtract))
            dv(lambda h, tt=tt, br_=br_, cib=cib: h.tensor_tensor(out=tt, in0=br_, in1=cib, op=ALU.mult))
            dv(lambda h, o_i=o_i, bi_=bi_, crb=crb: h.tensor_tensor(out=o_i, in0=bi_, in1=crb, op=ALU.mult))
            dv(lambda h, o_i=o_i, tt=tt: h.tensor_tensor(out=o_i, in0=o_i, in1=tt, op=ALU.add))
    for ct in range(16):
        q = ct % 4
        for i, (src, dst) in enumerate(((Bs_r, WBr), (Bs_i, WBi))):
            ps, psb = g.ps.get()
            S.op("pe", lambda h, ps=ps, src=src, ct=ct: h.transpose(out=ps[:, 0:128], in_=src[:, ct, :], identity=g.ident),
                 reads=[sb, g.cb], writes=[psb])
            er = slice(64, 128) if q == 3 else slice(32 * q, 32 * q + 32)
            evac(g, i, dst[er, ct, :], ps[er, 0:128], [psb], [wmat_b])
    A.reset(m1)
    S.barrier()
    Cs_r, Cs_i = f([16, 128]), f([16, 128])
    S.op("pool", lambda h: h.memset(Cs_r[0:64], 0.0), writes=[sb])
    S.op("pool", lambda h: h.memset(Cs_i[0:64], 0.0), writes=[sb])
    for gl in range(2):
        for par in range(2):
            for (dst, nm) in ((Cs_r, "s5_c_re"), (Cs_i, "s5_c_im")):
                p0 = 32 * par + 16 * gl
                S.op("sp", lambda h, gl=gl, par=par, dst=dst, nm=nm, p0=p0: h.dma_start(
                    out=dst[p0:p0 + 16, par::2, 64 * gl:64 * gl + 64],
                    in_=W(nm)[l].rearrange("(ct gl) c p -> gl c ct p", gl=2)[gl][:, par::2, :]), writes=[sb], dma=True)
    for ct in range(16):
        for i, (src, dst, scl) in enumerate(((Cs_r, WCr, None), (Cs_i, WCn, -1.0))):
            ps, psb = g.ps.get()
            S.op("pe", lambda h, ps=ps, src=src, ct=ct: h.transpose(out=ps[:, 0:64], in_=src[0:64, ct, :],
                                                                    identity=g.ident[0:64, 0:64]),
                 reads=[sb, g.cb], writes=[psb])
            evac(g, i, dst[:, ct, :], ps[:, 0:64], [psb], [wmat_b], scale=scl)
    A.reset(m1)
    S.barrier()
    xT = A.alloc([8, 512], BF16)
    xT_b = [Buf() for _ in range(8)]
    wS = A.alloc([8, 512], BF16)
    wS_b = Buf()
    load_w(g, wS, wS_b, W("mix_w_in")[l, :, 2304:2816].rearrange("(k p) n -> p k n", p=128))
    for c in range(4):
        gen_xT(g, xT, xT_b, resid_tiles(g, c * 4))
        for cc in range(4):
            ps, psb = g.ps.get()
            for k in range(8):
                mm(g, ps, wS[:, k, cc * 128:(cc + 1) * 128], xT[:, k, :], k == 0, k == 7, [wS_b, xT_b[k]], [psb])
            evac(g, cc, uT[:, cc, c * 512:(c + 1) * 512], ps, [psb], [uT_b[cc][c]])
    A.reset(m1)
    S.barrier()
    iota = f([512])
    S.op("sp", lambda h: h.dma_start(out=iota, in_=W("c_iota")), writes=[sb], dma=True)
    TB = RR([((f([512]), f([512])), Buf()) for _ in range(2)])
    angj, sj1, sj2 = f([512]), f([512]), f([512])
    sjb = Buf()
    brp, bip, ta, tb = f([512]), f([512]), f([512]), f([512])
    dB = Buf()
    brp_b, bip_b, ta_b, tb_b = Buf(), Buf(), Buf(), Buf()
    W2 = RR([((f([512]), f([512])), (Buf(), Buf())) for _ in range(2)])
    pa, pb2, pc, pd = f([512]), f([512]), f([512]), f([512])
    pa_b, pb_b, pc_b, pd_b = Buf(), Buf(), Buf(), Buf()
    INI = RR([(f([4]), Buf()) for _ in range(2)])
    XR = RR([(A.alloc([512], BF16), Buf()) for _ in range(2)])
    XI = RR([(A.alloc([512], BF16), Buf()) for _ in range(2)])
    vv, v2, vb = brp, bip, dB
    yacc = g.banks[0:4]
    rot = RR(g.banks[4:8])
    bu_q = []

    def bu_mm(cc, q, c):
        ct = 4 * cc + q
        rows = slice(64, 128) if q == 3 else slice(32 * q, 32 * q + 32)
        pbr, pbrb = rot.get()
        pbi, pbib = rot.get()
        mm(g, pbr, WBr[rows, ct, :], uT[rows, cc, c * 512:(c + 1) * 512], True, True, [wmat_b, uT_b[cc][c]], [pbrb])
        mm(g, pbi, WBi[rows, ct, :], uT[rows, cc, c * 512:(c + 1) * 512], True, True, [wmat_b, uT_b[cc][c]], [pbib])
        return pbr, pbrb, pbi, pbib
    for cc in range(4):
        for q in range(4):
            ct = 4 * cc + q
            rows = slice(64, 128) if q == 3 else slice(32 * q, 32 * q + 32)
            orow = slice(64 * (q // 2), 64 * (q // 2) + 64)
            (sinT, cosT), tbb = TB.get()
            S.op("dve", lambda h: h.tensor_scalar(out=angj, in0=iota, scalar1=th[:, ct:ct + 1], scalar2=None, op0=ALU.mult),
                 reads=[sb, sjb], writes=[sjb])
            sincos(g, angj, sinT, cosT, sj1, sj2, sjb, outb=tbb, t3=ta, t4=tb, cb2=[ta_b, tb_b])
            rho = mag[:, ct:ct + 1].to_broadcast([128, 512])
            cc_, ss_ = c128[:, ct:ct + 1], s128[:, ct:ct + 1]
            prev = None
            for c in range(4):
                py, pyb = yacc[c]
                if not bu_q:
                    bu_q.append(bu_mm(cc, q, c))
                pbr, pbrb, pbi, pbib = bu_q.pop(0)
                nx = (cc, q, c + 1) if c + 1 < 4 else ((cc, q + 1, 0) if q + 1 < 4 else ((cc + 1, 0, 0) if cc + 1 < 4 else None))
                if nx is not None:
                    bu_q.append(bu_mm(*nx))
                S.op("dve", lambda h: h.tensor_tensor(out=ta, in0=pbi, in1=sinT, op=ALU.mult), reads=[pbib, tbb], writes=[ta_b])
                S.op("dve", lambda h: h.tensor_tensor(out=tb, in0=pbr, in1=sinT, op=ALU.mult), reads=[pbrb, tbb], writes=[tb_b])
                S.op("dve", lambda h: h.tensor_tensor(out=brp, in0=pbr, in1=cosT, op=ALU.mult), reads=[pbrb, tbb], writes=[brp_b])
                S.op("dve", lambda h: h.tensor_tensor(out=bip, in0=pbi, in1=cosT, op=ALU.mult), reads=[pbib, tbb], writes=[bip_b])
                S.op("dve", lambda h: h.tensor_tensor(out=brp, in0=brp, in1=ta, op=ALU.add), reads=[brp_b, ta_b], writes=[brp_b])
                S.op("dve", lambda h: h.tensor_tensor(out=bip, in0=bip, in1=tb, op=ALU.subtract), reads=[bip_b, tb_b], writes=[bip_b])
                (wr, wi), (wr_b, wi_b) = W2.get()
                if prev is None:
                    i_re, i_im, ird = 0.0, 0.0, []
                else:
                    (pwr, pwi), (pwr_b, pwi_b) = prev
                    ini, inib = INI.get()
                    lr, li = pwr[:, 511:512], pwi[:, 511:512]
                    rdl = [pwr_b, pwi_b, sb]
                    S.op("dve", lambda h: h.tensor_scalar(out=ini[:, 0:1], in0=li, scalar1=ss_, scalar2=None, op0=ALU.mult),
                         reads=rdl, writes=[inib])
                    S.op("dve", lambda h: h.tensor_scalar(out=ini[:, 2:3], in0=li, scalar1=cc_, scalar2=None, op0=ALU.mult),
                         reads=rdl, writes=[inib])
                    S.op("dve", lambda h: h.scalar_tensor_tensor(out=ini[:, 1:2], in0=lr, scalar=cc_, in1=ini[:, 0:1],
                                                                 op0=ALU.mult, op1=ALU.subtract), reads=rdl + [inib], writes=[inib])
                    S.op("dve", lambda h: h.scalar_tensor_tensor(out=ini[:, 3:4], in0=lr, scalar=ss_, in1=ini[:, 2:3],
                                                                 op0=ALU.mult, op1=ALU.add), reads=rdl + [inib], writes=[inib])
                    i_re, i_im, ird = ini[:, 1:2], ini[:, 3:4], [inib]
                S.op("dve", lambda h: h.tensor_tensor_scan(out=wr, data0=rho, data1=brp, initial=i_re, op0=ALU.mult, op1=ALU.add),
                     reads=[brp_b, sb] + ird, writes=[wr_b])
                S.op("dve", lambda h: h.tensor_tensor_scan(out=wi, data0=rho, data1=bip, initial=i_im, op0=ALU.mult, op1=ALU.add),
                     reads=[bip_b, sb] + ird, writes=[wi_b])
                prev = ((wr, wi), (wr_b, wi_b))
                xr, xrb = XR.get()
                xi, xib = XI.get()
                S.op("pool", lambda h: h.tensor_tensor(out=pa, in0=wr, in1=cosT, op=ALU.mult), reads=[wr_b, tbb, sjb], writes=[pa_b])
                S.op("pool", lambda h: h.tensor_tensor(out=pb2, in0=wi, in1=sinT, op=ALU.mult), reads=[wi_b, tbb, sjb], writes=[pb_b])
                S.op("pool", lambda h: h.tensor_tensor(out=pc, in0=wi, in1=cosT, op=ALU.mult), reads=[wi_b, tbb, sjb], writes=[pc_b])
                S.op("pool", lambda h: h.tensor_tensor(out=pd, in0=wr, in1=sinT, op=ALU.mult), reads=[wr_b, tbb, sjb], writes=[pd_b])
                S.op("pool", lambda h: h.tensor_tensor(out=xr, in0=pa, in1=pb2, op=ALU.subtract), reads=[pa_b, pb_b], writes=[xrb])
                S.op("pool", lambda h: h.tensor_tensor(out=xi, in0=pc, in1=pd, op=ALU.add), reads=[pc_b, pd_b], writes=[xib])
                mm(g, py[orow, :], WCr[:, ct, :], xr, q % 2 == 0, False, [wmat_b, xrb], [pyb])
                mm(g, py[orow, :], WCn[:, ct, :], xi, False, q % 2 == 1, [wmat_b, xib], [pyb])
        for c in range(4):
            py, pyb = yacc[c]
            S.op("dve", lambda h: h.scalar_tensor_tensor(
                out=vv, in0=uT[:, cc, c * 512:(c + 1) * 512], scalar=dsk[:, cc:cc + 1], in1=py, op0=ALU.mult, op1=ALU.add),
                reads=[pyb, uT_b[cc][c], sb, vb], writes=[vb])
            S.op("act", lambda h: h.activation(out=v2, in_=vv, func=AF.Square), reads=[vb], writes=[vb])
            S.op("dve", lambda h: h.tensor_scalar(out=v2, in0=v2, scalar1=0.044715, scalar2=1.0, op0=ALU.mult, op1=ALU.add),
                 reads=[vb], writes=[vb])
            S.op("dve", lambda h: h.tensor_tensor(out=v2, in0=v2, in1=vv, op=ALU.mult), reads=[vb], writes=[vb])
            S.op("act", lambda h: h.activation(out=v2, in_=v2, func=AF.Sigmoid, scale=1.5957691216057308), reads=[vb], writes=[vb])
            S.op("dve", lambda h: h.tensor_tensor(out=yT[:, cc, c * 512:(c + 1) * 512], in0=vv, in1=v2, op=ALU.mult),
                 reads=[vb], writes=[yT_b[cc][c]])
    A.reset(m1)
    S.barrier()
    gw = A.alloc([4, 512], BF16)
    gw_b = Buf()
    load_w(g, gw, gw_b, W("s5_glu_w")[l].rearrange("(k p) n -> p k n", p=128))
    vv = f([512])
    vb = Buf()
    for c in range(4):
        zs = []
        for oc in range(4):
            ps, psb = g.banks[oc]
            for kc in range(4):
                mm(g, ps, gw[:, kc, oc * 128:(oc + 1) * 128], yT[:, kc, c * 512:(c + 1) * 512], kc == 0, kc == 3,
                   [gw_b, yT_b[kc][c]], [psb])
            zs.append((ps, psb))
        for oc in range(4):
            ps, psb = zs[oc]
            S.op("act", lambda h, ps=ps, oc=oc: h.activation(out=vv, in_=ps, func=AF.Sigmoid, bias=glb[:, oc:oc + 1], scale=1.0),
                 reads=[psb, sb, vb], writes=[vb])
            S.op("dve", lambda h, oc=oc, c=c: h.tensor_tensor(out=yT[:, oc, c * 512:(c + 1) * 512],
                                                              in0=yT[:, oc, c * 512:(c + 1) * 512], in1=vv, op=ALU.mult),
                 reads=[vb, yT_b[oc][c]], writes=[yT_b[oc][c]])


PARTS = ("ffn1", "mix", "cross", "ffn2")
_cache = {}


def run(inputs, layers, parts, xin, branches="abcd"):
    key = (tuple(layers), tuple(parts), branches)
    if key not in _cache:
        _cache[key] = build(layers, parts, branches)
    nc, g = _cache[key]
    in_maps = []
    shared = {}
    for name in g.dr:
        if name in ("x", "mem"):
            continue
        if name.startswith("c_"):
            shared[name] = consts()[name][0]
        else:
            shared[name] = np.ascontiguousarray(np.asarray(inputs[name], dtype=np.float32))
    for c in range(NCORES):
        m = dict(shared)
        m["x"] = np.ascontiguousarray(xin[c * SEQ_PER_CORE:(c + 1) * SEQ_PER_CORE])
        if "mem" in g.dr:
            m["mem"] = np.ascontiguousarray(np.asarray(inputs["mem"], dtype=np.float32)[c * SEQ_PER_CORE:(c + 1) * SEQ_PER_CORE])
        in_maps.append(m)
    res = run_bass_kernel_spmd(nc, in_maps, core_ids=list(range(NCORES)))
    if DEBUG:
        LAST.update({k: v for k, v in res.results[0].items()})
    return np.concatenate([r["out"] for r in res.results], axis=0)


FUSED = True


def kernel(**inputs):
    x = np.asarray(inputs["x"], dtype=np.float32)
    if FUSED:
        return run(inputs, list(range(DEPTH)), PARTS, x)
    for l in range(DEPTH):
        x = run(inputs, [l], PARTS, x)
    return x
```

```python
from contextlib import ExitStack
import math
import numpy as np
import concourse.bass as bass
import concourse.mybir as mybir
from concourse.bass_utils import run_bass_kernel_spmd

F32 = mybir.dt.float32
BF16 = mybir.dt.bfloat16
AF = mybir.ActivationFunctionType
ALU = mybir.AluOpType

D = 1024
L = 2048
NT = L // 128
DEPTH = 4
FFN = 2816
NJ = FFN // 128
MIX_IN = 7936
ALPHA = (2.0 * DEPTH) ** 0.25
EPS = 1e-5
NCORES = 8
SEQ_PER_CORE = 2

ENGS = ["pe", "act", "dve", "pool", "sp"]
NSEM = 4
PHASE = 2048
NDSEM = 40


class Buf:
    __slots__ = ("w", "r", "name")

    def __init__(self, name=""):
        self.w = None
        self.r = {}
        self.name = name


class _Rec:
    def __init__(self):
        self.call = None

    def __getattr__(self, name):
        def f(*a, **k):
            self.call = (name, a, k)
            return self
        return f


class Sched:
    def __init__(self, nc, es):
        self.nc = nc
        self.h = dict(pe=nc.tensor, act=nc.scalar, dve=nc.vector, pool=nc.gpsimd, sp=nc.sync)
        self.q = {e: [] for e in ENGS}
        self.cnt = {e: 0 for e in ENGS}
        self.semh = {}
        self.esval = {}
        for e in ENGS:
            for i in range(NSEM):
                self.semh[("e", e, i)] = es.enter_context(nc.semaphore(f"s_{e}{i}"))
                self.esval[("e", e, i)] = 0
        self.dval = [0] * NDSEM
        for i in range(NDSEM):
            self.semh[("d", i)] = es.enter_context(nc.semaphore(f"d{i}"))
        self.dnext = 0
        self.seen = {e: {} for e in ENGS}
        self.last = {}
        self.nwait = 0

    def _wait(self, eng, k, v):
        if self.seen[eng].get(k, 0) >= v:
            return
        self.seen[eng][k] = v
        sem = self.semh[k]
        self.q[eng].append(lambda h, sem=sem, v=v: h.wait_ge(sem, v))
        self.nwait += 1

    def op(self, eng, fn, reads=(), writes=(), dma=False):
        rec = _Rec()
        fn(rec)
        _name, _a, _k = rec.call

        def fn(h, _name=_name, _a=_a, _k=_k):
            return getattr(h, _name)(*_a, **_k)
        deps = {}

        def add(k, v):
            if deps.get(k, 0) < v:
                deps[k] = v

        cow = {}
        for b in reads:
            if b.w:
                for k, v in b.w.items():
                    add(k, v)
        for b in writes:
            co = bool(dma and b.w and not b.r and all(k[0] == "d" for k in b.w))
            cow[id(b)] = co
            if b.w and not co:
                for k, v in b.w.items():
                    add(k, v)
            for k, v in b.r.items():
                if k[0] == "e" and k[1] == eng:
                    continue
                add(k, v)
        for k, v in deps.items():
            if eng == "pe" and k[0] == "e" and k[1] == "pe":
                continue
            self._wait(eng, k, v)
        if dma:
            i = self.dnext
            self.dnext = (i + 1) % NDSEM
            k = ("d", i)
            prev = self.dval[i]
            if prev:
                self._wait(eng, k, prev)
            self.dval[i] = prev + 16
            tok = (k, prev + 16)
            sem = self.semh[k]
            self.q[eng].append(lambda h, fn=fn, sem=sem: fn(h).then_inc(sem, 16))
        else:
            c = self.cnt[eng]
            self.cnt[eng] = c + 1
            k = ("e", eng, (c // PHASE) % NSEM)
            self.esval[k] += 1
            tok = (k, self.esval[k])
            sem = self.semh[k]
            self.q[eng].append(lambda h, fn=fn, sem=sem: fn(h).then_inc(sem, 1))
        self.last[tok[0]] = tok[1]
        for b in writes:
            if cow.get(id(b)):
                b.w[tok[0]] = tok[1]
            else:
                b.w = {tok[0]: tok[1]}
            b.r = {}
        for b in reads:
            if b.r.get(tok[0], 0) < tok[1]:
                b.r[tok[0]] = tok[1]
        return tok

    def barrier(self):
        for e in ENGS:
            for k, v in self.last.items():
                if k[0] == "e" and k[1] == e:
                    continue
                self._wait(e, k, v)

    def finish(self, eng="sp"):
        for k, v in self.last.items():
            self._wait(eng, k, v)

    def replay(self, block):
        q = self.q

        @block.tensor
        def _(h):
            for f in q["pe"]:
                f(h)

        @block.scalar
        def _(h):
            for f in q["act"]:
                f(h)

        @block.vector
        def _(h):
            for f in q["dve"]:
                f(h)

        @block.gpsimd
        def _(h):
            for f in q["pool"]:
                f(h)

        @block.sync
        def _(h):
            for f in q["sp"]:
                f(h)


class RR:
    def __init__(self, aps, name="p"):
        self.slots = [a if isinstance(a, tuple) else (a, Buf(f"{name}{i}")) for i, a in enumerate(aps)]
        self.i = 0

    def get(self):
        s = self.slots[self.i]
        self.i = (self.i + 1) % len(self.slots)
        return s


class Arena:
    def __init__(self, t_bf):
        self.tb = t_bf
        self.tf = t_bf.bitcast(F32)
        self.off = 0
        self.cap = t_bf.shape[1] * 2

    def mark(self):
        return self.off

    def reset(self, m):
        self.off = m

    def alloc(self, shape, dt):
        esz = 4 if dt == F32 else 2
        n = int(np.prod(shape))
        self.off = (self.off + 63) // 64 * 64
        o = self.off
        self.off += n * esz
        assert self.off <= self.cap, (self.off, self.cap)
        t = self.tf if dt == F32 else self.tb
        ap = t[:, o // esz: o // esz + n]
        if len(shape) == 2:
            ap = ap.rearrange("p (a b) -> p a b", a=shape[0])
        elif len(shape) == 3:
            ap = ap.rearrange("p (a b c) -> p a b c", a=shape[0], b=shape[1])
        return ap


class K:
    pass


SHAPES = {
    "x": [SEQ_PER_CORE, L, D], "mem": [SEQ_PER_CORE, 256, D],
    "ffn1_w_in": [DEPTH, D, 2 * FFN], "ffn1_w_out": [DEPTH, FFN, D], "ffn1_ln_g": [DEPTH, D], "ffn1_ln_b": [DEPTH, D],
    "mix_w_in": [DEPTH, D, MIX_IN], "swa_sinks": [DEPTH, 8], "swa_proj": [DEPTH, 512, D],
    "s5_a_re": [DEPTH, 32, 64], "s5_a_im": [DEPTH, 32, 64], "s5_log_step": [DEPTH, 32],
    "s5_b_re": [DEPTH, 32, 64, 16], "s5_b_im": [DEPTH, 32, 64, 16], "s5_c_re": [DEPTH, 32, 16, 64],
    "s5_c_im": [DEPTH, 32, 16, 64], "s5_d": [DEPTH, 512], "s5_glu_w": [DEPTH, 512, 512], "s5_glu_b": [DEPTH, 512],
    "s5_proj": [DEPTH, 512, D], "conv_w": [DEPTH, 31, 512], "conv_b": [DEPTH, 512], "conv_ln_g": [DEPTH, 512],
    "conv_ln_b": [DEPTH, 512], "conv_proj": [DEPTH, 512, D], "diff_lq1": [DEPTH, 64], "diff_lk1": [DEPTH, 64],
    "diff_lq2": [DEPTH, 64], "diff_lk2": [DEPTH, 64], "diff_norm_g": [DEPTH, 128], "diff_proj": [DEPTH, 512, D],
    "mix_w_out": [DEPTH, D, D], "mix_ln_g": [DEPTH, D], "mix_ln_b": [DEPTH, D], "mem_ln_g": [D], "mem_ln_b": [D],
    "cross_wq": [DEPTH, D, 512], "cross_wkv": [DEPTH, D, D], "cross_wo": [DEPTH, 512, D],
    "cross_ln_g": [DEPTH, D], "cross_ln_b": [DEPTH, D],
    "ffn2_w_in": [DEPTH, D, 2 * FFN], "ffn2_w_out": [DEPTH, FFN, D], "ffn2_ln_g": [DEPTH, D], "ffn2_ln_b": [DEPTH, D],
}
NEG = -240000.0
TWO_PI = 2.0 * math.pi
CW1 = 6.28125
CW2 = TWO_PI - CW1
MAGIC = 12582912.0
DIFF_SLOPES = [2.0 ** (-8.0 * (h + 1) / 4) for h in range(4)]
SWA_SLOPES = [2.0 ** (-8.0 * (h + 1) / 8) for h in range(8)]
CONSTS = {}
DEBUG = False
LAST = {}


def consts():
    if not CONSTS:
        import ml_dtypes
        bf = ml_dtypes.bfloat16
        CONSTS["c_ident"] = (np.eye(128, dtype=np.float32), F32)
        CONSTS["c_iota"] = (np.tile(np.arange(512, dtype=np.float32)[None, :], (128, 1)), F32)
        pos = np.arange(L)
        pa, pb = (pos // 128).astype(np.float32), (pos % 128).astype(np.float32)
        kb = np.stack([np.ones(L), np.ones(L), 128.0 * pa, pb]).astype(np.float32)
        CONSTS["c_kb"] = (kb.astype(bf), BF16)

        def qb(slopes):
            o = np.zeros((4, len(slopes), L), np.float32)
            for h, s in enumerate(slopes):
                o[0, h] = -8.0 * s * 128.0 * pa
                o[1, h] = -8.0 * s * pb
                o[2, h] = 8.0 * s
                o[3, h] = 8.0 * s
            return o.astype(bf)
        CONSTS["c_qb_diff"] = (qb(DIFF_SLOPES), BF16)
        CONSTS["c_qb_swa"] = (qb(SWA_SLOPES), BF16)
        ki = np.arange(128)[:, None]
        md = np.zeros((128, 4, 512), np.float32)
        qi = np.arange(512)[None, :]
        for rel in range(4):
            md[:, rel, :] = np.where(qi >= 128 * rel + ki, 0.0, NEG)
        CONSTS["c_mask_diff"] = (md.astype(bf), BF16)
        ms = np.zeros((128, 2, 128), np.float32)
        q1 = np.arange(128)[None, :]
        ms[:, 0, :] = np.where(ki > q1, 0.0, NEG)
        ms[:, 1, :] = np.where(ki <= q1, 0.0, NEG)
        CONSTS["c_mask_swa"] = (ms.astype(bf), BF16)
    return CONSTS


def build(layers, parts, branches="abcd", nseq=SEQ_PER_CORE):
    nc = bass.Bass("TRN2", target_bir_lowering=False)
    es = ExitStack()
    with es:
        g = K()
        g.nc = nc
        S = Sched(nc, es)
        g.S = S
        g.dr = {}

        def W(name):
            if name not in g.dr:
                if name.startswith("c_"):
                    arr, dt = consts()[name]
                    g.dr[name] = nc.dram_tensor(name, list(arr.shape), dt, kind="ExternalInput").ap()
                else:
                    g.dr[name] = nc.dram_tensor(name, list(SHAPES[name]), F32, kind="ExternalInput").ap()
            return g.dr[name]
        g.W = W
        g.dbg = {}

        def dbg_out(name, ap, bufs):
            if not DEBUG or name in g.dbg:
                return
            t = nc.dram_tensor("dbg_" + name, list(ap.shape), ap.dtype, kind="ExternalOutput").ap()
            g.dbg[name] = t
            S.op("sp", lambda h: h.dma_start(out=t, in_=ap), reads=bufs, dma=True)
        g.dbg_out = dbg_out
        out = nc.dram_tensor("out", [nseq, L, D], F32, kind="ExternalOutput").ap()

        arena_t = es.enter_context(nc.sbuf_tensor("arena", [128, 106400], BF16))
        A = Arena(arena_t)
        g.A = A
        g.resid = A.alloc([NT, D], F32)
        g.resid_b = [Buf(f"resid{t}") for t in range(NT)]
        g.ident = A.alloc([128], F32)
        g.identb = A.alloc([128], BF16)
        g.ones = A.alloc([128], BF16)
        g.memT = A.alloc([8, 256], BF16)
        g.memT_b = [Buf() for _ in range(8)]
        g.eps_ap = A.alloc([1], F32)
        g.eps4_ap = A.alloc([1], F32)
        g.cb = Buf("consts")
        S.op("sp", lambda h: h.dma_start(out=g.ident, in_=W("c_ident")), writes=[g.cb], dma=True)
        S.op("dve", lambda h: h.tensor_copy(out=g.identb, in_=g.ident), reads=[g.cb], writes=[g.cb])
        S.op("dve", lambda h: h.memset(g.ones, 1.0), writes=[g.cb])
        S.op("dve", lambda h: h.memset(g.eps_ap, EPS), writes=[g.cb])
        S.op("dve", lambda h: h.memset(g.eps4_ap, 4.0 * EPS), writes=[g.cb])
        g.eps_b = g.cb
        g.ident_b = g.cb
        g.banks = [(es.enter_context(nc.psum_tensor(f"ps{i}", [128, 512], F32))[:], Buf(f"ps{i}")) for i in range(8)]
        g.ps = RR(g.banks)
        pmark = A.mark()

        for s in range(nseq):
            for t in range(NT):
                S.op("sp", lambda h, t=t, s=s: h.dma_start(out=g.resid[:, t, :], in_=W("x")[s, t * 128:(t + 1) * 128, :]),
                     writes=[g.resid_b[t]], dma=True)
            if "cross" in parts:
                A.reset(pmark)
                S.barrier()
                prep_mem(g, s)
            for l in layers:
                for p in parts:
                    A.reset(pmark)
                    S.barrier()
                    if p in ("ffn1", "ffn2"):
                        ffn(g, l, p)
                    elif p == "mix":
                        mixer(g, l, branches)
                    elif p == "cross":
                        cross(g, l)
            for t in range(NT):
                S.op("sp", lambda h, t=t, s=s: h.dma_start(out=out[s, t * 128:(t + 1) * 128, :], in_=g.resid[:, t, :]),
                     reads=[g.resid_b[t]], dma=True)
        S.finish("sp")
        with nc.Block() as block:
            S.replay(block)
        g.stats = dict(cnt=dict(S.cnt), nwait=S.nwait)
    return nc, g


def load_w(g, dst, dst_b, src, eng="pool"):
    g.S.op(eng, lambda h: h.dma_start(out=dst, in_=src), writes=[dst_b], dma=True)


def mm(g, out, lhsT, rhs, start, stop, reads, writes):
    g.S.op("pe", lambda h: h.matmul(out, lhsT=lhsT, rhs=rhs, start=start, stop=stop), reads=reads, writes=writes)


def evac(g, i, out, in_, reads, writes, scale=None):
    if i % 2 == 0:
        if scale is None:
            g.S.op("act", lambda h: h.activation(out=out, in_=in_, func=AF.Copy), reads=reads, writes=writes)
        else:
            g.S.op("act", lambda h: h.activation(out=out, in_=in_, func=AF.Copy, scale=scale), reads=reads, writes=writes)
    else:
        if scale is None:
            g.S.op("dve", lambda h: h.tensor_copy(out=out, in_=in_), reads=reads, writes=writes)
        else:
            g.S.op("dve", lambda h: h.tensor_scalar(out=out, in0=in_, scalar1=scale, scalar2=None, op0=ALU.mult),
                   reads=reads, writes=writes)


def gen_xT(g, xT, xT_b, srcs, act_only=False):
    S = g.S
    n = len(srcs)
    for k in range(8):
        ps, psb = g.ps.get()
        for t, (src, sb) in enumerate(srcs):
            S.op("pe", lambda h, ps=ps, t=t, k=k, src=src: h.transpose(
                out=ps[:, t * 128:(t + 1) * 128], in_=src[:, k * 128:(k + 1) * 128], identity=g.ident),
                reads=[sb, g.ident_b], writes=[psb])
        evac(g, 0 if act_only else k, xT[:, k, 0:128 * n], ps[:, 0:128 * n], [psb], [xT_b[k]])


def resid_tiles(g, t0, n=4):
    return [(g.resid[:, t0 + i, :], g.resid_b[t0 + i]) for i in range(n)]


def ln_alloc(g, n=3):
    A = g.A
    sets = RR([((A.alloc([12], F32), A.alloc([2], F32), A.alloc([1], F32), A.alloc([1], F32)), (Buf("ln1"), Buf("ln2")))
               for _ in range(n)])
    return sets, None


def load_ln(g, gname, bname, l):
    A, S, W = g.A, g.S, g.W
    lng = A.alloc([D], F32)
    lnb = A.alloc([D], F32)
    b = Buf("lnp")
    gs = W(gname)[l:l + 1, :] if l is not None else W(gname).rearrange("(o d) -> o d", o=1)
    bs = W(bname)[l:l + 1, :] if l is not None else W(bname).rearrange("(o d) -> o d", o=1)
    S.op("sp", lambda h: h.dma_start(out=lng, in_=gs.to_broadcast([128, D])), writes=[b], dma=True)
    S.op("sp", lambda h: h.dma_start(out=lnb, in_=bs.to_broadcast([128, D])), writes=[b], dma=True)
    return lng, lnb, b


def ln_tile(g, x, xb, lnp, tmp, tmp_b, eps=None):
    S = g.S
    lng, lnb, lnp_b = lnp
    (st, mv, sd, rs), (b1, b2) = tmp.get()
    for hh in range(2):
        S.op("dve", lambda h, hh=hh: h.bn_stats(out=st[:, hh * 6:(hh + 1) * 6], in_=x[:, hh * 512:(hh + 1) * 512]),
             reads=[xb], writes=[b1])
    S.op("dve", lambda h: h.bn_aggr(out=mv, in_=st), reads=[b1], writes=[b1])
    eps_ap = g.eps_ap if eps is None else eps
    S.op("act", lambda h: h.activation(out=sd, in_=mv[:, 1:2], func=AF.Sqrt, bias=eps_ap, scale=1.0),
         reads=[b1, g.eps_b], writes=[b2])
    S.op("dve", lambda h: h.scalar_tensor_tensor(out=x, in0=x, scalar=mv[:, 0:1], in1=lng, op0=ALU.subtract, op1=ALU.mult),
         reads=[xb, b1, lnp_b], writes=[xb])
    S.op("dve", lambda h: h.reciprocal(out=rs, in_=sd), reads=[b2], writes=[b2])
    S.op("dve", lambda h: h.scalar_tensor_tensor(out=x, in0=x, scalar=rs, in1=lnb, op0=ALU.mult, op1=ALU.add),
         reads=[xb, b2, lnp_b], writes=[xb])


def prep_mem(g, s):
    S, A, W = g.S, g.A, g.W
    mt = A.alloc([2, D], F32)
    mb = [Buf(), Buf()]
    lnp = load_ln(g, "mem_ln_g", "mem_ln_b", None)
    tmp, tmp_b = ln_alloc(g)
    for i in range(2):
        S.op("sp", lambda h, i=i: h.dma_start(out=mt[:, i, :], in_=W("mem")[s, i * 128:(i + 1) * 128, :]),
             writes=[mb[i]], dma=True)
        ln_tile(g, mt[:, i, :], mb[i], lnp, tmp, tmp_b)
    gen_xT(g, g.memT, g.memT_b, [(mt[:, i, :], mb[i]) for i in range(2)])


def ffn(g, l, which):
    S, A, W = g.S, g.A, g.W
    w_in = W(which + "_w_in")
    w_out = W(which + "_w_out")
    xT = A.alloc([8, 1024], BF16)
    xT_b = [[Buf() for _ in range(8)] for _ in range(2)]
    aT = A.alloc([NJ, 1024], BF16)
    aT_b = [[Buf() for _ in range(2)] for _ in range(NJ)]
    wo = RR([(A.alloc([NJ, 512], BF16), Buf()) for _ in range(2)])
    wi = RR([(A.alloc([8, 2, 256], BF16), Buf()) for _ in range(2)])
    lnp = load_ln(g, which + "_ln_g", which + "_ln_b", l)
    sg = RR([(A.alloc([512], F32), Buf()) for _ in range(2)])
    tmp, tmp_b = ln_alloc(g)
    pre = []

    def load_jp(jp):
        w, wb = wi.get()
        for gu in range(2):
            c0 = gu * FFN + jp * 256
            load_w(g, w[:, :, gu, :], wb, w_in[l, :, c0:c0 + 256].rearrange("(k p) n -> p k n", p=128))
        return w, wb
    for c in range(2):
        for sub in range(2):
            gen_xT(g, xT[:, :, sub * 512:(sub + 1) * 512], xT_b[sub], resid_tiles(g, c * 8 + sub * 4), act_only=True)
        for jp in range(NJ // 2):
            w, wb = pre.pop(0) if pre else load_jp(jp)
            for jj in range(2):
                j = jp * 2 + jj
                for sub in range(2):
                    pg, pgb = g.ps.get()
                    pu, pub = g.ps.get()
                    for gu, (pp, ppb) in enumerate(((pg, pgb), (pu, pub))):
                        for k in range(8):
                            mm(g, pp, w[:, k, gu, jj * 128:(jj + 1) * 128], xT[:, k, sub * 512:(sub + 1) * 512],
                               k == 0, k == 7, [wb, xT_b[sub][k]], [ppb])
                    sgt, sgb = sg.get()
                    S.op("act", lambda h, sgt=sgt, pg=pg: h.activation(out=sgt, in_=pg, func=AF.Silu),
                         reads=[pgb], writes=[sgb])
                    S.op("dve", lambda h, sgt=sgt, pu=pu, j=j, sub=sub: h.tensor_tensor(
                        out=aT[:, j, sub * 512:(sub + 1) * 512], in0=pu, in1=sgt, op=ALU.mult),
                        reads=[pub, sgb], writes=[aT_b[j][sub]])
        wos = []
        for q in range(2):
            w, wb = wo.get()
            load_w(g, w, wb, w_out[l, :, q * 512:(q + 1) * 512].rearrange("(j p) n -> p j n", p=128))
            wos.append((w, wb))
        if c == 0:
            pre = [load_jp(0), load_jp(1)]
        for q in range(2):
            w, wb = wos[q]
            for tl in range(8):
                t = c * 8 + tl
                pf, pfb = g.ps.get()
                for j in range(NJ):
                    mm(g, pf, aT[:, j, tl * 128:(tl + 1) * 128], w[:, j, :], j == 0, j == NJ - 1,
                       [wb, aT_b[j][tl // 4]], [pfb])
                xs = g.resid[:, t, q * 512:(q + 1) * 512]
                S.op("dve", lambda h: h.scalar_tensor_tensor(
                    out=xs, in0=xs, scalar=2.0 * ALPHA, in1=pf, op0=ALU.mult, op1=ALU.add),
                    reads=[g.resid_b[t], pfb], writes=[g.resid_b[t]])
        for tl in range(8):
            ln_tile(g, g.resid[:, c * 8 + tl, :], g.resid_b[c * 8 + tl], lnp, tmp, tmp_b, eps=g.eps4_ap)


def cross(g, l):
    S, A, W = g.S, g.A, g.W
    wq = A.alloc([8, 512], BF16)
    wkv = A.alloc([8, 1024], BF16)
    wo = A.alloc([4, 1024], BF16)
    wq_b, wkv_b, wo_b = Buf(), Buf(), Buf()
    load_w(g, wkv, wkv_b, W("cross_wkv")[l].rearrange("(k p) n -> p k n", p=128))
    load_w(g, wq, wq_b, W("cross_wq")[l].rearrange("(k p) n -> p k n", p=128))
    load_w(g, wo, wo_b, W("cross_wo")[l].rearrange("(k p) n -> p k n", p=128))
    KT = A.alloc([4, 256], BF16)
    KT_b = [Buf() for _ in range(4)]
    V = A.alloc([2, 512], BF16)
    V_b = [Buf() for _ in range(2)]
    for hd in range(4):
        ps, psb = g.ps.get()
        for k in range(8):
            mm(g, ps[:, 0:256], wkv[:, k, hd * 128:(hd + 1) * 128], g.memT[:, k, :], k == 0, k == 7,
               [wkv_b, g.memT_b[k]], [psb])
        evac(g, hd, KT[:, hd, :], ps[:, 0:256], [psb], [KT_b[hd]])
    for mt in range(2):
        ps, psb = g.ps.get()
        for k in range(8):
            mm(g, ps, g.memT[:, k, mt * 128:(mt + 1) * 128], wkv[:, k, 512:1024], k == 0, k == 7,
               [wkv_b, g.memT_b[k]], [psb])
        evac(g, mt, V[:, mt, :], ps, [psb], [V_b[mt]])
    xT = A.alloc([8, 512], BF16)
    xT_b = [Buf() for _ in range(8)]
    QT = RR([(A.alloc([512], BF16), Buf()) for _ in range(2)])
    PT = RR([(A.alloc([512], BF16), Buf()) for _ in range(4)])
    RI = RR([(A.alloc([512], F32), Buf()) for _ in range(2)])
    OT = A.alloc([4, 512], BF16)
    OT_b = [Buf() for _ in range(4)]
    lnp = load_ln(g, "cross_ln_g", "cross_ln_b", l)
    tmp, tmp_b = ln_alloc(g)
    sc = 1.0 / math.sqrt(128.0)
    for c in range(4):
        gen_xT(g, xT, xT_b, resid_tiles(g, c * 4))
        for hp in range(2):
            qts, ptss = {}, {}
            for hd in (2 * hp, 2 * hp + 1):
                ps, psb = g.ps.get()
                for k in range(8):
                    mm(g, ps, wq[:, k, hd * 128:(hd + 1) * 128], xT[:, k, :], k == 0, k == 7, [wq_b, xT_b[k]], [psb])
                qt, qtb = QT.get()
                evac(g, hd, qt, ps, [psb], [qtb])
                qts[hd] = (qt, qtb)
            for hd in (2 * hp, 2 * hp + 1):
                qt, qtb = qts[hd]
                pts = []
                for mt in range(2):
                    ps, psb = g.ps.get()
                    mm(g, ps, KT[:, hd, mt * 128:(mt + 1) * 128], qt, True, True, [KT_b[hd], qtb], [psb])
                    pt, ptb = PT.get()
                    S.op("act", lambda h: h.activation(out=pt, in_=ps, func=AF.Exp, scale=sc), reads=[psb], writes=[ptb])
                    pts.append((pt, ptb))
                ptss[hd] = pts
            for hd in (2 * hp, 2 * hp + 1):
                pts = ptss[hd]
                po, pob = g.ps.get()
                pr, prb = g.ps.get()
                for mt in range(2):
                    mm(g, po, V[:, mt, hd * 128:(hd + 1) * 128], pts[mt][0], mt == 0, mt == 1, [V_b[mt], pts[mt][1]], [pob])
                for mt in range(2):
                    mm(g, pr, g.ones, pts[mt][0], mt == 0, mt == 1, [g.cb, pts[mt][1]], [prb])
                ri, rib = RI.get()
                S.op("dve", lambda h: h.reciprocal(out=ri, in_=pr), reads=[prb], writes=[rib])
                S.op("dve", lambda h: h.tensor_tensor(out=OT[:, hd, :], in0=po, in1=ri, op=ALU.mult),
                     reads=[pob, rib], writes=[OT_b[hd]])
        for tl in range(4):
            t = c * 4 + tl
            for half in range(2):
                ps, psb = g.ps.get()
                for hd in range(4):
                    mm(g, ps, OT[:, hd, tl * 128:(tl + 1) * 128], wo[:, hd, half * 512:(half + 1) * 512],
                       hd == 0, hd == 3, [OT_b[hd], wo_b], [psb])
                xs = g.resid[:, t, half * 512:(half + 1) * 512]
                S.op("dve", lambda h, xs=xs, ps=ps: h.scalar_tensor_tensor(
                    out=xs, in0=xs, scalar=ALPHA, in1=ps, op0=ALU.mult, op1=ALU.add),
                    reads=[g.resid_b[t], psb], writes=[g.resid_b[t]])
            ln_tile(g, g.resid[:, t, :], g.resid_b[t], lnp, tmp, tmp_b)


def view3(ap, a):
    return ap.rearrange("p (a b) -> p a b", a=a)


def bc_mid(ap2, n):
    return ap2.unsqueeze(1).to_broadcast([ap2.shape[0], n, ap2.shape[1]])


def bc_last(ap2, n):
    return ap2.unsqueeze(2).to_broadcast([ap2.shape[0], ap2.shape[1], n])


def mixer(g, l, branches):
    S, A, W = g.S, g.A, g.W
    OT = {br: A.alloc([4, L], BF16) for br in "abcd"}
    OT_b = {br: [[Buf() for _ in range(4)] for _ in range(4)] for br in "abcd"}
    m0 = A.mark()
    fns = dict(a=swa_branch, b=s5_branch, c=conv_branch, d=diff_branch)
    for br in "dacb":
        A.reset(m0)
        S.barrier()
        if br in branches:
            fns[br](g, l, OT[br], OT_b[br])
        else:
            allb = [b for row in OT_b[br] for b in row]
            S.op("pool", lambda h, br=br: h.memset(OT[br], 0.0), writes=allb)
    A.reset(m0)
    S.barrier()
    lam_init = None
    mT = A.alloc([8, L], BF16)
    mT_b = [[Buf() for _ in range(4)] for _ in range(8)]
    m2 = A.mark()
    xT = A.alloc([8, 512], BF16)
    xT_b = [Buf() for _ in range(8)]
    wg = RR([(A.alloc([8, 4, 128], BF16), Buf()) for _ in range(2)])
    wp = RR([(A.alloc([4, 4, 128], BF16), Buf()) for _ in range(2)])
    SG = RR([(A.alloc([512], F32), Buf()) for _ in range(2)])
    PRT = [(A.alloc([512], F32), Buf()) for _ in range(3)]
    projs = dict(a="swa_proj", b="s5_proj", c="conv_proj", d="diff_proj")
    for jd in range(8):
        w, wb = wg.get()
        p, pb = wp.get()
        for i, br in enumerate("abcd"):
            c0 = 3840 + i * 1024 + jd * 128
            load_w(g, w[:, :, i, :], wb, W("mix_w_in")[l, :, c0:c0 + 128].rearrange("(k p) n -> p k n", p=128))
            load_w(g, p[:, i, :, :], pb, W(projs[br])[l, :, jd * 128:(jd + 1) * 128].rearrange("(k p) n -> p k n", p=128))
        for c in range(4):
            gen_xT(g, xT, xT_b, resid_tiles(g, c * 4), act_only=True)
            slot = [PRT[0], PRT[1], PRT[2], PRT[1]]
            for i, br in enumerate("abcd"):
                pgt, pgb = g.ps.get()
                for k in range(8):
                    mm(g, pgt, w[:, k, i, :], xT[:, k, :], k == 0, k == 7, [wb, xT_b[k]], [pgb])
                py, pyb = g.ps.get()
                for kc in range(4):
                    mm(g, py, p[:, i, kc, :], OT[br][:, kc, c * 512:(c + 1) * 512], kc == 0, kc == 3,
                       [pb, OT_b[br][kc][c]], [pyb])
                sg, sgb = SG.get()
                S.op("act", lambda h: h.activation(out=sg, in_=pgt, func=AF.Sigmoid), reads=[pgb], writes=[sgb])
                pr, prb = slot[i]
                S.op("dve", lambda h: h.tensor_tensor(out=pr, in0=py, in1=sg, op=ALU.mult), reads=[pyb, sgb], writes=[prb])
                if i == 1:
                    S.op("pool", lambda h: h.tensor_tensor(out=PRT[0][0], in0=PRT[0][0], in1=PRT[1][0], op=ALU.add),
                         reads=[PRT[0][1], PRT[1][1]], writes=[PRT[0][1]])
                if i == 3:
                    S.op("pool", lambda h: h.tensor_tensor(out=PRT[2][0], in0=PRT[2][0], in1=PRT[1][0], op=ALU.add),
                         reads=[PRT[2][1], PRT[1][1]], writes=[PRT[2][1]])
            S.op("dve", lambda h: h.tensor_tensor(out=mT[:, jd, c * 512:(c + 1) * 512], in0=PRT[0][0], in1=PRT[2][0], op=ALU.add),
                 reads=[PRT[0][1], PRT[2][1]], writes=[mT_b[jd][c]])
    A.reset(m2)
    S.barrier()
    wo = A.alloc([8, 1024], BF16)
    wo_b = Buf()
    load_w(g, wo, wo_b, W("mix_w_out")[l].rearrange("(k p) n -> p k n", p=128))
    lnp = load_ln(g, "mix_ln_g", "mix_ln_b", l)
    tmp, tmp_b = ln_alloc(g)
    for t in range(NT):
        for half in range(2):
            ps, psb = g.ps.get()
            for k in range(8):
                mm(g, ps, mT[:, k, t * 128:(t + 1) * 128], wo[:, k, half * 512:(half + 1) * 512], k == 0, k == 7,
                   [mT_b[k][t // 4], wo_b], [psb])
            xs = g.resid[:, t, half * 512:(half + 1) * 512]
            S.op("dve", lambda h, xs=xs, ps=ps: h.scalar_tensor_tensor(
                out=xs, in0=xs, scalar=ALPHA, in1=ps, op0=ALU.mult, op1=ALU.add),
                reads=[g.resid_b[t], psb], writes=[g.resid_b[t]])
        ln_tile(g, g.resid[:, t, :], g.resid_b[t], lnp, tmp, tmp_b)


def diff_branch(g, l, OT, OT_b):
    S, A, W = g.S, g.A, g.W
    lam_init = 0.8 - 0.6 * math.exp(-0.3 * l)
    xT = A.alloc([8, 512], BF16)
    xT_b = [Buf() for _ in range(8)]
    wD = RR([(A.alloc([8, 384], BF16), Buf()) for _ in range(2)])
    QT = A.alloc([2, L], BF16)
    KT = A.alloc([2, L], BF16)
    V = A.alloc([NT, 128], BF16)
    QT_b = [[Buf() for _ in range(4)] for _ in range(2)]
    KT_b = [[Buf() for _ in range(4)] for _ in range(2)]
    V_b = [Buf() for _ in range(NT)]
    qbias_b, kbias_b = Buf(), Buf()
    PT = RR([(A.alloc([512], BF16), Buf()) for _ in range(4)])
    SM = RR([(A.alloc([512], F32), Buf()) for _ in range(2)])
    F = RR([(A.alloc([512], F32), Buf()) for _ in range(6)])
    mask_diff = A.alloc([4, 512], BF16)
    mk_b = Buf()
    S.op("sp", lambda h: h.dma_start(out=mask_diff, in_=W("c_mask_diff")), writes=[mk_b], dma=True)
    lq = A.alloc([4, 64], F32)
    sc = A.alloc([8], F32)
    ng = A.alloc([1], F32)
    sb = Buf()
    for i, nm in enumerate(("diff_lq1", "diff_lk1", "diff_lq2", "diff_lk2")):
        S.op("sp", lambda h, i=i, nm=nm: h.dma_start(out=lq[:, i, :], in_=W(nm)[l:l + 1, :].to_broadcast([128, 64])),
             writes=[sb], dma=True)
    S.op("sp", lambda h: h.dma_start(out=ng, in_=W("diff_norm_g")[l].rearrange("(p o) -> p o", o=1)), writes=[sb], dma=True)
    for i in range(2):
        S.op("dve", lambda h, i=i: h.tensor_tensor(out=lq[:, 2 * i, :], in0=lq[:, 2 * i, :], in1=lq[:, 2 * i + 1, :], op=ALU.mult),
             reads=[sb], writes=[sb])
        S.op("dve", lambda h, i=i: h.tensor_reduce(out=sc[:, i:i + 1], in_=lq[:, 2 * i, :], axis=mybir.AxisListType.X, op=ALU.add),
             reads=[sb], writes=[sb])
        S.op("act", lambda h, i=i: h.activation(out=sc[:, 2 + i:3 + i], in_=sc[:, i:i + 1], func=AF.Exp), reads=[sb], writes=[sb])
    S.op("dve", lambda h: h.tensor_tensor(out=sc[:, 4:5], in0=sc[:, 3:4], in1=sc[:, 2:3], op=ALU.subtract), reads=[sb], writes=[sb])
    S.op("dve", lambda h: h.tensor_scalar(out=sc[:, 5:6], in0=sc[:, 4:5], scalar1=-lam_init, scalar2=None, op0=ALU.add),
         reads=[sb], writes=[sb])
    nlam = sc[:, 5:6]
    for c in range(2):
        S.op("sp", lambda h, c=c: h.dma_start(out=KT[64:68, c, :], in_=W("c_kb")), writes=[kbias_b], dma=True)
    acc = g.banks[0:4]
    rot = RR(g.banks[4:8])
    for hd in range(4):
        w, wb = wD.get()
        for i, c0 in enumerate((768, 1280, 1792)):
            load_w(g, w[:, :, i * 128:(i + 1) * 128],  wb,
                   W("mix_w_in")[l, :, c0 + hd * 128:c0 + (hd + 1) * 128].rearrange("(k p) n -> p k n", p=128))
        for c in range(2):
            S.op("sp", lambda h, c=c, hd=hd: h.dma_start(out=QT[64:68, c, :], in_=W("c_qb_diff")[:, hd, :]),
                 writes=[qbias_b], dma=True)
        for c in range(4):
            gen_xT(g, xT, xT_b, resid_tiles(g, c * 4))
            for qk, (T_, T_b) in enumerate(((QT, QT_b), (KT, KT_b))):
                for comp in range(2):
                    ps, psb = g.ps.get()
                    col = qk * 128 + comp * 64
                    for k in range(8):
                        mm(g, ps[0:64, :], w[:, k, col:col + 64], xT[:, k, :], k == 0, k == 7, [wb, xT_b[k]], [psb])
                    evac(g, comp + qk, T_[0:64, comp, c * 512:(c + 1) * 512], ps[0:64, :], [psb], [T_b[comp][c]])
            for tl in range(4):
                ps, psb = g.ps.get()
                for k in range(8):
                    mm(g, ps[:, 0:128], xT[:, k, tl * 128:(tl + 1) * 128], w[:, k, 256:384], k == 0, k == 7,
                       [wb, xT_b[k]], [psb])
                evac(g, tl, V[:, c * 4 + tl, :], ps[:, 0:128], [psb], [V_b[c * 4 + tl]])
        for qc in range(4):
            nkb = 4 * qc + 4
            for comp in range(2):
                po, pob = acc[2 * comp]
                pr, prb = acc[2 * comp + 1]
                def s_mm(kb):
                    ps, psb = rot.get()
                    mm(g, ps, KT[0:68, comp, kb * 128:(kb + 1) * 128], QT[0:68, comp, qc * 512:(qc + 1) * 512], True, True,
                       [KT_b[comp][kb // 4], QT_b[comp][qc], qbias_b, kbias_b], [psb])
                    return ps, psb
                nxt = [s_mm(0)] + ([s_mm(1)] if nkb > 1 else [])
                for kb in range(nkb):
                    ps, psb = nxt.pop(0)
                    if kb + 2 < nkb:
                        nxt.append(s_mm(kb + 2))
                    pt, ptb = PT.get()
                    if kb >= 4 * qc:
                        sm, smb = SM.get()
                        S.op("dve", lambda h: h.tensor_tensor(
                            out=sm, in0=ps, in1=mask_diff[:, kb - 4 * qc, :], op=ALU.add), reads=[psb, mk_b], writes=[smb])
                        S.op("act", lambda h: h.activation(out=pt, in_=sm, func=AF.Exp, scale=0.125),
                             reads=[smb], writes=[ptb])
                    else:
                        S.op("act", lambda h: h.activation(out=pt, in_=ps, func=AF.Exp, scale=0.125),
                             reads=[psb], writes=[ptb])
                    mm(g, po, V[:, kb, :], pt, kb == 0, kb == nkb - 1, [V_b[kb], ptb], [pob])
                    mm(g, pr, g.ones, pt, kb == 0, kb == nkb - 1, [g.cb, ptb], [prb])
            ts = []
            for comp in range(2):
                po, pob = acc[2 * comp]
                pr, prb = acc[2 * comp + 1]
                ri, rib = F.get()
                S.op("dve", lambda h, ri=ri, pr=pr: h.reciprocal(out=ri, in_=pr), reads=[prb], writes=[rib])
                t_, tb_ = F.get()
                S.op("dve", lambda h, t_=t_, po=po, ri=ri: h.tensor_tensor(out=t_, in0=po, in1=ri, op=ALU.mult),
                     reads=[pob, rib], writes=[tb_])
                ts.append((t_, tb_))
            o, ob = F.get()
            S.op("dve", lambda h, o=o, t0=ts[0][0], t1=ts[1][0]: h.scalar_tensor_tensor(
                out=o, in0=t1, scalar=nlam, in1=t0, op0=ALU.mult, op1=ALU.add),
                reads=[ts[0][1], ts[1][1], sb], writes=[ob])
            sq, sqb = PT.get()
            S.op("act", lambda h, sq=sq, o=o: h.activation(out=sq, in_=o, func=AF.Square), reads=[ob], writes=[sqb])
            pm, pmb = rot.get()
            mm(g, pm, g.ones, sq, True, True, [g.cb, sqb], [pmb])
            sd, sdb = F.get()
            S.op("act", lambda h, sd=sd, pm=pm: h.activation(out=sd, in_=pm, func=AF.Sqrt, bias=g.eps_ap, scale=1.0 / 128.0),
                 reads=[pmb, g.cb], writes=[sdb])
            S.op("dve", lambda h, sd=sd: h.reciprocal(out=sd, in_=sd), reads=[sdb], writes=[sdb])
            S.op("dve", lambda h, o=o, sd=sd: h.tensor_tensor(out=o, in0=o, in1=sd, op=ALU.mult), reads=[ob, sdb], writes=[ob])
            S.op("dve", lambda h, o=o, hd=hd, qc=qc: h.tensor_scalar(
                out=OT[:, hd, qc * 512:(qc + 1) * 512], in0=o, scalar1=ng, scalar2=1.0 - lam_init, op0=ALU.mult, op1=ALU.mult),
                reads=[ob, sb], writes=[OT_b[hd][qc]])


def swa_branch(g, l, OT, OT_b):
    S, A, W = g.S, g.A, g.W
    xT = A.alloc([8, 512], BF16)
    xT_b = [Buf() for _ in range(8)]
    wA = RR([(A.alloc([8, 384], BF16), Buf()) for _ in range(2)])
    QT = A.alloc([4, L], BF16)
    KT = A.alloc([L], BF16)
    V = A.alloc([NT, 64], BF16)
    QT_b = [[Buf() for _ in range(4)] for _ in range(4)]
    KT_b = [Buf() for _ in range(4)]
    V_b = [Buf() for _ in range(NT)]
    qbias_b, kbias_b = Buf(), Buf()
    PT = RR([(A.alloc([4, 128], BF16), Buf()) for _ in range(4)])
    SM = RR([(A.alloc([4, 128], F32), Buf()) for _ in range(2)])
    DN = RR([(A.alloc([2, 128], F32), Buf()) for _ in range(2)])
    esink = A.alloc([2], F32)
    es_b = Buf()
    mask_swa = A.alloc([2, 128], BF16)
    mk_b = Buf()
    S.op("sp", lambda h: h.dma_start(out=mask_swa, in_=W("c_mask_swa")), writes=[mk_b], dma=True)
    S.op("sp", lambda h: h.dma_start(out=KT[64:68, :], in_=W("c_kb")), writes=[kbias_b], dma=True)
    acc = g.banks[0:2]
    rot = RR(g.banks[2:8])
    for grp in range(2):
        w, wb = wA.get()
        for (d0, c0, n) in ((0, grp * 256, 256), (256, 512 + grp * 64, 64), (320, 640 + grp * 64, 64)):
            load_w(g, w[:, :, d0:d0 + n], wb, W("mix_w_in")[l, :, c0:c0 + n].rearrange("(k p) n -> p k n", p=128))
        for r in range(4):
            S.op("sp", lambda h, r=r, grp=grp: h.dma_start(out=QT[64:68, r, :], in_=W("c_qb_swa")[:, 4 * grp + r, :]),
                 writes=[qbias_b], dma=True)
        for par in range(2):
            for j in range(2):
                hidx = 4 * grp + 2 * j + par
                S.op("sp", lambda h, par=par, j=j, hidx=hidx: h.dma_start(
                    out=esink[64 * par:64 * par + 64, j:j + 1],
                    in_=W("swa_sinks")[l:l + 1, hidx:hidx + 1].to_broadcast([64, 1])), writes=[es_b], dma=True)
        S.op("act", lambda h: h.activation(out=esink, in_=esink, func=AF.Exp), reads=[es_b], writes=[es_b])
        for c in range(4):
            gen_xT(g, xT, xT_b, resid_tiles(g, c * 4))
            for r in range(4):
                ps, psb = g.ps.get()
                for k in range(8):
                    mm(g, ps[0:64, :], w[:, k, r * 64:(r + 1) * 64], xT[:, k, :], k == 0, k == 7, [wb, xT_b[k]], [psb])
                evac(g, r, QT[0:64, r, c * 512:(c + 1) * 512], ps[0:64, :], [psb], [QT_b[r][c]])
            ps, psb = g.ps.get()
            for k in range(8):
                mm(g, ps[0:64, :], w[:, k, 256:320], xT[:, k, :], k == 0, k == 7, [wb, xT_b[k]], [psb])
            evac(g, 1, KT[0:64, c * 512:(c + 1) * 512], ps[0:64, :], [psb], [KT_b[c]])
            for tl in range(4):
                ps, psb = g.ps.get()
                for k in range(8):
                    mm(g, ps[:, 0:64], xT[:, k, tl * 128:(tl + 1) * 128], w[:, k, 320:384], k == 0, k == 7,
                       [wb, xT_b[k]], [psb])
                evac(g, tl, V[:, c * 4 + tl, :], ps[:, 0:64], [psb], [V_b[c * 4 + tl]])
        def s_blk(n):
            kbs = [n - 1, n] if n > 0 else [n]
            pts = []
            for kb in kbs:
                mi = 0 if kb == n - 1 else 1
                ps, psb = rot.get()
                mm(g, ps, KT[0:68, kb * 128:(kb + 1) * 128], QT[0:68, :, n * 128:(n + 1) * 128], True, True,
                   [KT_b[kb // 4], kbias_b, qbias_b] + [QT_b[r][n // 4] for r in range(4)], [psb])
                sm, smb = SM.get()
                S.op("dve", lambda h: h.tensor_tensor(
                    out=sm, in0=view3(ps, 4), in1=bc_mid(mask_swa[:, mi, :], 4), op=ALU.add),
                    reads=[psb, mk_b], writes=[smb])
                pt, ptb = PT.get()
                S.op("act", lambda h: h.activation(out=pt, in_=sm, func=AF.Exp, scale=0.125),
                     reads=[smb], writes=[ptb])
                pts.append((pt, ptb, kb))
            return pts
        nxt = s_blk(0)
        for n in range(NT):
            pts = nxt
            if n + 1 < NT:
                nxt = s_blk(n + 1)
            po, pob = acc[0]
            pr, prb = acc[1]
            for par in range(2):
                for i, (pt, ptb, kb) in enumerate(pts):
                    mm(g, po[64 * par:64 * par + 64, 0:256], V[:, kb, :], pt[:, par::2, :], i == 0, i == len(pts) - 1,
                       [V_b[kb], ptb], [pob])
                for i, (pt, ptb, kb) in enumerate(pts):
                    mm(g, pr[64 * par:64 * par + 64, 0:256], g.ones[:, 0:64], pt[:, par::2, :], i == 0, i == len(pts) - 1,
                       [g.cb, ptb], [prb])
            dn, dnb = DN.get()
            S.op("dve", lambda h: h.tensor_tensor(
                out=dn, in0=view3(pr[:, 0:256], 2), in1=bc_last(esink, 128), op=ALU.add), reads=[prb, es_b], writes=[dnb])
            S.op("dve", lambda h: h.reciprocal(out=dn, in_=dn), reads=[dnb], writes=[dnb])
            S.op("dve", lambda h: h.tensor_tensor(
                out=OT[:, 2 * grp:2 * grp + 2, n * 128:(n + 1) * 128], in0=view3(po[:, 0:256], 2), in1=dn, op=ALU.mult),
                reads=[pob, dnb], writes=[OT_b[2 * grp][n // 4], OT_b[2 * grp + 1][n // 4]])


def conv_branch(g, l, OT, OT_b):
    S, A, W = g.S, g.A, g.W
    gT = A.alloc([4, 30 + L], BF16)
    gT_b = [[Buf() for _ in range(5)] for _ in range(4)]
    cwT = A.alloc([4, 32], F32)
    cbv = A.alloc([3, 4], F32)
    pb_ = Buf()
    m1 = A.mark()
    xT = A.alloc([8, 512], BF16)
    xT_b = [Buf() for _ in range(8)]
    wC = A.alloc([8, 1024], BF16)
    wC_b = Buf()
    cwn = A.alloc([512], F32)
    SG = RR([(A.alloc([512], F32), Buf()) for _ in range(2)])
    load_w(g, wC, wC_b, W("mix_w_in")[l, :, 2816:3840].rearrange("(k p) n -> p k n", p=128))
    S.op("pool", lambda h: h.memset(cwn[0:32, :], 0.0), writes=[pb_])
    S.op("sp", lambda h: h.dma_start(out=cwn[0:31, :], in_=W("conv_w")[l]), writes=[pb_], dma=True)
    for i, nm in enumerate(("conv_b", "conv_ln_g", "conv_ln_b")):
        S.op("sp", lambda h, i=i, nm=nm: h.dma_start(out=cbv[:, i, :], in_=W(nm)[l].rearrange("(c p) -> p c", p=128),
                                                     allow_slow_non_contiguous=True), writes=[pb_], dma=True)
    for cc in range(4):
        ps, psb = g.ps.get()
        S.op("pe", lambda h, ps=ps, cc=cc: h.transpose(out=ps[:, 0:32], in_=cwn[0:32, cc * 128:(cc + 1) * 128],
                                                       identity=g.ident[0:32, 0:32]), reads=[pb_, g.cb], writes=[psb])
        S.op("dve", lambda h, ps=ps, cc=cc: h.tensor_copy(out=cwT[:, cc, 0:31], in_=ps[:, 0:31]), reads=[psb], writes=[pb_])
        S.op("pool", lambda h, cc=cc: h.memset(gT[:, cc, 0:30], 0.0), writes=[gT_b[cc][4]])
    for c in range(4):
        gen_xT(g, xT, xT_b, resid_tiles(g, c * 4), act_only=True)
        for cc in range(4):
            pv, pvb = g.ps.get()
            pg, pgb = g.ps.get()
            for k in range(8):
                mm(g, pv, wC[:, k, cc * 128:(cc + 1) * 128], xT[:, k, :], k == 0, k == 7, [wC_b, xT_b[k]], [pvb])
            for k in range(8):
                mm(g, pg, wC[:, k, 512 + cc * 128:512 + (cc + 1) * 128], xT[:, k, :], k == 0, k == 7, [wC_b, xT_b[k]], [pgb])
            sg, sgb = SG.get()
            S.op("act", lambda h, sg=sg, pg=pg: h.activation(out=sg, in_=pg, func=AF.Sigmoid), reads=[pgb], writes=[sgb])
            S.op("dve", lambda h, sg=sg, pv=pv, cc=cc, c=c: h.tensor_tensor(
                out=gT[:, cc, 30 + c * 512:30 + (c + 1) * 512], in0=pv, in1=sg, op=ALU.mult),
                reads=[pvb, sgb], writes=[gT_b[cc][c]])
    g.dbg_out("gT", gT, [b for r in gT_b for b in r])
    g.dbg_out("cwT", cwT, [pb_])
    g.dbg_out("cbv", cbv, [pb_])
    A.reset(m1)
    S.barrier()
    Dg = A.alloc([4, 31, 128], BF16)
    Dg_b = [Buf() for _ in range(4)]
    yb = A.alloc([4, 512], F32)
    sq = A.alloc([4, 512], BF16)
    ybh = A.alloc([4, 512], BF16)
    ybh_b = [Buf() for _ in range(4)]
    yb_b = [Buf() for _ in range(4)]
    sq_b = [Buf() for _ in range(4)]
    Fm = [(A.alloc([512], F32), Buf()) for _ in range(3)]
    def build_dg(cc):
        for j in range(31):
            S.op("dve" if j % 2 == 0 else "pool", lambda h: h.tensor_scalar(
                out=Dg[:, cc, j, :], in0=g.identb, scalar1=cwT[:, cc, j:j + 1], scalar2=None, op0=ALU.mult),
                reads=[pb_, g.cb], writes=[Dg_b[cc]])

    def conv_mm(c):
        banks = []
        if c == 0:
            build_dg(0)
        for cc in range(4):
            if c == 0 and cc + 1 < 4:
                build_dg(cc + 1)
            ps, psb = g.ps.get()
            rd = [Dg_b[cc], gT_b[cc][c]] + ([gT_b[cc][c - 1]] if c > 0 else [gT_b[cc][4]])
            for j in range(31):
                mm(g, ps, Dg[:, cc, j, :], gT[:, cc, c * 512 + j:c * 512 + j + 512], j == 0, j == 30, rd, [psb])
            banks.append((ps, psb))
        return banks
    nxt_banks = conv_mm(0)
    for c in range(4):
        banks = nxt_banks
        for cc in range(4):
            ps, psb = banks[cc]
            S.op("act", lambda h: h.activation(out=yb[:, cc, :], in_=ps, func=AF.Identity,
                                               bias=cbv[:, 0, cc:cc + 1], scale=1.0),
                 reads=[psb, pb_], writes=[yb_b[cc]])
            S.op("act", lambda h: h.activation(out=sq[:, cc, :], in_=ps, func=AF.Square,
                                               bias=cbv[:, 0, cc:cc + 1], scale=1.0),
                 reads=[psb, pb_], writes=[sq_b[cc]])
            S.op("act", lambda h: h.activation(out=ybh[:, cc, :], in_=ps, func=AF.Identity,
                                               bias=cbv[:, 0, cc:cc + 1], scale=1.0),
                 reads=[psb, pb_], writes=[ybh_b[cc]])
        if c + 1 < 4:
            nxt_banks = conv_mm(c + 1)
        if c == 0:
            g.dbg_out("yb0", yb, yb_b)
            g.dbg_out("sq0", sq, sq_b)
        pm1, pm1b = g.ps.get()
        pm2, pm2b = g.ps.get()
        for cc in range(4):
            mm(g, pm1, g.ones, ybh[:, cc, :], cc == 0, cc == 3, [g.cb, ybh_b[cc]], [pm1b])
        for cc in range(4):
            mm(g, pm2, g.ones, sq[:, cc, :], cc == 0, cc == 3, [g.cb, sq_b[cc]], [pm2b])
        (mean, meb), (msq, msb), (rs, rsb) = Fm
        S.op("act", lambda h: h.activation(out=mean, in_=pm1, func=AF.Copy, scale=1.0 / 512.0), reads=[pm1b], writes=[meb])
        S.op("act", lambda h: h.activation(out=msq, in_=pm1, func=AF.Square, scale=1.0 / 512.0), reads=[pm1b], writes=[msb])
        S.op("dve", lambda h: h.scalar_tensor_tensor(out=rs, in0=pm2, scalar=1.0 / 512.0, in1=msq, op0=ALU.mult, op1=ALU.subtract),
             reads=[pm2b, msb], writes=[rsb])
        S.op("act", lambda h: h.activation(out=rs, in_=rs, func=AF.Sqrt, bias=g.eps_ap, scale=1.0), reads=[rsb, g.cb], writes=[rsb])
        S.op("dve", lambda h: h.reciprocal(out=rs, in_=rs), reads=[rsb], writes=[rsb])
        if c == 0:
            g.dbg_out("mean0", mean, [meb])
            g.dbg_out("msq0", msq, [msb])
            g.dbg_out("rs0", rs, [rsb])
        for cc in range(4):
            S.op("pool", lambda h, cc=cc: h.tensor_tensor(out=yb[:, cc, :], in0=yb[:, cc, :], in1=mean, op=ALU.subtract),
                 reads=[yb_b[cc], meb], writes=[yb_b[cc]])
            S.op("dve", lambda h, cc=cc: h.tensor_tensor(out=yb[:, cc, :], in0=yb[:, cc, :], in1=rs, op=ALU.mult),
                 reads=[yb_b[cc], rsb], writes=[yb_b[cc]])
            S.op("act", lambda h, cc=cc, c=c: h.activation(out=OT[:, cc, c * 512:(c + 1) * 512], in_=yb[:, cc, :], func=AF.Silu,
                                                           bias=cbv[:, 2, cc:cc + 1], scale=cbv[:, 1, cc:cc + 1]),
                 reads=[yb_b[cc], pb_], writes=[OT_b[cc][c]])
    g.dbg_out("OTc", OT, [b for r in OT_b for b in r])


def sincos(g, x, s_out, c_out, t1, t2, xb, outb=None, t3=None, t4=None, cb2=None):
    S = g.S
    chains = [(0.0, s_out, t1, t2, Buf()), (math.pi / 2.0, c_out, t3 if t3 is not None else t1, t4 if t4 is not None else t2,
                                           Buf() if t3 is not None else None)]
    if t3 is None:
        chains[1] = chains[1][:4] + (chains[0][4],)
    steps = []
    for shift, out, a, b, cb in chains:
        st = [
            ("dve", lambda h, a=a, shift=shift: h.tensor_scalar(out=a, in0=x, scalar1=1.0 / TWO_PI, scalar2=shift / TWO_PI + MAGIC,
                                                                 op0=ALU.mult, op1=ALU.add)),
            ("dve", lambda h, a=a: h.tensor_scalar(out=a, in0=a, scalar1=-MAGIC, scalar2=None, op0=ALU.add)),
            ("dve", lambda h, a=a, b=b: h.scalar_tensor_tensor(out=b, in0=a, scalar=-CW1, in1=x, op0=ALU.mult, op1=ALU.add)),
            ("dve", lambda h, a=a, b=b: h.scalar_tensor_tensor(out=b, in0=a, scalar=-CW2, in1=b, op0=ALU.mult, op1=ALU.add)),
            ("dve", lambda h, b=b, shift=shift: h.tensor_scalar(out=b, in0=b, scalar1=shift, scalar2=3.1415925, op0=ALU.add, op1=ALU.min)),
            ("dve", lambda h, b=b: h.tensor_scalar(out=b, in0=b, scalar1=-3.1415925, scalar2=None, op0=ALU.max)),
            ("act", lambda h, b=b, out=out: h.activation(out=out, in_=b, func=AF.Sin)),
        ]
        steps.append((st, cb))
    order = []
    if t3 is None:
        for st, cb in steps:
            order += [(e, fn, cb) for e, fn in st]
    else:
        for i in range(7):
            for st, cb in steps:
                order.append(st[i] + (cb,))
    c2 = chains[1][4]
    for idx, (e, fn, cb) in enumerate(order):
        last = (e == "act")
        cbs = cb2 if (cb2 is not None and cb is c2 and t3 is not None) else [cb]
        wr = list(cbs) + ([xb] if last else []) + ([outb] if (last and outb is not None) else [])
        S.op(e, fn, reads=[xb] + list(cbs), writes=wr)


def s5_branch(g, l, OT, OT_b):
    S, A, W = g.S, g.A, g.W
    yT, yT_b = OT, OT_b

    def f(shape):
        return A.alloc(shape, F32)
    sb = Buf("s5small")

    def dv(fn):
        S.op("dve", fn, reads=[sb], writes=[sb])

    def ac(fn):
        S.op("act", fn, reads=[sb], writes=[sb])
    WBr = A.alloc([16, 128], BF16)
    WBi = A.alloc([16, 128], BF16)
    WCr = A.alloc([16, 64], BF16)
    WCn = A.alloc([16, 64], BF16)
    wmat_b = Buf()
    mag, s128, c128 = f([16]), f([16]), f([16])
    dsk, glb = f([4]), f([4])
    uT = A.alloc([4, L], BF16)
    uT_b = [[Buf() for _ in range(4)] for _ in range(4)]
    th = f([16])
    m1 = A.mark()
    ar, ai, ls, sn, cs, t1, t2 = (f([16]) for _ in range(7))
    abr, abi, den, nr, cr, ci, t3 = (f([16]) for _ in range(7))
    Bs_r, Bs_i = f([16, 128]), f([16, 128])
    braw_r, braw_i = f([16, 16]), f([16, 16])
    tb1 = f([4, 16])
    S.op("sp", lambda h: h.dma_start(out=ar, in_=W("s5_a_re")[l].rearrange("(ct gl) p -> (gl p) ct", gl=2),
                                     allow_slow_non_contiguous=True), writes=[sb], dma=True)
    S.op("sp", lambda h: h.dma_start(out=ai, in_=W("s5_a_im")[l].rearrange("(ct gl) p -> (gl p) ct", gl=2),
                                     allow_slow_non_contiguous=True), writes=[sb], dma=True)
    for gl in range(2):
        S.op("sp", lambda h, gl=gl: h.dma_start(
            out=ls[64 * gl:64 * gl + 64, :],
            in_=W("s5_log_step")[l].rearrange("(ct gl) -> gl ct", gl=2)[gl:gl + 1, :].to_broadcast([64, 16]),
            allow_slow_non_contiguous=True), writes=[sb], dma=True)
    S.op("sp", lambda h: h.dma_start(out=braw_r, in_=W("s5_b_re")[l].rearrange("(ct gl) p c -> (gl p) ct c", gl=2)),
         writes=[sb], dma=True)
    S.op("sp", lambda h: h.dma_start(out=braw_i, in_=W("s5_b_im")[l].rearrange("(ct gl) p c -> (gl p) ct c", gl=2)),
         writes=[sb], dma=True)
    for i, nm in enumerate(("s5_d", "s5_glu_b")):
        dst = (dsk, glb)[i]
        S.op("sp", lambda h, dst=dst, nm=nm: h.dma_start(out=dst, in_=W(nm)[l].rearrange("(c p) -> p c", p=128),
                                                         allow_slow_non_contiguous=True), writes=[sb], dma=True)
    S.op("pool", lambda h: h.memset(Bs_r, 0.0), writes=[sb])
    S.op("pool", lambda h: h.memset(Bs_i, 0.0), writes=[sb])
    ac(lambda h: h.activation(out=ls, in_=ls, func=AF.Exp))
    dv(lambda h: h.tensor_tensor(out=t1, in0=ar, in1=ls, op=ALU.mult))
    ac(lambda h: h.activation(out=mag, in_=t1, func=AF.Exp))
    dv(lambda h: h.tensor_tensor(out=th, in0=ai, in1=ls, op=ALU.mult))
    sincos(g, th, sn, cs, t1, t2, sb)
    dv(lambda h: h.tensor_tensor(out=abr, in0=mag, in1=cs, op=ALU.mult))
    dv(lambda h: h.tensor_tensor(out=abi, in0=mag, in1=sn, op=ALU.mult))
    dv(lambda h: h.tensor_tensor(out=t1, in0=ar, in1=ar, op=ALU.mult))
    dv(lambda h: h.tensor_tensor(out=den, in0=ai, in1=ai, op=ALU.mult))
    dv(lambda h: h.tensor_tensor(out=den, in0=den, in1=t1, op=ALU.add))
    dv(lambda h: h.reciprocal(out=den, in_=den))
    dv(lambda h: h.tensor_scalar(out=nr, in0=abr, scalar1=-1.0, scalar2=None, op0=ALU.add))
    dv(lambda h: h.tensor_tensor(out=t1, in0=nr, in1=ar, op=ALU.mult))
    dv(lambda h: h.tensor_tensor(out=t2, in0=abi, in1=ai, op=ALU.mult))
    dv(lambda h: h.tensor_tensor(out=t1, in0=t1, in1=t2, op=ALU.add))
    dv(lambda h: h.tensor_tensor(out=cr, in0=t1, in1=den, op=ALU.mult))
    dv(lambda h: h.tensor_tensor(out=t1, in0=abi, in1=ar, op=ALU.mult))
    dv(lambda h: h.tensor_tensor(out=t2, in0=nr, in1=ai, op=ALU.mult))
    dv(lambda h: h.tensor_tensor(out=t1, in0=t1, in1=t2, op=ALU.subtract))
    dv(lambda h: h.tensor_tensor(out=ci, in0=t1, in1=den, op=ALU.mult))
    dv(lambda h: h.tensor_scalar(out=t3, in0=th, scalar1=512.0, scalar2=None, op0=ALU.mult))
    sincos(g, t3, s128, c128, t1, t2, sb)
    for q in range(4):
        for gl in range(2):
            rows = slice(64 * gl, 64 * gl + 64)
            c0 = 32 * q + 16 * gl
            crb = bc_last(cr[rows, q::4], 16)
            cib = bc_last(ci[rows, q::4], 16)
            br_, bi_ = braw_r[rows, q::4, :], braw_i[rows, q::4, :]
            o_r, o_i = Bs_r[rows, q::4, c0:c0 + 16], Bs_i[rows, q::4, c0:c0 + 16]
            tt = tb1[rows]
            dv(lambda h, tt=tt, bi_=bi_, cib=cib: h.tensor_tensor(out=tt, in0=bi_, in1=cib, op=ALU.mult))
            dv(lambda h, o_r=o_r, br_=br_, crb=crb: h.tensor_tensor(out=o_r, in0=br_, in1=crb, op=ALU.mult))
            dv(lambda h, o_r=o_r, tt=tt: h.tensor_tensor(out=o_r, in0=o_r, in1=tt, op=ALU.subtract))
            dv(lambda h, tt=tt, br_=br_, cib=cib: h.tensor_tensor(out=tt, in0=br_, in1=cib, op=ALU.mult))
            dv(lambda h, o_i=o_i, bi_=bi_, crb=crb: h.tensor_tensor(out=o_i, in0=bi_, in1=crb, op=ALU.mult))
            dv(lambda h, o_i=o_i, tt=tt: h.tensor_tensor(out=o_i, in0=o_i, in1=tt, op=ALU.add))
    for ct in range(16):
        q = ct % 4
        for i, (src, dst) in enumerate(((Bs_r, WBr), (Bs_i, WBi))):
            ps, psb = g.ps.get()
            S.op("pe", lambda h, ps=ps, src=src, ct=ct: h.transpose(out=ps[:, 0:128], in_=src[:, ct, :], identity=g.ident),
                 reads=[sb, g.cb], writes=[psb])
            er = slice(64, 128) if q == 3 else slice(32 * q, 32 * q + 32)
            evac(g, i, dst[er, ct, :], ps[er, 0:128], [psb], [wmat_b])
    A.reset(m1)
    S.barrier()
    Cs_r, Cs_i = f([16, 128]), f([16, 128])
    S.op("pool", lambda h: h.memset(Cs_r[0:64], 0.0), writes=[sb])
    S.op("pool", lambda h: h.memset(Cs_i[0:64], 0.0), writes=[sb])
    for gl in range(2):
        for par in range(2):
            for (dst, nm) in ((Cs_r, "s5_c_re"), (Cs_i, "s5_c_im")):
                p0 = 32 * par + 16 * gl
                S.op("sp", lambda h, gl=gl, par=par, dst=dst, nm=nm, p0=p0: h.dma_start(
                    out=dst[p0:p0 + 16, par::2, 64 * gl:64 * gl + 64],
                    in_=W(nm)[l].rearrange("(ct gl) c p -> gl c ct p", gl=2)[gl][:, par::2, :]), writes=[sb], dma=True)
    for ct in range(16):
        for i, (src, dst, scl) in enumerate(((Cs_r, WCr, None), (Cs_i, WCn, -1.0))):
            ps, psb = g.ps.get()
            S.op("pe", lambda h, ps=ps, src=src, ct=ct: h.transpose(out=ps[:, 0:64], in_=src[0:64, ct, :],
                                                                    identity=g.ident[0:64, 0:64]),
                 reads=[sb, g.cb], writes=[psb])
            evac(g, i, dst[:, ct, :], ps[:, 0:64], [psb], [wmat_b], scale=scl)
    A.reset(m1)
    S.barrier()
    xT = A.alloc([8, 512], BF16)
    xT_b = [Buf() for _ in range(8)]
    wS = A.alloc([8, 512], BF16)
    wS_b = Buf()
    load_w(g, wS, wS_b, W("mix_w_in")[l, :, 2304:2816].rearrange("(k p) n -> p k n", p=128))
    for c in range(4):
        gen_xT(g, xT, xT_b, resid_tiles(g, c * 4), act_only=True)
        for cc in range(4):
            ps, psb = g.ps.get()
            for k in range(8):
                mm(g, ps, wS[:, k, cc * 128:(cc + 1) * 128], xT[:, k, :], k == 0, k == 7, [wS_b, xT_b[k]], [psb])
            evac(g, cc, uT[:, cc, c * 512:(c + 1) * 512], ps, [psb], [uT_b[cc][c]])
    A.reset(m1)
    S.barrier()
    iota = f([512])
    S.op("sp", lambda h: h.dma_start(out=iota, in_=W("c_iota")), writes=[sb], dma=True)
    TB = RR([((f([512]), f([512])), Buf()) for _ in range(2)])
    angj, sj1, sj2 = f([512]), f([512]), f([512])
    sjb = Buf()
    brp, bip, ta, tb = f([512]), f([512]), f([512]), f([512])
    dB = Buf()
    brp_b, bip_b, ta_b, tb_b = Buf(), Buf(), Buf(), Buf()
    W2 = RR([((f([512]), f([512])), (Buf(), Buf())) for _ in range(2)])
    pa, pb2, pc, pd = f([512]), f([512]), f([512]), f([512])
    pa_b, pb_b, pc_b, pd_b = Buf(), Buf(), Buf(), Buf()
    INI = RR([(f([4]), Buf()) for _ in range(2)])
    XR = RR([(A.alloc([512], BF16), Buf()) for _ in range(2)])
    XI = RR([(A.alloc([512], BF16), Buf()) for _ in range(2)])
    vv, v2, vb = brp, bip, dB
    yacc = g.banks[0:4]
    rot = RR(g.banks[4:8])
    bu_q = []

    def bu_mm(cc, q, c):
        ct = 4 * cc + q
        rows = slice(64, 128) if q == 3 else slice(32 * q, 32 * q + 32)
        pbr, pbrb = rot.get()
        pbi, pbib = rot.get()
        mm(g, pbr, WBr[rows, ct, :], uT[rows, cc, c * 512:(c + 1) * 512], True, True, [wmat_b, uT_b[cc][c]], [pbrb])
        mm(g, pbi, WBi[rows, ct, :], uT[rows, cc, c * 512:(c + 1) * 512], True, True, [wmat_b, uT_b[cc][c]], [pbib])
        return pbr, pbrb, pbi, pbib
    for cc in range(4):
        for q in range(4):
            ct = 4 * cc + q
            rows = slice(64, 128) if q == 3 else slice(32 * q, 32 * q + 32)
            orow = slice(64 * (q // 2), 64 * (q // 2) + 64)
            (sinT, cosT), tbb = TB.get()
            S.op("dve", lambda h: h.tensor_scalar(out=angj, in0=iota, scalar1=th[:, ct:ct + 1], scalar2=None, op0=ALU.mult),
                 reads=[sb, sjb], writes=[sjb])
            sincos(g, angj, sinT, cosT, sj1, sj2, sjb, outb=tbb, t3=ta, t4=tb, cb2=[ta_b, tb_b])
            rho = mag[:, ct:ct + 1].to_broadcast([128, 512])
            cc_, ss_ = c128[:, ct:ct + 1], s128[:, ct:ct + 1]
            prev = None
            for c in range(4):
                py, pyb = yacc[c]
                if not bu_q:
                    bu_q.append(bu_mm(cc, q, c))
                pbr, pbrb, pbi, pbib = bu_q.pop(0)
                nx = (cc, q, c + 1) if c + 1 < 4 else ((cc, q + 1, 0) if q + 1 < 4 else ((cc + 1, 0, 0) if cc + 1 < 4 else None))
                if nx is not None:
                    bu_q.append(bu_mm(*nx))
                S.op("dve", lambda h: h.tensor_tensor(out=ta, in0=pbi, in1=sinT, op=ALU.mult), reads=[pbib, tbb], writes=[ta_b])
                S.op("dve", lambda h: h.tensor_tensor(out=tb, in0=pbr, in1=sinT, op=ALU.mult), reads=[pbrb, tbb], writes=[tb_b])
                S.op("dve", lambda h: h.tensor_tensor(out=brp, in0=pbr, in1=cosT, op=ALU.mult), reads=[pbrb, tbb], writes=[brp_b])
                S.op("dve", lambda h: h.tensor_tensor(out=bip, in0=pbi, in1=cosT, op=ALU.mult), reads=[pbib, tbb], writes=[bip_b])
                S.op("dve", lambda h: h.tensor_tensor(out=brp, in0=brp, in1=ta, op=ALU.add), reads=[brp_b, ta_b], writes=[brp_b])
                S.op("dve", lambda h: h.tensor_tensor(out=bip, in0=bip, in1=tb, op=ALU.subtract), reads=[bip_b, tb_b], writes=[bip_b])
                (wr, wi), (wr_b, wi_b) = W2.get()
                if prev is None:
                    i_re, i_im, ird = 0.0, 0.0, []
                else:
                    (pwr, pwi), (pwr_b, pwi_b) = prev
                    ini, inib = INI.get()
                    lr, li = pwr[:, 511:512], pwi[:, 511:512]
                    rdl = [pwr_b, pwi_b, sb]
                    S.op("dve", lambda h: h.tensor_scalar(out=ini[:, 0:1], in0=li, scalar1=ss_, scalar2=None, op0=ALU.mult),
                         reads=rdl, writes=[inib])
                    S.op("dve", lambda h: h.tensor_scalar(out=ini[:, 2:3], in0=li, scalar1=cc_, scalar2=None, op0=ALU.mult),
                         reads=rdl, writes=[inib])
                    S.op("dve", lambda h: h.scalar_tensor_tensor(out=ini[:, 1:2], in0=lr, scalar=cc_, in1=ini[:, 0:1],
                                                                 op0=ALU.mult, op1=ALU.subtract), reads=rdl + [inib], writes=[inib])
                    S.op("dve", lambda h: h.scalar_tensor_tensor(out=ini[:, 3:4], in0=lr, scalar=ss_, in1=ini[:, 2:3],
                                                                 op0=ALU.mult, op1=ALU.add), reads=rdl + [inib], writes=[inib])
                    i_re, i_im, ird = ini[:, 1:2], ini[:, 3:4], [inib]
                S.op("dve", lambda h: h.tensor_tensor_scan(out=wr, data0=rho, data1=brp, initial=i_re, op0=ALU.mult, op1=ALU.add),
                     reads=[brp_b, sb] + ird, writes=[wr_b])
                S.op("dve", lambda h: h.tensor_tensor_scan(out=wi, data0=rho, data1=bip, initial=i_im, op0=ALU.mult, op1=ALU.add),
                     reads=[bip_b, sb] + ird, writes=[wi_b])
                prev = ((wr, wi), (wr_b, wi_b))
                xr, xrb = XR.get()
                xi, xib = XI.get()
                S.op("pool", lambda h: h.tensor_tensor(out=pa, in0=wr, in1=cosT, op=ALU.mult), reads=[wr_b, tbb, sjb], writes=[pa_b])
                S.op("pool", lambda h: h.tensor_tensor(out=pb2, in0=wi, in1=sinT, op=ALU.mult), reads=[wi_b, tbb, sjb], writes=[pb_b])
                S.op("pool", lambda h: h.tensor_tensor(out=pc, in0=wi, in1=cosT, op=ALU.mult), reads=[wi_b, tbb, sjb], writes=[pc_b])
                S.op("pool", lambda h: h.tensor_tensor(out=pd, in0=wr, in1=sinT, op=ALU.mult), reads=[wr_b, tbb, sjb], writes=[pd_b])
                S.op("pool", lambda h: h.tensor_tensor(out=xr, in0=pa, in1=pb2, op=ALU.subtract), reads=[pa_b, pb_b], writes=[xrb])
                S.op("pool", lambda h: h.tensor_tensor(out=xi, in0=pc, in1=pd, op=ALU.add), reads=[pc_b, pd_b], writes=[xib])
                mm(g, py[orow, :], WCr[:, ct, :], xr, q % 2 == 0, False, [wmat_b, xrb], [pyb])
                mm(g, py[orow, :], WCn[:, ct, :], xi, False, q % 2 == 1, [wmat_b, xib], [pyb])
        for c in range(4):
            py, pyb = yacc[c]
            S.op("dve", lambda h: h.scalar_tensor_tensor(
                out=vv, in0=uT[:, cc, c * 512:(c + 1) * 512], scalar=dsk[:, cc:cc + 1], in1=py, op0=ALU.mult, op1=ALU.add),
                reads=[pyb, uT_b[cc][c], sb, vb], writes=[vb])
            S.op("act", lambda h: h.activation(out=v2, in_=vv, func=AF.Square), reads=[vb], writes=[vb])
            S.op("dve", lambda h: h.tensor_scalar(out=v2, in0=v2, scalar1=0.044715, scalar2=1.0, op0=ALU.mult, op1=ALU.add),
                 reads=[vb], writes=[vb])
            S.op("dve", lambda h: h.tensor_tensor(out=v2, in0=v2, in1=vv, op=ALU.mult), reads=[vb], writes=[vb])
            S.op("act", lambda h: h.activation(out=v2, in_=v2, func=AF.Sigmoid, scale=1.5957691216057308), reads=[vb], writes=[vb])
            S.op("dve", lambda h: h.tensor_tensor(out=yT[:, cc, c * 512:(c + 1) * 512], in0=vv, in1=v2, op=ALU.mult),
                 reads=[vb], writes=[yT_b[cc][c]])
    A.reset(m1)
    S.barrier()
    gw = A.alloc([4, 512], BF16)
    gw_b = Buf()
    load_w(g, gw, gw_b, W("s5_glu_w")[l].rearrange("(k p) n -> p k n", p=128))
    vv = f([512])
    vb = Buf()
    for c in range(4):
        zs = []
        for oc in range(4):
            ps, psb = g.banks[oc]
            for kc in range(4):
                mm(g, ps, gw[:, kc, oc * 128:(oc + 1) * 128], yT[:, kc, c * 512:(c + 1) * 512], kc == 0, kc == 3,
                   [gw_b, yT_b[kc][c]], [psb])
            zs.append((ps, psb))
        for oc in range(4):
            ps, psb = zs[oc]
            S.op("act", lambda h, ps=ps, oc=oc: h.activation(out=vv, in_=ps, func=AF.Sigmoid, bias=glb[:, oc:oc + 1], scale=1.0),
                 reads=[psb, sb, vb], writes=[vb])
            S.op("dve", lambda h, oc=oc, c=c: h.tensor_tensor(out=yT[:, oc, c * 512:(c + 1) * 512],
                                                              in0=yT[:, oc, c * 512:(c + 1) * 512], in1=vv, op=ALU.mult),
                 reads=[vb, yT_b[oc][c]], writes=[yT_b[oc][c]])


PARTS = ("ffn1", "mix", "cross", "ffn2")
_cache = {}


def run(inputs, layers, parts, xin, branches="abcd"):
    key = (tuple(layers), tuple(parts), branches)
    if key not in _cache:
        _cache[key] = build(layers, parts, branches)
    nc, g = _cache[key]
    in_maps = []
    shared = {}
    for name in g.dr:
        if name in ("x", "mem"):
            continue
        if name.startswith("c_"):
            shared[name] = consts()[name][0]
        else:
            shared[name] = np.ascontiguousarray(np.asarray(inputs[name], dtype=np.float32))
    for c in range(NCORES):
        m = dict(shared)
        m["x"] = np.ascontiguousarray(xin[c * SEQ_PER_CORE:(c + 1) * SEQ_PER_CORE])
        if "mem" in g.dr:
            m["mem"] = np.ascontiguousarray(np.asarray(inputs["mem"], dtype=np.float32)[c * SEQ_PER_CORE:(c + 1) * SEQ_PER_CORE])
        in_maps.append(m)
    res = run_bass_kernel_spmd(nc, in_maps, core_ids=list(range(NCORES)))
    if DEBUG:
        LAST.update({k: v for k, v in res.results[0].items()})
    return np.concatenate([r["out"] for r in res.results], axis=0)


FUSED = True


def kernel(**inputs):
    x = np.asarray(inputs["x"], dtype=np.float32)
    if FUSED:
        return run(inputs, list(range(DEPTH)), PARTS, x)
    for l in range(DEPTH):
        x = run(inputs, [l], PARTS, x)
    return x
```
